# Optimizing a Trainium2 kernel written in Bass

```python
import math
import jax, jax.numpy as jnp
from jax import lax
import numpy as np

D_MODEL = 2048
BATCH = 2
SEQ = 4096
DEPTH = 1
DEC_BATCH = 128
DEC_SEQ = 8
PAST_LEN = 8192
PAGE_SIZE = 128

SSM_WIDTH = D_MODEL // 4
SSM_GROUP = 16
SSM_GROUPS = SSM_WIDTH // SSM_GROUP
SSM_STATE = 64
MLA_HEADS = 8
MLA_NOPE = 128
MLA_ROPE = 64
MLA_QK = MLA_NOPE + MLA_ROPE
MLA_V = 128
MLA_WIDTH = MLA_HEADS * MLA_V
Q_LORA = D_MODEL // 4
KV_LORA = D_MODEL // 8
ROPE_THETA = 10000.0
MEM_TOKENS = 256
MEM_HEADS = 4
MEM_HEAD_DIM = 128
MEM_WIDTH = MEM_HEADS * MEM_HEAD_DIM
MIX_WIDTH = SSM_WIDTH + MLA_WIDTH + MEM_WIDTH
Q_BLOCK = 128
EPS = 1e-6
NEG_INF = -1e30
F32 = jnp.float32

IN_SIZES = (SSM_WIDTH, SSM_WIDTH, Q_LORA, KV_LORA, MLA_ROPE, MLA_WIDTH, MEM_WIDTH, MEM_WIDTH)
IN_WIDTH = sum(IN_SIZES)
IN_SPLITS = tuple(int(s) for s in np.cumsum(IN_SIZES)[:-1])

kernel_name = 'hymba_s5_mla_memxattn_step'


def rmsnorm(x, g):
    xf = x.astype(F32)
    xf = xf * lax.rsqrt(jnp.mean(xf * xf, axis=-1, keepdims=True) + EPS)
    return xf.astype(x.dtype) * g.astype(x.dtype)


def rope(x, pos):
    half = x.shape[-1] // 2
    inv_freq = ROPE_THETA ** (-jnp.arange(half, dtype=F32) / half)
    ang = pos.astype(F32)[:, None] * inv_freq[None, :]
    shape = (pos.shape[0],) + (1,) * (x.ndim - 3) + (half,)
    cos = jnp.cos(ang).reshape(shape).astype(x.dtype)
    sin = jnp.sin(ang).reshape(shape).astype(x.dtype)
    x1, x2 = x[..., :half], x[..., half:]
    return jnp.concatenate([x1 * cos - x2 * sin, x1 * sin + x2 * cos], axis=-1)


def attend(q, k, v, q_pos, k_pos):
    s = jnp.einsum('bqhd,bkhd->bhqk', q, k, preferred_element_type=F32) * (q.shape[-1] ** -0.5)
    s = jnp.where(k_pos[None, :] <= q_pos[:, None], s, NEG_INF)
    pr = jax.nn.softmax(s, axis=-1).astype(v.dtype)
    return jnp.einsum('bhqk,bkhd->bqhd', pr, v)


def in_projection(x, p):
    z = rmsnorm(x, p['norm_g']) @ p['w_in']
    return jnp.split(z, IN_SPLITS, axis=-1)


def ssm_mixer(u, h0_re, h0_im, p):
    B, T, _ = u.shape
    dt = jnp.exp(p['ssm_log_dt'].astype(F32))[:, None]
    ar, ai = p['ssm_a_re'].astype(F32), p['ssm_a_im'].astype(F32)
    mag = jnp.exp(dt * ar)
    abar_re, abar_im = mag * jnp.cos(dt * ai), mag * jnp.sin(dt * ai)
    den = ar * ar + ai * ai
    nr, ni = abar_re - 1.0, abar_im
    f_re = ((nr * ar + ni * ai) / den)[..., None]
    f_im = ((ni * ar - nr * ai) / den)[..., None]
    b_re, b_im = p['ssm_b_re'].astype(F32), p['ssm_b_im'].astype(F32)
    bbar_re = f_re * b_re - f_im * b_im
    bbar_im = f_re * b_im + f_im * b_re
    uf = u.astype(F32)
    ug = uf.reshape(B, T, SSM_GROUPS, SSM_GROUP)
    bu_re = jnp.einsum('btgp,gnp->btgn', ug, bbar_re)
    bu_im = jnp.einsum('btgp,gnp->btgn', ug, bbar_im)
    a_re = jnp.broadcast_to(abar_re, bu_re.shape)
    a_im = jnp.broadcast_to(abar_im, bu_re.shape)

    def combine(l, r):
        a1r, a1i, b1r, b1i = l
        a2r, a2i, b2r, b2i = r
        return (a2r * a1r - a2i * a1i, a2r * a1i + a2i * a1r,
                a2r * b1r - a2i * b1i + b2r, a2r * b1i + a2i * b1r + b2i)

    acr, aci, xr, xi = lax.associative_scan(combine, (a_re, a_im, bu_re, bu_im), axis=1)
    h0r = h0_re.astype(F32)[:, None]
    h0i = h0_im.astype(F32)[:, None]
    xr = xr + acr * h0r - aci * h0i
    xi = xi + acr * h0i + aci * h0r
    y = (jnp.einsum('btgn,gpn->btgp', xr, p['ssm_c_re'].astype(F32))
         - jnp.einsum('btgn,gpn->btgp', xi, p['ssm_c_im'].astype(F32)))
    y = y.reshape(B, T, SSM_WIDTH) + p['ssm_d'].astype(F32) * uf
    g = jax.nn.gelu(y)
    y = g * jax.nn.sigmoid(g @ p['ssm_glu_w'].astype(F32) + p['ssm_glu_b'].astype(F32))
    return y.astype(u.dtype), xr[:, -1].astype(h0_re.dtype), xi[:, -1].astype(h0_im.dtype)


def mla_project(cq, ckv, kr, pos, p):
    B, T, _ = cq.shape
    q = (rmsnorm(cq, p['mla_q_norm_g']) @ p['mla_w_uq']).reshape(B, T, MLA_HEADS, MLA_QK)
    q = jnp.concatenate([q[..., :MLA_NOPE], rope(q[..., MLA_NOPE:], pos)], axis=-1)
    q = rmsnorm(q, p['mla_qk_norm_q'])
    return q, rmsnorm(ckv, p['mla_kv_norm_g']), rope(kr, pos)


def mla_keys(ckv, kr, p):
    B, T, _ = ckv.shape
    kv = (ckv @ p['mla_w_ukv']).reshape(B, T, MLA_HEADS, MLA_NOPE + MLA_V)
    k = jnp.concatenate([kv[..., :MLA_NOPE],
                         jnp.broadcast_to(kr[:, :, None, :], (B, T, MLA_HEADS, MLA_ROPE))], axis=-1)
    return rmsnorm(k, p['mla_qk_norm_k']), kv[..., MLA_NOPE:]


def mem_kv(mem, p):
    B, M, _ = mem.shape
    m = rmsnorm(mem, p['mem_norm_g'])
    k = rmsnorm((m @ p['mem_w_k']).reshape(B, M, MEM_HEADS, MEM_HEAD_DIM), p['mem_qk_norm_k'])
    v = (m @ p['mem_w_v']).reshape(B, M, MEM_HEADS, MEM_HEAD_DIM)
    return k, v


def mem_attend(qm, mk, mv, p):
    B, T, _ = qm.shape
    q = rmsnorm(qm.reshape(B, T, MEM_HEADS, MEM_HEAD_DIM), p['mem_qk_norm_q'])
    s = jnp.einsum('bqhd,bkhd->bhqk', q, mk, preferred_element_type=F32) * (MEM_HEAD_DIM ** -0.5)
    pr = jax.nn.softmax(s, axis=-1).astype(mv.dtype)
    return jnp.einsum('bhqk,bkhd->bqhd', pr, mv).reshape(B, T, MEM_WIDTH)


def merge(x, y_ssm, y_mla, y_mem, g_ssm, g_mla, g_mem, p):
    o = jnp.concatenate([rmsnorm(y_ssm, p['out_norm_ssm']) * jax.nn.silu(g_ssm),
                         rmsnorm(y_mla, p['out_norm_mla']) * jax.nn.silu(g_mla),
                         rmsnorm(y_mem, p['out_norm_mem']) * jax.nn.silu(g_mem)], axis=-1)
    return x + o @ p['w_out']


def setup_inputs(seed: int = 0) -> dict:
    key = jax.random.key(seed)
    ks = iter(jax.random.split(key, 64))

    def nrm(shape, scale):
        return jax.random.normal(next(ks), shape, F32) * scale

    def gain(n):
        return 1.0 + nrm((n,), 0.01)

    n_pages = PAST_LEN // PAGE_SIZE
    n_used = DEC_BATCH * n_pages
    n_pool = (n_used * 5) // 4
    inp = {}
    inp['x_prompt'] = nrm((BATCH, SEQ, D_MODEL), 1.0)
    inp['x_sample'] = nrm((DEC_BATCH, DEC_SEQ, D_MODEL), 1.0)
    inp['mem_prompt'] = nrm((BATCH, MEM_TOKENS, D_MODEL), 1.0)
    inp['cache_ckv'] = nrm((n_pool, PAGE_SIZE, KV_LORA), 1.0)
    inp['cache_krope'] = nrm((n_pool, PAGE_SIZE, MLA_ROPE), 1.0)
    inp['cache_mem_k'] = nrm((DEC_BATCH, MEM_TOKENS, MEM_HEADS, MEM_HEAD_DIM), 1.0)
    inp['cache_mem_v'] = nrm((DEC_BATCH, MEM_TOKENS, MEM_HEADS, MEM_HEAD_DIM), 1.0)
    inp['state_ssm_re'] = nrm((DEC_BATCH, SSM_GROUPS, SSM_STATE), 1.0)
    inp['state_ssm_im'] = nrm((DEC_BATCH, SSM_GROUPS, SSM_STATE), 1.0)
    inp['page_table'] = jax.random.permutation(next(ks), n_pool)[:n_used].reshape(DEC_BATCH, n_pages).astype(jnp.int32)
    inp['norm_g'] = gain(D_MODEL)
    inp['w_in'] = nrm((D_MODEL, IN_WIDTH), D_MODEL ** -0.5)
    inp['ssm_a_re'] = -0.5 + nrm((SSM_GROUPS, SSM_STATE), 0.01)
    inp['ssm_a_im'] = math.pi * jnp.arange(SSM_STATE, dtype=F32)[None, :] + nrm((SSM_GROUPS, SSM_STATE), 0.01)
    inp['ssm_log_dt'] = jax.random.uniform(next(ks), (SSM_GROUPS,), F32, math.log(1e-3), math.log(1e-1))
    inp['ssm_b_re'] = nrm((SSM_GROUPS, SSM_STATE, SSM_GROUP), (2.0 * SSM_GROUP) ** -0.5)
    inp['ssm_b_im'] = nrm((SSM_GROUPS, SSM_STATE, SSM_GROUP), (2.0 * SSM_GROUP) ** -0.5)
    inp['ssm_c_re'] = nrm((SSM_GROUPS, SSM_GROUP, SSM_STATE), (2.0 * SSM_STATE) ** -0.5)
    inp['ssm_c_im'] = nrm((SSM_GROUPS, SSM_GROUP, SSM_STATE), (2.0 * SSM_STATE) ** -0.5)
    inp['ssm_d'] = 1.0 + nrm((SSM_WIDTH,), 0.1)
    inp['ssm_glu_w'] = nrm((SSM_WIDTH, SSM_WIDTH), SSM_WIDTH ** -0.5)
    inp['ssm_glu_b'] = nrm((SSM_WIDTH,), 0.01)
    inp['mla_q_norm_g'] = gain(Q_LORA)
    inp['mla_w_uq'] = nrm((Q_LORA, MLA_HEADS * MLA_QK), Q_LORA ** -0.5)
    inp['mla_kv_norm_g'] = gain(KV_LORA)
    inp['mla_w_ukv'] = nrm((KV_LORA, MLA_HEADS * (MLA_NOPE + MLA_V)), KV_LORA ** -0.5)
    inp['mla_qk_norm_q'] = gain(MLA_QK)
    inp['mla_qk_norm_k'] = gain(MLA_QK)
    inp['mem_norm_g'] = gain(D_MODEL)
    inp['mem_w_k'] = nrm((D_MODEL, MEM_WIDTH), D_MODEL ** -0.5)
    inp['mem_w_v'] = nrm((D_MODEL, MEM_WIDTH), D_MODEL ** -0.5)
    inp['mem_qk_norm_q'] = gain(MEM_HEAD_DIM)
    inp['mem_qk_norm_k'] = gain(MEM_HEAD_DIM)
    inp['out_norm_ssm'] = gain(SSM_WIDTH)
    inp['out_norm_mla'] = gain(MLA_WIDTH)
    inp['out_norm_mem'] = gain(MEM_WIDTH)
    inp['w_out'] = nrm((MIX_WIDTH, D_MODEL), MIX_WIDTH ** -0.5)
    return inp


def reference(x_prompt, x_sample, mem_prompt, cache_ckv, cache_krope, cache_mem_k, cache_mem_v,
              state_ssm_re, state_ssm_im, page_table,
              norm_g, w_in, ssm_a_re, ssm_a_im, ssm_log_dt, ssm_b_re, ssm_b_im, ssm_c_re, ssm_c_im,
              ssm_d, ssm_glu_w, ssm_glu_b,
              mla_q_norm_g, mla_w_uq, mla_kv_norm_g, mla_w_ukv, mla_qk_norm_q, mla_qk_norm_k,
              mem_norm_g, mem_w_k, mem_w_v, mem_qk_norm_q, mem_qk_norm_k,
              out_norm_ssm, out_norm_mla, out_norm_mem, w_out):
    p = dict(norm_g=norm_g, w_in=w_in, ssm_a_re=ssm_a_re, ssm_a_im=ssm_a_im, ssm_log_dt=ssm_log_dt,
             ssm_b_re=ssm_b_re, ssm_b_im=ssm_b_im, ssm_c_re=ssm_c_re, ssm_c_im=ssm_c_im,
             ssm_d=ssm_d, ssm_glu_w=ssm_glu_w, ssm_glu_b=ssm_glu_b,
             mla_q_norm_g=mla_q_norm_g, mla_w_uq=mla_w_uq, mla_kv_norm_g=mla_kv_norm_g,
             mla_w_ukv=mla_w_ukv, mla_qk_norm_q=mla_qk_norm_q, mla_qk_norm_k=mla_qk_norm_k,
             mem_norm_g=mem_norm_g, mem_w_k=mem_w_k, mem_w_v=mem_w_v,
             mem_qk_norm_q=mem_qk_norm_q, mem_qk_norm_k=mem_qk_norm_k,
             out_norm_ssm=out_norm_ssm, out_norm_mla=out_norm_mla, out_norm_mem=out_norm_mem,
             w_out=w_out)

    B = x_prompt.shape[0]
    pos_p = jnp.arange(SEQ, dtype=jnp.int32)
    u, g_ssm, cq, ckv, kr, g_mla, qm, g_mem = in_projection(x_prompt, p)
    h0 = jnp.zeros((B, SSM_GROUPS, SSM_STATE), x_prompt.dtype)
    y_ssm_p, ssm_re_p, ssm_im_p = ssm_mixer(u, h0, h0, p)
    q_p, ckv_p, kr_p = mla_project(cq, ckv, kr, pos_p, p)
    k_p, v_p = mla_keys(ckv_p, kr_p, p)
    nb = SEQ // Q_BLOCK
    qb = q_p.reshape(B, nb, Q_BLOCK, MLA_HEADS, MLA_QK).transpose(1, 0, 2, 3, 4)
    ob = lax.map(lambda a: attend(a[0], k_p, v_p, a[1], pos_p), (qb, pos_p.reshape(nb, Q_BLOCK)))
    y_mla_p = ob.transpose(1, 0, 2, 3, 4).reshape(B, SEQ, MLA_WIDTH)
    memk_p, memv_p = mem_kv(mem_prompt, p)
    y_mem_p = mem_attend(qm, memk_p, memv_p, p)
    y_prompt = merge(x_prompt, y_ssm_p, y_mla_p, y_mem_p, g_ssm, g_mla, g_mem, p)

    Bs = x_sample.shape[0]
    pos_s = PAST_LEN + jnp.arange(DEC_SEQ, dtype=jnp.int32)
    k_pos_s = jnp.arange(PAST_LEN + DEC_SEQ, dtype=jnp.int32)
    u, g_ssm, cq, ckv, kr, g_mla, qm, g_mem = in_projection(x_sample, p)
    y_ssm_s, ssm_re_s, ssm_im_s = ssm_mixer(u, state_ssm_re, state_ssm_im, p)
    q_s, ckv_s, kr_s = mla_project(cq, ckv, kr, pos_s, p)

    def one_seq(a):
        pages, q1, ckv1, kr1 = a
        ckv_all = jnp.concatenate([cache_ckv[pages].reshape(-1, KV_LORA), ckv1.astype(cache_ckv.dtype)], axis=0)
        kr_all = jnp.concatenate([cache_krope[pages].reshape(-1, MLA_ROPE), kr1.astype(cache_krope.dtype)], axis=0)
        k1, v1 = mla_keys(ckv_all[None], kr_all[None], p)
        return attend(q1[None], k1, v1, pos_s, k_pos_s)[0]

    y_mla_s = lax.map(one_seq, (page_table, q_s, ckv_s, kr_s)).reshape(Bs, DEC_SEQ, MLA_WIDTH)
    y_mem_s = mem_attend(qm, cache_mem_k, cache_mem_v, p)
    y_sample = merge(x_sample, y_ssm_s, y_mla_s, y_mem_s, g_ssm, g_mla, g_mem, p)

    return (y_prompt, y_sample, ckv_p, kr_p, ckv_s, kr_s, memk_p, memv_p,
            ssm_re_p, ssm_im_p, ssm_re_s, ssm_im_s)
```

```python
import contextlib
import numpy as np
import concourse.bass as bass
import concourse.mybir as mybir
from concourse.bass_utils import run_bass_kernel_spmd

F32 = mybir.dt.float32
BF16 = mybir.dt.bfloat16
I32 = mybir.dt.int32
AF = mybir.ActivationFunctionType
ALU = mybir.AluOpType
AX = mybir.AxisListType

D = 2048
IN_W = 3904
EPS = 1e-6
NEG = -30000.0
NSEQ = 16
LS = 64
GT = 2
GC = GT * 128
C_U, C_GS, C_CQ, C_CKV, C_KR, C_GM, C_QM, C_GME = 0, 512, 1024, 1536, 1792, 1856, 2880, 3392
GELU_K = 2.0 * 0.7978845608028654


class Tracker:
    def __init__(self, nc, es):
        self.nc = nc
        self.es = es
        self.eng = {'pe': nc.tensor, 'act': nc.scalar, 'dve': nc.vector, 'pool': nc.gpsimd, 'sp': nc.sync}
        self.sem = {}
        self.cnt = {}
        for n in ['pe', 'act', 'dve', 'pool']:
            self.sem[n] = es.enter_context(nc.semaphore('c_' + n))
            self.cnt[n] = 0
        self.seen = {n: {} for n in self.eng}
        self.lastw = {}
        self.readers = {}
        self.nins = 0

    def _waits(self, e, reads, writes):
        need = {}

        def add(t):
            if t is None:
                return
            sn, v = t
            if need.get(sn, 0) < v:
                need[sn] = v
        for k in reads:
            add(self.lastw.get(k))
            if k.startswith('ps'):
                for t in self.readers.get(k, ()):
                    if t[0] != e:
                        add(t)
        for k in writes:
            add(self.lastw.get(k))
            for t in self.readers.get(k, ()):
                add(t)
        for sn, v in need.items():
            if sn == 'pe' and e == 'pe':
                continue
            if self.seen[e].get(sn, 0) < v:
                self.eng[e].wait_ge(self.sem[sn], v)
                self.seen[e][sn] = v
                self.nins += 1

    def _commit(self, tok, reads, writes):
        for k in reads:
            self.readers.setdefault(k, []).append(tok)
        for k in writes:
            self.lastw[k] = tok
            self.readers[k] = []

    def op(self, e, fn, R=(), W=()):
        self._waits(e, R, W)
        ins = fn(self.eng[e])
        self.cnt[e] += 1
        ins.then_inc(self.sem[e], 1)
        self.nins += 1
        self._commit((e, self.cnt[e]), R, W)

    def dma(self, q, stream, fn, R=(), W=()):
        if stream not in self.sem:
            self.sem[stream] = self.es.enter_context(self.nc.semaphore('d_' + stream))
            self.cnt[stream] = 0
        self._waits(q, R, W)
        if self.cnt[stream] and self.seen[q].get(stream, 0) < self.cnt[stream]:
            self.eng[q].wait_ge(self.sem[stream], self.cnt[stream])
            self.seen[q][stream] = self.cnt[stream]
            self.nins += 1
        ins = fn(self.eng[q])
        self.cnt[stream] += 16
        ins.then_inc(self.sem[stream], 16)
        self.nins += 1
        self._commit((stream, self.cnt[stream]), R, W)

    def barrier(self):
        for e in self.eng:
            for sn, v in self.cnt.items():
                if v == 0:
                    continue
                if self.seen[e].get(sn, 0) < v:
                    self.eng[e].wait_ge(self.sem[sn], v)
                    self.seen[e][sn] = v
                    self.nins += 1
        self.lastw = {}
        self.readers = {}

    def finish(self, q='sp'):
        for sn, v in self.cnt.items():
            if v == 0:
                continue
            self.eng[q].wait_ge(self.sem[sn], v)


class _Stop(Exception):
    pass


def build(SEQ, NPG, NPOOL, stop=None):
    CH = SEQ // 4
    NOWN = CH // 128
    NPREV = 3 * NOWN
    NKT = NPREV + NOWN
    NT = NKT + 1
    nc = bass.Bass("TRN2", target_bir_lowering=False)

    def din(name, shape, dt=F32):
        return nc.dram_tensor(name, list(shape), dt, kind="ExternalInput").ap()

    def dout(name, shape):
        return nc.dram_tensor(name, list(shape), F32, kind="ExternalOutput").ap()

    xall = din("xall", [NT * 128, D])
    rope = din("rope", [NT * 128, 128])
    kbias_d = din("kbias", [128, NKT])
    mem_d = din("mem", [256, D])
    cckv = din("cckv", [NPOOL * 128, 256])
    ckr = din("ckr", [NPOOL * 128, 64])
    cmk = din("cmk", [NSEQ * 256, 512])
    cmv = din("cmv", [NSEQ * 256, 512])
    st_re = din("st_re", [NSEQ, 2048])
    st_im = din("st_im", [NSEQ, 2048])
    ptab = din("ptab", [NSEQ, NPG], I32)
    ident_d = din("ident", [128, 128])
    tri_d = din("tri", [128, 128])
    smask_d = din("smask", [128, 256])
    tau_d = din("tau", [128, LS])
    norm_g = din("norm_g", [D]); w_in = din("w_in", [D, IN_W])
    a_re_d = din("ssm_a_re", [32, 64]); a_im_d = din("ssm_a_im", [32, 64]); ldt_d = din("ssm_log_dt", [32])
    b_re_d = din("ssm_b_re", [32, 64, 16]); b_im_d = din("ssm_b_im", [32, 64, 16])
    c_re_d = din("ssm_c_re", [32, 16, 64]); c_im_d = din("ssm_c_im", [32, 16, 64])
    ssm_d_d = din("ssm_d", [512]); glu_w_d = din("ssm_glu_w", [512, 512]); glu_b_d = din("ssm_glu_b", [512])
    gq_d = din("mla_q_norm_g", [512]); w_uq_d = din("mla_w_uq", [512, 1536])
    gkv_d = din("mla_kv_norm_g", [256]); w_ukv_d = din("mla_w_ukv", [256, 2048])
    gqq_d = din("mla_qk_norm_q", [192]); gqk_d = din("mla_qk_norm_k", [192])
    gmem_d = din("mem_norm_g", [D]); w_mk_d = din("mem_w_k", [D, 512]); w_mv_d = din("mem_w_v", [D, 512])
    gmq_d = din("mem_qk_norm_q", [128]); gmk_d = din("mem_qk_norm_k", [128])
    go_s_d = din("out_norm_ssm", [512]); go_m_d = din("out_norm_mla", [1024]); go_e_d = din("out_norm_mem", [512])
    w_out_d = din("w_out", [D, D])

    y_p = dout("y_p", [CH, D]); y_s = dout("y_s", [128, D])
    ckv_p = dout("ckv_p", [CH, 256]); kr_p = dout("kr_p", [CH, 64])
    ckv_s = dout("ckv_s", [128, 256]); kr_s = dout("kr_s", [128, 64])
    memk_o = dout("memk", [256, 512]); memv_o = dout("memv", [256, 512])
    sp_re = dout("sp_re", [16, 128]); sp_im = dout("sp_im", [16, 128])
    ss_re = dout("ss_re", [NSEQ, 2048]); ss_im = dout("ss_im", [NSEQ, 2048])

    es = contextlib.ExitStack()
    with es:
        T = Tracker(nc, es)

        def chk(name):
            if stop == name:
                raise _Stop()

        import os as _os
        DBG = _os.environ.get('KDBG')

        def dbg(name, ap, shape, keys, dt=F32):
            if not DBG:
                return
            d = nc.dram_tensor("dbg_" + name, list(shape), dt, kind="ExternalOutput").ap()
            T.dma('sp', 'dbg', lambda e: e.dma_start(out=d, in_=ap, allow_slow_non_contiguous=True), R=keys)

        try:
            def sb(name, shape, dt=F32):
                return es.enter_context(nc.sbuf_tensor("s_" + name, list(shape), dt))

            def ps(name, shape, dt=F32):
                return es.enter_context(nc.psum_tensor("p_" + name, list(shape), dt))

            psT = [ps("psT%d" % i, [128, 1024], BF16) for i in range(2)]
            psA = [ps("psA%d" % i, [128, 512], F32) for i in range(3)]
            psC = [ps("psC%d" % i, [128, 512], F32) for i in range(3)]
            rr = {'T': 0, 'A': 0}

            def nextT():
                i = rr['T']; rr['T'] = (i + 1) % 2
                return psT[i], 'psT%d' % i

            def nextA():
                i = rr['A']; rr['A'] = (i + 1) % 3
                return psA[i], 'psA%d' % i

            identf = sb("identf", [128, 128]); identb = sb("identb", [128, 128], BF16)
            trib = sb("trib", [128, 128], BF16); onesb = sb("onesb", [128, 128], BF16)
            smask = sb("smask", [128, 256]); taur = sb("taur", [128, LS])
            stg = sb("stg", [128, 1024])
            T.dma('sp', 'c0', lambda e: e.dma_start(out=identf[:], in_=ident_d[:, :]), W=['identf'])
            T.dma('sp', 'c0', lambda e: e.dma_start(out=stg[:, 0:128], in_=tri_d[:, :]), W=['stg'])
            T.dma('sp', 'c0', lambda e: e.dma_start(out=smask[:], in_=smask_d[:, :]), W=['smask'])
            T.dma('sp', 'c0', lambda e: e.dma_start(out=taur[:], in_=tau_d[:, :]), W=['taur'])
            T.op('dve', lambda e: e.tensor_copy(out=identb[:], in_=identf[:]), R=['identf'], W=['identb'])
            T.op('dve', lambda e: e.tensor_copy(out=trib[:], in_=stg[:, 0:128]), R=['stg'], W=['trib'])
            T.op('dve', lambda e: e.memset(onesb[:], 1.0), W=['onesb'])

            def transpose_f32(dst_ap, src_ap, npart, nfree, dkey, skey, scale=None):
                p, pk = nextA()
                T.op('pe', lambda e: e.transpose(out=p[0:nfree, 0:npart], in_=src_ap, identity=identf[0:npart, 0:npart]),
                     R=[skey, 'identf'], W=[pk])
                if scale is None:
                    T.op('dve', lambda e: e.tensor_copy(out=dst_ap, in_=p[0:nfree, 0:npart]), R=[pk], W=[dkey])
                else:
                    T.op('dve', lambda e: e.tensor_scalar(out=dst_ap, in0=p[0:nfree, 0:npart], scalar1=scale, scalar2=None,
                                                          op0=ALU.mult), R=[pk], W=[dkey])

            def load_col_into(dst_ap, dkey, vec_d, n):
                k = n // 128
                T.dma('sp', 'c0', lambda e: e.dma_start(out=stg[0:k, 0:128], in_=vec_d.rearrange("(k p) -> k p", p=128)),
                      W=['stg'])
                transpose_f32(dst_ap, stg[0:k, 0:128], k, 128, dkey, 'stg')

            def load_col(name, vec_d, n):
                t = sb(name, [128, n // 128])
                load_col_into(t[:], name, vec_d, n)
                return t

            def load_bc(name, vec_d, n):
                t = sb(name, [128, n])
                T.dma('sp', 'c0', lambda e: e.dma_start(out=t[:], in_=vec_d.partition_broadcast(128)), W=[name])
                return t

            gcol_in = load_col("gcol_in", norm_g, D)
            gcol_q = load_col("gcol_q", gq_d, 512)
            gcol_mem = load_col("gcol_mem", gmem_d, D)
            gcol_out = sb("gcol_out", [128, 16])
            load_col_into(gcol_out[:, 0:4], 'gcol_out', go_s_d, 512)
            load_col_into(gcol_out[:, 4:12], 'gcol_out', go_m_d, 1024)
            load_col_into(gcol_out[:, 12:16], 'gcol_out', go_e_d, 512)
            dcol = load_col("dcol", ssm_d_d, 512)
            diagD_b = sb("diagD_b", [128, 4, 128], BF16)
            for ct in range(4):
                T.op('dve', lambda e, ct=ct: e.tensor_scalar(out=diagD_b[:, ct, :], in0=identf[:], scalar1=dcol[:, ct:ct + 1],
                                                             scalar2=None, op0=ALU.mult), R=['identf', 'dcol'], W=['diagD_b'])
            gkv_bc = load_bc("gkv_bc", gkv_d, 256)
            gmk_bc = load_bc("gmk_bc", gmk_d, 128)
            gqk_bc = load_bc("gqk_bc", gqq_d, 192)
            T.dma('sp', 'c0', lambda e: e.dma_start(out=stg[:, 0:192], in_=gqk_d.partition_broadcast(128)), W=['stg'])
            T.op('dve', lambda e: e.scalar_tensor_tensor(out=gqk_bc[:], in0=gqk_bc[:], scalar=192.0 ** -0.5, in1=stg[:, 0:192],
                                                         op0=ALU.mult, op1=ALU.mult), R=['gqk_bc', 'stg'], W=['gqk_bc'])
            gmq_bc = load_bc("gmq_bc", gmq_d, 128)
            T.op('dve', lambda e: e.tensor_scalar(out=gmq_bc[:], in0=gmq_bc[:], scalar1=128.0 ** -0.5, scalar2=None,
                                                  op0=ALU.mult), R=['gmq_bc'], W=['gmq_bc'])
            glub_b = sb("glub_b", [1, 512], BF16)
            T.dma('sp', 'c0', lambda e: e.dma_start(out=stg[0:1, 0:512], in_=glu_b_d.rearrange("(o n) -> o n", o=1)),
                  W=['stg'])
            T.op('dve', lambda e: e.tensor_copy(out=glub_b[:], in_=stg[0:1, 0:512]), R=['stg'], W=['glub_b'])

            def load_w_rows(dst, dkey, w_d, kc_n, c0, ncols, gcol, dcol0=0):
                for kc in range(kc_n):
                    for cc0 in range(0, ncols, 1024):
                        n = min(1024, ncols - cc0)
                        T.dma('sp', 'wst', lambda e, kc=kc, cc0=cc0, n=n: e.dma_start(
                            out=stg[:, 0:n], in_=w_d[kc * 128:(kc + 1) * 128, c0 + cc0:c0 + cc0 + n]), W=['stg'])
                        if gcol is None:
                            T.op('dve', lambda e, kc=kc, cc0=cc0, n=n: e.tensor_copy(out=dst[:, kc, dcol0 + cc0:dcol0 + cc0 + n],
                                                                                     in_=stg[:, 0:n]), R=['stg'], W=[dkey])
                        else:
                            T.op('dve', lambda e, kc=kc, cc0=cc0, n=n: e.tensor_scalar(out=dst[:, kc, dcol0 + cc0:dcol0 + cc0 + n],
                                                                                       in0=stg[:, 0:n], scalar1=gcol[:, kc:kc + 1],
                                                                                       scalar2=None, op0=ALU.mult), R=['stg'], W=[dkey])

            w_uq_b = sb("w_uq_b", [128, 4, 1536], BF16)
            load_w_rows(w_uq_b, 'w_uq_b', w_uq_d, 4, 0, 1536, gcol_q)
            glu_w_b = sb("glu_w_b", [128, 4, 512], BF16)
            load_w_rows(glu_w_b, 'glu_w_b', glu_w_d, 4, 0, 512, None)
            w_uk_b = sb("w_uk_b", [128, 2, 1024], BF16)
            w_uv_b = sb("w_uv_b", [128, 2, 1024], BF16)
            for kc in range(2):
                for hf in range(2):
                    T.dma('sp', 'wst', lambda e, kc=kc, hf=hf: e.dma_start(
                        out=stg[:, 0:1024], in_=w_ukv_d[kc * 128:(kc + 1) * 128, hf * 1024:(hf + 1) * 1024]), W=['stg'])
                    sv = stg[:, 0:1024].rearrange("p (h c) -> p h c", c=256)
                    T.op('dve', lambda e, kc=kc, hf=hf, sv=sv: e.tensor_copy(
                        out=w_uk_b[:, kc, hf * 512:(hf + 1) * 512].rearrange("p (h c) -> p h c", c=128), in_=sv[:, :, 0:128]),
                        R=['stg'], W=['w_uk_b'])
                    T.op('dve', lambda e, kc=kc, hf=hf, sv=sv: e.tensor_copy(
                        out=w_uv_b[:, kc, hf * 512:(hf + 1) * 512].rearrange("p (h c) -> p h c", c=128), in_=sv[:, :, 128:256]),
                        R=['stg'], W=['w_uv_b'])
            w_ukT_b = sb("w_ukT_b", [128, 8, 256], BF16)
            for h in range(8):
                p, pk = nextT()
                for kc in range(2):
                    T.op('pe', lambda e, h=h, kc=kc, p=p: e.transpose(out=p[:, kc * 128:(kc + 1) * 128],
                                                                      in_=w_uk_b[:, kc, h * 128:(h + 1) * 128],
                                                                      identity=identb[:]), R=['w_uk_b', 'identb'], W=[pk])
                T.op('act', lambda e, h=h, p=p: e.copy(out=w_ukT_b[:, h, :], in_=p[:, 0:256]), R=[pk], W=['w_ukT_b'])

            chk('w')
            arena = sb("arena", [128, 15360], BF16)
            w_up_b = arena[:, 0:13312].rearrange("p (k c) -> p k c", k=16)
            xnT1 = arena[:, 13312:15360].rearrange("p (k c) -> p k c", k=16)
            memT = arena[:, 8192:12288].rearrange("p (t k c) -> p t k c", t=2, k=16)
            gates = arena[:, 0:GT * 2048].rearrange("p (t c) -> p t c", t=GT)
            QTn = arena[:, 4096:4096 + 8 * GC].rearrange("p (h c) -> p h c", h=8)
            QTr = arena[:, 6144:6144 + 8 * GC].rearrange("p (h c) -> p h c", h=8)
            xnT_g = arena[:, 8192:8192 + GT * 2048].rearrange("p (t k c) -> p t k c", t=GT, k=16)
            ymla = arena[:, 12288:12288 + GT * 1024].rearrange("p (t c) -> p t c", t=GT)
            QmT = arena[:, 14336:14336 + 4 * GC].rearrange("p (h c) -> p h c", h=4)

            load_w_rows(w_up_b, 'w_up_b', w_in, 16, C_U, 512, gcol_in, 0)
            load_w_rows(w_up_b, 'w_up_b', w_in, 16, C_CKV, 320, gcol_in, 512)

            xt = sb("xt", [128, D])
            are = sb("are", [128, 16]); aim = sb("aim", [128, 16]); dts = sb("dts", [128, 16])
            with nc.allow_non_contiguous_dma(reason="small one-time parameter loads"):
                T.dma('sp', 'c0', lambda e: e.dma_start(out=are[:], in_=a_re_d.rearrange("(t two) n -> (two n) t", two=2)),
                      W=['are'])
                T.dma('sp', 'c0', lambda e: e.dma_start(out=aim[:], in_=a_im_d.rearrange("(t two) n -> (two n) t", two=2)),
                      W=['aim'])
                lv = ldt_d.rearrange("(t two) -> two t", two=2)
                for hh in range(2):
                    T.dma('sp', 'c0', lambda e, hh=hh: e.dma_start(out=dts[hh * 64:(hh + 1) * 64, :],
                                                                   in_=lv[hh].partition_broadcast(64)), W=['dts'])
            T.op('act', lambda e: e.activation(out=dts[:], in_=dts[:], func=AF.Exp), R=['dts'], W=['dts'])
            theta = sb("theta", [128, 16]); rmag = sb("rmag", [128, 16])
            T.op('dve', lambda e: e.tensor_tensor(out=theta[:], in0=dts[:], in1=aim[:], op=ALU.mult), R=['dts', 'aim'], W=['theta'])
            T.op('dve', lambda e: e.tensor_tensor(out=rmag[:], in0=dts[:], in1=are[:], op=ALU.mult), R=['dts', 'are'], W=['rmag'])
            T.op('act', lambda e: e.activation(out=rmag[:], in_=rmag[:], func=AF.Exp), R=['rmag'], W=['rmag'])
            costab = sb("costab", [128, 16, LS]); sintab = sb("sintab", [128, 16, LS])
            angw = xt[:, 0:4 * LS]; angk = sb("angk", [128, 4 * LS], I32); angm = xt[:, 256:256 + 4 * LS]
            TAB = ['sintab', 'costab']

            def sin_table(dst, dkey, phase):
                for q4 in range(4):
                    for j in range(4):
                        i = q4 * 4 + j
                        T.op('dve', lambda e, i=i, j=j: e.tensor_scalar(out=angw[:, j * LS:(j + 1) * LS], in0=taur[:],
                                                                        scalar1=theta[:, i:i + 1], scalar2=None, op0=ALU.mult),
                             R=['taur', 'theta', 'angw'], W=['angw'])
                    T.op('dve', lambda e: e.tensor_scalar(out=angw[:], in0=angw[:], scalar1=1.0 / (2 * np.pi), scalar2=phase,
                                                          op0=ALU.mult, op1=ALU.add), R=['angw'], W=['angw'])
                    T.op('dve', lambda e: e.tensor_copy(out=angk[:], in_=angw[:]), R=['angw'], W=['angk'])
                    T.op('dve', lambda e: e.tensor_copy(out=angm[:], in_=angk[:]), R=['angk'], W=['angm'])
                    T.op('dve', lambda e: e.tensor_tensor(out=angw[:], in0=angw[:], in1=angm[:], op=ALU.subtract),
                         R=['angw', 'angm'], W=['angw'])
                    T.op('dve', lambda e: e.tensor_scalar(out=angm[:], in0=angw[:], scalar1=0.5, scalar2=None, op0=ALU.is_gt),
                         R=['angw'], W=['angm'])
                    T.op('dve', lambda e: e.tensor_tensor(out=angw[:], in0=angw[:], in1=angm[:], op=ALU.subtract),
                         R=['angw', 'angm'], W=['angw'])
                    T.op('dve', lambda e: e.tensor_scalar(out=angm[:], in0=angw[:], scalar1=-0.5, scalar2=None, op0=ALU.is_lt),
                         R=['angw'], W=['angm'])
                    T.op('dve', lambda e: e.tensor_tensor(out=angw[:], in0=angw[:], in1=angm[:], op=ALU.add),
                         R=['angw', 'angm'], W=['angw'])
                    T.op('act', lambda e, q4=q4: e.activation(out=dst[:, q4 * 4:(q4 + 1) * 4, :].rearrange("p a b -> p (a b)"), in_=angw[:],
                                                              func=AF.Sin, scale=2 * np.pi), R=['angw'], W=[dkey])

            sin_table(sintab, 'sintab', 0.0)
            sin_table(costab, 'costab', 0.25)
            s5t = xt[:, 512:640].rearrange("p (a b) -> p a b", a=8)
            abr, abi, den, fre, fim, nfim, t0_, t1_ = [s5t[:, i, :] for i in range(8)]
            S5 = ['s5t']
            T.op('dve', lambda e: e.tensor_tensor(out=abr, in0=rmag[:], in1=costab[:, :, 0], op=ALU.mult), R=['rmag'] + TAB, W=S5)
            T.op('dve', lambda e: e.tensor_tensor(out=abi, in0=rmag[:], in1=sintab[:, :, 0], op=ALU.mult), R=['rmag'] + TAB + S5, W=S5)
            T.op('dve', lambda e: e.tensor_tensor(out=den, in0=are[:], in1=are[:], op=ALU.mult), R=['are'] + S5, W=S5)
            T.op('dve', lambda e: e.tensor_tensor(out=t0_, in0=aim[:], in1=aim[:], op=ALU.mult), R=['aim'] + S5, W=S5)
            T.op('dve', lambda e: e.tensor_tensor(out=den, in0=den, in1=t0_, op=ALU.add), R=S5, W=S5)
            T.op('dve', lambda e: e.reciprocal(out=den, in_=den), R=S5, W=S5)
            T.op('dve', lambda e: e.tensor_scalar(out=t1_, in0=abr, scalar1=-1.0, scalar2=None, op0=ALU.add), R=S5, W=S5)
            T.op('dve', lambda e: e.tensor_tensor(out=fre, in0=t1_, in1=are[:], op=ALU.mult), R=S5 + ['are'], W=S5)
            T.op('dve', lambda e: e.tensor_tensor(out=t0_, in0=abi, in1=aim[:], op=ALU.mult), R=S5 + ['aim'], W=S5)
            T.op('dve', lambda e: e.tensor_tensor(out=fre, in0=fre, in1=t0_, op=ALU.add), R=S5, W=S5)
            T.op('dve', lambda e: e.tensor_tensor(out=fre, in0=fre, in1=den, op=ALU.mult), R=S5, W=S5)
            T.op('dve', lambda e: e.tensor_tensor(out=fim, in0=abi, in1=are[:], op=ALU.mult), R=S5 + ['are'], W=S5)
            T.op('dve', lambda e: e.tensor_tensor(out=t0_, in0=t1_, in1=aim[:], op=ALU.mult), R=S5 + ['aim'], W=S5)
            T.op('dve', lambda e: e.tensor_tensor(out=fim, in0=fim, in1=t0_, op=ALU.subtract), R=S5, W=S5)
            T.op('dve', lambda e: e.tensor_tensor(out=fim, in0=fim, in1=den, op=ALU.mult), R=S5, W=S5)
            T.op('dve', lambda e: e.tensor_scalar(out=nfim, in0=fim, scalar1=-1.0, scalar2=None, op0=ALU.mult), R=S5, W=S5)
            bst = xt[:, 640:1152].rearrange("p (c t q) -> p c t q", c=2, t=16)
            with nc.allow_non_contiguous_dma(reason="small one-time parameter loads"):
                T.dma('sp', 'c0', lambda e: e.dma_start(out=bst[:, 0, :, :],
                                                        in_=b_re_d.rearrange("(t two) n q -> (two n) t q", two=2)), W=['bst'])
                T.dma('sp', 'c0', lambda e: e.dma_start(out=bst[:, 1, :, :],
                                                        in_=b_im_d.rearrange("(t two) n q -> (two n) t q", two=2)), W=['bst'])
            BT_b = sb("BT_b", [128, 16, 2, 128], BF16)
            bexp = xt[:, 1152:1408].rearrange("p (c n) -> p c n", c=2); bt1 = xt[:, 1408:1440].rearrange("p (c n) -> p c n", c=2)
            for i in range(16):
                ga, gb = (2 * i) % 8, (2 * i + 1) % 8
                T.op('dve', lambda e: e.memset(bexp[:], 0.0), W=['bexp'])
                T.op('dve', lambda e, i=i: e.tensor_scalar(out=bt1[:, 0, :], in0=bst[:, 0, i, :], scalar1=s5t[:, 3, i:i + 1],
                                                           scalar2=None, op0=ALU.mult), R=['bst', 's5t'], W=['bt1'])
                T.op('dve', lambda e, i=i: e.scalar_tensor_tensor(out=bt1[:, 0, :], in0=bst[:, 1, i, :], scalar=s5t[:, 5, i:i + 1],
                                                                  in1=bt1[:, 0, :], op0=ALU.mult, op1=ALU.add),
                     R=['bst', 's5t', 'bt1'], W=['bt1'])
                T.op('dve', lambda e, i=i: e.tensor_scalar(out=bt1[:, 1, :], in0=bst[:, 1, i, :], scalar1=s5t[:, 3, i:i + 1],
                                                           scalar2=None, op0=ALU.mult), R=['bst', 's5t', 'bt1'], W=['bt1'])
                T.op('dve', lambda e, i=i: e.scalar_tensor_tensor(out=bt1[:, 1, :], in0=bst[:, 0, i, :], scalar=s5t[:, 4, i:i + 1],
                                                                  in1=bt1[:, 1, :], op0=ALU.mult, op1=ALU.add),
                     R=['bst', 's5t', 'bt1'], W=['bt1'])
                for c in range(2):
                    T.op('dve', lambda e, c=c, ga=ga: e.tensor_copy(out=bexp[0:64, c, ga * 16:ga * 16 + 16], in_=bt1[0:64, c, :]),
                         R=['bt1', 'bexp'], W=['bexp'])
                    T.op('dve', lambda e, c=c, gb=gb: e.tensor_copy(out=bexp[64:128, c, gb * 16:gb * 16 + 16], in_=bt1[64:128, c, :]),
                         R=['bt1', 'bexp'], W=['bexp'])
                for c in range(2):
                    transpose_f32(BT_b[:, i, c, :], bexp[:, c, :], 128, 128, 'BT_b', 'bexp')
            CT_b = sb("CT_b", [128, 16, 2, 32], BF16)
            T.op('dve', lambda e: e.memset(CT_b[:], 0.0), W=['CT_b'])
            cpad = xt[:, 1536:1664]; cT = xt[:, 1664:1792]
            for c, cd in enumerate((c_re_d, c_im_d)):
                cv = cd.rearrange("g p n -> (g p) n")
                for c4 in range(4):
                    T.dma('sp', 'c0', lambda e, c4=c4, cv=cv: e.dma_start(out=cpad[:, 0:64], in_=cv[c4 * 128:(c4 + 1) * 128, :]), W=['cpad'])
                    T.dma('sp', 'c0', lambda e, c4=c4, cv=cv: e.dma_start(out=cpad[:, 64:128], in_=cv[c4 * 128:(c4 + 1) * 128, :]), W=['cpad'])
                    transpose_f32(cT[:], cpad[:], 128, 128, 'cT', 'cpad', scale=(1.0 if c == 0 else -1.0))
                    for k in range(4):
                        i = 4 * c4 + k
                        T.op('dve', lambda e, i=i, k=k, c=c: e.tensor_copy(out=CT_b[0:64, i, c, 0:16],
                                                                          in_=cT[0:64, (2 * k) * 16:(2 * k) * 16 + 16]),
                             R=['cT', 'CT_b'], W=['CT_b'])
                        T.op('dve', lambda e, i=i, k=k, c=c: e.tensor_copy(out=CT_b[64:128, i, c, 16:32],
                                                                          in_=cT[64:128, (2 * k + 1) * 16:(2 * k + 1) * 16 + 16]),
                             R=['cT', 'CT_b'], W=['CT_b'])

            dbg('theta', theta[:], [128, 16], ['theta']); dbg('rmag', rmag[:], [128, 16], ['rmag'])
            dbg('costab', costab[:].rearrange("p a b -> p (a b)"), [128, 16 * LS], ['costab'])
            dbg('sintab', sintab[:].rearrange("p a b -> p (a b)"), [128, 16 * LS], ['sintab'])
            dbg('s5t', xt[:, 512:640], [128, 128], ['s5t'])
            chk('s5')
            hre = sb("hre", [128, 16, NSEQ]); him = sb("him", [128, 16, NSEQ])
            T.op('dve', lambda e: e.memset(hre[:], 0.0), W=['hst'])
            T.op('dve', lambda e: e.memset(him[:], 0.0), R=['hst'], W=['hst'])
            GS = 2
            s5tmp = [sb("s5tmp%d" % k, [128, GS, 128]) for k in range(4)]
            bpr = sb("bpr", [128, GS, 128]); bpi = sb("bpi", [128, GS, 128])
            d0t = sb("d0t", [128, GS, 128])
            h_b = sb("h_b", [128, GS, 2, 128], BF16)
            hend = sb("hend", [128, 4, GS, NSEQ])

            dbg_once = []

            def ssm_step(uT_get, ncols, S, L, col0, want_out, gis=None):
                mcol = 0 if S == 1 else 128
                for gi in (range(16 // GS) if gis is None else gis):
                    p, pk = nextA()
                    pv = p[:, 0:GS * 2 * ncols].rearrange("p (j c n) -> p j c n", j=GS, c=2)
                    for j in range(GS):
                        i = gi * GS + j
                        for c in range(2):
                            T.op('pe', lambda e, i=i, j=j, c=c, pv=pv: e.matmul(pv[:, j, c, :], lhsT=BT_b[:, i, c, :],
                                                                               rhs=uT_get(i // 4), start=True, stop=True),
                                 R=['BT_b', 'uT'], W=[pk])
                    isl = slice(gi * GS, gi * GS + GS)

                    def tab(tb):
                        a = tb[:, isl, 0:L]
                        if S == 1:
                            return a
                        return a.unsqueeze(2).to_broadcast([128, GS, S, L])

                    def v4(t):
                        a = t[:, :, 0:ncols]
                        if S == 1:
                            return a
                        return a.rearrange("p j (s l) -> p j s l", l=L)

                    def pvc(c):
                        a = pv[:, :, c, :]
                        if S == 1:
                            return a
                        return a.rearrange("p j (s l) -> p j s l", l=L)
                    t1, t2, t3, t4 = s5tmp
                    T.op('dve', lambda e: e.tensor_tensor(out=v4(t1), in0=pvc(0), in1=tab(costab), op=ALU.mult), R=[pk] + TAB, W=['s5tmp0'])
                    T.op('dve', lambda e: e.tensor_tensor(out=v4(t2), in0=pvc(1), in1=tab(sintab), op=ALU.mult), R=[pk] + TAB, W=['s5tmp1'])
                    T.op('dve', lambda e: e.tensor_tensor(out=v4(t3), in0=pvc(1), in1=tab(costab), op=ALU.mult), R=[pk] + TAB, W=['s5tmp2'])
                    T.op('dve', lambda e: e.tensor_tensor(out=v4(t4), in0=pvc(0), in1=tab(sintab), op=ALU.mult), R=[pk] + TAB, W=['s5tmp3'])
                    T.op('pool', lambda e: e.tensor_tensor(out=bpr[:, :, 0:ncols], in0=t1[:, :, 0:ncols], in1=t2[:, :, 0:ncols], op=ALU.add),
                         R=['s5tmp0', 's5tmp1'], W=['bpr'])
                    T.op('pool', lambda e: e.tensor_tensor(out=bpi[:, :, 0:ncols], in0=t3[:, :, 0:ncols], in1=t4[:, :, 0:ncols], op=ALU.subtract),
                         R=['s5tmp2', 's5tmp3'], W=['bpi'])
                    if DBG and not dbg_once and gi == 0:
                        dbg('t1', t1[:].rearrange("p a b -> p (a b)"), [128, 256], ['s5tmp0'])
                        dbg('uT', uT[:].rearrange("p a b -> p (a b)"), [128, 512], ['uT'], BF16)
                        dbg('ubg', u_bg[:, 0, :], [128, 512], ['u_bg'], BF16)
                        dbg('wup', w_up_b[:, 0:2, :].rearrange("p a b -> p (a b)"), [128, 1664], ['w_up_b'], BF16)
                        dbg('gcol', gcol_in[:], [128, 16], ['gcol_in'])
                        dbg('xnT', xnT1[:].rearrange("p a b -> p (a b)"), [128, 2048], ['xnT1'], BF16)
                        dbg('BT', BT_b[:, 0:2, :, :].rearrange("p a b c -> p (a b c)"), [128, 512], ['BT_b'], BF16)
                        dbg('t4', t4[:].rearrange("p a b -> p (a b)"), [128, 256], ['s5tmp3'])
                        dbg('bpr0', bpr[:].rearrange("p a b -> p (a b)"), [128, 256], ['bpr'])
                        dbg('bpi0', bpi[:].rearrange("p a b -> p (a b)"), [128, 256], ['bpi'])
                    for j in range(GS):
                        i = gi * GS + j
                        T.op('pool', lambda e, i=i, j=j: e.tensor_scalar(out=d0t[:, j, 0:ncols], in0=smask[:, mcol:mcol + ncols],
                                                                         scalar1=rmag[:, i:i + 1], scalar2=None, op0=ALU.mult),
                             R=['smask', 'rmag', 'd0t'], W=['d0t'])
                        fr = bpr[:, j, 0:ncols].rearrange("p (s l) -> p s l", l=L)[:, :, 0]
                        fi = bpi[:, j, 0:ncols].rearrange("p (s l) -> p s l", l=L)[:, :, 0]
                        T.op('dve', lambda e, i=i, fr=fr: e.scalar_tensor_tensor(out=fr, in0=hre[:, i, 0:S], scalar=rmag[:, i:i + 1],
                                                                                  in1=fr, op0=ALU.mult, op1=ALU.add),
                             R=['hst', 'rmag', 'bpr'], W=['bpr'])
                        T.op('dve', lambda e, i=i, fi=fi: e.scalar_tensor_tensor(out=fi, in0=him[:, i, 0:S], scalar=rmag[:, i:i + 1],
                                                                                  in1=fi, op0=ALU.mult, op1=ALU.add),
                             R=['hst', 'rmag', 'bpi'], W=['bpi'])
                    for j in range(GS):
                        T.op('dve', lambda e, j=j: e.tensor_tensor_scan(out=bpr[:, j, 0:ncols], data0=d0t[:, j, 0:ncols],
                                                                        data1=bpr[:, j, 0:ncols], initial=0.0,
                                                                        op0=ALU.mult, op1=ALU.add), R=['d0t', 'bpr'], W=['bpr'])
                        T.op('dve', lambda e, j=j: e.tensor_tensor_scan(out=bpi[:, j, 0:ncols], data0=d0t[:, j, 0:ncols],
                                                                        data1=bpi[:, j, 0:ncols], initial=0.0,
                                                                        op0=ALU.mult, op1=ALU.add), R=['d0t', 'bpi'], W=['bpi'])
                    if DBG and not dbg_once and gi == 0:
                        dbg('bpr1', bpr[:].rearrange("p a b -> p (a b)"), [128, 256], ['bpr'])
                        dbg('bpi1', bpi[:].rearrange("p a b -> p (a b)"), [128, 256], ['bpi'])
                        dbg('d0t', d0t[:].rearrange("p a b -> p (a b)"), [128, 256], ['d0t'])
                        dbg_once.append(1)
                    gl_r = bpr[:, :, 0:ncols].rearrange("p j (s l) -> p j s l", l=L)[:, :, :, L - 1]
                    gl_i = bpi[:, :, 0:ncols].rearrange("p j (s l) -> p j s l", l=L)[:, :, :, L - 1]
                    cl = costab[:, isl, L - 1:L].to_broadcast([128, GS, S])
                    sl = sintab[:, isl, L - 1:L].to_broadcast([128, GS, S])
                    e1, e2, e3, e4 = [hend[:, k, :, 0:S] for k in range(4)]
                    T.op('pool', lambda e: e.tensor_tensor(out=e1, in0=gl_r, in1=cl, op=ALU.mult), R=['bpr'] + TAB, W=['hend'])
                    T.op('pool', lambda e: e.tensor_tensor(out=e2, in0=gl_i, in1=sl, op=ALU.mult), R=['bpi', 'hend'] + TAB, W=['hend'])
                    T.op('pool', lambda e: e.tensor_tensor(out=e3, in0=gl_r, in1=sl, op=ALU.mult), R=['bpr', 'hend'] + TAB, W=['hend'])
                    T.op('pool', lambda e: e.tensor_tensor(out=e4, in0=gl_i, in1=cl, op=ALU.mult), R=['bpi', 'hend'] + TAB, W=['hend'])
                    T.op('pool', lambda e: e.tensor_tensor(out=hre[:, isl, 0:S], in0=e1, in1=e2, op=ALU.subtract), R=['hend', 'hst'], W=['hst'])
                    T.op('pool', lambda e: e.tensor_tensor(out=him[:, isl, 0:S], in0=e3, in1=e4, op=ALU.add), R=['hend', 'hst'], W=['hst'])
                    if want_out:
                        T.op('pool', lambda e: e.tensor_tensor(out=v4(t1), in0=v4(bpr), in1=tab(costab), op=ALU.mult), R=['bpr'] + TAB, W=['s5tmp0'])
                        T.op('pool', lambda e: e.tensor_tensor(out=v4(t2), in0=v4(bpi), in1=tab(sintab), op=ALU.mult), R=['bpi'] + TAB, W=['s5tmp1'])
                        T.op('dve', lambda e: e.tensor_tensor(out=v4(t3), in0=v4(bpr), in1=tab(sintab), op=ALU.mult), R=['bpr'] + TAB, W=['s5tmp2'])
                        T.op('dve', lambda e: e.tensor_tensor(out=v4(t4), in0=v4(bpi), in1=tab(costab), op=ALU.mult), R=['bpi'] + TAB, W=['s5tmp3'])
                        T.op('pool', lambda e: e.tensor_tensor(out=h_b[:, :, 0, col0:col0 + ncols], in0=t1[:, :, 0:ncols],
                                                               in1=t2[:, :, 0:ncols], op=ALU.subtract), R=['s5tmp0', 's5tmp1', 'h_b'], W=['h_b'])
                        T.op('pool', lambda e: e.tensor_tensor(out=h_b[:, :, 1, col0:col0 + ncols], in0=t3[:, :, 0:ncols],
                                                               in1=t4[:, :, 0:ncols], op=ALU.add), R=['s5tmp2', 's5tmp3', 'h_b'], W=['h_b'])

            T.barrier()
            xnb = sb("xnb", [128, D], BF16)
            ropet = sb("ropet", [128, 128])
            sm = sb("sm", [128, 64])
            junk = sb("junk", [128, 512], BF16)
            ckvn_f = sb("ckvn_f", [128, 256]); krr_f = sb("krr_f", [128, 64]); krt = sb("krt", [128, 64])
            ckvn_b = sb("ckvn_b", [128, 256], BF16)
            krr_b = sb("krr_b", [128, 64], BF16)
            uT = sb("uT", [128, 4, 128], BF16); u_bg = sb("u_bg", [128, GT, 512], BF16)
            ckvT_all = sb("ckvT_all", [128, 2, NKT * 128], BF16)
            krT_all = sb("krT_all", [64, NKT * 128], BF16)
            KA = max(NKT, 16)
            ckv1_all = sb("ckv1_all", [128, KA, 256], BF16)
            c1flat = ckv1_all[:].rearrange("p a b -> p (a b)")
            rk_all = sb("rk_all", [128, NKT, 8])
            kbias = sb("kbias", [128, NKT])
            T.dma('sp', 'c0', lambda e: e.dma_start(out=kbias[:], in_=kbias_d[:, :]), W=['kbias'])
            ckvT_s = sb("ckvT_s", [128, 2, 128], BF16); krT_s = sb("krT_s", [64, 128], BF16)

            def rstd_of(src_ap, n, dst_ap, skey):
                T.op('act', lambda e: e.activation(out=junk[:, 0:n], in_=src_ap, func=AF.Square, accum_out=dst_ap),
                     R=[skey, 'sm'], W=['junk', 'sm'])
                T.op('act', lambda e: e.activation(out=dst_ap, in_=dst_ap, func=AF.Sqrt, scale=1.0 / n, bias=EPS), R=['sm'], W=['sm'])
                T.op('dve', lambda e: e.reciprocal(out=dst_ap, in_=dst_ap), R=['sm'], W=['sm'])

            def norm_transpose(src_d_ap, dst_fn, dkey):
                T.dma('sp', 'xld', lambda e: e.dma_start(out=xt[:], in_=src_d_ap), W=['xt'])
                T.op('act', lambda e: e.activation(out=xnb[:, 0:1024], in_=xt[:, 0:1024], func=AF.Square, accum_out=sm[:, 0:1]),
                     R=['xt', 'sm'], W=['xnb', 'sm'])
                T.op('act', lambda e: e.activation(out=xnb[:, 1024:2048], in_=xt[:, 1024:2048], func=AF.Square, accum_out=sm[:, 1:2]),
                     R=['xt', 'sm'], W=['xnb', 'sm'])
                T.op('dve', lambda e: e.tensor_tensor(out=sm[:, 0:1], in0=sm[:, 0:1], in1=sm[:, 1:2], op=ALU.add), R=['sm'], W=['sm'])
                T.op('act', lambda e: e.activation(out=sm[:, 0:1], in_=sm[:, 0:1], func=AF.Sqrt, scale=1.0 / D, bias=EPS), R=['sm'], W=['sm'])
                T.op('dve', lambda e: e.reciprocal(out=sm[:, 0:1], in_=sm[:, 0:1]), R=['sm'], W=['sm'])
                T.op('dve', lambda e: e.tensor_scalar(out=xnb[:], in0=xt[:], scalar1=sm[:, 0:1], scalar2=None, op0=ALU.mult),
                     R=['xt', 'sm', 'xnb'], W=['xnb'])
                for half in range(2):
                    p, pk = nextT()
                    for k in range(8):
                        kc = half * 8 + k
                        T.op('pe', lambda e, k=k, kc=kc, p=p: e.transpose(out=p[:, k * 128:(k + 1) * 128],
                                                                          in_=xnb[:, kc * 128:(kc + 1) * 128], identity=identb[:]),
                             R=['xnb', 'identb'], W=[pk])
                    if half == 0:
                        T.op('act', lambda e, p=p, half=half: e.copy(out=dst_fn(half), in_=p[:, :]), R=[pk, dkey], W=[dkey])
                    else:
                        T.op('dve', lambda e, p=p, half=half: e.tensor_copy(out=dst_fn(half), in_=p[:, :]), R=[pk, dkey], W=[dkey])

            def key_norms(cT, cTk, kr_ap, krk, nk, rk_dst, rkk):
                pa, pak = nextA()
                pb, pbk = nextA()
                for hb, (pp, ppk) in enumerate(((pa, pak), (pb, pbk))):
                    for kc in range(2):
                        T.op('pe', lambda e, kc=kc, pp=pp, hb=hb: e.matmul(pp[0:nk, :], lhsT=cT[:, kc, :],
                                                                          rhs=w_uk_b[:, kc, hb * 512:(hb + 1) * 512],
                                                                          start=(kc == 0), stop=(kc == 1)),
                             R=[cTk, 'w_uk_b'], W=[ppk])
                for h in range(8):
                    pp, ppk = (pa, pak) if h < 4 else (pb, pbk)
                    hh = h % 4
                    T.op('act', lambda e, pp=pp, hh=hh, h=h: e.activation(out=junk[0:nk, 0:128], in_=pp[0:nk, hh * 128:(hh + 1) * 128],
                                                                           func=AF.Square, accum_out=sm[0:nk, 8 + h:9 + h]),
                         R=[ppk, 'sm'], W=['junk', 'sm'])
                T.op('act', lambda e: e.activation(out=junk[0:nk, 0:64], in_=kr_ap, func=AF.Square, accum_out=sm[0:nk, 16:17]),
                     R=[krk, 'sm'], W=['junk', 'sm'])
                T.op('dve', lambda e: e.tensor_scalar(out=sm[0:nk, 8:16], in0=sm[0:nk, 8:16], scalar1=sm[0:nk, 16:17], scalar2=None,
                                                      op0=ALU.add), R=['sm'], W=['sm'])
                T.op('act', lambda e: e.activation(out=sm[0:nk, 8:16], in_=sm[0:nk, 8:16], func=AF.Sqrt, scale=1.0 / 192, bias=EPS),
                     R=['sm'], W=['sm'])
                T.op('dve', lambda e: e.reciprocal(out=rk_dst, in_=sm[0:nk, 8:16]), R=['sm', rkk], W=[rkk])

            def latent_post(pck, pckk, row0, out_ckv, out_kr, orow0, cT_dst, cTk, kT_dst, kTk, c1_dst, c1k, rk_dst):
                T.dma('sp', 'rld', lambda e: e.dma_start(out=ropet[:], in_=rope[row0:row0 + 128, :]), W=['ropet'])
                rstd_of(pck[:, 0:256], 256, sm[:, 2:3], pckk)
                T.op('dve', lambda e: e.scalar_tensor_tensor(out=ckvn_f[:], in0=pck[:, 0:256], scalar=sm[:, 2:3], in1=gkv_bc[:],
                                                             op0=ALU.mult, op1=ALU.mult), R=[pckk, 'sm', 'gkv_bc'], W=['ckvn_f'])
                T.op('dve', lambda e: e.tensor_tensor(out=krr_f[:], in0=pck[:, 256:320], in1=ropet[:, 0:64], op=ALU.mult),
                     R=[pckk, 'ropet'], W=['krr_f'])
                T.op('dve', lambda e: e.tensor_tensor(out=krt[:, 0:32], in0=pck[:, 288:320], in1=ropet[:, 64:96], op=ALU.mult),
                     R=[pckk, 'ropet'], W=['krt'])
                T.op('dve', lambda e: e.tensor_tensor(out=krt[:, 32:64], in0=pck[:, 256:288], in1=ropet[:, 96:128], op=ALU.mult),
                     R=[pckk, 'ropet', 'krt'], W=['krt'])
                T.op('dve', lambda e: e.tensor_tensor(out=krr_f[:], in0=krr_f[:], in1=krt[:], op=ALU.add), R=['krr_f', 'krt'], W=['krr_f'])
                if out_ckv is not None:
                    T.dma('sp', 'ost', lambda e: e.dma_start(out=out_ckv[orow0:orow0 + 128, :], in_=ckvn_f[:]), R=['ckvn_f'])
                    T.dma('sp', 'ost', lambda e: e.dma_start(out=out_kr[orow0:orow0 + 128, :], in_=krr_f[:]), R=['krr_f'])
                T.op('pool', lambda e: e.tensor_copy(out=ckvn_b[:], in_=ckvn_f[:]), R=['ckvn_f'], W=['ckvn_b'])
                T.op('pool', lambda e: e.tensor_copy(out=krr_b[:], in_=krr_f[:]), R=['krr_f'], W=['krr_b'])
                if c1_dst is not None:
                    T.op('pool', lambda e: e.tensor_copy(out=c1_dst, in_=ckvn_f[:]), R=['ckvn_f', c1k], W=[c1k])
                p, pk = nextT()
                for kc in range(2):
                    T.op('pe', lambda e, kc=kc: e.transpose(out=p[:, kc * 128:(kc + 1) * 128], in_=ckvn_b[:, kc * 128:(kc + 1) * 128],
                                                            identity=identb[:]), R=['ckvn_b', 'identb'], W=[pk])
                T.op('pe', lambda e: e.transpose(out=p[0:64, 256:384], in_=krr_b[:], identity=identb[:]), R=['krr_b', 'identb'], W=[pk])
                T.op('act', lambda e: e.copy(out=cT_dst, in_=p[:, 0:256].rearrange("p (a b) -> p a b", a=2)), R=[pk, cTk], W=[cTk])
                T.op('act', lambda e: e.copy(out=kT_dst, in_=p[0:64, 256:384]), R=[pk, kTk], W=[kTk])
                if rk_dst is not None:
                    key_norms(cT_dst, cTk, krr_f[:], 'krr_f', 128, rk_dst, 'rk_all')

            def u_transpose(src_b_ap, skey):
                p, pk = nextT()
                for k in range(4):
                    T.op('pe', lambda e, k=k: e.transpose(out=p[:, k * 128:(k + 1) * 128], in_=src_b_ap[:, k * 128:(k + 1) * 128],
                                                          identity=identb[:]), R=[skey, 'identb'], W=[pk])
                T.op('dve', lambda e: e.tensor_copy(out=uT[:].rearrange("p a b -> p (a b)"), in_=p[:, 0:512]), R=[pk], W=['uT'])

            for t in range(NPREV):
                norm_transpose(xall[t * 128:(t + 1) * 128, :],
                               lambda half: xnT1[:, half * 8:(half + 1) * 8, :].rearrange("p a b -> p (a b)"), 'xnT1')
                pu, puk = nextA()
                for kc in range(16):
                    T.op('pe', lambda e, kc=kc: e.matmul(pu[:, 0:512], lhsT=xnT1[:, kc, :], rhs=w_up_b[:, kc, 0:512],
                                                         start=(kc == 0), stop=(kc == 15)), R=['xnT1', 'w_up_b'], W=[puk])
                pc, pck = nextA()
                for kc in range(16):
                    T.op('pe', lambda e, kc=kc: e.matmul(pc[:, 0:320], lhsT=xnT1[:, kc, :], rhs=w_up_b[:, kc, 512:832],
                                                         start=(kc == 0), stop=(kc == 15)), R=['xnT1', 'w_up_b'], W=[pck])
                T.op('act', lambda e: e.copy(out=u_bg[:, 0, :], in_=pu[:, 0:512]), R=[puk], W=['u_bg'])
                latent_post(pc, pck, t * 128, None, None, 0, ckvT_all[:, :, t * 128:(t + 1) * 128], 'ckvT_all',
                            krT_all[0:64, t * 128:(t + 1) * 128], 'krT_all', ckv1_all[:, t, :], 'ckv1_all', rk_all[:, t, :])
                u_transpose(u_bg[:, 0, :], 'u_bg')
                for sub in range(128 // LS):
                    ssm_step(lambda ct, sub=sub: uT[:, ct, sub * LS:(sub + 1) * LS], LS, 1, LS, 0, False)
            dbg('hre', hre[:, :, 0], [128, 16], ['hst']); dbg('him', him[:, :, 0], [128, 16], ['hst'])
            T.barrier()

            chk('P')
            memKT_b = sb("memKT_b", [128, 4, 256], BF16)
            memV_b = sb("memV_b", [128, 2, 512], BF16)
            wblk = sb("wblk", [128, 16, 256], BF16)
            yf = sb("yf", [128, 512]); yg = sb("yg", [128, 512])
            ob = sb("ob", [128, 512], BF16)

            def stream_wblock(w_d, c0, ncols, gcol):
                for q4 in range(4):
                    T.dma('sp', 'wst', lambda e, q4=q4: e.dma_start(
                        out=stg[:, 0:4 * ncols].rearrange("p (k c) -> p k c", k=4),
                        in_=w_d[q4 * 512:(q4 + 1) * 512, c0:c0 + ncols].rearrange("(k p) c -> p k c", p=128)), W=['stg'])
                    for k in range(4):
                        kc = q4 * 4 + k
                        eng = 'dve' if (k % 2 == 0) else 'pool'
                        T.op(eng, lambda e, k=k, kc=kc: e.tensor_scalar(out=wblk[:, kc, 0:ncols], in0=stg[:, k * ncols:(k + 1) * ncols],
                                                                        scalar1=gcol[:, kc:kc + 1], scalar2=None, op0=ALU.mult),
                             R=['stg', 'wblk'], W=['wblk'])
                return wblk, 'wblk'

            for mt in range(2):
                norm_transpose(mem_d[mt * 128:(mt + 1) * 128, :],
                               lambda half, mt=mt: memT[:, mt, half * 8:(half + 1) * 8, :].rearrange("p a b -> p (a b)"), 'memT')
            chk('M0')
            for which, w_d in enumerate((w_mk_d, w_mv_d)):
                for cb in range(2):
                    wb, wk = stream_wblock(w_d, cb * 256, 256, gcol_mem)
                    chk('M1_%d_%d' % (which, cb))
                    for mt in range(2):
                        p, pk = nextA()
                        for kc in range(16):
                            T.op('pe', lambda e, kc=kc, mt=mt, p=p: e.matmul(p[:, 0:256], lhsT=memT[:, mt, kc, :], rhs=wb[:, kc, 0:256],
                                                                             start=(kc == 0), stop=(kc == 15)), R=['memT', wk], W=[pk])
                        chk('M2')
                        if which == 1:
                            T.op('act', lambda e, p=p: e.copy(out=yf[:, 0:256], in_=p[:, 0:256]), R=[pk], W=['yf'])
                            T.op('dve', lambda e, p=p, mt=mt, cb=cb: e.tensor_copy(out=memV_b[:, mt, cb * 256:(cb + 1) * 256], in_=p[:, 0:256]),
                                 R=[pk, 'memV_b'], W=['memV_b'])
                            T.dma('sp', 'ost', lambda e, mt=mt, cb=cb: e.dma_start(out=memv_o[mt * 128:(mt + 1) * 128, cb * 256:(cb + 1) * 256],
                                                                                   in_=yf[:, 0:256]), R=['yf'])
                        else:
                            for hh in range(2):
                                rstd_of(p[:, hh * 128:(hh + 1) * 128], 128, sm[:, 3:4], pk)
                                T.op('dve', lambda e, p=p, hh=hh: e.scalar_tensor_tensor(out=yf[:, hh * 128:(hh + 1) * 128],
                                                                                         in0=p[:, hh * 128:(hh + 1) * 128], scalar=sm[:, 3:4],
                                                                                         in1=gmk_bc[:], op0=ALU.mult, op1=ALU.mult),
                                     R=[pk, 'sm', 'gmk_bc', 'yf'], W=['yf'])
                            chk('M3')
                            T.op('dve', lambda e: e.tensor_copy(out=ob[:, 0:256], in_=yf[:, 0:256]), R=['yf'], W=['ob'])
                            T.dma('sp', 'ost', lambda e, mt=mt, cb=cb: e.dma_start(out=memk_o[mt * 128:(mt + 1) * 128, cb * 256:(cb + 1) * 256],
                                                                                   in_=yf[:, 0:256]), R=['yf'])
                            chk('M4')
                            pt_, ptk = nextT()
                            for hh in range(2):
                                T.op('pe', lambda e, hh=hh, pt_=pt_: e.transpose(out=pt_[:, hh * 128:(hh + 1) * 128], in_=ob[:, hh * 128:(hh + 1) * 128],
                                                                                 identity=identb[:]), R=['ob', 'identb'], W=[ptk])
                            T.op('act', lambda e, cb=cb, mt=mt, pt_=pt_: e.copy(out=memKT_b[:, cb * 2:cb * 2 + 2, mt * 128:(mt + 1) * 128],
                                                                                in_=pt_[:, 0:256].rearrange("p (a b) -> p a b", a=2)),
                                 R=[ptk, 'memKT_b'], W=['memKT_b'])
                            chk('M5')
                            if cb == 1 and mt == 1:
                                chk('M6')
            T.barrier()

            chk('M')
            ssq = sb("ssq", [128, GT, 8])
            qf = xt[:, 0:1536].rearrange("p (h c) -> p h c", h=8); qs_b = sb("qs_b", [128, 8, 192], BF16)
            wk6 = sb("wk6", [128, 1536])
            qsq = wk6[:, :].rearrange("p (h c) -> p h c", h=8)
            ymT8 = wk6[:, 0:1024].rearrange("p (h c) -> p h c", h=8)
            ymT4 = wk6[:, 0:4 * GC].rearrange("p (h c) -> p h c", h=4)
            cq_b = sb("cq_b", [128, 512], BF16); cqT = sb("cqT", [128, 4, 128], BF16)
            qabsT = sb("qabsT", [128, 2, 8, 128], BF16)
            qav = qabsT[:].rearrange("p a h c -> p a (h c)")
            PT = sb("PT", [128, GC], BF16)
            OTn = sb("OTn", [128, 2, GC], BF16)
            rcp = sb("rcp", [128, GC])
            xres = sb("xres", [128, 256]); yout = sb("yout", [128, 256])
            idx_b = sb("idx_b", [128, NPG], I32); idxf = sb("idxf", [128, NPG])
            iot = sb("iot", [128, 1], I32); iotf = sb("iotf", [128, 1])
            pgc = [sb("pgc%d" % i, [128, 256]) for i in range(2)]
            pgk = [sb("pgk%d" % i, [128, 64]) for i in range(2)]
            pgcb = sb("pgcb", [128, 256], BF16); pgkb = sb("pgkb", [128, 64], BF16)
            pgT = sb("pgT", [128, 2, 128], BF16); pgkT = sb("pgkT", [64, 128], BF16)
            rkp = sb("rkp", [128, 8]); scf = sb("scf", [128, 64])
            mini = sb("mini", [8, 256], BF16)
            mkb = c1flat[:, 0:1024].rearrange("p (t c) -> p t c", t=2); mvb = c1flat[:, 1024:2048].rearrange("p (t c) -> p t c", t=2)
            mkT_s = c1flat[:, 2048:3072].rearrange("p (h c) -> p h c", h=4)
            mcf = [xt[:, 0:1024].rearrange("p (t c) -> p t c", t=2), xt[:, 1024:2048].rearrange("p (t c) -> p t c", t=2)]

            def in_proj_block(ng, c0, ncols, consume):
                stream_wblock(w_in, c0, ncols, gcol_in)
                for ti in range(ng):
                    p, pk = nextA()
                    for kc in range(16):
                        T.op('pe', lambda e, kc=kc, ti=ti, p=p: e.matmul(p[:, 0:ncols], lhsT=xnT_g[:, ti, kc, :], rhs=wblk[:, kc, 0:ncols],
                                                                         start=(kc == 0), stop=(kc == 15)), R=['xnT_g', 'wblk'], W=[pk])
                    consume(ti, p, pk)

            def mem_attend(ncols, qcols, KT, KTk, V, Vk, out_fn):
                for h in range(4):
                    acc, acck = psC[0], 'psC0'
                    lb, lbk = psC[1], 'psC1'
                    for kt in range(2):
                        p, pk = nextA()
                        T.op('pe', lambda e, kt=kt, h=h, p=p: e.matmul(p[:, 0:ncols], lhsT=KT[:, h, kt * 128:(kt + 1) * 128],
                                                                       rhs=QmT[:, h, qcols], start=True, stop=True), R=[KTk, 'QmT'], W=[pk])
                        T.op('act', lambda e, p=p: e.activation(out=PT[:, 0:ncols], in_=p[:, 0:ncols], func=AF.Exp), R=[pk], W=['PT'])
                        T.op('pe', lambda e, kt=kt, h=h: e.matmul(acc[:, 0:ncols], lhsT=V[:, kt, h * 128:(h + 1) * 128], rhs=PT[:, 0:ncols],
                                                                  start=(kt == 0), stop=(kt == 1)), R=[Vk, 'PT'], W=[acck])
                        T.op('pe', lambda e, kt=kt: e.matmul(lb[:, 0:ncols], lhsT=onesb[:], rhs=PT[:, 0:ncols],
                                                             start=(kt == 0), stop=(kt == 1)), R=['onesb', 'PT'], W=[lbk])
                    T.op('dve', lambda e: e.reciprocal(out=rcp[:, 0:ncols], in_=lb[:, 0:ncols]), R=[lbk], W=['rcp'])
                    T.op('dve', lambda e, h=h: e.tensor_tensor(out=out_fn(h), in0=acc[:, 0:ncols], in1=rcp[:, 0:ncols], op=ALU.mult),
                         R=[acck, 'rcp', 'wk6'], W=['wk6'])

            def finish_branch(ti, src_ap, skey, n, gate0):
                rstd_of(src_ap, n, sm[:, 4:5], skey)
                T.op('dve', lambda e: e.scalar_tensor_tensor(out=gates[:, ti, gate0:gate0 + n], in0=src_ap, scalar=sm[:, 4:5],
                                                             in1=gates[:, ti, gate0:gate0 + n], op0=ALU.mult, op1=ALU.mult),
                     R=[skey, 'sm', 'gates'], W=['gates'])

            def mem_branch_finish(ng, get_cols):
                for ti in range(ng):
                    p, pk = nextA()
                    for h in range(4):
                        T.op('pe', lambda e, h=h, ti=ti, p=p: e.transpose(out=p[:, h * 128:(h + 1) * 128], in_=get_cols(h, ti), identity=identf[:]),
                             R=['wk6', 'identf'], W=[pk])
                    T.op('act', lambda e, p=p: e.copy(out=yf[:], in_=p[:, 0:512]), R=[pk], W=['yf'])
                    finish_branch(ti, yf[:], 'yf', 512, 1536)

            def q_path(ti, row0):
                T.dma('sp', 'rld', lambda e: e.dma_start(out=ropet[:], in_=rope[row0:row0 + 128, :]), W=['ropet'])
                rstd_of(yf[:], 512, sm[:, 5:6], 'yf')
                T.op('dve', lambda e: e.tensor_scalar(out=cq_b[:], in0=yf[:], scalar1=sm[:, 5:6], scalar2=None, op0=ALU.mult),
                     R=['yf', 'sm'], W=['cq_b'])
                p, pk = nextT()
                for k in range(4):
                    T.op('pe', lambda e, k=k, p=p: e.transpose(out=p[:, k * 128:(k + 1) * 128], in_=cq_b[:, k * 128:(k + 1) * 128],
                                                               identity=identb[:]), R=['cq_b', 'identb'], W=[pk])
                T.op('dve', lambda e, p=p: e.tensor_copy(out=cqT[:].rearrange("p a b -> p (a b)"), in_=p[:, 0:512]), R=[pk], W=['cqT'])
                for blk in range(3):
                    pq, pqk = nextA()
                    for k in range(4):
                        T.op('pe', lambda e, k=k, blk=blk, pq=pq: e.matmul(pq[:, 0:512], lhsT=cqT[:, k, :],
                                                                          rhs=w_uq_b[:, k, blk * 512:(blk + 1) * 512],
                                                                          start=(k == 0), stop=(k == 3)), R=['cqT', 'w_uq_b'], W=[pqk])
                    T.op('act', lambda e, blk=blk, pq=pq: e.copy(out=qf[:].rearrange("p h c -> p (h c)")[:, blk * 512:(blk + 1) * 512],
                                                                 in_=pq[:, 0:512]), R=[pqk, 'xt'], W=['xt'])
                cc = ropet[:, 0:64].unsqueeze(1).to_broadcast([128, 8, 64])
                ns = ropet[:, 64:96].unsqueeze(1).to_broadcast([128, 8, 32])
                ps_ = ropet[:, 96:128].unsqueeze(1).to_broadcast([128, 8, 32])
                T.op('dve', lambda e: e.tensor_tensor(out=qsq[:, :, 0:32], in0=qf[:, :, 160:192], in1=ns, op=ALU.mult), R=['xt', 'ropet'], W=['wk6'])
                T.op('dve', lambda e: e.tensor_tensor(out=qsq[:, :, 32:64], in0=qf[:, :, 128:160], in1=ps_, op=ALU.mult), R=['xt', 'ropet', 'wk6'], W=['wk6'])
                T.op('dve', lambda e: e.tensor_tensor(out=qf[:, :, 128:192], in0=qf[:, :, 128:192], in1=cc, op=ALU.mult), R=['xt', 'ropet', 'wk6'], W=['xt'])
                T.op('dve', lambda e: e.tensor_tensor(out=qf[:, :, 128:192], in0=qf[:, :, 128:192], in1=qsq[:, :, 0:64], op=ALU.add), R=['xt', 'wk6'], W=['xt'])
                T.op('pool', lambda e: e.tensor_tensor(out=qsq[:], in0=qf[:], in1=qf[:], op=ALU.mult), R=['xt', 'wk6'], W=['wk6'])
                T.op('dve', lambda e: e.tensor_reduce(out=sm[:, 24:32], in_=qsq[:], axis=AX.X, op=ALU.add), R=['wk6', 'sm'], W=['sm'])
                T.op('act', lambda e: e.activation(out=sm[:, 24:32], in_=sm[:, 24:32], func=AF.Sqrt, scale=1.0 / 192, bias=EPS), R=['sm'], W=['sm'])
                T.op('dve', lambda e: e.reciprocal(out=sm[:, 24:32], in_=sm[:, 24:32]), R=['sm'], W=['sm'])
                T.op('dve', lambda e: e.tensor_tensor(out=qf[:], in0=qf[:], in1=sm[:, 24:32].unsqueeze(2).to_broadcast([128, 8, 192]), op=ALU.mult),
                     R=['xt', 'sm'], W=['xt'])
                T.op('dve', lambda e: e.tensor_tensor(out=qs_b[:], in0=qf[:], in1=gqk_bc[:].unsqueeze(1).to_broadcast([128, 8, 192]), op=ALU.mult),
                     R=['xt', 'gqk_bc'], W=['qs_b'])
                for h in range(8):
                    p, pk = nextT()
                    T.op('pe', lambda e, h=h, p=p: e.transpose(out=p[:, 0:128], in_=qs_b[:, h, 0:128], identity=identb[:]), R=['qs_b', 'identb'], W=[pk])
                    T.op('pe', lambda e, h=h, p=p: e.transpose(out=p[0:64, 128:256], in_=qs_b[:, h, 128:192], identity=identb[:]), R=['qs_b', 'identb'], W=[pk])
                    T.op('act', lambda e, h=h, p=p: e.copy(out=QTn[:, h, ti * 128:(ti + 1) * 128], in_=p[:, 0:128]), R=[pk, 'QTn'], W=['QTn'])
                    T.op('dve', lambda e, h=h, p=p: e.tensor_copy(out=QTr[0:64, h, ti * 128:(ti + 1) * 128], in_=p[0:64, 128:256]), R=[pk, 'QTr'], W=['QTr'])

            def prompt_attention(ng, ncg, own0):
                for h in range(8):
                    for kc in range(2):
                        p, pk = nextA()
                        T.op('pe', lambda e, kc=kc, h=h, p=p: e.matmul(p[:, 0:ncg], lhsT=w_ukT_b[:, h, kc * 128:(kc + 1) * 128],
                                                                       rhs=QTn[:, h, 0:ncg], start=True, stop=True), R=['w_ukT_b', 'QTn'], W=[pk])
                        T.op('act', lambda e, kc=kc, p=p: e.copy(out=qav[:, kc, 0:ncg], in_=p[:, 0:ncg]), R=[pk, 'qabsT'], W=['qabsT'])
                    nkt = NPREV + own0 + ng
                    for kt in range(nkt):
                        rel = kt - (NPREV + own0)
                        c0 = max(rel, 0) * 128
                        ncol = ncg - c0
                        kcols = slice(kt * 128, (kt + 1) * 128)
                        p, pk = nextA()
                        for kc in range(2):
                            T.op('pe', lambda e, kc=kc, p=p, kcols=kcols, c0=c0, ncol=ncol: e.matmul(
                                p[:, 0:ncol], lhsT=ckvT_all[:, kc, kcols], rhs=qav[:, kc, c0:ncg], start=(kc == 0), stop=False),
                                R=['ckvT_all', 'qabsT'], W=[pk])
                        T.op('pe', lambda e, h=h, p=p, kcols=kcols, c0=c0, ncol=ncol: e.matmul(
                            p[:, 0:ncol], lhsT=krT_all[0:64, kcols], rhs=QTr[0:64, h, c0:ncg], start=False, stop=True),
                            R=['krT_all', 'QTr'], W=[pk])
                        T.op('act', lambda e, kt=kt, h=h, p=p, c0=c0, ncol=ncol: e.activation(
                            out=PT[:, c0:ncg], in_=p[:, 0:ncol], func=AF.Exp, scale=rk_all[:, kt, h:h + 1], bias=kbias[:, kt:kt + 1]),
                            R=[pk, 'rk_all', 'kbias'], W=['PT'])
                        if rel >= 0:
                            T.op('pool', lambda e, c0=c0: e.tensor_tensor(out=PT[:, c0:c0 + 128], in0=PT[:, c0:c0 + 128], in1=trib[:], op=ALU.mult),
                                 R=['PT', 'trib'], W=['PT'])
                        first, last = (kt == 0), (kt == nkt - 1)
                        for kc in range(2):
                            T.op('pe', lambda e, kc=kc, kt=kt, c0=c0, first=first, last=last: e.matmul(
                                psC[kc][:, c0:ncg], lhsT=ckv1_all[:, kt, kc * 128:(kc + 1) * 128], rhs=PT[:, c0:ncg], start=first, stop=last),
                                R=['ckv1_all', 'PT'], W=['psC%d' % kc])
                        T.op('pe', lambda e, c0=c0, first=first, last=last: e.matmul(psC[2][:, c0:ncg], lhsT=onesb[:], rhs=PT[:, c0:ncg],
                                                                                     start=first, stop=last),
                             R=['onesb', 'PT'], W=['psC2'])
                    T.op('dve', lambda e: e.reciprocal(out=rcp[:, 0:ncg], in_=psC[2][:, 0:ncg]), R=['psC2'], W=['rcp'])
                    for kc in range(2):
                        T.op('dve', lambda e, kc=kc: e.tensor_tensor(out=OTn[:, kc, 0:ncg], in0=psC[kc][:, 0:ncg], in1=rcp[:, 0:ncg], op=ALU.mult),
                             R=['psC%d' % kc, 'rcp', 'OTn'], W=['OTn'])
                    for ti in range(ng):
                        p, pk = nextA()
                        for kc in range(2):
                            T.op('pe', lambda e, kc=kc, ti=ti, h=h, p=p: e.matmul(p[:, 0:128], lhsT=OTn[:, kc, ti * 128:(ti + 1) * 128],
                                                                                 rhs=w_uv_b[:, kc, h * 128:(h + 1) * 128],
                                                                                 start=(kc == 0), stop=(kc == 1)), R=['OTn', 'w_uv_b'], W=[pk])
                        T.op('dve', lambda e, ti=ti, h=h, p=p: e.tensor_copy(out=ymla[:, ti, h * 128:(h + 1) * 128], in_=p[:, 0:128]),
                             R=[pk, 'ymla'], W=['ymla'])
                        T.op('act', lambda e, ti=ti, h=h, p=p: e.activation(out=junk[:, 0:128], in_=p[:, 0:128], func=AF.Square,
                                                                            accum_out=ssq[:, ti, h:h + 1]), R=[pk, 'ssq'], W=['junk', 'ssq'])
                mem_attend(ncg, slice(0, ncg), memKT_b, 'memKT_b', memV_b, 'memV_b', lambda h: ymT4[:, h, 0:ncg])
                mem_branch_finish(ng, lambda h, ti: ymT4[:, h, ti * 128:(ti + 1) * 128])

            def sample_attention():
                T.op('pool', lambda e: e.iota(iot[:], pattern=[[0, 1]], base=0, channel_multiplier=1), W=['iot'])
                T.op('dve', lambda e: e.tensor_copy(out=iotf[:], in_=iot[:]), R=['iot'], W=['iotf'])
                for h in range(8):
                    for kc in range(2):
                        p, pk = nextA()
                        T.op('pe', lambda e, kc=kc, h=h, p=p: e.matmul(p[:, 0:128], lhsT=w_ukT_b[:, h, kc * 128:(kc + 1) * 128],
                                                                       rhs=QTn[:, h, 0:128], start=True, stop=True), R=['w_ukT_b', 'QTn'], W=[pk])
                        T.op('act', lambda e, kc=kc, h=h, p=p: e.copy(out=qabsT[:, kc, h, :], in_=p[:, 0:128]), R=[pk, 'qabsT'], W=['qabsT'])
                acc = psC[0]
                o0 = acc[:, 0:64]; o1 = acc[:, 64:128]; lb = acc[:, 128:192]
                for b in range(NSEQ):
                    bc = slice(b * 8, (b + 1) * 8)
                    T.dma('sp', 'c0', lambda e, b=b: e.dma_start(out=idx_b[:], in_=ptab[b].partition_broadcast(128)), W=['idx_b'])
                    T.op('dve', lambda e: e.tensor_copy(out=idxf[:], in_=idx_b[:]), R=['idx_b'], W=['idxf'])
                    T.op('dve', lambda e: e.tensor_scalar(out=idxf[:], in0=idxf[:], scalar1=128.0, scalar2=iotf[:, 0:1], op0=ALU.mult, op1=ALU.add),
                         R=['idxf', 'iotf'], W=['idxf'])
                    T.op('dve', lambda e: e.tensor_copy(out=idx_b[:], in_=idxf[:]), R=['idxf'], W=['idx_b'])

                    def attend(cT, cTk, krT_ap, krTk, c1, c1k, rk_ap, rkk, nk, first, last, mask):
                        p, pk = nextA()
                        pv = p[0:nk, 0:64]
                        for kc in range(2):
                            T.op('pe', lambda e, kc=kc: e.matmul(pv, lhsT=cT[:, kc, :], rhs=qabsT[:, kc, :, bc], start=(kc == 0), stop=False),
                                 R=[cTk, 'qabsT'], W=[pk])
                        T.op('pe', lambda e: e.matmul(pv, lhsT=krT_ap, rhs=QTr[0:64, :, bc], start=False, stop=True), R=[krTk, 'QTr'], W=[pk])
                        T.op('dve', lambda e: e.tensor_tensor(out=scf[0:nk, :].rearrange("p (h q) -> p h q", q=8),
                                                              in0=pv.rearrange("p (h q) -> p h q", q=8),
                                                              in1=rk_ap.unsqueeze(2).to_broadcast([nk, 8, 8]), op=ALU.mult),
                             R=[pk, rkk], W=['scf'])
                        T.op('act', lambda e: e.activation(out=PT[0:nk, 0:64], in_=scf[0:nk, :], func=AF.Exp), R=['scf'], W=['PT'])
                        if mask:
                            T.op('dve', lambda e: e.tensor_tensor(out=PT[0:nk, 0:64].rearrange("p (h q) -> p h q", q=8),
                                                                  in0=PT[0:nk, 0:64].rearrange("p (h q) -> p h q", q=8),
                                                                  in1=trib[0:nk, 0:8].unsqueeze(1).to_broadcast([nk, 8, 8]), op=ALU.mult),
                                 R=['PT', 'trib'], W=['PT'])
                        T.op('pe', lambda e: e.matmul(o0, lhsT=c1[0:nk, 0:128], rhs=PT[0:nk, 0:64], start=first, stop=last,
                                                      skip_group_check=True), R=[c1k, 'PT'], W=['psC0'])
                        T.op('pe', lambda e: e.matmul(o1, lhsT=c1[0:nk, 128:256], rhs=PT[0:nk, 0:64], start=False, stop=last,
                                                      skip_group_check=True), R=[c1k, 'PT'], W=['psC0'])
                        T.op('pe', lambda e: e.matmul(lb, lhsT=onesb[0:nk, :], rhs=PT[0:nk, 0:64], start=False, stop=last,
                                                      skip_group_check=True), R=['onesb', 'PT'], W=['psC0'])

                    for pg in range(NPG):
                        s = pg % 2
                        T.dma('pool', 'pgc%d' % s, lambda e, s=s, pg=pg: e.indirect_dma_start(
                            out=pgc[s][:], out_offset=None, in_=cckv[:, :],
                            in_offset=bass.IndirectOffsetOnAxis(ap=idx_b[:, pg:pg + 1], axis=0)), R=['idx_b'], W=['pgc%d' % s])
                        T.dma('pool', 'pgk%d' % s, lambda e, s=s, pg=pg: e.indirect_dma_start(
                            out=pgk[s][:], out_offset=None, in_=ckr[:, :],
                            in_offset=bass.IndirectOffsetOnAxis(ap=idx_b[:, pg:pg + 1], axis=0)), R=['idx_b'], W=['pgk%d' % s])
                        T.op('dve', lambda e, s=s: e.tensor_copy(out=pgcb[:], in_=pgc[s][:]), R=['pgc%d' % s], W=['pgcb'])
                        T.op('dve', lambda e, s=s: e.tensor_copy(out=pgkb[:], in_=pgk[s][:]), R=['pgk%d' % s], W=['pgkb'])
                        p, pk = nextT()
                        for kc in range(2):
                            T.op('pe', lambda e, kc=kc, p=p: e.transpose(out=p[:, kc * 128:(kc + 1) * 128], in_=pgcb[:, kc * 128:(kc + 1) * 128],
                                                                         identity=identb[:]), R=['pgcb', 'identb'], W=[pk])
                        T.op('pe', lambda e, p=p: e.transpose(out=p[0:64, 256:384], in_=pgkb[:], identity=identb[:]), R=['pgkb', 'identb'], W=[pk])
                        T.op('act', lambda e, p=p: e.copy(out=pgT[:], in_=p[:, 0:256].rearrange("p (a b) -> p a b", a=2)), R=[pk], W=['pgT'])
                        T.op('act', lambda e, p=p: e.copy(out=pgkT[:], in_=p[0:64, 256:384]), R=[pk], W=['pgkT'])
                        key_norms(pgT[:], 'pgT', pgk[s][:], 'pgk%d' % s, 128, rkp[:], 'rkp')
                        attend(pgT[:], 'pgT', pgkT[:], 'pgkT', pgcb, 'pgcb', rkp[:], 'rkp', 128, pg == 0, False, False)
                    p, pk = nextT()
                    for kc in range(2):
                        T.op('pe', lambda e, kc=kc, p=p: e.transpose(out=p[0:8, kc * 128:(kc + 1) * 128], in_=ckvT_s[:, kc, bc], identity=identb[:]),
                             R=['ckvT_s', 'identb'], W=[pk])
                    T.op('pe', lambda e, p=p: e.transpose(out=p[0:8, 256:320], in_=krT_s[0:64, bc], identity=identb[0:64, 0:64]),
                         R=['krT_s', 'identb'], W=[pk])
                    T.op('act', lambda e, p=p: e.copy(out=mini[:], in_=p[0:8, 0:256]), R=[pk], W=['mini'])
                    T.op('dve', lambda e, p=p: e.tensor_copy(out=scf[0:8, 0:64], in_=p[0:8, 256:320]), R=[pk], W=['scf'])
                    key_norms(ckvT_s[:, :, bc], 'ckvT_s', scf[0:8, 0:64], 'scf', 8, rkp[0:8, :], 'rkp')
                    attend(ckvT_s[:, :, bc], 'ckvT_s', krT_s[0:64, bc], 'krT_s', mini, 'mini', rkp[0:8, :], 'rkp', 8, NPG == 0, True, True)
                    T.op('dve', lambda e: e.reciprocal(out=rcp[:, 0:64], in_=lb), R=['psC0'], W=['rcp'])
                    T.op('dve', lambda e: e.tensor_tensor(out=OTn[:, 0, 0:64], in0=o0, in1=rcp[:, 0:64], op=ALU.mult), R=['psC0', 'rcp'], W=['OTn'])
                    T.op('dve', lambda e: e.tensor_tensor(out=OTn[:, 1, 0:64], in0=o1, in1=rcp[:, 0:64], op=ALU.mult), R=['psC0', 'rcp', 'OTn'], W=['OTn'])
                    p, pk = nextA()
                    for h in range(8):
                        for kc in range(2):
                            T.op('pe', lambda e, kc=kc, h=h, p=p: e.matmul(p[:, h * 8:(h + 1) * 8], lhsT=w_uv_b[:, kc, h * 128:(h + 1) * 128],
                                                                           rhs=OTn[:, kc, h * 8:(h + 1) * 8], start=(kc == 0), stop=(kc == 1)),
                                 R=['w_uv_b', 'OTn'], W=[pk])
                    T.op('act', lambda e, p=p: e.copy(out=ymT8[:, :, bc], in_=p[:, 0:64].rearrange("p (h q) -> p h q", q=8)), R=[pk, 'wk6'], W=['wk6'])
                for h in range(8):
                    p, pk = nextA()
                    T.op('pe', lambda e, h=h, p=p: e.transpose(out=p[:, 0:128], in_=ymT8[:, h, :], identity=identf[:]), R=['wk6', 'identf'], W=[pk])
                    T.op('dve', lambda e, h=h, p=p: e.tensor_copy(out=ymla[:, 0, h * 128:(h + 1) * 128], in_=p[:, 0:128]), R=[pk, 'ymla'], W=['ymla'])
                    T.op('act', lambda e, h=h, p=p: e.activation(out=junk[:, 0:128], in_=p[:, 0:128], func=AF.Square, accum_out=ssq[:, 0, h:h + 1]),
                         R=[pk, 'ssq'], W=['junk', 'ssq'])
                for b in range(NSEQ):
                    bc = slice(b * 8, (b + 1) * 8)
                    T.dma('sp', 'mck', lambda e, b=b: e.dma_start(out=mcf[0], in_=cmk[b * 256:(b + 1) * 256, :].rearrange("(t p) c -> p t c", p=128)),
                          W=['xt'])
                    T.dma('sp', 'mcv', lambda e, b=b: e.dma_start(out=mcf[1], in_=cmv[b * 256:(b + 1) * 256, :].rearrange("(t p) c -> p t c", p=128)),
                          W=['xt'])
                    T.op('pool', lambda e: e.tensor_copy(out=mkb[:], in_=mcf[0]), R=['xt'], W=['mkb'])
                    T.op('pool', lambda e: e.tensor_copy(out=mvb[:], in_=mcf[1]), R=['xt'], W=['mvb'])
                    for kt in range(2):
                        p, pk = nextT()
                        for h in range(4):
                            T.op('pe', lambda e, kt=kt, h=h, p=p: e.transpose(out=p[:, h * 128:(h + 1) * 128], in_=mkb[:, kt, h * 128:(h + 1) * 128],
                                                                              identity=identb[:]), R=['mkb', 'identb'], W=[pk])
                        T.op('act', lambda e, kt=kt, p=p: e.copy(out=mkT_s[:, :, kt * 128:(kt + 1) * 128],
                                                                in_=p[:, 0:512].rearrange("p (a b) -> p a b", a=4)), R=[pk, 'mkT_s'], W=['mkT_s'])
                    mem_attend(8, bc, mkT_s, 'mkT_s', mvb, 'mvb', lambda h, bc=bc: ymT4[:, h, bc])
                mem_branch_finish(1, lambda h, ti: ymT4[:, h, 0:128])

            def run_group(tiles, is_sample):
                ng = len(tiles)
                ncg = ng * 128
                for ti, (row0, _) in enumerate(tiles):
                    norm_transpose(xall[row0:row0 + 128, :],
                                   lambda half, ti=ti: xnT_g[:, ti, half * 8:(half + 1) * 8, :].rearrange("p a b -> p (a b)"), 'xnT_g')

                def c_u(half):
                    def f(ti, p, pk):
                        T.op('act', lambda e: e.copy(out=u_bg[:, ti, half * 256:(half + 1) * 256], in_=p[:, 0:256]), R=[pk, 'u_bg'], W=['u_bg'])
                    return f
                in_proj_block(ng, C_U, 256, c_u(0))
                in_proj_block(ng, C_U + 256, 256, c_u(1))

                def c_gate(goff):
                    def f(ti, p, pk):
                        T.op('act', lambda e: e.activation(out=gates[:, ti, goff:goff + 256], in_=p[:, 0:256], func=AF.Silu),
                             R=[pk, 'gates'], W=['gates'])
                    return f
                in_proj_block(ng, C_GS, 256, c_gate(0))
                in_proj_block(ng, C_GS + 256, 256, c_gate(256))
                chk('G0')
                for ti, (row0, own) in enumerate(tiles):
                    u_transpose(u_bg[:, ti, :], 'u_bg')
                    py, pyk = psC[2], 'psC2'
                    for gi in range(16 // GS):
                        if is_sample:
                            ssm_step(lambda ct: uT[:, ct, :], 128, NSEQ, 8, 0, True, gis=[gi])
                        else:
                            for sub in range(128 // LS):
                                ssm_step(lambda ct, sub=sub: uT[:, ct, sub * LS:(sub + 1) * LS], LS, 1, LS, sub * LS, True, gis=[gi])
                        for j in range(GS):
                            i = gi * GS + j
                            T.op('pe', lambda e, i=i, j=j: e.matmul(py[:, i * 32:(i + 1) * 32], lhsT=h_b[:, j, 0, :], rhs=CT_b[:, i, 0, :],
                                                                    start=True, stop=False), R=['h_b', 'CT_b'], W=[pyk])
                            T.op('pe', lambda e, i=i, j=j: e.matmul(py[:, i * 32:(i + 1) * 32], lhsT=h_b[:, j, 1, :], rhs=CT_b[:, i, 1, :],
                                                                    start=False, stop=False), R=['h_b', 'CT_b'], W=[pyk])
                            ct, off = i // 4, (i % 4) * 32
                            T.op('pe', lambda e, i=i, ct=ct, off=off: e.matmul(py[:, i * 32:(i + 1) * 32], lhsT=uT[:, ct, :],
                                                                              rhs=diagD_b[:, ct, off:off + 32], start=False, stop=True),
                                 R=['uT', 'diagD_b'], W=[pyk])
                    T.op('act', lambda e: e.copy(out=yf[:], in_=py[:, 0:512]), R=[pyk], W=['yf'])
                    T.op('pool', lambda e: e.tensor_tensor(out=yg[:], in0=yf[:], in1=yf[:], op=ALU.mult), R=['yf'], W=['yg'])
                    T.op('pool', lambda e: e.tensor_scalar(out=yg[:], in0=yg[:], scalar1=0.044715, scalar2=1.0, op0=ALU.mult, op1=ALU.add),
                         R=['yg'], W=['yg'])
                    T.op('pool', lambda e: e.tensor_tensor(out=yg[:], in0=yg[:], in1=yf[:], op=ALU.mult), R=['yg', 'yf'], W=['yg'])
                    T.op('act', lambda e: e.activation(out=yg[:], in_=yg[:], func=AF.Sigmoid, scale=GELU_K), R=['yg'], W=['yg'])
                    T.op('dve', lambda e: e.tensor_tensor(out=yg[:], in0=yg[:], in1=yf[:], op=ALU.mult), R=['yg', 'yf'], W=['yg'])
                    T.op('dve', lambda e: e.tensor_copy(out=ob[:, 0:512], in_=yg[:]), R=['yg'], W=['ob'])
                    p, pk = nextT()
                    for k in range(4):
                        T.op('pe', lambda e, k=k, p=p: e.transpose(out=p[:, k * 128:(k + 1) * 128], in_=ob[:, k * 128:(k + 1) * 128],
                                                                   identity=identb[:]), R=['ob', 'identb'], W=[pk])
                    T.op('act', lambda e, p=p: e.copy(out=cqT[:].rearrange("p a b -> p (a b)"), in_=p[:, 0:512]), R=[pk], W=['cqT'])
                    pz, pzk = nextA()
                    for k in range(4):
                        T.op('pe', lambda e, k=k, pz=pz: e.matmul(pz[:, 0:512], lhsT=cqT[:, k, :], rhs=glu_w_b[:, k, :], start=(k == 0), stop=False),
                             R=['cqT', 'glu_w_b'], W=[pzk])
                    T.op('pe', lambda e, pz=pz: e.matmul(pz[:, 0:512], lhsT=onesb[0:1, :], rhs=glub_b[0:1, :], start=False, stop=True),
                         R=['onesb', 'glub_b'], W=[pzk])
                    T.op('act', lambda e, pz=pz: e.activation(out=yf[:], in_=pz[:, 0:512], func=AF.Sigmoid), R=[pzk, 'yf'], W=['yf'])
                    T.op('dve', lambda e: e.tensor_tensor(out=yf[:], in0=yf[:], in1=yg[:], op=ALU.mult), R=['yf', 'yg'], W=['yf'])
                    finish_branch(ti, yf[:], 'yf', 512, 0)

                chk('G1')
                lat_ps = []
                stream_wblock(w_in, C_CKV, 256, gcol_in)
                for ti in range(ng):
                    p, pk = psC[ti], 'psC%d' % ti
                    for kc in range(16):
                        T.op('pe', lambda e, kc=kc, ti=ti, p=p: e.matmul(p[:, 0:256], lhsT=xnT_g[:, ti, kc, :], rhs=wblk[:, kc, 0:256],
                                                                         start=(kc == 0), stop=(kc == 15)), R=['xnT_g', 'wblk'], W=[pk])
                    lat_ps.append((p, pk))
                stream_wblock(w_in, C_KR, 64, gcol_in)
                for ti, (row0, own) in enumerate(tiles):
                    p, pk = lat_ps[ti]
                    for kc in range(16):
                        T.op('pe', lambda e, kc=kc, ti=ti, p=p: e.matmul(p[:, 256:320], lhsT=xnT_g[:, ti, kc, :], rhs=wblk[:, kc, 0:64],
                                                                         start=(kc == 0), stop=(kc == 15)), R=['xnT_g', 'wblk', pk], W=[pk])
                    if is_sample:
                        latent_post(p, pk, row0, ckv_s, kr_s, 0, ckvT_s[:], 'ckvT_s', krT_s[0:64, :], 'krT_s', None, None, None)
                    else:
                        kt = NPREV + own
                        latent_post(p, pk, row0, ckv_p, kr_p, own * 128, ckvT_all[:, :, kt * 128:(kt + 1) * 128], 'ckvT_all',
                                    krT_all[0:64, kt * 128:(kt + 1) * 128], 'krT_all', ckv1_all[:, kt, :], 'ckv1_all', rk_all[:, kt, :])

                chk('G2')
                cq_ps = []
                stream_wblock(w_in, C_CQ, 256, gcol_in)
                for ti in range(ng):
                    p, pk = psC[ti], 'psC%d' % ti
                    for kc in range(16):
                        T.op('pe', lambda e, kc=kc, ti=ti, p=p: e.matmul(p[:, 0:256], lhsT=xnT_g[:, ti, kc, :], rhs=wblk[:, kc, 0:256],
                                                                         start=(kc == 0), stop=(kc == 15)), R=['xnT_g', 'wblk'], W=[pk])
                    cq_ps.append((p, pk))
                stream_wblock(w_in, C_CQ + 256, 256, gcol_in)
                for ti, (row0, own) in enumerate(tiles):
                    p, pk = cq_ps[ti]
                    for kc in range(16):
                        T.op('pe', lambda e, kc=kc, ti=ti, p=p: e.matmul(p[:, 256:512], lhsT=xnT_g[:, ti, kc, :], rhs=wblk[:, kc, 0:256],
                                                                         start=(kc == 0), stop=(kc == 15)), R=['xnT_g', 'wblk', pk], W=[pk])
                    T.op('act', lambda e, p=p: e.copy(out=yf[:], in_=p[:, 0:512]), R=[pk], W=['yf'])
                    q_path(ti, row0)

                chk('G3')
                for b4 in range(4):
                    in_proj_block(ng, C_GM + b4 * 256, 256, c_gate(512 + b4 * 256))

                def c_qm(half):
                    def f(ti, p, pk):
                        for hh in range(2):
                            h = half * 2 + hh
                            rstd_of(p[:, hh * 128:(hh + 1) * 128], 128, sm[:, 6:7], pk)
                            T.op('dve', lambda e, hh=hh, h=h: e.scalar_tensor_tensor(out=cq_b[:, h * 128:(h + 1) * 128],
                                                                                     in0=p[:, hh * 128:(hh + 1) * 128], scalar=sm[:, 6:7],
                                                                                     in1=gmq_bc[:], op0=ALU.mult, op1=ALU.mult),
                                 R=[pk, 'sm', 'gmq_bc', 'cq_b'], W=['cq_b'])
                        pt_, ptk = nextT()
                        for hh in range(2):
                            h = half * 2 + hh
                            T.op('pe', lambda e, hh=hh, h=h, pt_=pt_: e.transpose(out=pt_[:, hh * 128:(hh + 1) * 128], in_=cq_b[:, h * 128:(h + 1) * 128],
                                                                                  identity=identb[:]), R=['cq_b', 'identb'], W=[ptk])
                        T.op('act', lambda e, pt_=pt_: e.copy(out=QmT[:, half * 2:half * 2 + 2, ti * 128:(ti + 1) * 128],
                                                              in_=pt_[:, 0:256].rearrange("p (a b) -> p a b", a=2)), R=[ptk, 'QmT'], W=['QmT'])
                    return f
                in_proj_block(ng, C_QM, 256, c_qm(0))
                in_proj_block(ng, C_QM + 256, 256, c_qm(1))
                in_proj_block(ng, C_GME, 256, c_gate(1536))
                in_proj_block(ng, C_GME + 256, 256, c_gate(1792))

                chk('G4')
                T.op('dve', lambda e: e.memset(ssq[:], 0.0), W=['ssq'])
                if not is_sample:
                    prompt_attention(ng, ncg, tiles[0][1])
                else:
                    sample_attention()

                chk('G5')
                for ti in range(ng):
                    T.op('dve', lambda e, ti=ti: e.tensor_reduce(out=sm[:, 7:8], in_=ssq[:, ti, 0:8], axis=AX.X, op=ALU.add), R=['ssq', 'sm'], W=['sm'])
                    T.op('act', lambda e: e.activation(out=sm[:, 7:8], in_=sm[:, 7:8], func=AF.Sqrt, scale=1.0 / 1024, bias=EPS), R=['sm'], W=['sm'])
                    T.op('dve', lambda e: e.reciprocal(out=sm[:, 7:8], in_=sm[:, 7:8]), R=['sm'], W=['sm'])
                    T.op('dve', lambda e, ti=ti: e.scalar_tensor_tensor(out=gates[:, ti, 512:1536], in0=ymla[:, ti, :], scalar=sm[:, 7:8],
                                                                        in1=gates[:, ti, 512:1536], op0=ALU.mult, op1=ALU.mult),
                         R=['ymla', 'sm', 'gates'], W=['gates'])
                    for half in range(2):
                        p, pk = nextT()
                        for k in range(8):
                            kc = half * 8 + k
                            T.op('pe', lambda e, k=k, kc=kc, p=p, ti=ti: e.transpose(out=p[:, k * 128:(k + 1) * 128],
                                                                                     in_=gates[:, ti, kc * 128:(kc + 1) * 128], identity=identb[:]),
                                 R=['gates', 'identb'], W=[pk])
                        T.op('act', lambda e, p=p, half=half, ti=ti: e.copy(out=xnT_g[:, ti, half * 8:(half + 1) * 8, :].rearrange("p a b -> p (a b)"),
                                                                           in_=p[:, :]), R=[pk, 'xnT_g'], W=['xnT_g'])
                for cb in range(8):
                    stream_wblock(w_out_d, cb * 256, 256, gcol_out)
                    for ti, (row0, own) in enumerate(tiles):
                        T.dma('sp', 'xres', lambda e, row0=row0, cb=cb: e.dma_start(out=xres[:], in_=xall[row0:row0 + 128, cb * 256:(cb + 1) * 256]),
                              W=['xres'])
                        p, pk = nextA()
                        for kc in range(16):
                            T.op('pe', lambda e, kc=kc, ti=ti, p=p: e.matmul(p[:, 0:256], lhsT=xnT_g[:, ti, kc, :], rhs=wblk[:, kc, 0:256],
                                                                             start=(kc == 0), stop=(kc == 15)), R=['xnT_g', 'wblk'], W=[pk])
                        T.op('dve', lambda e, p=p: e.tensor_tensor(out=yout[:], in0=p[:, 0:256], in1=xres[:], op=ALU.add), R=[pk, 'xres'], W=['yout'])
                        dst = y_s if is_sample else y_p
                        r0 = 0 if is_sample else own * 128
                        T.dma('sp', 'ost', lambda e, dst=dst, r0=r0, cb=cb: e.dma_start(out=dst[r0:r0 + 128, cb * 256:(cb + 1) * 256], in_=yout[:]),
                              R=['yout'])

            own_tiles = [((NPREV + o) * 128, o) for o in range(NOWN)]
            for g0 in range(0, NOWN, GT):
                run_group(own_tiles[g0:g0 + GT], False)
            stf = sb("stf", [16, 128])
            for src, dst in ((hre, sp_re), (him, sp_im)):
                transpose_f32(stf[:], src[:, :, 0], 128, 16, 'stf', 'hst')
                T.dma('sp', 'ost', lambda e, dst=dst: e.dma_start(out=dst[:, :], in_=stf[:]), R=['stf'])
            chk('O')
            T.barrier()
            stin = xt[0:NSEQ, :]
            for src_d, dstt in ((st_re, hre), (st_im, him)):
                T.dma('sp', 'c0', lambda e, src_d=src_d: e.dma_start(out=stin, in_=src_d[:, :]), W=['xt'])
                for i in range(16):
                    transpose_f32(dstt[:, i, :], stin[:, i * 128:(i + 1) * 128], NSEQ, 128, 'hst', 'xt')
            run_group([(NKT * 128, None)], True)
            for src, dst in ((hre, ss_re), (him, ss_im)):
                for i in range(16):
                    transpose_f32(stin[:, i * 128:(i + 1) * 128], src[:, i, :], 128, NSEQ, 'xt', 'hst')
                T.dma('sp', 'ost', lambda e, dst=dst: e.dma_start(out=dst[:, :], in_=stin), R=['xt'])
        except _Stop:
            pass
        T.finish('sp')
        print("[kernel] instructions emitted:", T.nins, "sbuf left:", nc.sbuf_bytes_remaining, {k: v for k, v in T.cnt.items() if k in ("pe", "act", "dve", "pool")}, flush=True)
    return nc


def rope_table(pos):
    half = 32
    inv = (10000.0 ** (-np.arange(half, dtype=np.float32) / half)).astype(np.float32)
    ang = pos.astype(np.float32)[:, None] * inv[None, :]
    c, s = np.cos(ang).astype(np.float32), np.sin(ang).astype(np.float32)
    return np.concatenate([c, c, -s, s], axis=1).astype(np.float32)


WNAMES = ['norm_g', 'w_in', 'ssm_a_re', 'ssm_a_im', 'ssm_log_dt', 'ssm_b_re', 'ssm_b_im', 'ssm_c_re', 'ssm_c_im', 'ssm_d',
          'ssm_glu_w', 'ssm_glu_b', 'mla_q_norm_g', 'mla_w_uq', 'mla_kv_norm_g', 'mla_w_ukv', 'mla_qk_norm_q', 'mla_qk_norm_k',
          'mem_norm_g', 'mem_w_k', 'mem_w_v', 'mem_qk_norm_q', 'mem_qk_norm_k', 'out_norm_ssm', 'out_norm_mla', 'out_norm_mem',
          'w_out']


def make_in_maps(inp, SEQ, NPG, NPOOL, PAST):
    CH = SEQ // 4
    NOWN = CH // 128
    NPREV = 3 * NOWN
    NKT = NPREV + NOWN
    f32 = np.float32
    xp = np.asarray(inp['x_prompt'], f32); xs = np.asarray(inp['x_sample'], f32)
    ident = np.eye(128, dtype=f32)
    tri = np.triu(np.ones((128, 128), f32))
    smask = np.ones((128, 256), f32); smask[:, 0] = 0.0
    smask[:, 128::8] = 0.0
    tau = np.tile(np.arange(1, LS + 1, dtype=f32)[None, :], (128, 1))
    cckv = np.ascontiguousarray(np.asarray(inp['cache_ckv'], f32).reshape(NPOOL * 128, 256))
    ckr = np.ascontiguousarray(np.asarray(inp['cache_krope'], f32).reshape(NPOOL * 128, 64))
    wts = {n: np.ascontiguousarray(np.asarray(inp[n], f32)) for n in WNAMES}
    maps = []
    for c in range(8):
        s, j = c // 4, c % 4
        xall = np.zeros(((NKT + 1) * 128, D), f32)
        pos = np.zeros((NKT + 1) * 128, f32)
        kb = np.zeros((128, NKT), f32)
        for k in range(3):
            src = j - 3 + k
            r0 = k * CH
            if src >= 0:
                xall[r0:r0 + CH] = xp[s, src * CH:(src + 1) * CH]
                pos[r0:r0 + CH] = np.arange(src * CH, (src + 1) * CH)
            else:
                kb[:, k * NOWN:(k + 1) * NOWN] = NEG
        xall[3 * CH:4 * CH] = xp[s, j * CH:(j + 1) * CH]
        pos[3 * CH:4 * CH] = np.arange(j * CH, (j + 1) * CH)
        xall[4 * CH:] = xs[c * NSEQ:(c + 1) * NSEQ].reshape(128, D)
        pos[4 * CH:] = np.tile(PAST + np.arange(8), NSEQ)
        m = {
            'xall': xall, 'rope': rope_table(pos), 'kbias': kb,
            'mem': np.ascontiguousarray(np.asarray(inp['mem_prompt'], f32)[s]),
            'cckv': cckv, 'ckr': ckr,
            'cmk': np.ascontiguousarray(np.asarray(inp['cache_mem_k'], f32)[c * NSEQ:(c + 1) * NSEQ].reshape(NSEQ * 256, 512)),
            'cmv': np.ascontiguousarray(np.asarray(inp['cache_mem_v'], f32)[c * NSEQ:(c + 1) * NSEQ].reshape(NSEQ * 256, 512)),
            'st_re': np.ascontiguousarray(np.asarray(inp['state_ssm_re'], f32)[c * NSEQ:(c + 1) * NSEQ].reshape(NSEQ, 2048)),
            'st_im': np.ascontiguousarray(np.asarray(inp['state_ssm_im'], f32)[c * NSEQ:(c + 1) * NSEQ].reshape(NSEQ, 2048)),
            'ptab': np.ascontiguousarray(np.asarray(inp['page_table'], np.int32)[c * NSEQ:(c + 1) * NSEQ]),
            'ident': ident, 'tri': tri, 'smask': smask, 'tau': tau,
        }
        m.update(wts)
        maps.append(m)
    return maps


def assemble(r, SEQ):
    y_prompt = np.stack([np.concatenate([r[s * 4 + j]['y_p'] for j in range(4)], 0) for s in range(2)])
    ckv_p = np.stack([np.concatenate([r[s * 4 + j]['ckv_p'] for j in range(4)], 0) for s in range(2)])
    kr_p = np.stack([np.concatenate([r[s * 4 + j]['kr_p'] for j in range(4)], 0) for s in range(2)])
    y_sample = np.concatenate([r[c]['y_s'].reshape(NSEQ, 8, D) for c in range(8)], 0)
    ckv_s = np.concatenate([r[c]['ckv_s'].reshape(NSEQ, 8, 256) for c in range(8)], 0)
    kr_s = np.concatenate([r[c]['kr_s'].reshape(NSEQ, 8, 64) for c in range(8)], 0)
    memk = np.stack([r[s * 4]['memk'].reshape(256, 4, 128) for s in range(2)])
    memv = np.stack([r[s * 4]['memv'].reshape(256, 4, 128) for s in range(2)])
    sp_re = np.stack([r[s * 4 + 3]['sp_re'].reshape(32, 64) for s in range(2)])
    sp_im = np.stack([r[s * 4 + 3]['sp_im'].reshape(32, 64) for s in range(2)])
    ss_re = np.concatenate([r[c]['ss_re'].reshape(NSEQ, 32, 64) for c in range(8)], 0)
    ss_im = np.concatenate([r[c]['ss_im'].reshape(NSEQ, 32, 64) for c in range(8)], 0)
    outs = (y_prompt, y_sample, ckv_p, kr_p, ckv_s, kr_s, memk, memv, sp_re, sp_im, ss_re, ss_im)
    return tuple(np.ascontiguousarray(o, dtype=np.float32) for o in outs)


def run(inp, SEQ, PAST, stop=None, cores=None):
    NPG = PAST // 128
    NPOOL = int(np.asarray(inp['cache_ckv']).shape[0])
    nc = build(SEQ, NPG, NPOOL, stop)
    maps = make_in_maps(inp, SEQ, NPG, NPOOL, PAST)
    if cores is not None:
        res = run_bass_kernel_spmd(nc, [maps[c] for c in cores], core_ids=list(range(len(cores))))
        return {c: res.results[i] for i, c in enumerate(cores)}
    res = run_bass_kernel_spmd(nc, maps, core_ids=list(range(8)))
    return assemble(res.results, SEQ)


def kernel(**inputs):
    return run(inputs, 4096, 8192)
```

```python
import contextlib
import numpy as np
import concourse.bass as bass
import concourse.mybir as mybir
from concourse.bass_utils import run_bass_kernel_spmd

F32 = mybir.dt.float32
BF16 = mybir.dt.bfloat16
I32 = mybir.dt.int32
AF = mybir.ActivationFunctionType
ALU = mybir.AluOpType
AX = mybir.AxisListType

D = 2048
IN_W = 3904
EPS = 1e-6
NEG = -30000.0
NSEQ = 16
LS = 64
GT = 2
GC = GT * 128
C_U, C_GS, C_CQ, C_CKV, C_KR, C_GM, C_QM, C_GME = 0, 512, 1024, 1536, 1792, 1856, 2880, 3392
GELU_K = 2.0 * 0.7978845608028654


class Tracker:
    def __init__(self, nc, es):
        self.nc = nc
        self.es = es
        self.eng = {'pe': nc.tensor, 'act': nc.scalar, 'dve': nc.vector, 'pool': nc.gpsimd, 'sp': nc.sync}
        self.sem = {}
        self.cnt = {}
        for n in ['pe', 'act', 'dve', 'pool']:
            self.sem[n] = es.enter_context(nc.semaphore('c_' + n))
            self.cnt[n] = 0
        self.seen = {n: {} for n in self.eng}
        self.lastw = {}
        self.readers = {}
        self.nins = 0

    def _waits(self, e, reads, writes):
        need = {}

        def add(t):
            if t is None:
                return
            sn, v = t
            if need.get(sn, 0) < v:
                need[sn] = v
        for k in reads:
            add(self.lastw.get(k))
            if k.startswith('ps'):
                for t in self.readers.get(k, ()):
                    if t[0] != e:
                        add(t)
        for k in writes:
            add(self.lastw.get(k))
            for t in self.readers.get(k, ()):
                add(t)
        for sn, v in need.items():
            if sn == 'pe' and e == 'pe':
                continue
            if self.seen[e].get(sn, 0) < v:
                self.eng[e].wait_ge(self.sem[sn], v)
                self.seen[e][sn] = v
                self.nins += 1

    def _commit(self, tok, reads, writes):
        for k in reads:
            self.readers.setdefault(k, []).append(tok)
        for k in writes:
            self.lastw[k] = tok
            self.readers[k] = []

    def op(self, e, fn, R=(), W=()):
        self._waits(e, R, W)
        ins = fn(self.eng[e])
        self.cnt[e] += 1
        ins.then_inc(self.sem[e], 1)
        self.nins += 1
        self._commit((e, self.cnt[e]), R, W)

    def dma(self, q, stream, fn, R=(), W=()):
        if stream not in self.sem:
            self.sem[stream] = self.es.enter_context(self.nc.semaphore('d_' + stream))
            self.cnt[stream] = 0
        self._waits(q, R, W)
        if self.cnt[stream] and self.seen[q].get(stream, 0) < self.cnt[stream]:
            self.eng[q].wait_ge(self.sem[stream], self.cnt[stream])
            self.seen[q][stream] = self.cnt[stream]
            self.nins += 1
        ins = fn(self.eng[q])
        self.cnt[stream] += 16
        ins.then_inc(self.sem[stream], 16)
        self.nins += 1
        self._commit((stream, self.cnt[stream]), R, W)

    def barrier(self):
        for e in self.eng:
            for sn, v in self.cnt.items():
                if v == 0:
                    continue
                if self.seen[e].get(sn, 0) < v:
                    self.eng[e].wait_ge(self.sem[sn], v)
                    self.seen[e][sn] = v
                    self.nins += 1
        self.lastw = {}
        self.readers = {}

    def finish(self, q='sp'):
        for sn, v in self.cnt.items():
            if v == 0:
                continue
            self.eng[q].wait_ge(self.sem[sn], v)


class _Stop(Exception):
    pass


def build(SEQ, NPG, NPOOL, stop=None):
    CH = SEQ // 4
    NOWN = CH // 128
    NPREV = 3 * NOWN
    NKT = NPREV + NOWN
    NT = NKT + 1
    nc = bass.Bass("TRN2", target_bir_lowering=False)

    def din(name, shape, dt=F32):
        return nc.dram_tensor(name, list(shape), dt, kind="ExternalInput").ap()

    def dout(name, shape):
        return nc.dram_tensor(name, list(shape), F32, kind="ExternalOutput").ap()

    xall = din("xall", [NT * 128, D])
    rope = din("rope", [NT * 128, 128])
    kbias_d = din("kbias", [128, NKT])
    mem_d = din("mem", [256, D])
    cckv = din("cckv", [NPOOL * 128, 256])
    ckr = din("ckr", [NPOOL * 128, 64])
    cmk = din("cmk", [NSEQ * 256, 512])
    cmv = din("cmv", [NSEQ * 256, 512])
    st_re = din("st_re", [NSEQ, 2048])
    st_im = din("st_im", [NSEQ, 2048])
    ptab = din("ptab", [NSEQ, NPG], I32)
    ident_d = din("ident", [128, 128])
    tri_d = din("tri", [128, 128])
    smask_d = din("smask", [128, 256])
    tau_d = din("tau", [128, LS])
    norm_g = din("norm_g", [D]); w_in = din("w_in", [D, IN_W])
    a_re_d = din("ssm_a_re", [32, 64]); a_im_d = din("ssm_a_im", [32, 64]); ldt_d = din("ssm_log_dt", [32])
    b_re_d = din("ssm_b_re", [32, 64, 16]); b_im_d = din("ssm_b_im", [32, 64, 16])
    c_re_d = din("ssm_c_re", [32, 16, 64]); c_im_d = din("ssm_c_im", [32, 16, 64])
    ssm_d_d = din("ssm_d", [512]); glu_w_d = din("ssm_glu_w", [512, 512]); glu_b_d = din("ssm_glu_b", [512])
    gq_d = din("mla_q_norm_g", [512]); w_uq_d = din("mla_w_uq", [512, 1536])
    gkv_d = din("mla_kv_norm_g", [256]); w_ukv_d = din("mla_w_ukv", [256, 2048])
    gqq_d = din("mla_qk_norm_q", [192]); gqk_d = din("mla_qk_norm_k", [192])
    gmem_d = din("mem_norm_g", [D]); w_mk_d = din("mem_w_k", [D, 512]); w_mv_d = din("mem_w_v", [D, 512])
    gmq_d = din("mem_qk_norm_q", [128]); gmk_d = din("mem_qk_norm_k", [128])
    go_s_d = din("out_norm_ssm", [512]); go_m_d = din("out_norm_mla", [1024]); go_e_d = din("out_norm_mem", [512])
    w_out_d = din("w_out", [D, D])

    y_p = dout("y_p", [CH, D]); y_s = dout("y_s", [128, D])
    ckv_p = dout("ckv_p", [CH, 256]); kr_p = dout("kr_p", [CH, 64])
    ckv_s = dout("ckv_s", [128, 256]); kr_s = dout("kr_s", [128, 64])
    memk_o = dout("memk", [256, 512]); memv_o = dout("memv", [256, 512])
    sp_re = dout("sp_re", [16, 128]); sp_im = dout("sp_im", [16, 128])
    ss_re = dout("ss_re", [NSEQ, 2048]); ss_im = dout("ss_im", [NSEQ, 2048])

    es = contextlib.ExitStack()
    with es:
        T = Tracker(nc, es)

        def chk(name):
            if stop == name:
                raise _Stop()

        import os as _os
        DBG = _os.environ.get('KDBG')

        def dbg(name, ap, shape, keys, dt=F32):
            if not DBG:
                return
            d = nc.dram_tensor("dbg_" + name, list(shape), dt, kind="ExternalOutput").ap()
            T.dma('sp', 'dbg', lambda e: e.dma_start(out=d, in_=ap, allow_slow_non_contiguous=True), R=keys)

        try:
            def sb(name, shape, dt=F32):
                return es.enter_context(nc.sbuf_tensor("s_" + name, list(shape), dt))

            def ps(name, shape, dt=F32):
                return es.enter_context(nc.psum_tensor("p_" + name, list(shape), dt))

            psT = [ps("psT%d" % i, [128, 1024], BF16) for i in range(2)]
            psA = [ps("psA%d" % i, [128, 512], F32) for i in range(3)]
            psC = [ps("psC%d" % i, [128, 512], F32) for i in range(3)]
            rr = {'T': 0, 'A': 0}

            def nextT():
                i = rr['T']; rr['T'] = (i + 1) % 2
                return psT[i], 'psT%d' % i

            def nextA():
                i = rr['A']; rr['A'] = (i + 1) % 3
                return psA[i], 'psA%d' % i

            identf = sb("identf", [128, 128]); identb = sb("identb", [128, 128], BF16)
            trib = sb("trib", [128, 128], BF16); onesb = sb("onesb", [128, 128], BF16)
            smask = sb("smask", [128, 256]); taur = sb("taur", [128, LS])
            stg = sb("stg", [128, 1024])
            T.dma('sp', 'c0', lambda e: e.dma_start(out=identf[:], in_=ident_d[:, :]), W=['identf'])
            T.dma('sp', 'c0', lambda e: e.dma_start(out=stg[:, 0:128], in_=tri_d[:, :]), W=['stg'])
            T.dma('sp', 'c0', lambda e: e.dma_start(out=smask[:], in_=smask_d[:, :]), W=['smask'])
            T.dma('sp', 'c0', lambda e: e.dma_start(out=taur[:], in_=tau_d[:, :]), W=['taur'])
            T.op('dve', lambda e: e.tensor_copy(out=identb[:], in_=identf[:]), R=['identf'], W=['identb'])
            T.op('dve', lambda e: e.tensor_copy(out=trib[:], in_=stg[:, 0:128]), R=['stg'], W=['trib'])
            T.op('dve', lambda e: e.memset(onesb[:], 1.0), W=['onesb'])

            def transpose_f32(dst_ap, src_ap, npart, nfree, dkey, skey, scale=None):
                p, pk = nextA()
                T.op('pe', lambda e: e.transpose(out=p[0:nfree, 0:npart], in_=src_ap, identity=identf[0:npart, 0:npart]),
                     R=[skey, 'identf'], W=[pk])
                if scale is None:
                    T.op('dve', lambda e: e.tensor_copy(out=dst_ap, in_=p[0:nfree, 0:npart]), R=[pk], W=[dkey])
                else:
                    T.op('dve', lambda e: e.tensor_scalar(out=dst_ap, in0=p[0:nfree, 0:npart], scalar1=scale, scalar2=None,
                                                          op0=ALU.mult), R=[pk], W=[dkey])

            def load_col_into(dst_ap, dkey, vec_d, n):
                k = n // 128
                T.dma('sp', 'c0', lambda e: e.dma_start(out=stg[0:k, 0:128], in_=vec_d.rearrange("(k p) -> k p", p=128)),
                      W=['stg'])
                transpose_f32(dst_ap, stg[0:k, 0:128], k, 128, dkey, 'stg')

            def load_col(name, vec_d, n):
                t = sb(name, [128, n // 128])
                load_col_into(t[:], name, vec_d, n)
                return t

            def load_bc(name, vec_d, n):
                t = sb(name, [128, n])
                T.dma('sp', 'c0', lambda e: e.dma_start(out=t[:], in_=vec_d.partition_broadcast(128)), W=[name])
                return t

            gcol_in = load_col("gcol_in", norm_g, D)
            gcol_q = load_col("gcol_q", gq_d, 512)
            gcol_mem = load_col("gcol_mem", gmem_d, D)
            gcol_out = sb("gcol_out", [128, 16])
            load_col_into(gcol_out[:, 0:4], 'gcol_out', go_s_d, 512)
            load_col_into(gcol_out[:, 4:12], 'gcol_out', go_m_d, 1024)
            load_col_into(gcol_out[:, 12:16], 'gcol_out', go_e_d, 512)
            dcol = load_col("dcol", ssm_d_d, 512)
            diagD_b = sb("diagD_b", [128, 4, 128], BF16)
            for ct in range(4):
                T.op('dve', lambda e, ct=ct: e.tensor_scalar(out=diagD_b[:, ct, :], in0=identf[:], scalar1=dcol[:, ct:ct + 1],
                                                             scalar2=None, op0=ALU.mult), R=['identf', 'dcol'], W=['diagD_b'])
            gkv_bc = load_bc("gkv_bc", gkv_d, 256)
            gmk_bc = load_bc("gmk_bc", gmk_d, 128)
            gqk_bc = load_bc("gqk_bc", gqq_d, 192)
            T.dma('sp', 'c0', lambda e: e.dma_start(out=stg[:, 0:192], in_=gqk_d.partition_broadcast(128)), W=['stg'])
            T.op('dve', lambda e: e.scalar_tensor_tensor(out=gqk_bc[:], in0=gqk_bc[:], scalar=192.0 ** -0.5, in1=stg[:, 0:192],
                                                         op0=ALU.mult, op1=ALU.mult), R=['gqk_bc', 'stg'], W=['gqk_bc'])
            gmq_bc = load_bc("gmq_bc", gmq_d, 128)
            T.op('dve', lambda e: e.tensor_scalar(out=gmq_bc[:], in0=gmq_bc[:], scalar1=128.0 ** -0.5, scalar2=None,
                                                  op0=ALU.mult), R=['gmq_bc'], W=['gmq_bc'])
            glub_b = sb("glub_b", [1, 512], BF16)
            T.dma('sp', 'c0', lambda e: e.dma_start(out=stg[0:1, 0:512], in_=glu_b_d.rearrange("(o n) -> o n", o=1)),
                  W=['stg'])
            T.op('dve', lambda e: e.tensor_copy(out=glub_b[:], in_=stg[0:1, 0:512]), R=['stg'], W=['glub_b'])

            def load_w_rows(dst, dkey, w_d, kc_n, c0, ncols, gcol, dcol0=0):
                for kc in range(kc_n):
                    for cc0 in range(0, ncols, 1024):
                        n = min(1024, ncols - cc0)
                        T.dma('sp', 'wst', lambda e, kc=kc, cc0=cc0, n=n: e.dma_start(
                            out=stg[:, 0:n], in_=w_d[kc * 128:(kc + 1) * 128, c0 + cc0:c0 + cc0 + n]), W=['stg'])
                        if gcol is None:
                            T.op('dve', lambda e, kc=kc, cc0=cc0, n=n: e.tensor_copy(out=dst[:, kc, dcol0 + cc0:dcol0 + cc0 + n],
                                                                                     in_=stg[:, 0:n]), R=['stg'], W=[dkey])
                        else:
                            T.op('dve', lambda e, kc=kc, cc0=cc0, n=n: e.tensor_scalar(out=dst[:, kc, dcol0 + cc0:dcol0 + cc0 + n],
                                                                                       in0=stg[:, 0:n], scalar1=gcol[:, kc:kc + 1],
                                                                                       scalar2=None, op0=ALU.mult), R=['stg'], W=[dkey])

            w_uq_b = sb("w_uq_b", [128, 4, 1536], BF16)
            load_w_rows(w_uq_b, 'w_uq_b', w_uq_d, 4, 0, 1536, gcol_q)
            glu_w_b = sb("glu_w_b", [128, 4, 512], BF16)
            load_w_rows(glu_w_b, 'glu_w_b', glu_w_d, 4, 0, 512, None)
            w_uk_b = sb("w_uk_b", [128, 2, 1024], BF16)
            w_uv_b = sb("w_uv_b", [128, 2, 1024], BF16)
            for kc in range(2):
                for hf in range(2):
                    T.dma('sp', 'wst', lambda e, kc=kc, hf=hf: e.dma_start(
                        out=stg[:, 0:1024], in_=w_ukv_d[kc * 128:(kc + 1) * 128, hf * 1024:(hf + 1) * 1024]), W=['stg'])
                    sv = stg[:, 0:1024].rearrange("p (h c) -> p h c", c=256)
                    T.op('dve', lambda e, kc=kc, hf=hf, sv=sv: e.tensor_copy(
                        out=w_uk_b[:, kc, hf * 512:(hf + 1) * 512].rearrange("p (h c) -> p h c", c=128), in_=sv[:, :, 0:128]),
                        R=['stg'], W=['w_uk_b'])
                    T.op('dve', lambda e, kc=kc, hf=hf, sv=sv: e.tensor_copy(
                        out=w_uv_b[:, kc, hf * 512:(hf + 1) * 512].rearrange("p (h c) -> p h c", c=128), in_=sv[:, :, 128:256]),
                        R=['stg'], W=['w_uv_b'])
            w_ukT_b = sb("w_ukT_b", [128, 8, 256], BF16)
            for h in range(8):
                p, pk = nextT()
                for kc in range(2):
                    T.op('pe', lambda e, h=h, kc=kc, p=p: e.transpose(out=p[:, kc * 128:(kc + 1) * 128],
                                                                      in_=w_uk_b[:, kc, h * 128:(h + 1) * 128],
                                                                      identity=identb[:]), R=['w_uk_b', 'identb'], W=[pk])
                T.op('act', lambda e, h=h, p=p: e.copy(out=w_ukT_b[:, h, :], in_=p[:, 0:256]), R=[pk], W=['w_ukT_b'])

            chk('w')
            arena = sb("arena", [128, 15360], BF16)
            w_up_b = arena[:, 0:13312].rearrange("p (k c) -> p k c", k=16)
            xnT1 = arena[:, 13312:15360].rearrange("p (k c) -> p k c", k=16)
            memT = arena[:, 8192:12288].rearrange("p (t k c) -> p t k c", t=2, k=16)
            gates = arena[:, 0:GT * 2048].rearrange("p (t c) -> p t c", t=GT)
            QTn = arena[:, 4096:4096 + 8 * GC].rearrange("p (h c) -> p h c", h=8)
            QTr = arena[:, 6144:6144 + 8 * GC].rearrange("p (h c) -> p h c", h=8)
            xnT_g = arena[:, 8192:8192 + GT * 2048].rearrange("p (t k c) -> p t k c", t=GT, k=16)
            ymla = arena[:, 12288:12288 + GT * 1024].rearrange("p (t c) -> p t c", t=GT)
            QmT = arena[:, 14336:14336 + 4 * GC].rearrange("p (h c) -> p h c", h=4)

            load_w_rows(w_up_b, 'w_up_b', w_in, 16, C_U, 512, gcol_in, 0)
            load_w_rows(w_up_b, 'w_up_b', w_in, 16, C_CKV, 320, gcol_in, 512)

            xt = sb("xt", [128, D])
            are = sb("are", [128, 16]); aim = sb("aim", [128, 16]); dts = sb("dts", [128, 16])
            with nc.allow_non_contiguous_dma(reason="small one-time parameter loads"):
                T.dma('sp', 'c0', lambda e: e.dma_start(out=are[:], in_=a_re_d.rearrange("(t two) n -> (two n) t", two=2)),
                      W=['are'])
                T.dma('sp', 'c0', lambda e: e.dma_start(out=aim[:], in_=a_im_d.rearrange("(t two) n -> (two n) t", two=2)),
                      W=['aim'])
                lv = ldt_d.rearrange("(t two) -> two t", two=2)
                for hh in range(2):
                    T.dma('sp', 'c0', lambda e, hh=hh: e.dma_start(out=dts[hh * 64:(hh + 1) * 64, :],
                                                                   in_=lv[hh].partition_broadcast(64)), W=['dts'])
            T.op('act', lambda e: e.activation(out=dts[:], in_=dts[:], func=AF.Exp), R=['dts'], W=['dts'])
            theta = sb("theta", [128, 16]); rmag = sb("rmag", [128, 16])
            T.op('dve', lambda e: e.tensor_tensor(out=theta[:], in0=dts[:], in1=aim[:], op=ALU.mult), R=['dts', 'aim'], W=['theta'])
            T.op('dve', lambda e: e.tensor_tensor(out=rmag[:], in0=dts[:], in1=are[:], op=ALU.mult), R=['dts', 'are'], W=['rmag'])
            T.op('act', lambda e: e.activation(out=rmag[:], in_=rmag[:], func=AF.Exp), R=['rmag'], W=['rmag'])
            costab = sb("costab", [128, 16, LS]); sintab = sb("sintab", [128, 16, LS])
            angw = xt[:, 0:4 * LS]; angk = sb("angk", [128, 4 * LS], I32); angm = xt[:, 256:256 + 4 * LS]
            TAB = ['sintab', 'costab']

            def sin_table(dst, dkey, phase):
                for q4 in range(4):
                    for j in range(4):
                        i = q4 * 4 + j
                        T.op('dve', lambda e, i=i, j=j: e.tensor_scalar(out=angw[:, j * LS:(j + 1) * LS], in0=taur[:],
                                                                        scalar1=theta[:, i:i + 1], scalar2=None, op0=ALU.mult),
                             R=['taur', 'theta', 'angw'], W=['angw'])
                    T.op('dve', lambda e: e.tensor_scalar(out=angw[:], in0=angw[:], scalar1=1.0 / (2 * np.pi), scalar2=phase,
                                                          op0=ALU.mult, op1=ALU.add), R=['angw'], W=['angw'])
                    T.op('dve', lambda e: e.tensor_copy(out=angk[:], in_=angw[:]), R=['angw'], W=['angk'])
                    T.op('dve', lambda e: e.tensor_copy(out=angm[:], in_=angk[:]), R=['angk'], W=['angm'])
                    T.op('dve', lambda e: e.tensor_tensor(out=angw[:], in0=angw[:], in1=angm[:], op=ALU.subtract),
                         R=['angw', 'angm'], W=['angw'])
                    T.op('dve', lambda e: e.tensor_scalar(out=angm[:], in0=angw[:], scalar1=0.5, scalar2=None, op0=ALU.is_gt),
                         R=['angw'], W=['angm'])
                    T.op('dve', lambda e: e.tensor_tensor(out=angw[:], in0=angw[:], in1=angm[:], op=ALU.subtract),
                         R=['angw', 'angm'], W=['angw'])
                    T.op('dve', lambda e: e.tensor_scalar(out=angm[:], in0=angw[:], scalar1=-0.5, scalar2=None, op0=ALU.is_lt),
                         R=['angw'], W=['angm'])
                    T.op('dve', lambda e: e.tensor_tensor(out=angw[:], in0=angw[:], in1=angm[:], op=ALU.add),
                         R=['angw', 'angm'], W=['angw'])
                    T.op('act', lambda e, q4=q4: e.activation(out=dst[:, q4 * 4:(q4 + 1) * 4, :].rearrange("p a b -> p (a b)"), in_=angw[:],
                                                              func=AF.Sin, scale=2 * np.pi), R=['angw'], W=[dkey])

            sin_table(sintab, 'sintab', 0.0)
            sin_table(costab, 'costab', 0.25)
            s5t = xt[:, 512:640].rearrange("p (a b) -> p a b", a=8)
            abr, abi, den, fre, fim, nfim, t0_, t1_ = [s5t[:, i, :] for i in range(8)]
            S5 = ['s5t']
            T.op('dve', lambda e: e.tensor_tensor(out=abr, in0=rmag[:], in1=costab[:, :, 0], op=ALU.mult), R=['rmag'] + TAB, W=S5)
            T.op('dve', lambda e: e.tensor_tensor(out=abi, in0=rmag[:], in1=sintab[:, :, 0], op=ALU.mult), R=['rmag'] + TAB + S5, W=S5)
            T.op('dve', lambda e: e.tensor_tensor(out=den, in0=are[:], in1=are[:], op=ALU.mult), R=['are'] + S5, W=S5)
            T.op('dve', lambda e: e.tensor_tensor(out=t0_, in0=aim[:], in1=aim[:], op=ALU.mult), R=['aim'] + S5, W=S5)
            T.op('dve', lambda e: e.tensor_tensor(out=den, in0=den, in1=t0_, op=ALU.add), R=S5, W=S5)
            T.op('dve', lambda e: e.reciprocal(out=den, in_=den), R=S5, W=S5)
            T.op('dve', lambda e: e.tensor_scalar(out=t1_, in0=abr, scalar1=-1.0, scalar2=None, op0=ALU.add), R=S5, W=S5)
            T.op('dve', lambda e: e.tensor_tensor(out=fre, in0=t1_, in1=are[:], op=ALU.mult), R=S5 + ['are'], W=S5)
            T.op('dve', lambda e: e.tensor_tensor(out=t0_, in0=abi, in1=aim[:], op=ALU.mult), R=S5 + ['aim'], W=S5)
            T.op('dve', lambda e: e.tensor_tensor(out=fre, in0=fre, in1=t0_, op=ALU.add), R=S5, W=S5)
            T.op('dve', lambda e: e.tensor_tensor(out=fre, in0=fre, in1=den, op=ALU.mult), R=S5, W=S5)
            T.op('dve', lambda e: e.tensor_tensor(out=fim, in0=abi, in1=are[:], op=ALU.mult), R=S5 + ['are'], W=S5)
            T.op('dve', lambda e: e.tensor_tensor(out=t0_, in0=t1_, in1=aim[:], op=ALU.mult), R=S5 + ['aim'], W=S5)
            T.op('dve', lambda e: e.tensor_tensor(out=fim, in0=fim, in1=t0_, op=ALU.subtract), R=S5, W=S5)
            T.op('dve', lambda e: e.tensor_tensor(out=fim, in0=fim, in1=den, op=ALU.mult), R=S5, W=S5)
            T.op('dve', lambda e: e.tensor_scalar(out=nfim, in0=fim, scalar1=-1.0, scalar2=None, op0=ALU.mult), R=S5, W=S5)
            bst = xt[:, 640:1152].rearrange("p (c t q) -> p c t q", c=2, t=16)
            with nc.allow_non_contiguous_dma(reason="small one-time parameter loads"):
                T.dma('sp', 'c0', lambda e: e.dma_start(out=bst[:, 0, :, :],
                                                        in_=b_re_d.rearrange("(t two) n q -> (two n) t q", two=2)), W=['bst'])
                T.dma('sp', 'c0', lambda e: e.dma_start(out=bst[:, 1, :, :],
                                                        in_=b_im_d.rearrange("(t two) n q -> (two n) t q", two=2)), W=['bst'])
            BT_b = sb("BT_b", [128, 16, 2, 128], BF16)
            bexp = xt[:, 1152:1408].rearrange("p (c n) -> p c n", c=2); bt1 = xt[:, 1408:1440].rearrange("p (c n) -> p c n", c=2)
            for i in range(16):
                ga, gb = (2 * i) % 8, (2 * i + 1) % 8
                T.op('dve', lambda e: e.memset(bexp[:], 0.0), W=['bexp'])
                T.op('dve', lambda e, i=i: e.tensor_scalar(out=bt1[:, 0, :], in0=bst[:, 0, i, :], scalar1=s5t[:, 3, i:i + 1],
                                                           scalar2=None, op0=ALU.mult), R=['bst', 's5t'], W=['bt1'])
                T.op('dve', lambda e, i=i: e.scalar_tensor_tensor(out=bt1[:, 0, :], in0=bst[:, 1, i, :], scalar=s5t[:, 5, i:i + 1],
                                                                  in1=bt1[:, 0, :], op0=ALU.mult, op1=ALU.add),
                     R=['bst', 's5t', 'bt1'], W=['bt1'])
                T.op('dve', lambda e, i=i: e.tensor_scalar(out=bt1[:, 1, :], in0=bst[:, 1, i, :], scalar1=s5t[:, 3, i:i + 1],
                                                           scalar2=None, op0=ALU.mult), R=['bst', 's5t', 'bt1'], W=['bt1'])
                T.op('dve', lambda e, i=i: e.scalar_tensor_tensor(out=bt1[:, 1, :], in0=bst[:, 0, i, :], scalar=s5t[:, 4, i:i + 1],
                                                                  in1=bt1[:, 1, :], op0=ALU.mult, op1=ALU.add),
                     R=['bst', 's5t', 'bt1'], W=['bt1'])
                for c in range(2):
                    T.op('dve', lambda e, c=c, ga=ga: e.tensor_copy(out=bexp[0:64, c, ga * 16:ga * 16 + 16], in_=bt1[0:64, c, :]),
                         R=['bt1', 'bexp'], W=['bexp'])
                    T.op('dve', lambda e, c=c, gb=gb: e.tensor_copy(out=bexp[64:128, c, gb * 16:gb * 16 + 16], in_=bt1[64:128, c, :]),
                         R=['bt1', 'bexp'], W=['bexp'])
                for c in range(2):
                    transpose_f32(BT_b[:, i, c, :], bexp[:, c, :], 128, 128, 'BT_b', 'bexp')
            CT_b = sb("CT_b", [128, 16, 2, 32], BF16)
            T.op('dve', lambda e: e.memset(CT_b[:], 0.0), W=['CT_b'])
            cpad = xt[:, 1536:1664]; cT = xt[:, 1664:1792]
            for c, cd in enumerate((c_re_d, c_im_d)):
                cv = cd.rearrange("g p n -> (g p) n")
                for c4 in range(4):
                    T.dma('sp', 'c0', lambda e, c4=c4, cv=cv: e.dma_start(out=cpad[:, 0:64], in_=cv[c4 * 128:(c4 + 1) * 128, :]), W=['cpad'])
                    T.dma('sp', 'c0', lambda e, c4=c4, cv=cv: e.dma_start(out=cpad[:, 64:128], in_=cv[c4 * 128:(c4 + 1) * 128, :]), W=['cpad'])
                    transpose_f32(cT[:], cpad[:], 128, 128, 'cT', 'cpad', scale=(1.0 if c == 0 else -1.0))
                    for k in range(4):
                        i = 4 * c4 + k
                        T.op('dve', lambda e, i=i, k=k, c=c: e.tensor_copy(out=CT_b[0:64, i, c, 0:16],
                                                                          in_=cT[0:64, (2 * k) * 16:(2 * k) * 16 + 16]),
                             R=['cT', 'CT_b'], W=['CT_b'])
                        T.op('dve', lambda e, i=i, k=k, c=c: e.tensor_copy(out=CT_b[64:128, i, c, 16:32],
                                                                          in_=cT[64:128, (2 * k + 1) * 16:(2 * k + 1) * 16 + 16]),
                             R=['cT', 'CT_b'], W=['CT_b'])

            dbg('theta', theta[:], [128, 16], ['theta']); dbg('rmag', rmag[:], [128, 16], ['rmag'])
            dbg('costab', costab[:].rearrange("p a b -> p (a b)"), [128, 16 * LS], ['costab'])
            dbg('sintab', sintab[:].rearrange("p a b -> p (a b)"), [128, 16 * LS], ['sintab'])
            dbg('s5t', xt[:, 512:640], [128, 128], ['s5t'])
            chk('s5')
            hre = sb("hre", [128, 16, NSEQ]); him = sb("him", [128, 16, NSEQ])
            T.op('dve', lambda e: e.memset(hre[:], 0.0), W=['hst'])
            T.op('dve', lambda e: e.memset(him[:], 0.0), R=['hst'], W=['hst'])
            GS = 2
            s5tmp = [sb("s5tmp%d" % k, [128, GS, 128]) for k in range(4)]
            bpr = sb("bpr", [128, GS, 128]); bpi = sb("bpi", [128, GS, 128])
            d0t = sb("d0t", [128, GS, 128])
            h_b = sb("h_b", [128, GS, 2, 128], BF16)
            hend = sb("hend", [128, 4, GS, NSEQ])

            dbg_once = []

            def ssm_step(uT_get, ncols, S, L, col0, want_out, gis=None):
                mcol = 0 if S == 1 else 128
                for gi in (range(16 // GS) if gis is None else gis):
                    p, pk = nextA()
                    pv = p[:, 0:GS * 2 * ncols].rearrange("p (j c n) -> p j c n", j=GS, c=2)
                    for j in range(GS):
                        i = gi * GS + j
                        for c in range(2):
                            T.op('pe', lambda e, i=i, j=j, c=c, pv=pv: e.matmul(pv[:, j, c, :], lhsT=BT_b[:, i, c, :],
                                                                               rhs=uT_get(i // 4), start=True, stop=True),
                                 R=['BT_b', 'uT'], W=[pk])
                    isl = slice(gi * GS, gi * GS + GS)

                    def tab(tb):
                        a = tb[:, isl, 0:L]
                        if S == 1:
                            return a
                        return a.unsqueeze(2).to_broadcast([128, GS, S, L])

                    def v4(t):
                        a = t[:, :, 0:ncols]
                        if S == 1:
                            return a
                        return a.rearrange("p j (s l) -> p j s l", l=L)

                    def pvc(c):
                        a = pv[:, :, c, :]
                        if S == 1:
                            return a
                        return a.rearrange("p j (s l) -> p j s l", l=L)
                    t1, t2, t3, t4 = s5tmp
                    T.op('dve', lambda e: e.tensor_tensor(out=v4(t1), in0=pvc(0), in1=tab(costab), op=ALU.mult), R=[pk] + TAB, W=['s5tmp0'])
                    T.op('dve', lambda e: e.tensor_tensor(out=v4(t2), in0=pvc(1), in1=tab(sintab), op=ALU.mult), R=[pk] + TAB, W=['s5tmp1'])
                    T.op('dve', lambda e: e.tensor_tensor(out=v4(t3), in0=pvc(1), in1=tab(costab), op=ALU.mult), R=[pk] + TAB, W=['s5tmp2'])
                    T.op('dve', lambda e: e.tensor_tensor(out=v4(t4), in0=pvc(0), in1=tab(sintab), op=ALU.mult), R=[pk] + TAB, W=['s5tmp3'])
                    T.op('pool', lambda e: e.tensor_tensor(out=bpr[:, :, 0:ncols], in0=t1[:, :, 0:ncols], in1=t2[:, :, 0:ncols], op=ALU.add),
                         R=['s5tmp0', 's5tmp1'], W=['bpr'])
                    T.op('pool', lambda e: e.tensor_tensor(out=bpi[:, :, 0:ncols], in0=t3[:, :, 0:ncols], in1=t4[:, :, 0:ncols], op=ALU.subtract),
                         R=['s5tmp2', 's5tmp3'], W=['bpi'])
                    if DBG and not dbg_once and gi == 0:
                        dbg('t1', t1[:].rearrange("p a b -> p (a b)"), [128, 256], ['s5tmp0'])
                        dbg('uT', uT[:].rearrange("p a b -> p (a b)"), [128, 512], ['uT'], BF16)
                        dbg('ubg', u_bg[:, 0, :], [128, 512], ['u_bg'], BF16)
                        dbg('wup', w_up_b[:, 0:2, :].rearrange("p a b -> p (a b)"), [128, 1664], ['w_up_b'], BF16)
                        dbg('gcol', gcol_in[:], [128, 16], ['gcol_in'])
                        dbg('xnT', xnT1[:].rearrange("p a b -> p (a b)"), [128, 2048], ['xnT1'], BF16)
                        dbg('BT', BT_b[:, 0:2, :, :].rearrange("p a b c -> p (a b c)"), [128, 512], ['BT_b'], BF16)
                        dbg('t4', t4[:].rearrange("p a b -> p (a b)"), [128, 256], ['s5tmp3'])
                        dbg('bpr0', bpr[:].rearrange("p a b -> p (a b)"), [128, 256], ['bpr'])
                        dbg('bpi0', bpi[:].rearrange("p a b -> p (a b)"), [128, 256], ['bpi'])
                    for j in range(GS):
                        i = gi * GS + j
                        T.op('pool', lambda e, i=i, j=j: e.tensor_scalar(out=d0t[:, j, 0:ncols], in0=smask[:, mcol:mcol + ncols],
                                                                         scalar1=rmag[:, i:i + 1], scalar2=None, op0=ALU.mult),
                             R=['smask', 'rmag', 'd0t'], W=['d0t'])
                        fr = bpr[:, j, 0:ncols].rearrange("p (s l) -> p s l", l=L)[:, :, 0]
                        fi = bpi[:, j, 0:ncols].rearrange("p (s l) -> p s l", l=L)[:, :, 0]
                        T.op('dve', lambda e, i=i, fr=fr: e.scalar_tensor_tensor(out=fr, in0=hre[:, i, 0:S], scalar=rmag[:, i:i + 1],
                                                                                  in1=fr, op0=ALU.mult, op1=ALU.add),
                             R=['hst', 'rmag', 'bpr'], W=['bpr'])
                        T.op('dve', lambda e, i=i, fi=fi: e.scalar_tensor_tensor(out=fi, in0=him[:, i, 0:S], scalar=rmag[:, i:i + 1],
                                                                                  in1=fi, op0=ALU.mult, op1=ALU.add),
                             R=['hst', 'rmag', 'bpi'], W=['bpi'])
                    for j in range(GS):
                        T.op('dve', lambda e, j=j: e.tensor_tensor_scan(out=bpr[:, j, 0:ncols], data0=d0t[:, j, 0:ncols],
                                                                        data1=bpr[:, j, 0:ncols], initial=0.0,
                                                                        op0=ALU.mult, op1=ALU.add), R=['d0t', 'bpr'], W=['bpr'])
                        T.op('dve', lambda e, j=j: e.tensor_tensor_scan(out=bpi[:, j, 0:ncols], data0=d0t[:, j, 0:ncols],
                                                                        data1=bpi[:, j, 0:ncols], initial=0.0,
                                                                        op0=ALU.mult, op1=ALU.add), R=['d0t', 'bpi'], W=['bpi'])
                    if DBG and not dbg_once and gi == 0:
                        dbg('bpr1', bpr[:].rearrange("p a b -> p (a b)"), [128, 256], ['bpr'])
                        dbg('bpi1', bpi[:].rearrange("p a b -> p (a b)"), [128, 256], ['bpi'])
                        dbg('d0t', d0t[:].rearrange("p a b -> p (a b)"), [128, 256], ['d0t'])
                        dbg_once.append(1)
                    gl_r = bpr[:, :, 0:ncols].rearrange("p j (s l) -> p j s l", l=L)[:, :, :, L - 1]
                    gl_i = bpi[:, :, 0:ncols].rearrange("p j (s l) -> p j s l", l=L)[:, :, :, L - 1]
                    cl = costab[:, isl, L - 1:L].to_broadcast([128, GS, S])
                    sl = sintab[:, isl, L - 1:L].to_broadcast([128, GS, S])
                    e1, e2, e3, e4 = [hend[:, k, :, 0:S] for k in range(4)]
                    T.op('pool', lambda e: e.tensor_tensor(out=e1, in0=gl_r, in1=cl, op=ALU.mult), R=['bpr'] + TAB, W=['hend'])
                    T.op('pool', lambda e: e.tensor_tensor(out=e2, in0=gl_i, in1=sl, op=ALU.mult), R=['bpi', 'hend'] + TAB, W=['hend'])
                    T.op('pool', lambda e: e.tensor_tensor(out=e3, in0=gl_r, in1=sl, op=ALU.mult), R=['bpr', 'hend'] + TAB, W=['hend'])
                    T.op('pool', lambda e: e.tensor_tensor(out=e4, in0=gl_i, in1=cl, op=ALU.mult), R=['bpi', 'hend'] + TAB, W=['hend'])
                    T.op('pool', lambda e: e.tensor_tensor(out=hre[:, isl, 0:S], in0=e1, in1=e2, op=ALU.subtract), R=['hend', 'hst'], W=['hst'])
                    T.op('pool', lambda e: e.tensor_tensor(out=him[:, isl, 0:S], in0=e3, in1=e4, op=ALU.add), R=['hend', 'hst'], W=['hst'])
                    if want_out:
                        T.op('pool', lambda e: e.tensor_tensor(out=v4(t1), in0=v4(bpr), in1=tab(costab), op=ALU.mult), R=['bpr'] + TAB, W=['s5tmp0'])
                        T.op('pool', lambda e: e.tensor_tensor(out=v4(t2), in0=v4(bpi), in1=tab(sintab), op=ALU.mult), R=['bpi'] + TAB, W=['s5tmp1'])
                        T.op('dve', lambda e: e.tensor_tensor(out=v4(t3), in0=v4(bpr), in1=tab(sintab), op=ALU.mult), R=['bpr'] + TAB, W=['s5tmp2'])
                        T.op('dve', lambda e: e.tensor_tensor(out=v4(t4), in0=v4(bpi), in1=tab(costab), op=ALU.mult), R=['bpi'] + TAB, W=['s5tmp3'])
                        T.op('pool', lambda e: e.tensor_tensor(out=h_b[:, :, 0, col0:col0 + ncols], in0=t1[:, :, 0:ncols],
                                                               in1=t2[:, :, 0:ncols], op=ALU.subtract), R=['s5tmp0', 's5tmp1', 'h_b'], W=['h_b'])
                        T.op('pool', lambda e: e.tensor_tensor(out=h_b[:, :, 1, col0:col0 + ncols], in0=t3[:, :, 0:ncols],
                                                               in1=t4[:, :, 0:ncols], op=ALU.add), R=['s5tmp2', 's5tmp3', 'h_b'], W=['h_b'])

            T.barrier()
            xnb = sb("xnb", [128, D], BF16)
            ropet = sb("ropet", [128, 128])
            sm = sb("sm", [128, 64])
            junk = sb("junk", [128, 512], BF16)
            ckvn_f = sb("ckvn_f", [128, 256]); krr_f = sb("krr_f", [128, 64]); krt = sb("krt", [128, 64])
            ckvn_b = sb("ckvn_b", [128, 256], BF16)
            krr_b = sb("krr_b", [128, 64], BF16)
            uT = sb("uT", [128, 4, 128], BF16); u_bg = sb("u_bg", [128, GT, 512], BF16)
            ckvT_all = sb("ckvT_all", [128, 2, NKT * 128], BF16)
            krT_all = sb("krT_all", [64, NKT * 128], BF16)
            KA = max(NKT, 16)
            ckv1_all = sb("ckv1_all", [128, KA, 256], BF16)
            c1flat = ckv1_all[:].rearrange("p a b -> p (a b)")
            rk_all = sb("rk_all", [128, NKT, 8])
            kbias = sb("kbias", [128, NKT])
            T.dma('sp', 'c0', lambda e: e.dma_start(out=kbias[:], in_=kbias_d[:, :]), W=['kbias'])
            ckvT_s = sb("ckvT_s", [128, 2, 128], BF16); krT_s = sb("krT_s", [64, 128], BF16)

            def rstd_of(src_ap, n, dst_ap, skey):
                T.op('act', lambda e: e.activation(out=junk[:, 0:n], in_=src_ap, func=AF.Square, accum_out=dst_ap),
                     R=[skey, 'sm'], W=['junk', 'sm'])
                T.op('act', lambda e: e.activation(out=dst_ap, in_=dst_ap, func=AF.Sqrt, scale=1.0 / n, bias=EPS), R=['sm'], W=['sm'])
                T.op('dve', lambda e: e.reciprocal(out=dst_ap, in_=dst_ap), R=['sm'], W=['sm'])

            def norm_transpose(src_d_ap, dst_fn, dkey):
                T.dma('sp', 'xld', lambda e: e.dma_start(out=xt[:], in_=src_d_ap), W=['xt'])
                T.op('act', lambda e: e.activation(out=xnb[:, 0:1024], in_=xt[:, 0:1024], func=AF.Square, accum_out=sm[:, 0:1]),
                     R=['xt', 'sm'], W=['xnb', 'sm'])
                T.op('act', lambda e: e.activation(out=xnb[:, 1024:2048], in_=xt[:, 1024:2048], func=AF.Square, accum_out=sm[:, 1:2]),
                     R=['xt', 'sm'], W=['xnb', 'sm'])
                T.op('dve', lambda e: e.tensor_tensor(out=sm[:, 0:1], in0=sm[:, 0:1], in1=sm[:, 1:2], op=ALU.add), R=['sm'], W=['sm'])
                T.op('act', lambda e: e.activation(out=sm[:, 0:1], in_=sm[:, 0:1], func=AF.Sqrt, scale=1.0 / D, bias=EPS), R=['sm'], W=['sm'])
                T.op('dve', lambda e: e.reciprocal(out=sm[:, 0:1], in_=sm[:, 0:1]), R=['sm'], W=['sm'])
                T.op('dve', lambda e: e.tensor_scalar(out=xnb[:], in0=xt[:], scalar1=sm[:, 0:1], scalar2=None, op0=ALU.mult),
                     R=['xt', 'sm', 'xnb'], W=['xnb'])
                for half in range(2):
                    p, pk = nextT()
                    for k in range(8):
                        kc = half * 8 + k
                        T.op('pe', lambda e, k=k, kc=kc, p=p: e.transpose(out=p[:, k * 128:(k + 1) * 128],
                                                                          in_=xnb[:, kc * 128:(kc + 1) * 128], identity=identb[:]),
                             R=['xnb', 'identb'], W=[pk])
                    if half == 0:
                        T.op('act', lambda e, p=p, half=half: e.copy(out=dst_fn(half), in_=p[:, :]), R=[pk, dkey], W=[dkey])
                    else:
                        T.op('dve', lambda e, p=p, half=half: e.tensor_copy(out=dst_fn(half), in_=p[:, :]), R=[pk, dkey], W=[dkey])

            sqb = sb("sqb", [128, 1024], BF16)
            kst = [sb("kst%d" % i, [128, 16]) for i in range(2)]
            kslot = [0]

            def key_norms(cT, cTk, kr_ap, krk, nk, rk_dst, rkk):
                sl_ = kslot[0]; kslot[0] = 1 - sl_
                ks, ksk = kst[sl_], 'kst%d' % sl_
                pa, pak = nextA()
                pb, pbk = nextA()
                for hb, (pp, ppk) in enumerate(((pa, pak), (pb, pbk))):
                    for kc in range(2):
                        T.op('pe', lambda e, kc=kc, pp=pp, hb=hb: e.matmul(pp[0:nk, :], lhsT=cT[:, kc, :],
                                                                          rhs=w_uk_b[:, kc, hb * 512:(hb + 1) * 512],
                                                                          start=(kc == 0), stop=(kc == 1)),
                             R=[cTk, 'w_uk_b'], W=[ppk])
                    T.op('act', lambda e, pp=pp, hb=hb: e.activation(out=sqb[0:nk, hb * 512:(hb + 1) * 512], in_=pp[0:nk, :],
                                                                     func=AF.Square), R=[ppk], W=['sqb%d' % hb])
                T.op('dve', lambda e: e.tensor_reduce(out=ks[0:nk, 0:8], in_=sqb[0:nk, :].rearrange("p (h d) -> p h d", d=128),
                                                      axis=AX.X, op=ALU.add), R=['sqb0', 'sqb1'], W=[ksk])
                T.op('act', lambda e: e.activation(out=junk[0:nk, 0:64], in_=kr_ap, func=AF.Square, accum_out=ks[0:nk, 8:9]),
                     R=[krk, ksk], W=['junk', ksk])
                T.op('dve', lambda e: e.tensor_scalar(out=ks[0:nk, 0:8], in0=ks[0:nk, 0:8], scalar1=ks[0:nk, 8:9], scalar2=None,
                                                      op0=ALU.add), R=[ksk], W=[ksk])
                T.op('act', lambda e: e.activation(out=ks[0:nk, 0:8], in_=ks[0:nk, 0:8], func=AF.Sqrt, scale=1.0 / 192, bias=EPS),
                     R=[ksk], W=[ksk])
                T.op('dve', lambda e: e.reciprocal(out=rk_dst, in_=ks[0:nk, 0:8]), R=[ksk, rkk], W=[rkk])

            def latent_post(pck, pckk, row0, out_ckv, out_kr, orow0, cT_dst, cTk, kT_dst, kTk, c1_dst, c1k, rk_dst):
                T.dma('sp', 'rld', lambda e: e.dma_start(out=ropet[:], in_=rope[row0:row0 + 128, :]), W=['ropet'])
                rstd_of(pck[:, 0:256], 256, sm[:, 2:3], pckk)
                T.op('dve', lambda e: e.scalar_tensor_tensor(out=ckvn_f[:], in0=pck[:, 0:256], scalar=sm[:, 2:3], in1=gkv_bc[:],
                                                             op0=ALU.mult, op1=ALU.mult), R=[pckk, 'sm', 'gkv_bc'], W=['ckvn_f'])
                T.op('dve', lambda e: e.tensor_tensor(out=krr_f[:], in0=pck[:, 256:320], in1=ropet[:, 0:64], op=ALU.mult),
                     R=[pckk, 'ropet'], W=['krr_f'])
                T.op('dve', lambda e: e.tensor_tensor(out=krt[:, 0:32], in0=pck[:, 288:320], in1=ropet[:, 64:96], op=ALU.mult),
                     R=[pckk, 'ropet'], W=['krt'])
                T.op('dve', lambda e: e.tensor_tensor(out=krt[:, 32:64], in0=pck[:, 256:288], in1=ropet[:, 96:128], op=ALU.mult),
                     R=[pckk, 'ropet', 'krt'], W=['krt'])
                T.op('dve', lambda e: e.tensor_tensor(out=krr_f[:], in0=krr_f[:], in1=krt[:], op=ALU.add), R=['krr_f', 'krt'], W=['krr_f'])
                if out_ckv is not None:
                    T.dma('sp', 'ost', lambda e: e.dma_start(out=out_ckv[orow0:orow0 + 128, :], in_=ckvn_f[:]), R=['ckvn_f'])
                    T.dma('sp', 'ost', lambda e: e.dma_start(out=out_kr[orow0:orow0 + 128, :], in_=krr_f[:]), R=['krr_f'])
                T.op('pool', lambda e: e.tensor_copy(out=ckvn_b[:], in_=ckvn_f[:]), R=['ckvn_f'], W=['ckvn_b'])
                T.op('pool', lambda e: e.tensor_copy(out=krr_b[:], in_=krr_f[:]), R=['krr_f'], W=['krr_b'])
                if c1_dst is not None:
                    T.op('pool', lambda e: e.tensor_copy(out=c1_dst, in_=ckvn_f[:]), R=['ckvn_f', c1k], W=[c1k])
                p, pk = nextT()
                for kc in range(2):
                    T.op('pe', lambda e, kc=kc: e.transpose(out=p[:, kc * 128:(kc + 1) * 128], in_=ckvn_b[:, kc * 128:(kc + 1) * 128],
                                                            identity=identb[:]), R=['ckvn_b', 'identb'], W=[pk])
                T.op('pe', lambda e: e.transpose(out=p[0:64, 256:384], in_=krr_b[:], identity=identb[:]), R=['krr_b', 'identb'], W=[pk])
                T.op('act', lambda e: e.copy(out=cT_dst, in_=p[:, 0:256].rearrange("p (a b) -> p a b", a=2)), R=[pk, cTk], W=[cTk])
                T.op('act', lambda e: e.copy(out=kT_dst, in_=p[0:64, 256:384]), R=[pk, kTk], W=[kTk])
                if rk_dst is not None:
                    key_norms(cT_dst, cTk, krr_f[:], 'krr_f', 128, rk_dst, 'rk_all')

            def u_transpose(src_b_ap, skey):
                p, pk = nextT()
                for k in range(4):
                    T.op('pe', lambda e, k=k: e.transpose(out=p[:, k * 128:(k + 1) * 128], in_=src_b_ap[:, k * 128:(k + 1) * 128],
                                                          identity=identb[:]), R=[skey, 'identb'], W=[pk])
                T.op('dve', lambda e: e.tensor_copy(out=uT[:].rearrange("p a b -> p (a b)"), in_=p[:, 0:512]), R=[pk], W=['uT'])

            for t in range(NPREV):
                norm_transpose(xall[t * 128:(t + 1) * 128, :],
                               lambda half: xnT1[:, half * 8:(half + 1) * 8, :].rearrange("p a b -> p (a b)"), 'xnT1')
                pu, puk = nextA()
                for kc in range(16):
                    T.op('pe', lambda e, kc=kc: e.matmul(pu[:, 0:512], lhsT=xnT1[:, kc, :], rhs=w_up_b[:, kc, 0:512],
                                                         start=(kc == 0), stop=(kc == 15)), R=['xnT1', 'w_up_b'], W=[puk])
                pc, pck = nextA()
                for kc in range(16):
                    T.op('pe', lambda e, kc=kc: e.matmul(pc[:, 0:320], lhsT=xnT1[:, kc, :], rhs=w_up_b[:, kc, 512:832],
                                                         start=(kc == 0), stop=(kc == 15)), R=['xnT1', 'w_up_b'], W=[pck])
                T.op('act', lambda e: e.copy(out=u_bg[:, 0, :], in_=pu[:, 0:512]), R=[puk], W=['u_bg'])
                latent_post(pc, pck, t * 128, None, None, 0, ckvT_all[:, :, t * 128:(t + 1) * 128], 'ckvT_all',
                            krT_all[0:64, t * 128:(t + 1) * 128], 'krT_all', ckv1_all[:, t, :], 'ckv1_all', rk_all[:, t, :])
                u_transpose(u_bg[:, 0, :], 'u_bg')
                for sub in range(128 // LS):
                    ssm_step(lambda ct, sub=sub: uT[:, ct, sub * LS:(sub + 1) * LS], LS, 1, LS, 0, False)
            dbg('hre', hre[:, :, 0], [128, 16], ['hst']); dbg('him', him[:, :, 0], [128, 16], ['hst'])
            T.barrier()

            chk('P')
            memKT_b = sb("memKT_b", [128, 4, 256], BF16)
            memV_b = sb("memV_b", [128, 2, 512], BF16)
            wblk = sb("wblk", [128, 16, 256], BF16)
            yf = sb("yf", [128, 512]); yg = sb("yg", [128, 512])
            ob = sb("ob", [128, 512], BF16)

            def stream_wblock(w_d, c0, ncols, gcol):
                for q4 in range(4):
                    T.dma('sp', 'wst', lambda e, q4=q4: e.dma_start(
                        out=stg[:, 0:4 * ncols].rearrange("p (k c) -> p k c", k=4),
                        in_=w_d[q4 * 512:(q4 + 1) * 512, c0:c0 + ncols].rearrange("(k p) c -> p k c", p=128)), W=['stg'])
                    for k in range(4):
                        kc = q4 * 4 + k
                        eng = 'dve' if (k % 2 == 0) else 'pool'
                        T.op(eng, lambda e, k=k, kc=kc: e.tensor_scalar(out=wblk[:, kc, 0:ncols], in0=stg[:, k * ncols:(k + 1) * ncols],
                                                                        scalar1=gcol[:, kc:kc + 1], scalar2=None, op0=ALU.mult),
                             R=['stg', 'wblk'], W=['wblk'])
                return wblk, 'wblk'

            for mt in range(2):
                norm_transpose(mem_d[mt * 128:(mt + 1) * 128, :],
                               lambda half, mt=mt: memT[:, mt, half * 8:(half + 1) * 8, :].rearrange("p a b -> p (a b)"), 'memT')
            chk('M0')
            for which, w_d in enumerate((w_mk_d, w_mv_d)):
                for cb in range(2):
                    wb, wk = stream_wblock(w_d, cb * 256, 256, gcol_mem)
                    chk('M1_%d_%d' % (which, cb))
                    for mt in range(2):
                        p, pk = nextA()
                        for kc in range(16):
                            T.op('pe', lambda e, kc=kc, mt=mt, p=p: e.matmul(p[:, 0:256], lhsT=memT[:, mt, kc, :], rhs=wb[:, kc, 0:256],
                                                                             start=(kc == 0), stop=(kc == 15)), R=['memT', wk], W=[pk])
                        chk('M2')
                        if which == 1:
                            T.op('act', lambda e, p=p: e.copy(out=yf[:, 0:256], in_=p[:, 0:256]), R=[pk], W=['yf'])
                            T.op('dve', lambda e, p=p, mt=mt, cb=cb: e.tensor_copy(out=memV_b[:, mt, cb * 256:(cb + 1) * 256], in_=p[:, 0:256]),
                                 R=[pk, 'memV_b'], W=['memV_b'])
                            T.dma('sp', 'ost', lambda e, mt=mt, cb=cb: e.dma_start(out=memv_o[mt * 128:(mt + 1) * 128, cb * 256:(cb + 1) * 256],
                                                                                   in_=yf[:, 0:256]), R=['yf'])
                        else:
                            for hh in range(2):
                                rstd_of(p[:, hh * 128:(hh + 1) * 128], 128, sm[:, 3:4], pk)
                                T.op('dve', lambda e, p=p, hh=hh: e.scalar_tensor_tensor(out=yf[:, hh * 128:(hh + 1) * 128],
                                                                                         in0=p[:, hh * 128:(hh + 1) * 128], scalar=sm[:, 3:4],
                                                                                         in1=gmk_bc[:], op0=ALU.mult, op1=ALU.mult),
                                     R=[pk, 'sm', 'gmk_bc', 'yf'], W=['yf'])
                            chk('M3')
                            T.op('dve', lambda e: e.tensor_copy(out=ob[:, 0:256], in_=yf[:, 0:256]), R=['yf'], W=['ob'])
                            T.dma('sp', 'ost', lambda e, mt=mt, cb=cb: e.dma_start(out=memk_o[mt * 128:(mt + 1) * 128, cb * 256:(cb + 1) * 256],
                                                                                   in_=yf[:, 0:256]), R=['yf'])
                            chk('M4')
                            pt_, ptk = nextT()
                            for hh in range(2):
                                T.op('pe', lambda e, hh=hh, pt_=pt_: e.transpose(out=pt_[:, hh * 128:(hh + 1) * 128], in_=ob[:, hh * 128:(hh + 1) * 128],
                                                                                 identity=identb[:]), R=['ob', 'identb'], W=[ptk])
                            T.op('act', lambda e, cb=cb, mt=mt, pt_=pt_: e.copy(out=memKT_b[:, cb * 2:cb * 2 + 2, mt * 128:(mt + 1) * 128],
                                                                                in_=pt_[:, 0:256].rearrange("p (a b) -> p a b", a=2)),
                                 R=[ptk, 'memKT_b'], W=['memKT_b'])
                            chk('M5')
                            if cb == 1 and mt == 1:
                                chk('M6')
            T.barrier()

            chk('M')
            ssq = sb("ssq", [128, GT, 8])
            qf = xt[:, 0:1536].rearrange("p (h c) -> p h c", h=8); qs_b = sb("qs_b", [128, 8, 192], BF16)
            wk6 = sb("wk6", [128, 1536])
            qsq = wk6[:, :].rearrange("p (h c) -> p h c", h=8)
            ymT8 = wk6[:, 0:1024].rearrange("p (h c) -> p h c", h=8)
            ymT4 = wk6[:, 0:4 * GC].rearrange("p (h c) -> p h c", h=4)
            cq_b = sb("cq_b", [128, 512], BF16); cqT = sb("cqT", [128, 4, 128], BF16)
            qabsT = sb("qabsT", [128, 2, 8, 128], BF16)
            qav = qabsT[:].rearrange("p a h c -> p a (h c)")
            PT = sb("PT", [128, GC], BF16)
            OTn = sb("OTn", [128, 2, GC], BF16)
            rcp = sb("rcp", [128, GC])
            xres = sb("xres", [128, 256]); yout = sb("yout", [128, 256])
            idx_b = sb("idx_b", [128, NPG], I32); idxf = sb("idxf", [128, NPG])
            iot = sb("iot", [128, 1], I32); iotf = sb("iotf", [128, 1])
            pgc = [sb("pgc%d" % i, [128, 256]) for i in range(2)]
            pgk = [sb("pgk%d" % i, [128, 64]) for i in range(2)]
            pgcb2 = [sb("pgcb%d" % i, [128, 256], BF16) for i in range(2)]; pgkb2 = [sb("pgkb%d" % i, [128, 64], BF16) for i in range(2)]
            pgT2 = [sb("pgT%d" % i, [128, 2, 128], BF16) for i in range(2)]; pgkT2 = [sb("pgkT%d" % i, [64, 128], BF16) for i in range(2)]
            rkp2 = [sb("rkp%d" % i, [128, 8]) for i in range(2)]; scf2 = [sb("scf%d" % i, [128, 64]) for i in range(2)]
            PTs2 = [sb("PTs%d" % i, [128, 64], BF16) for i in range(2)]
            mini = sb("mini", [8, 256], BF16)
            mkb = c1flat[:, 0:1024].rearrange("p (t c) -> p t c", t=2); mvb = c1flat[:, 1024:2048].rearrange("p (t c) -> p t c", t=2)
            mkT_s = c1flat[:, 2048:3072].rearrange("p (h c) -> p h c", h=4)
            mcf = [xt[:, 0:1024].rearrange("p (t c) -> p t c", t=2), xt[:, 1024:2048].rearrange("p (t c) -> p t c", t=2)]

            def in_proj_block(ng, c0, ncols, consume):
                stream_wblock(w_in, c0, ncols, gcol_in)
                for ti in range(ng):
                    p, pk = nextA()
                    for kc in range(16):
                        T.op('pe', lambda e, kc=kc, ti=ti, p=p: e.matmul(p[:, 0:ncols], lhsT=xnT_g[:, ti, kc, :], rhs=wblk[:, kc, 0:ncols],
                                                                         start=(kc == 0), stop=(kc == 15)), R=['xnT_g', 'wblk'], W=[pk])
                    consume(ti, p, pk)

            def mem_attend(ncols, qcols, KT, KTk, V, Vk, out_fn):
                for h in range(4):
                    acc, acck = psC[0], 'psC0'
                    lb, lbk = psC[1], 'psC1'
                    for kt in range(2):
                        p, pk = nextA()
                        T.op('pe', lambda e, kt=kt, h=h, p=p: e.matmul(p[:, 0:ncols], lhsT=KT[:, h, kt * 128:(kt + 1) * 128],
                                                                       rhs=QmT[:, h, qcols], start=True, stop=True), R=[KTk, 'QmT'], W=[pk])
                        T.op('act', lambda e, p=p: e.activation(out=PT[:, 0:ncols], in_=p[:, 0:ncols], func=AF.Exp), R=[pk], W=['PT'])
                        T.op('pe', lambda e, kt=kt, h=h: e.matmul(acc[:, 0:ncols], lhsT=V[:, kt, h * 128:(h + 1) * 128], rhs=PT[:, 0:ncols],
                                                                  start=(kt == 0), stop=(kt == 1)), R=[Vk, 'PT'], W=[acck])
                        T.op('pe', lambda e, kt=kt: e.matmul(lb[:, 0:ncols], lhsT=onesb[:], rhs=PT[:, 0:ncols],
                                                             start=(kt == 0), stop=(kt == 1)), R=['onesb', 'PT'], W=[lbk])
                    T.op('dve', lambda e: e.reciprocal(out=rcp[:, 0:ncols], in_=lb[:, 0:ncols]), R=[lbk], W=['rcp'])
                    T.op('dve', lambda e, h=h: e.tensor_tensor(out=out_fn(h), in0=acc[:, 0:ncols], in1=rcp[:, 0:ncols], op=ALU.mult),
                         R=[acck, 'rcp', 'wk6'], W=['wk6'])

            def finish_branch(ti, src_ap, skey, n, gate0):
                rstd_of(src_ap, n, sm[:, 4:5], skey)
                T.op('dve', lambda e: e.scalar_tensor_tensor(out=gates[:, ti, gate0:gate0 + n], in0=src_ap, scalar=sm[:, 4:5],
                                                             in1=gates[:, ti, gate0:gate0 + n], op0=ALU.mult, op1=ALU.mult),
                     R=[skey, 'sm', 'gates'], W=['gates'])

            def mem_branch_finish(ng, get_cols):
                for ti in range(ng):
                    p, pk = nextA()
                    for h in range(4):
                        T.op('pe', lambda e, h=h, ti=ti, p=p: e.transpose(out=p[:, h * 128:(h + 1) * 128], in_=get_cols(h, ti), identity=identf[:]),
                             R=['wk6', 'identf'], W=[pk])
                    T.op('act', lambda e, p=p: e.copy(out=yf[:], in_=p[:, 0:512]), R=[pk], W=['yf'])
                    finish_branch(ti, yf[:], 'yf', 512, 1536)

            def q_path(ti, row0):
                T.dma('sp', 'rld', lambda e: e.dma_start(out=ropet[:], in_=rope[row0:row0 + 128, :]), W=['ropet'])
                rstd_of(yf[:], 512, sm[:, 5:6], 'yf')
                T.op('dve', lambda e: e.tensor_scalar(out=cq_b[:], in0=yf[:], scalar1=sm[:, 5:6], scalar2=None, op0=ALU.mult),
                     R=['yf', 'sm'], W=['cq_b'])
                p, pk = nextT()
                for k in range(4):
                    T.op('pe', lambda e, k=k, p=p: e.transpose(out=p[:, k * 128:(k + 1) * 128], in_=cq_b[:, k * 128:(k + 1) * 128],
                                                               identity=identb[:]), R=['cq_b', 'identb'], W=[pk])
                T.op('dve', lambda e, p=p: e.tensor_copy(out=cqT[:].rearrange("p a b -> p (a b)"), in_=p[:, 0:512]), R=[pk], W=['cqT'])
                for blk in range(3):
                    pq, pqk = nextA()
                    for k in range(4):
                        T.op('pe', lambda e, k=k, blk=blk, pq=pq: e.matmul(pq[:, 0:512], lhsT=cqT[:, k, :],
                                                                          rhs=w_uq_b[:, k, blk * 512:(blk + 1) * 512],
                                                                          start=(k == 0), stop=(k == 3)), R=['cqT', 'w_uq_b'], W=[pqk])
                    T.op('act', lambda e, blk=blk, pq=pq: e.copy(out=qf[:].rearrange("p h c -> p (h c)")[:, blk * 512:(blk + 1) * 512],
                                                                 in_=pq[:, 0:512]), R=[pqk, 'xt'], W=['xt'])
                cc = ropet[:, 0:64].unsqueeze(1).to_broadcast([128, 8, 64])
                ns = ropet[:, 64:96].unsqueeze(1).to_broadcast([128, 8, 32])
                ps_ = ropet[:, 96:128].unsqueeze(1).to_broadcast([128, 8, 32])
                T.op('dve', lambda e: e.tensor_tensor(out=qsq[:, :, 0:32], in0=qf[:, :, 160:192], in1=ns, op=ALU.mult), R=['xt', 'ropet'], W=['wk6'])
                T.op('dve', lambda e: e.tensor_tensor(out=qsq[:, :, 32:64], in0=qf[:, :, 128:160], in1=ps_, op=ALU.mult), R=['xt', 'ropet', 'wk6'], W=['wk6'])
                T.op('dve', lambda e: e.tensor_tensor(out=qf[:, :, 128:192], in0=qf[:, :, 128:192], in1=cc, op=ALU.mult), R=['xt', 'ropet', 'wk6'], W=['xt'])
                T.op('dve', lambda e: e.tensor_tensor(out=qf[:, :, 128:192], in0=qf[:, :, 128:192], in1=qsq[:, :, 0:64], op=ALU.add), R=['xt', 'wk6'], W=['xt'])
                T.op('pool', lambda e: e.tensor_tensor(out=qsq[:], in0=qf[:], in1=qf[:], op=ALU.mult), R=['xt', 'wk6'], W=['wk6'])
                T.op('dve', lambda e: e.tensor_reduce(out=sm[:, 24:32], in_=qsq[:], axis=AX.X, op=ALU.add), R=['wk6', 'sm'], W=['sm'])
                T.op('act', lambda e: e.activation(out=sm[:, 24:32], in_=sm[:, 24:32], func=AF.Sqrt, scale=1.0 / 192, bias=EPS), R=['sm'], W=['sm'])
                T.op('dve', lambda e: e.reciprocal(out=sm[:, 24:32], in_=sm[:, 24:32]), R=['sm'], W=['sm'])
                T.op('dve', lambda e: e.tensor_tensor(out=qf[:], in0=qf[:], in1=sm[:, 24:32].unsqueeze(2).to_broadcast([128, 8, 192]), op=ALU.mult),
                     R=['xt', 'sm'], W=['xt'])
                T.op('dve', lambda e: e.tensor_tensor(out=qs_b[:], in0=qf[:], in1=gqk_bc[:].unsqueeze(1).to_broadcast([128, 8, 192]), op=ALU.mult),
                     R=['xt', 'gqk_bc'], W=['qs_b'])
                for h in range(8):
                    p, pk = nextT()
                    T.op('pe', lambda e, h=h, p=p: e.transpose(out=p[:, 0:128], in_=qs_b[:, h, 0:128], identity=identb[:]), R=['qs_b', 'identb'], W=[pk])
                    T.op('pe', lambda e, h=h, p=p: e.transpose(out=p[0:64, 128:256], in_=qs_b[:, h, 128:192], identity=identb[:]), R=['qs_b', 'identb'], W=[pk])
                    T.op('act', lambda e, h=h, p=p: e.copy(out=QTn[:, h, ti * 128:(ti + 1) * 128], in_=p[:, 0:128]), R=[pk, 'QTn'], W=['QTn'])
                    T.op('dve', lambda e, h=h, p=p: e.tensor_copy(out=QTr[0:64, h, ti * 128:(ti + 1) * 128], in_=p[0:64, 128:256]), R=[pk, 'QTr'], W=['QTr'])

            def prompt_attention(ng, ncg, own0):
                for h in range(8):
                    for kc in range(2):
                        p, pk = nextA()
                        T.op('pe', lambda e, kc=kc, h=h, p=p: e.matmul(p[:, 0:ncg], lhsT=w_ukT_b[:, h, kc * 128:(kc + 1) * 128],
                                                                       rhs=QTn[:, h, 0:ncg], start=True, stop=True), R=['w_ukT_b', 'QTn'], W=[pk])
                        T.op('act', lambda e, kc=kc, p=p: e.copy(out=qav[:, kc, 0:ncg], in_=p[:, 0:ncg]), R=[pk, 'qabsT'], W=['qabsT'])
                    nkt = NPREV + own0 + ng
                    for kt in range(nkt):
                        rel = kt - (NPREV + own0)
                        c0 = max(rel, 0) * 128
                        ncol = ncg - c0
                        kcols = slice(kt * 128, (kt + 1) * 128)
                        p, pk = nextA()
                        for kc in range(2):
                            T.op('pe', lambda e, kc=kc, p=p, kcols=kcols, c0=c0, ncol=ncol: e.matmul(
                                p[:, 0:ncol], lhsT=ckvT_all[:, kc, kcols], rhs=qav[:, kc, c0:ncg], start=(kc == 0), stop=False),
                                R=['ckvT_all', 'qabsT'], W=[pk])
                        T.op('pe', lambda e, h=h, p=p, kcols=kcols, c0=c0, ncol=ncol: e.matmul(
                            p[:, 0:ncol], lhsT=krT_all[0:64, kcols], rhs=QTr[0:64, h, c0:ncg], start=False, stop=True),
                            R=['krT_all', 'QTr'], W=[pk])
                        T.op('act', lambda e, kt=kt, h=h, p=p, c0=c0, ncol=ncol: e.activation(
                            out=PT[:, c0:ncg], in_=p[:, 0:ncol], func=AF.Exp, scale=rk_all[:, kt, h:h + 1], bias=kbias[:, kt:kt + 1]),
                            R=[pk, 'rk_all', 'kbias'], W=['PT'])
                        if rel >= 0:
                            T.op('pool', lambda e, c0=c0: e.tensor_tensor(out=PT[:, c0:c0 + 128], in0=PT[:, c0:c0 + 128], in1=trib[:], op=ALU.mult),
                                 R=['PT', 'trib'], W=['PT'])
                        first, last = (kt == 0), (kt == nkt - 1)
                        for kc in range(2):
                            T.op('pe', lambda e, kc=kc, kt=kt, c0=c0, first=first, last=last: e.matmul(
                                psC[kc][:, c0:ncg], lhsT=ckv1_all[:, kt, kc * 128:(kc + 1) * 128], rhs=PT[:, c0:ncg], start=first, stop=last),
                                R=['ckv1_all', 'PT'], W=['psC%d' % kc])
                        T.op('pe', lambda e, c0=c0, first=first, last=last: e.matmul(psC[2][:, c0:ncg], lhsT=onesb[:], rhs=PT[:, c0:ncg],
                                                                                     start=first, stop=last),
                             R=['onesb', 'PT'], W=['psC2'])
                    T.op('dve', lambda e: e.reciprocal(out=rcp[:, 0:ncg], in_=psC[2][:, 0:ncg]), R=['psC2'], W=['rcp'])
                    for kc in range(2):
                        T.op('dve', lambda e, kc=kc: e.tensor_tensor(out=OTn[:, kc, 0:ncg], in0=psC[kc][:, 0:ncg], in1=rcp[:, 0:ncg], op=ALU.mult),
                             R=['psC%d' % kc, 'rcp', 'OTn'], W=['OTn'])
                    for ti in range(ng):
                        p, pk = nextA()
                        for kc in range(2):
                            T.op('pe', lambda e, kc=kc, ti=ti, h=h, p=p: e.matmul(p[:, 0:128], lhsT=OTn[:, kc, ti * 128:(ti + 1) * 128],
                                                                                 rhs=w_uv_b[:, kc, h * 128:(h + 1) * 128],
                                                                                 start=(kc == 0), stop=(kc == 1)), R=['OTn', 'w_uv_b'], W=[pk])
                        T.op('dve', lambda e, ti=ti, h=h, p=p: e.tensor_copy(out=ymla[:, ti, h * 128:(h + 1) * 128], in_=p[:, 0:128]),
                             R=[pk, 'ymla'], W=['ymla'])
                        T.op('act', lambda e, ti=ti, h=h, p=p: e.activation(out=junk[:, 0:128], in_=p[:, 0:128], func=AF.Square,
                                                                            accum_out=ssq[:, ti, h:h + 1]), R=[pk, 'ssq'], W=['junk', 'ssq'])
                mem_attend(ncg, slice(0, ncg), memKT_b, 'memKT_b', memV_b, 'memV_b', lambda h: ymT4[:, h, 0:ncg])
                mem_branch_finish(ng, lambda h, ti: ymT4[:, h, ti * 128:(ti + 1) * 128])

            def sample_attention():
                T.op('pool', lambda e: e.iota(iot[:], pattern=[[0, 1]], base=0, channel_multiplier=1), W=['iot'])
                T.op('dve', lambda e: e.tensor_copy(out=iotf[:], in_=iot[:]), R=['iot'], W=['iotf'])
                for h in range(8):
                    for kc in range(2):
                        p, pk = nextA()
                        T.op('pe', lambda e, kc=kc, h=h, p=p: e.matmul(p[:, 0:128], lhsT=w_ukT_b[:, h, kc * 128:(kc + 1) * 128],
                                                                       rhs=QTn[:, h, 0:128], start=True, stop=True), R=['w_ukT_b', 'QTn'], W=[pk])
                        T.op('act', lambda e, kc=kc, h=h, p=p: e.copy(out=qabsT[:, kc, h, :], in_=p[:, 0:128]), R=[pk, 'qabsT'], W=['qabsT'])
                acc = psC[0]
                o0 = acc[:, 0:64]; o1 = acc[:, 64:128]; lb = acc[:, 128:192]
                for b in range(NSEQ):
                    bc = slice(b * 8, (b + 1) * 8)
                    T.dma('sp', 'c0', lambda e, b=b: e.dma_start(out=idx_b[:], in_=ptab[b].partition_broadcast(128)), W=['idx_b'])
                    T.op('dve', lambda e: e.tensor_copy(out=idxf[:], in_=idx_b[:]), R=['idx_b'], W=['idxf'])
                    T.op('dve', lambda e: e.tensor_scalar(out=idxf[:], in0=idxf[:], scalar1=128.0, scalar2=iotf[:, 0:1], op0=ALU.mult, op1=ALU.add),
                         R=['idxf', 'iotf'], W=['idxf'])
                    T.op('dve', lambda e: e.tensor_copy(out=idx_b[:], in_=idxf[:]), R=['idxf'], W=['idx_b'])

                    def attend(sl_, cT, cTk, krT_ap, krTk, c1, c1k, rk_ap, rkk, nk, first, last, mask):
                        p, pk = psC[1 + sl_], 'psC%d' % (1 + sl_)
                        scf, scfk = scf2[sl_], 'scf%d' % sl_
                        PTs, PTk = PTs2[sl_], 'PTs%d' % sl_
                        pv = p[0:nk, 0:64]
                        for kc in range(2):
                            T.op('pe', lambda e, kc=kc: e.matmul(pv, lhsT=cT[:, kc, :], rhs=qabsT[:, kc, :, bc], start=(kc == 0), stop=False),
                                 R=[cTk, 'qabsT'], W=[pk])
                        T.op('pe', lambda e: e.matmul(pv, lhsT=krT_ap, rhs=QTr[0:64, :, bc], start=False, stop=True), R=[krTk, 'QTr'], W=[pk])
                        T.op('dve', lambda e: e.tensor_tensor(out=scf[0:nk, :].rearrange("p (h q) -> p h q", q=8),
                                                              in0=pv.rearrange("p (h q) -> p h q", q=8),
                                                              in1=rk_ap.unsqueeze(2).to_broadcast([nk, 8, 8]), op=ALU.mult),
                             R=[pk, rkk], W=[scfk])
                        T.op('act', lambda e: e.activation(out=PTs[0:nk, 0:64], in_=scf[0:nk, :], func=AF.Exp), R=[scfk], W=[PTk])
                        if mask:
                            T.op('dve', lambda e: e.tensor_tensor(out=PTs[0:nk, 0:64].rearrange("p (h q) -> p h q", q=8),
                                                                  in0=PTs[0:nk, 0:64].rearrange("p (h q) -> p h q", q=8),
                                                                  in1=trib[0:nk, 0:8].unsqueeze(1).to_broadcast([nk, 8, 8]), op=ALU.mult),
                                 R=[PTk, 'trib'], W=[PTk])
                        T.op('pe', lambda e: e.matmul(o0, lhsT=c1[0:nk, 0:128], rhs=PTs[0:nk, 0:64], start=first, stop=last,
                                                      skip_group_check=True), R=[c1k, PTk], W=['psC0'])
                        T.op('pe', lambda e: e.matmul(o1, lhsT=c1[0:nk, 128:256], rhs=PTs[0:nk, 0:64], start=False, stop=last,
                                                      skip_group_check=True), R=[c1k, PTk], W=['psC0'])
                        T.op('pe', lambda e: e.matmul(lb, lhsT=onesb[0:nk, :], rhs=PTs[0:nk, 0:64], start=False, stop=last,
                                                      skip_group_check=True), R=['onesb', PTk], W=['psC0'])

                    for pg in range(NPG):
                        s = pg % 2
                        pgcb, pgkb, pgT, pgkT, rkp = pgcb2[s], pgkb2[s], pgT2[s], pgkT2[s], rkp2[s]
                        T.dma('pool', 'pgc%d' % s, lambda e, s=s, pg=pg: e.indirect_dma_start(
                            out=pgc[s][:], out_offset=None, in_=cckv[:, :],
                            in_offset=bass.IndirectOffsetOnAxis(ap=idx_b[:, pg:pg + 1], axis=0)), R=['idx_b'], W=['pgc%d' % s])
                        T.dma('pool', 'pgk%d' % s, lambda e, s=s, pg=pg: e.indirect_dma_start(
                            out=pgk[s][:], out_offset=None, in_=ckr[:, :],
                            in_offset=bass.IndirectOffsetOnAxis(ap=idx_b[:, pg:pg + 1], axis=0)), R=['idx_b'], W=['pgk%d' % s])
                        T.op('pool', lambda e, s=s, pgcb=pgcb: e.tensor_copy(out=pgcb[:], in_=pgc[s][:]), R=['pgc%d' % s], W=['pgcb%d' % s])
                        T.op('pool', lambda e, s=s, pgkb=pgkb: e.tensor_copy(out=pgkb[:], in_=pgk[s][:]), R=['pgk%d' % s], W=['pgkb%d' % s])
                        p, pk = nextT()
                        for kc in range(2):
                            T.op('pe', lambda e, kc=kc, p=p, pgcb=pgcb: e.transpose(out=p[:, kc * 128:(kc + 1) * 128], in_=pgcb[:, kc * 128:(kc + 1) * 128],
                                                                                   identity=identb[:]), R=['pgcb%d' % s, 'identb'], W=[pk])
                        T.op('pe', lambda e, p=p, pgkb=pgkb: e.transpose(out=p[0:64, 256:384], in_=pgkb[:], identity=identb[:]), R=['pgkb%d' % s, 'identb'], W=[pk])
                        T.op('dve', lambda e, p=p, pgT=pgT: e.tensor_copy(out=pgT[:], in_=p[:, 0:256].rearrange("p (a b) -> p a b", a=2)), R=[pk], W=['pgT%d' % s])
                        T.op('dve', lambda e, p=p, pgkT=pgkT: e.tensor_copy(out=pgkT[:], in_=p[0:64, 256:384]), R=[pk], W=['pgkT%d' % s])
                        key_norms(pgT[:], 'pgT%d' % s, pgk[s][:], 'pgk%d' % s, 128, rkp[:], 'rkp%d' % s)
                        attend(s, pgT[:], 'pgT%d' % s, pgkT[:], 'pgkT%d' % s, pgcb, 'pgcb%d' % s, rkp[:], 'rkp%d' % s, 128, pg == 0, False, False)
                    s = NPG % 2
                    rkp = rkp2[s]; scfm = scf2[1 - s]
                    p, pk = nextT()
                    for kc in range(2):
                        T.op('pe', lambda e, kc=kc, p=p: e.transpose(out=p[0:8, kc * 128:(kc + 1) * 128], in_=ckvT_s[:, kc, bc], identity=identb[:]),
                             R=['ckvT_s', 'identb'], W=[pk])
                    T.op('pe', lambda e, p=p: e.transpose(out=p[0:8, 256:320], in_=krT_s[0:64, bc], identity=identb[0:64, 0:64]),
                         R=['krT_s', 'identb'], W=[pk])
                    T.op('act', lambda e, p=p: e.copy(out=mini[:], in_=p[0:8, 0:256]), R=[pk], W=['mini'])
                    T.op('dve', lambda e, p=p: e.tensor_copy(out=scfm[0:8, 0:64], in_=p[0:8, 256:320]), R=[pk], W=['scf%d' % (1 - s)])
                    key_norms(ckvT_s[:, :, bc], 'ckvT_s', scfm[0:8, 0:64], 'scf%d' % (1 - s), 8, rkp[0:8, :], 'rkp%d' % s)
                    attend(s, ckvT_s[:, :, bc], 'ckvT_s', krT_s[0:64, bc], 'krT_s', mini, 'mini', rkp[0:8, :], 'rkp%d' % s, 8, NPG == 0, True, True)
                    T.op('dve', lambda e: e.reciprocal(out=rcp[:, 0:64], in_=lb), R=['psC0'], W=['rcp'])
                    T.op('dve', lambda e: e.tensor_tensor(out=OTn[:, 0, 0:64], in0=o0, in1=rcp[:, 0:64], op=ALU.mult), R=['psC0', 'rcp'], W=['OTn'])
                    T.op('dve', lambda e: e.tensor_tensor(out=OTn[:, 1, 0:64], in0=o1, in1=rcp[:, 0:64], op=ALU.mult), R=['psC0', 'rcp', 'OTn'], W=['OTn'])
                    p, pk = nextA()
                    for h in range(8):
                        for kc in range(2):
                            T.op('pe', lambda e, kc=kc, h=h, p=p: e.matmul(p[:, h * 8:(h + 1) * 8], lhsT=w_uv_b[:, kc, h * 128:(h + 1) * 128],
                                                                           rhs=OTn[:, kc, h * 8:(h + 1) * 8], start=(kc == 0), stop=(kc == 1)),
                                 R=['w_uv_b', 'OTn'], W=[pk])
                    T.op('act', lambda e, p=p: e.copy(out=ymT8[:, :, bc], in_=p[:, 0:64].rearrange("p (h q) -> p h q", q=8)), R=[pk, 'wk6'], W=['wk6'])
                for h in range(8):
                    p, pk = nextA()
                    T.op('pe', lambda e, h=h, p=p: e.transpose(out=p[:, 0:128], in_=ymT8[:, h, :], identity=identf[:]), R=['wk6', 'identf'], W=[pk])
                    T.op('dve', lambda e, h=h, p=p: e.tensor_copy(out=ymla[:, 0, h * 128:(h + 1) * 128], in_=p[:, 0:128]), R=[pk, 'ymla'], W=['ymla'])
                    T.op('act', lambda e, h=h, p=p: e.activation(out=junk[:, 0:128], in_=p[:, 0:128], func=AF.Square, accum_out=ssq[:, 0, h:h + 1]),
                         R=[pk, 'ssq'], W=['junk', 'ssq'])
                for b in range(NSEQ):
                    bc = slice(b * 8, (b + 1) * 8)
                    T.dma('sp', 'mck', lambda e, b=b: e.dma_start(out=mcf[0], in_=cmk[b * 256:(b + 1) * 256, :].rearrange("(t p) c -> p t c", p=128)),
                          W=['xt'])
                    T.dma('sp', 'mcv', lambda e, b=b: e.dma_start(out=mcf[1], in_=cmv[b * 256:(b + 1) * 256, :].rearrange("(t p) c -> p t c", p=128)),
                          W=['xt'])
                    T.op('pool', lambda e: e.tensor_copy(out=mkb[:], in_=mcf[0]), R=['xt'], W=['mkb'])
                    T.op('pool', lambda e: e.tensor_copy(out=mvb[:], in_=mcf[1]), R=['xt'], W=['mvb'])
                    for kt in range(2):
                        p, pk = nextT()
                        for h in range(4):
                            T.op('pe', lambda e, kt=kt, h=h, p=p: e.transpose(out=p[:, h * 128:(h + 1) * 128], in_=mkb[:, kt, h * 128:(h + 1) * 128],
                                                                              identity=identb[:]), R=['mkb', 'identb'], W=[pk])
                        T.op('act', lambda e, kt=kt, p=p: e.copy(out=mkT_s[:, :, kt * 128:(kt + 1) * 128],
                                                                in_=p[:, 0:512].rearrange("p (a b) -> p a b", a=4)), R=[pk, 'mkT_s'], W=['mkT_s'])
                    mem_attend(8, bc, mkT_s, 'mkT_s', mvb, 'mvb', lambda h, bc=bc: ymT4[:, h, bc])
                mem_branch_finish(1, lambda h, ti: ymT4[:, h, 0:128])

            def run_group(tiles, is_sample):
                ng = len(tiles)
                ncg = ng * 128
                for ti, (row0, _) in enumerate(tiles):
                    norm_transpose(xall[row0:row0 + 128, :],
                                   lambda half, ti=ti: xnT_g[:, ti, half * 8:(half + 1) * 8, :].rearrange("p a b -> p (a b)"), 'xnT_g')

                def c_u(half):
                    def f(ti, p, pk):
                        T.op('act', lambda e: e.copy(out=u_bg[:, ti, half * 256:(half + 1) * 256], in_=p[:, 0:256]), R=[pk, 'u_bg'], W=['u_bg'])
                    return f
                in_proj_block(ng, C_U, 256, c_u(0))
                in_proj_block(ng, C_U + 256, 256, c_u(1))

                def c_gate(goff):
                    def f(ti, p, pk):
                        T.op('act', lambda e: e.activation(out=gates[:, ti, goff:goff + 256], in_=p[:, 0:256], func=AF.Silu),
                             R=[pk, 'gates'], W=['gates'])
                    return f
                in_proj_block(ng, C_GS, 256, c_gate(0))
                in_proj_block(ng, C_GS + 256, 256, c_gate(256))
                chk('G0')
                for ti, (row0, own) in enumerate(tiles):
                    u_transpose(u_bg[:, ti, :], 'u_bg')
                    py, pyk = psC[2], 'psC2'
                    for gi in range(16 // GS):
                        if is_sample:
                            ssm_step(lambda ct: uT[:, ct, :], 128, NSEQ, 8, 0, True, gis=[gi])
                        else:
                            for sub in range(128 // LS):
                                ssm_step(lambda ct, sub=sub: uT[:, ct, sub * LS:(sub + 1) * LS], LS, 1, LS, sub * LS, True, gis=[gi])
                        for j in range(GS):
                            i = gi * GS + j
                            T.op('pe', lambda e, i=i, j=j: e.matmul(py[:, i * 32:(i + 1) * 32], lhsT=h_b[:, j, 0, :], rhs=CT_b[:, i, 0, :],
                                                                    start=True, stop=False), R=['h_b', 'CT_b'], W=[pyk])
                            T.op('pe', lambda e, i=i, j=j: e.matmul(py[:, i * 32:(i + 1) * 32], lhsT=h_b[:, j, 1, :], rhs=CT_b[:, i, 1, :],
                                                                    start=False, stop=False), R=['h_b', 'CT_b'], W=[pyk])
                            ct, off = i // 4, (i % 4) * 32
                            T.op('pe', lambda e, i=i, ct=ct, off=off: e.matmul(py[:, i * 32:(i + 1) * 32], lhsT=uT[:, ct, :],
                                                                              rhs=diagD_b[:, ct, off:off + 32], start=False, stop=True),
                                 R=['uT', 'diagD_b'], W=[pyk])
                    T.op('act', lambda e: e.copy(out=yf[:], in_=py[:, 0:512]), R=[pyk], W=['yf'])
                    T.op('pool', lambda e: e.tensor_tensor(out=yg[:], in0=yf[:], in1=yf[:], op=ALU.mult), R=['yf'], W=['yg'])
                    T.op('pool', lambda e: e.tensor_scalar(out=yg[:], in0=yg[:], scalar1=0.044715, scalar2=1.0, op0=ALU.mult, op1=ALU.add),
                         R=['yg'], W=['yg'])
                    T.op('pool', lambda e: e.tensor_tensor(out=yg[:], in0=yg[:], in1=yf[:], op=ALU.mult), R=['yg', 'yf'], W=['yg'])
                    T.op('act', lambda e: e.activation(out=yg[:], in_=yg[:], func=AF.Sigmoid, scale=GELU_K), R=['yg'], W=['yg'])
                    T.op('dve', lambda e: e.tensor_tensor(out=yg[:], in0=yg[:], in1=yf[:], op=ALU.mult), R=['yg', 'yf'], W=['yg'])
                    T.op('dve', lambda e: e.tensor_copy(out=ob[:, 0:512], in_=yg[:]), R=['yg'], W=['ob'])
                    p, pk = nextT()
                    for k in range(4):
                        T.op('pe', lambda e, k=k, p=p: e.transpose(out=p[:, k * 128:(k + 1) * 128], in_=ob[:, k * 128:(k + 1) * 128],
                                                                   identity=identb[:]), R=['ob', 'identb'], W=[pk])
                    T.op('act', lambda e, p=p: e.copy(out=cqT[:].rearrange("p a b -> p (a b)"), in_=p[:, 0:512]), R=[pk], W=['cqT'])
                    pz, pzk = nextA()
                    for k in range(4):
                        T.op('pe', lambda e, k=k, pz=pz: e.matmul(pz[:, 0:512], lhsT=cqT[:, k, :], rhs=glu_w_b[:, k, :], start=(k == 0), stop=False),
                             R=['cqT', 'glu_w_b'], W=[pzk])
                    T.op('pe', lambda e, pz=pz: e.matmul(pz[:, 0:512], lhsT=onesb[0:1, :], rhs=glub_b[0:1, :], start=False, stop=True),
                         R=['onesb', 'glub_b'], W=[pzk])
                    T.op('act', lambda e, pz=pz: e.activation(out=yf[:], in_=pz[:, 0:512], func=AF.Sigmoid), R=[pzk, 'yf'], W=['yf'])
                    T.op('dve', lambda e: e.tensor_tensor(out=yf[:], in0=yf[:], in1=yg[:], op=ALU.mult), R=['yf', 'yg'], W=['yf'])
                    finish_branch(ti, yf[:], 'yf', 512, 0)

                chk('G1')
                lat_ps = []
                stream_wblock(w_in, C_CKV, 256, gcol_in)
                for ti in range(ng):
                    p, pk = psC[ti], 'psC%d' % ti
                    for kc in range(16):
                        T.op('pe', lambda e, kc=kc, ti=ti, p=p: e.matmul(p[:, 0:256], lhsT=xnT_g[:, ti, kc, :], rhs=wblk[:, kc, 0:256],
                                                                         start=(kc == 0), stop=(kc == 15)), R=['xnT_g', 'wblk'], W=[pk])
                    lat_ps.append((p, pk))
                stream_wblock(w_in, C_KR, 64, gcol_in)
                for ti, (row0, own) in enumerate(tiles):
                    p, pk = lat_ps[ti]
                    for kc in range(16):
                        T.op('pe', lambda e, kc=kc, ti=ti, p=p: e.matmul(p[:, 256:320], lhsT=xnT_g[:, ti, kc, :], rhs=wblk[:, kc, 0:64],
                                                                         start=(kc == 0), stop=(kc == 15)), R=['xnT_g', 'wblk', pk], W=[pk])
                    if is_sample:
                        latent_post(p, pk, row0, ckv_s, kr_s, 0, ckvT_s[:], 'ckvT_s', krT_s[0:64, :], 'krT_s', None, None, None)
                    else:
                        kt = NPREV + own
                        latent_post(p, pk, row0, ckv_p, kr_p, own * 128, ckvT_all[:, :, kt * 128:(kt + 1) * 128], 'ckvT_all',
                                    krT_all[0:64, kt * 128:(kt + 1) * 128], 'krT_all', ckv1_all[:, kt, :], 'ckv1_all', rk_all[:, kt, :])

                chk('G2')
                cq_ps = []
                stream_wblock(w_in, C_CQ, 256, gcol_in)
                for ti in range(ng):
                    p, pk = psC[ti], 'psC%d' % ti
                    for kc in range(16):
                        T.op('pe', lambda e, kc=kc, ti=ti, p=p: e.matmul(p[:, 0:256], lhsT=xnT_g[:, ti, kc, :], rhs=wblk[:, kc, 0:256],
                                                                         start=(kc == 0), stop=(kc == 15)), R=['xnT_g', 'wblk'], W=[pk])
                    cq_ps.append((p, pk))
                stream_wblock(w_in, C_CQ + 256, 256, gcol_in)
                for ti, (row0, own) in enumerate(tiles):
                    p, pk = cq_ps[ti]
                    for kc in range(16):
                        T.op('pe', lambda e, kc=kc, ti=ti, p=p: e.matmul(p[:, 256:512], lhsT=xnT_g[:, ti, kc, :], rhs=wblk[:, kc, 0:256],
                                                                         start=(kc == 0), stop=(kc == 15)), R=['xnT_g', 'wblk', pk], W=[pk])
                    T.op('act', lambda e, p=p: e.copy(out=yf[:], in_=p[:, 0:512]), R=[pk], W=['yf'])
                    q_path(ti, row0)

                chk('G3')
                for b4 in range(4):
                    in_proj_block(ng, C_GM + b4 * 256, 256, c_gate(512 + b4 * 256))

                def c_qm(half):
                    def f(ti, p, pk):
                        for hh in range(2):
                            h = half * 2 + hh
                            rstd_of(p[:, hh * 128:(hh + 1) * 128], 128, sm[:, 6:7], pk)
                            T.op('dve', lambda e, hh=hh, h=h: e.scalar_tensor_tensor(out=cq_b[:, h * 128:(h + 1) * 128],
                                                                                     in0=p[:, hh * 128:(hh + 1) * 128], scalar=sm[:, 6:7],
                                                                                     in1=gmq_bc[:], op0=ALU.mult, op1=ALU.mult),
                                 R=[pk, 'sm', 'gmq_bc', 'cq_b'], W=['cq_b'])
                        pt_, ptk = nextT()
                        for hh in range(2):
                            h = half * 2 + hh
                            T.op('pe', lambda e, hh=hh, h=h, pt_=pt_: e.transpose(out=pt_[:, hh * 128:(hh + 1) * 128], in_=cq_b[:, h * 128:(h + 1) * 128],
                                                                                  identity=identb[:]), R=['cq_b', 'identb'], W=[ptk])
                        T.op('act', lambda e, pt_=pt_: e.copy(out=QmT[:, half * 2:half * 2 + 2, ti * 128:(ti + 1) * 128],
                                                              in_=pt_[:, 0:256].rearrange("p (a b) -> p a b", a=2)), R=[ptk, 'QmT'], W=['QmT'])
                    return f
                in_proj_block(ng, C_QM, 256, c_qm(0))
                in_proj_block(ng, C_QM + 256, 256, c_qm(1))
                in_proj_block(ng, C_GME, 256, c_gate(1536))
                in_proj_block(ng, C_GME + 256, 256, c_gate(1792))

                chk('G4')
                T.op('dve', lambda e: e.memset(ssq[:], 0.0), W=['ssq'])
                if not is_sample:
                    prompt_attention(ng, ncg, tiles[0][1])
                else:
                    sample_attention()

                chk('G5')
                for ti in range(ng):
                    T.op('dve', lambda e, ti=ti: e.tensor_reduce(out=sm[:, 7:8], in_=ssq[:, ti, 0:8], axis=AX.X, op=ALU.add), R=['ssq', 'sm'], W=['sm'])
                    T.op('act', lambda e: e.activation(out=sm[:, 7:8], in_=sm[:, 7:8], func=AF.Sqrt, scale=1.0 / 1024, bias=EPS), R=['sm'], W=['sm'])
                    T.op('dve', lambda e: e.reciprocal(out=sm[:, 7:8], in_=sm[:, 7:8]), R=['sm'], W=['sm'])
                    T.op('dve', lambda e, ti=ti: e.scalar_tensor_tensor(out=gates[:, ti, 512:1536], in0=ymla[:, ti, :], scalar=sm[:, 7:8],
                                                                        in1=gates[:, ti, 512:1536], op0=ALU.mult, op1=ALU.mult),
                         R=['ymla', 'sm', 'gates'], W=['gates'])
                    for half in range(2):
                        p, pk = nextT()
                        for k in range(8):
                            kc = half * 8 + k
                            T.op('pe', lambda e, k=k, kc=kc, p=p, ti=ti: e.transpose(out=p[:, k * 128:(k + 1) * 128],
                                                                                     in_=gates[:, ti, kc * 128:(kc + 1) * 128], identity=identb[:]),
                                 R=['gates', 'identb'], W=[pk])
                        T.op('act', lambda e, p=p, half=half, ti=ti: e.copy(out=xnT_g[:, ti, half * 8:(half + 1) * 8, :].rearrange("p a b -> p (a b)"),
                                                                           in_=p[:, :]), R=[pk, 'xnT_g'], W=['xnT_g'])
                for cb in range(8):
                    stream_wblock(w_out_d, cb * 256, 256, gcol_out)
                    for ti, (row0, own) in enumerate(tiles):
                        T.dma('sp', 'xres', lambda e, row0=row0, cb=cb: e.dma_start(out=xres[:], in_=xall[row0:row0 + 128, cb * 256:(cb + 1) * 256]),
                              W=['xres'])
                        p, pk = nextA()
                        for kc in range(16):
                            T.op('pe', lambda e, kc=kc, ti=ti, p=p: e.matmul(p[:, 0:256], lhsT=xnT_g[:, ti, kc, :], rhs=wblk[:, kc, 0:256],
                                                                             start=(kc == 0), stop=(kc == 15)), R=['xnT_g', 'wblk'], W=[pk])
                        T.op('dve', lambda e, p=p: e.tensor_tensor(out=yout[:], in0=p[:, 0:256], in1=xres[:], op=ALU.add), R=[pk, 'xres'], W=['yout'])
                        dst = y_s if is_sample else y_p
                        r0 = 0 if is_sample else own * 128
                        T.dma('sp', 'ost', lambda e, dst=dst, r0=r0, cb=cb: e.dma_start(out=dst[r0:r0 + 128, cb * 256:(cb + 1) * 256], in_=yout[:]),
                              R=['yout'])

            own_tiles = [((NPREV + o) * 128, o) for o in range(NOWN)]
            for g0 in range(0, NOWN, GT):
                run_group(own_tiles[g0:g0 + GT], False)
            stf = sb("stf", [16, 128])
            for src, dst in ((hre, sp_re), (him, sp_im)):
                transpose_f32(stf[:], src[:, :, 0], 128, 16, 'stf', 'hst')
                T.dma('sp', 'ost', lambda e, dst=dst: e.dma_start(out=dst[:, :], in_=stf[:]), R=['stf'])
            chk('O')
            T.barrier()
            stin = xt[0:NSEQ, :]
            for src_d, dstt in ((st_re, hre), (st_im, him)):
                T.dma('sp', 'c0', lambda e, src_d=src_d: e.dma_start(out=stin, in_=src_d[:, :]), W=['xt'])
                for i in range(16):
                    transpose_f32(dstt[:, i, :], stin[:, i * 128:(i + 1) * 128], NSEQ, 128, 'hst', 'xt')
            run_group([(NKT * 128, None)], True)
            for src, dst in ((hre, ss_re), (him, ss_im)):
                for i in range(16):
                    transpose_f32(stin[:, i * 128:(i + 1) * 128], src[:, i, :], 128, NSEQ, 'xt', 'hst')
                T.dma('sp', 'ost', lambda e, dst=dst: e.dma_start(out=dst[:, :], in_=stin), R=['xt'])
        except _Stop:
            pass
        T.finish('sp')
        print("[kernel] instructions emitted:", T.nins, "sbuf left:", nc.sbuf_bytes_remaining, {k: v for k, v in T.cnt.items() if k in ("pe", "act", "dve", "pool")}, flush=True)
    return nc


def rope_table(pos):
    half = 32
    inv = (10000.0 ** (-np.arange(half, dtype=np.float32) / half)).astype(np.float32)
    ang = pos.astype(np.float32)[:, None] * inv[None, :]
    c, s = np.cos(ang).astype(np.float32), np.sin(ang).astype(np.float32)
    return np.concatenate([c, c, -s, s], axis=1).astype(np.float32)


WNAMES = ['norm_g', 'w_in', 'ssm_a_re', 'ssm_a_im', 'ssm_log_dt', 'ssm_b_re', 'ssm_b_im', 'ssm_c_re', 'ssm_c_im', 'ssm_d',
          'ssm_glu_w', 'ssm_glu_b', 'mla_q_norm_g', 'mla_w_uq', 'mla_kv_norm_g', 'mla_w_ukv', 'mla_qk_norm_q', 'mla_qk_norm_k',
          'mem_norm_g', 'mem_w_k', 'mem_w_v', 'mem_qk_norm_q', 'mem_qk_norm_k', 'out_norm_ssm', 'out_norm_mla', 'out_norm_mem',
          'w_out']


def make_in_maps(inp, SEQ, NPG, NPOOL, PAST):
    CH = SEQ // 4
    NOWN = CH // 128
    NPREV = 3 * NOWN
    NKT = NPREV + NOWN
    f32 = np.float32
    xp = np.asarray(inp['x_prompt'], f32); xs = np.asarray(inp['x_sample'], f32)
    ident = np.eye(128, dtype=f32)
    tri = np.triu(np.ones((128, 128), f32))
    smask = np.ones((128, 256), f32); smask[:, 0] = 0.0
    smask[:, 128::8] = 0.0
    tau = np.tile(np.arange(1, LS + 1, dtype=f32)[None, :], (128, 1))
    cckv = np.ascontiguousarray(np.asarray(inp['cache_ckv'], f32).reshape(NPOOL * 128, 256))
    ckr = np.ascontiguousarray(np.asarray(inp['cache_krope'], f32).reshape(NPOOL * 128, 64))
    wts = {n: np.ascontiguousarray(np.asarray(inp[n], f32)) for n in WNAMES}
    maps = []
    for c in range(8):
        s, j = c // 4, c % 4
        xall = np.zeros(((NKT + 1) * 128, D), f32)
        pos = np.zeros((NKT + 1) * 128, f32)
        kb = np.zeros((128, NKT), f32)
        for k in range(3):
            src = j - 3 + k
            r0 = k * CH
            if src >= 0:
                xall[r0:r0 + CH] = xp[s, src * CH:(src + 1) * CH]
                pos[r0:r0 + CH] = np.arange(src * CH, (src + 1) * CH)
            else:
                kb[:, k * NOWN:(k + 1) * NOWN] = NEG
        xall[3 * CH:4 * CH] = xp[s, j * CH:(j + 1) * CH]
        pos[3 * CH:4 * CH] = np.arange(j * CH, (j + 1) * CH)
        xall[4 * CH:] = xs[c * NSEQ:(c + 1) * NSEQ].reshape(128, D)
        pos[4 * CH:] = np.tile(PAST + np.arange(8), NSEQ)
        m = {
            'xall': xall, 'rope': rope_table(pos), 'kbias': kb,
            'mem': np.ascontiguousarray(np.asarray(inp['mem_prompt'], f32)[s]),
            'cckv': cckv, 'ckr': ckr,
            'cmk': np.ascontiguousarray(np.asarray(inp['cache_mem_k'], f32)[c * NSEQ:(c + 1) * NSEQ].reshape(NSEQ * 256, 512)),
            'cmv': np.ascontiguousarray(np.asarray(inp['cache_mem_v'], f32)[c * NSEQ:(c + 1) * NSEQ].reshape(NSEQ * 256, 512)),
            'st_re': np.ascontiguousarray(np.asarray(inp['state_ssm_re'], f32)[c * NSEQ:(c + 1) * NSEQ].reshape(NSEQ, 2048)),
            'st_im': np.ascontiguousarray(np.asarray(inp['state_ssm_im'], f32)[c * NSEQ:(c + 1) * NSEQ].reshape(NSEQ, 2048)),
            'ptab': np.ascontiguousarray(np.asarray(inp['page_table'], np.int32)[c * NSEQ:(c + 1) * NSEQ]),
            'ident': ident, 'tri': tri, 'smask': smask, 'tau': tau,
        }
        m.update(wts)
        maps.append(m)
    return maps


def assemble(r, SEQ):
    y_prompt = np.stack([np.concatenate([r[s * 4 + j]['y_p'] for j in range(4)], 0) for s in range(2)])
    ckv_p = np.stack([np.concatenate([r[s * 4 + j]['ckv_p'] for j in range(4)], 0) for s in range(2)])
    kr_p = np.stack([np.concatenate([r[s * 4 + j]['kr_p'] for j in range(4)], 0) for s in range(2)])
    y_sample = np.concatenate([r[c]['y_s'].reshape(NSEQ, 8, D) for c in range(8)], 0)
    ckv_s = np.concatenate([r[c]['ckv_s'].reshape(NSEQ, 8, 256) for c in range(8)], 0)
    kr_s = np.concatenate([r[c]['kr_s'].reshape(NSEQ, 8, 64) for c in range(8)], 0)
    memk = np.stack([r[s * 4]['memk'].reshape(256, 4, 128) for s in range(2)])
    memv = np.stack([r[s * 4]['memv'].reshape(256, 4, 128) for s in range(2)])
    sp_re = np.stack([r[s * 4 + 3]['sp_re'].reshape(32, 64) for s in range(2)])
    sp_im = np.stack([r[s * 4 + 3]['sp_im'].reshape(32, 64) for s in range(2)])
    ss_re = np.concatenate([r[c]['ss_re'].reshape(NSEQ, 32, 64) for c in range(8)], 0)
    ss_im = np.concatenate([r[c]['ss_im'].reshape(NSEQ, 32, 64) for c in range(8)], 0)
    outs = (y_prompt, y_sample, ckv_p, kr_p, ckv_s, kr_s, memk, memv, sp_re, sp_im, ss_re, ss_im)
    return tuple(np.ascontiguousarray(o, dtype=np.float32) for o in outs)


def run(inp, SEQ, PAST, stop=None, cores=None):
    NPG = PAST // 128
    NPOOL = int(np.asarray(inp['cache_ckv']).shape[0])
    nc = build(SEQ, NPG, NPOOL, stop)
    maps = make_in_maps(inp, SEQ, NPG, NPOOL, PAST)
    if cores is not None:
        res = run_bass_kernel_spmd(nc, [maps[c] for c in cores], core_ids=list(range(len(cores))))
        return {c: res.results[i] for i, c in enumerate(cores)}
    res = run_bass_kernel_spmd(nc, maps, core_ids=list(range(8)))
    return assemble(res.results, SEQ)


def kernel(**inputs):
    return run(inputs, 4096, 8192)
```

```python
import contextlib
import numpy as np
import concourse.bass as bass
import concourse.mybir as mybir
from concourse.bass_utils import run_bass_kernel_spmd

F32 = mybir.dt.float32
BF16 = mybir.dt.bfloat16
I32 = mybir.dt.int32
AF = mybir.ActivationFunctionType
ALU = mybir.AluOpType
AX = mybir.AxisListType

D = 2048
IN_W = 3904
EPS = 1e-6
NEG = -30000.0
NSEQ = 16
LS = 64
GT = 2
GC = GT * 128
C_U, C_GS, C_CQ, C_CKV, C_KR, C_GM, C_QM, C_GME = 0, 512, 1024, 1536, 1792, 1856, 2880, 3392
GELU_K = 2.0 * 0.7978845608028654


class Tracker:
    def __init__(self, nc, es):
        self.nc = nc
        self.es = es
        self.eng = {'pe': nc.tensor, 'act': nc.scalar, 'dve': nc.vector, 'pool': nc.gpsimd, 'sp': nc.sync}
        self.sem = {}
        self.cnt = {}
        for n in ['pe', 'act', 'dve', 'pool']:
            self.sem[n] = es.enter_context(nc.semaphore('c_' + n))
            self.cnt[n] = 0
        self.seen = {n: {} for n in self.eng}
        self.lastw = {}
        self.readers = {}
        self.nins = 0

    def _waits(self, e, reads, writes):
        need = {}

        def add(t):
            if t is None:
                return
            sn, v = t
            if need.get(sn, 0) < v:
                need[sn] = v
        for k in reads:
            add(self.lastw.get(k))
            if k.startswith('ps'):
                for t in self.readers.get(k, ()):
                    if t[0] != e:
                        add(t)
        for k in writes:
            add(self.lastw.get(k))
            for t in self.readers.get(k, ()):
                add(t)
        for sn, v in need.items():
            if sn == 'pe' and e == 'pe':
                continue
            if self.seen[e].get(sn, 0) < v:
                self.eng[e].wait_ge(self.sem[sn], v)
                self.seen[e][sn] = v
                self.nins += 1

    def _commit(self, tok, reads, writes):
        for k in reads:
            self.readers.setdefault(k, []).append(tok)
        for k in writes:
            self.lastw[k] = tok
            self.readers[k] = []

    def op(self, e, fn, R=(), W=()):
        self._waits(e, R, W)
        ins = fn(self.eng[e])
        self.cnt[e] += 1
        ins.then_inc(self.sem[e], 1)
        self.nins += 1
        self._commit((e, self.cnt[e]), R, W)

    def dma(self, q, stream, fn, R=(), W=()):
        if stream not in self.sem:
            self.sem[stream] = self.es.enter_context(self.nc.semaphore('d_' + stream))
            self.cnt[stream] = 0
        self._waits(q, R, W)
        if self.cnt[stream] and self.seen[q].get(stream, 0) < self.cnt[stream]:
            self.eng[q].wait_ge(self.sem[stream], self.cnt[stream])
            self.seen[q][stream] = self.cnt[stream]
            self.nins += 1
        ins = fn(self.eng[q])
        self.cnt[stream] += 16
        ins.then_inc(self.sem[stream], 16)
        self.nins += 1
        self._commit((stream, self.cnt[stream]), R, W)

    def barrier(self):
        for e in self.eng:
            for sn, v in self.cnt.items():
                if v == 0:
                    continue
                if self.seen[e].get(sn, 0) < v:
                    self.eng[e].wait_ge(self.sem[sn], v)
                    self.seen[e][sn] = v
                    self.nins += 1
        self.lastw = {}
        self.readers = {}

    def finish(self, q='sp'):
        for sn, v in self.cnt.items():
            if v == 0:
                continue
            self.eng[q].wait_ge(self.sem[sn], v)


class _Stop(Exception):
    pass


def build(SEQ, NPG, NPOOL, stop=None):
    CH = SEQ // 4
    NOWN = CH // 128
    NPREV = 3 * NOWN
    NKT = NPREV + NOWN
    NT = NKT + 1
    nc = bass.Bass("TRN2", target_bir_lowering=False)

    def din(name, shape, dt=F32):
        return nc.dram_tensor(name, list(shape), dt, kind="ExternalInput").ap()

    def dout(name, shape):
        return nc.dram_tensor(name, list(shape), F32, kind="ExternalOutput").ap()

    xall = din("xall", [NT * 128, D])
    rope = din("rope", [NT * 128, 128])
    kbias_d = din("kbias", [128, NKT])
    mem_d = din("mem", [256, D])
    cckv = din("cckv", [NPOOL * 128, 256])
    ckr = din("ckr", [NPOOL * 128, 64])
    cmk = din("cmk", [NSEQ * 256, 512])
    cmv = din("cmv", [NSEQ * 256, 512])
    st_re = din("st_re", [NSEQ, 2048])
    st_im = din("st_im", [NSEQ, 2048])
    ptab = din("ptab", [NSEQ, NPG], I32)
    ident_d = din("ident", [128, 128])
    tri_d = din("tri", [128, 128])
    smask_d = din("smask", [128, 256])
    tau_d = din("tau", [128, LS])
    norm_g = din("norm_g", [D]); w_in = din("w_in", [D, IN_W])
    a_re_d = din("ssm_a_re", [32, 64]); a_im_d = din("ssm_a_im", [32, 64]); ldt_d = din("ssm_log_dt", [32])
    b_re_d = din("ssm_b_re", [32, 64, 16]); b_im_d = din("ssm_b_im", [32, 64, 16])
    c_re_d = din("ssm_c_re", [32, 16, 64]); c_im_d = din("ssm_c_im", [32, 16, 64])
    ssm_d_d = din("ssm_d", [512]); glu_w_d = din("ssm_glu_w", [512, 512]); glu_b_d = din("ssm_glu_b", [512])
    gq_d = din("mla_q_norm_g", [512]); w_uq_d = din("mla_w_uq", [512, 1536])
    gkv_d = din("mla_kv_norm_g", [256]); w_ukv_d = din("mla_w_ukv", [256, 2048])
    gqq_d = din("mla_qk_norm_q", [192]); gqk_d = din("mla_qk_norm_k", [192])
    gmem_d = din("mem_norm_g", [D]); w_mk_d = din("mem_w_k", [D, 512]); w_mv_d = din("mem_w_v", [D, 512])
    gmq_d = din("mem_qk_norm_q", [128]); gmk_d = din("mem_qk_norm_k", [128])
    go_s_d = din("out_norm_ssm", [512]); go_m_d = din("out_norm_mla", [1024]); go_e_d = din("out_norm_mem", [512])
    w_out_d = din("w_out", [D, D])

    y_p = dout("y_p", [CH, D]); y_s = dout("y_s", [128, D])
    ckv_p = dout("ckv_p", [CH, 256]); kr_p = dout("kr_p", [CH, 64])
    ckv_s = dout("ckv_s", [128, 256]); kr_s = dout("kr_s", [128, 64])
    memk_o = dout("memk", [256, 512]); memv_o = dout("memv", [256, 512])
    sp_re = dout("sp_re", [16, 128]); sp_im = dout("sp_im", [16, 128])
    ss_re = dout("ss_re", [NSEQ, 2048]); ss_im = dout("ss_im", [NSEQ, 2048])

    es = contextlib.ExitStack()
    with es:
        T = Tracker(nc, es)

        def chk(name):
            if stop == name:
                raise _Stop()

        import os as _os
        DBG = _os.environ.get('KDBG')

        def dbg(name, ap, shape, keys, dt=F32):
            if not DBG:
                return
            d = nc.dram_tensor("dbg_" + name, list(shape), dt, kind="ExternalOutput").ap()
            T.dma('sp', 'dbg', lambda e: e.dma_start(out=d, in_=ap, allow_slow_non_contiguous=True), R=keys)

        try:
            def sb(name, shape, dt=F32):
                return es.enter_context(nc.sbuf_tensor("s_" + name, list(shape), dt))

            def ps(name, shape, dt=F32):
                return es.enter_context(nc.psum_tensor("p_" + name, list(shape), dt))

            psT = [ps("psT%d" % i, [128, 1024], BF16) for i in range(2)]
            psA = [ps("psA%d" % i, [128, 512], F32) for i in range(3)]
            psC = [ps("psC%d" % i, [128, 512], F32) for i in range(3)]
            rr = {'T': 0, 'A': 0}

            def nextT():
                i = rr['T']; rr['T'] = (i + 1) % 2
                return psT[i], 'psT%d' % i

            def nextA():
                i = rr['A']; rr['A'] = (i + 1) % 3
                return psA[i], 'psA%d' % i

            identf = sb("identf", [128, 128]); identb = sb("identb", [128, 128], BF16)
            trib = sb("trib", [128, 128], BF16); onesb = sb("onesb", [128, 128], BF16)
            smask = sb("smask", [128, 256]); taur = sb("taur", [128, LS])
            stg = sb("stg", [128, 1024])
            T.dma('sp', 'c0', lambda e: e.dma_start(out=identf[:], in_=ident_d[:, :]), W=['identf'])
            T.dma('sp', 'c0', lambda e: e.dma_start(out=stg[:, 0:128], in_=tri_d[:, :]), W=['stg'])
            T.dma('sp', 'c0', lambda e: e.dma_start(out=smask[:], in_=smask_d[:, :]), W=['smask'])
            T.dma('sp', 'c0', lambda e: e.dma_start(out=taur[:], in_=tau_d[:, :]), W=['taur'])
            T.op('dve', lambda e: e.tensor_copy(out=identb[:], in_=identf[:]), R=['identf'], W=['identb'])
            T.op('dve', lambda e: e.tensor_copy(out=trib[:], in_=stg[:, 0:128]), R=['stg'], W=['trib'])
            T.op('dve', lambda e: e.memset(onesb[:], 1.0), W=['onesb'])

            def transpose_f32(dst_ap, src_ap, npart, nfree, dkey, skey, scale=None):
                p, pk = nextA()
                T.op('pe', lambda e: e.transpose(out=p[0:nfree, 0:npart], in_=src_ap, identity=identf[0:npart, 0:npart]),
                     R=[skey, 'identf'], W=[pk])
                if scale is None:
                    T.op('dve', lambda e: e.tensor_copy(out=dst_ap, in_=p[0:nfree, 0:npart]), R=[pk], W=[dkey])
                else:
                    T.op('dve', lambda e: e.tensor_scalar(out=dst_ap, in0=p[0:nfree, 0:npart], scalar1=scale, scalar2=None,
                                                          op0=ALU.mult), R=[pk], W=[dkey])

            def load_col_into(dst_ap, dkey, vec_d, n):
                k = n // 128
                T.dma('sp', 'c0', lambda e: e.dma_start(out=stg[0:k, 0:128], in_=vec_d.rearrange("(k p) -> k p", p=128)),
                      W=['stg'])
                transpose_f32(dst_ap, stg[0:k, 0:128], k, 128, dkey, 'stg')

            def load_col(name, vec_d, n):
                t = sb(name, [128, n // 128])
                load_col_into(t[:], name, vec_d, n)
                return t

            def load_bc(name, vec_d, n):
                t = sb(name, [128, n])
                T.dma('sp', 'c0', lambda e: e.dma_start(out=t[:], in_=vec_d.partition_broadcast(128)), W=[name])
                return t

            gcol_in = load_col("gcol_in", norm_g, D)
            gcol_q = load_col("gcol_q", gq_d, 512)
            gcol_mem = load_col("gcol_mem", gmem_d, D)
            gcol_out = sb("gcol_out", [128, 16])
            load_col_into(gcol_out[:, 0:4], 'gcol_out', go_s_d, 512)
            load_col_into(gcol_out[:, 4:12], 'gcol_out', go_m_d, 1024)
            load_col_into(gcol_out[:, 12:16], 'gcol_out', go_e_d, 512)
            dcol = load_col("dcol", ssm_d_d, 512)
            diagD_b = sb("diagD_b", [128, 4, 128], BF16)
            for ct in range(4):
                T.op('dve', lambda e, ct=ct: e.tensor_scalar(out=diagD_b[:, ct, :], in0=identf[:], scalar1=dcol[:, ct:ct + 1],
                                                             scalar2=None, op0=ALU.mult), R=['identf', 'dcol'], W=['diagD_b'])
            gkv_bc = load_bc("gkv_bc", gkv_d, 256)
            gmk_bc = load_bc("gmk_bc", gmk_d, 128)
            gqk_bc = load_bc("gqk_bc", gqq_d, 192)
            T.dma('sp', 'c0', lambda e: e.dma_start(out=stg[:, 0:192], in_=gqk_d.partition_broadcast(128)), W=['stg'])
            T.op('dve', lambda e: e.scalar_tensor_tensor(out=gqk_bc[:], in0=gqk_bc[:], scalar=192.0 ** -0.5, in1=stg[:, 0:192],
                                                         op0=ALU.mult, op1=ALU.mult), R=['gqk_bc', 'stg'], W=['gqk_bc'])
            gmq_bc = load_bc("gmq_bc", gmq_d, 128)
            T.op('dve', lambda e: e.tensor_scalar(out=gmq_bc[:], in0=gmq_bc[:], scalar1=128.0 ** -0.5, scalar2=None,
                                                  op0=ALU.mult), R=['gmq_bc'], W=['gmq_bc'])
            glub_b = sb("glub_b", [1, 512], BF16)
            T.dma('sp', 'c0', lambda e: e.dma_start(out=stg[0:1, 0:512], in_=glu_b_d.rearrange("(o n) -> o n", o=1)),
                  W=['stg'])
            T.op('dve', lambda e: e.tensor_copy(out=glub_b[:], in_=stg[0:1, 0:512]), R=['stg'], W=['glub_b'])

            def load_w_rows(dst, dkey, w_d, kc_n, c0, ncols, gcol, dcol0=0):
                for kc in range(kc_n):
                    for cc0 in range(0, ncols, 1024):
                        n = min(1024, ncols - cc0)
                        T.dma('sp', 'wst', lambda e, kc=kc, cc0=cc0, n=n: e.dma_start(
                            out=stg[:, 0:n], in_=w_d[kc * 128:(kc + 1) * 128, c0 + cc0:c0 + cc0 + n]), W=['stg'])
                        if gcol is None:
                            T.op('dve', lambda e, kc=kc, cc0=cc0, n=n: e.tensor_copy(out=dst[:, kc, dcol0 + cc0:dcol0 + cc0 + n],
                                                                                     in_=stg[:, 0:n]), R=['stg'], W=[dkey])
                        else:
                            T.op('dve', lambda e, kc=kc, cc0=cc0, n=n: e.tensor_scalar(out=dst[:, kc, dcol0 + cc0:dcol0 + cc0 + n],
                                                                                       in0=stg[:, 0:n], scalar1=gcol[:, kc:kc + 1],
                                                                                       scalar2=None, op0=ALU.mult), R=['stg'], W=[dkey])

            w_uq_b = sb("w_uq_b", [128, 4, 1536], BF16)
            load_w_rows(w_uq_b, 'w_uq_b', w_uq_d, 4, 0, 1536, gcol_q)
            glu_w_b = sb("glu_w_b", [128, 4, 512], BF16)
            load_w_rows(glu_w_b, 'glu_w_b', glu_w_d, 4, 0, 512, None)
            w_uk_b = sb("w_uk_b", [128, 2, 1024], BF16)
            w_uv_b = sb("w_uv_b", [128, 2, 1024], BF16)
            for kc in range(2):
                for hf in range(2):
                    T.dma('sp', 'wst', lambda e, kc=kc, hf=hf: e.dma_start(
                        out=stg[:, 0:1024], in_=w_ukv_d[kc * 128:(kc + 1) * 128, hf * 1024:(hf + 1) * 1024]), W=['stg'])
                    sv = stg[:, 0:1024].rearrange("p (h c) -> p h c", c=256)
                    T.op('dve', lambda e, kc=kc, hf=hf, sv=sv: e.tensor_copy(
                        out=w_uk_b[:, kc, hf * 512:(hf + 1) * 512].rearrange("p (h c) -> p h c", c=128), in_=sv[:, :, 0:128]),
                        R=['stg'], W=['w_uk_b'])
                    T.op('dve', lambda e, kc=kc, hf=hf, sv=sv: e.tensor_copy(
                        out=w_uv_b[:, kc, hf * 512:(hf + 1) * 512].rearrange("p (h c) -> p h c", c=128), in_=sv[:, :, 128:256]),
                        R=['stg'], W=['w_uv_b'])
            w_ukT_b = sb("w_ukT_b", [128, 8, 256], BF16)
            for h in range(8):
                p, pk = nextT()
                for kc in range(2):
                    T.op('pe', lambda e, h=h, kc=kc, p=p: e.transpose(out=p[:, kc * 128:(kc + 1) * 128],
                                                                      in_=w_uk_b[:, kc, h * 128:(h + 1) * 128],
                                                                      identity=identb[:]), R=['w_uk_b', 'identb'], W=[pk])
                T.op('act', lambda e, h=h, p=p: e.copy(out=w_ukT_b[:, h, :], in_=p[:, 0:256]), R=[pk], W=['w_ukT_b'])

            chk('w')
            arena = sb("arena", [128, 15360], BF16)
            w_up_b = arena[:, 0:13312].rearrange("p (k c) -> p k c", k=16)
            xnT1 = arena[:, 13312:15360].rearrange("p (k c) -> p k c", k=16)
            memT = arena[:, 8192:12288].rearrange("p (t k c) -> p t k c", t=2, k=16)
            gates = arena[:, 0:GT * 2048].rearrange("p (t c) -> p t c", t=GT)
            QTn = arena[:, 4096:4096 + 8 * GC].rearrange("p (h c) -> p h c", h=8)
            QTr = arena[:, 6144:6144 + 8 * GC].rearrange("p (h c) -> p h c", h=8)
            xnT_g = arena[:, 8192:8192 + GT * 2048].rearrange("p (t k c) -> p t k c", t=GT, k=16)
            ymla = arena[:, 12288:12288 + GT * 1024].rearrange("p (t c) -> p t c", t=GT)
            QmT = arena[:, 14336:14336 + 4 * GC].rearrange("p (h c) -> p h c", h=4)

            load_w_rows(w_up_b, 'w_up_b', w_in, 16, C_U, 512, gcol_in, 0)
            load_w_rows(w_up_b, 'w_up_b', w_in, 16, C_CKV, 320, gcol_in, 512)

            xt = sb("xt", [128, D])
            are = sb("are", [128, 16]); aim = sb("aim", [128, 16]); dts = sb("dts", [128, 16])
            with nc.allow_non_contiguous_dma(reason="small one-time parameter loads"):
                T.dma('sp', 'c0', lambda e: e.dma_start(out=are[:], in_=a_re_d.rearrange("(t two) n -> (two n) t", two=2)),
                      W=['are'])
                T.dma('sp', 'c0', lambda e: e.dma_start(out=aim[:], in_=a_im_d.rearrange("(t two) n -> (two n) t", two=2)),
                      W=['aim'])
                lv = ldt_d.rearrange("(t two) -> two t", two=2)
                for hh in range(2):
                    T.dma('sp', 'c0', lambda e, hh=hh: e.dma_start(out=dts[hh * 64:(hh + 1) * 64, :],
                                                                   in_=lv[hh].partition_broadcast(64)), W=['dts'])
            T.op('act', lambda e: e.activation(out=dts[:], in_=dts[:], func=AF.Exp), R=['dts'], W=['dts'])
            theta = sb("theta", [128, 16]); rmag = sb("rmag", [128, 16])
            T.op('dve', lambda e: e.tensor_tensor(out=theta[:], in0=dts[:], in1=aim[:], op=ALU.mult), R=['dts', 'aim'], W=['theta'])
            T.op('dve', lambda e: e.tensor_tensor(out=rmag[:], in0=dts[:], in1=are[:], op=ALU.mult), R=['dts', 'are'], W=['rmag'])
            T.op('act', lambda e: e.activation(out=rmag[:], in_=rmag[:], func=AF.Exp), R=['rmag'], W=['rmag'])
            costab = sb("costab", [128, 16, LS]); sintab = sb("sintab", [128, 16, LS])
            angw = xt[:, 0:4 * LS]; angk = sb("angk", [128, 4 * LS], I32); angm = xt[:, 256:256 + 4 * LS]
            TAB = ['sintab', 'costab']

            def sin_table(dst, dkey, phase):
                for q4 in range(4):
                    for j in range(4):
                        i = q4 * 4 + j
                        T.op('dve', lambda e, i=i, j=j: e.tensor_scalar(out=angw[:, j * LS:(j + 1) * LS], in0=taur[:],
                                                                        scalar1=theta[:, i:i + 1], scalar2=None, op0=ALU.mult),
                             R=['taur', 'theta', 'angw'], W=['angw'])
                    T.op('dve', lambda e: e.tensor_scalar(out=angw[:], in0=angw[:], scalar1=1.0 / (2 * np.pi), scalar2=phase,
                                                          op0=ALU.mult, op1=ALU.add), R=['angw'], W=['angw'])
                    T.op('dve', lambda e: e.tensor_copy(out=angk[:], in_=angw[:]), R=['angw'], W=['angk'])
                    T.op('dve', lambda e: e.tensor_copy(out=angm[:], in_=angk[:]), R=['angk'], W=['angm'])
                    T.op('dve', lambda e: e.tensor_tensor(out=angw[:], in0=angw[:], in1=angm[:], op=ALU.subtract),
                         R=['angw', 'angm'], W=['angw'])
                    T.op('dve', lambda e: e.tensor_scalar(out=angm[:], in0=angw[:], scalar1=0.5, scalar2=None, op0=ALU.is_gt),
                         R=['angw'], W=['angm'])
                    T.op('dve', lambda e: e.tensor_tensor(out=angw[:], in0=angw[:], in1=angm[:], op=ALU.subtract),
                         R=['angw', 'angm'], W=['angw'])
                    T.op('dve', lambda e: e.tensor_scalar(out=angm[:], in0=angw[:], scalar1=-0.5, scalar2=None, op0=ALU.is_lt),
                         R=['angw'], W=['angm'])
                    T.op('dve', lambda e: e.tensor_tensor(out=angw[:], in0=angw[:], in1=angm[:], op=ALU.add),
                         R=['angw', 'angm'], W=['angw'])
                    T.op('act', lambda e, q4=q4: e.activation(out=dst[:, q4 * 4:(q4 + 1) * 4, :].rearrange("p a b -> p (a b)"), in_=angw[:],
                                                              func=AF.Sin, scale=2 * np.pi), R=['angw'], W=[dkey])

            sin_table(sintab, 'sintab', 0.0)
            sin_table(costab, 'costab', 0.25)
            s5t = xt[:, 512:640].rearrange("p (a b) -> p a b", a=8)
            abr, abi, den, fre, fim, nfim, t0_, t1_ = [s5t[:, i, :] for i in range(8)]
            S5 = ['s5t']
            T.op('dve', lambda e: e.tensor_tensor(out=abr, in0=rmag[:], in1=costab[:, :, 0], op=ALU.mult), R=['rmag'] + TAB, W=S5)
            T.op('dve', lambda e: e.tensor_tensor(out=abi, in0=rmag[:], in1=sintab[:, :, 0], op=ALU.mult), R=['rmag'] + TAB + S5, W=S5)
            T.op('dve', lambda e: e.tensor_tensor(out=den, in0=are[:], in1=are[:], op=ALU.mult), R=['are'] + S5, W=S5)
            T.op('dve', lambda e: e.tensor_tensor(out=t0_, in0=aim[:], in1=aim[:], op=ALU.mult), R=['aim'] + S5, W=S5)
            T.op('dve', lambda e: e.tensor_tensor(out=den, in0=den, in1=t0_, op=ALU.add), R=S5, W=S5)
            T.op('dve', lambda e: e.reciprocal(out=den, in_=den), R=S5, W=S5)
            T.op('dve', lambda e: e.tensor_scalar(out=t1_, in0=abr, scalar1=-1.0, scalar2=None, op0=ALU.add), R=S5, W=S5)
            T.op('dve', lambda e: e.tensor_tensor(out=fre, in0=t1_, in1=are[:], op=ALU.mult), R=S5 + ['are'], W=S5)
            T.op('dve', lambda e: e.tensor_tensor(out=t0_, in0=abi, in1=aim[:], op=ALU.mult), R=S5 + ['aim'], W=S5)
            T.op('dve', lambda e: e.tensor_tensor(out=fre, in0=fre, in1=t0_, op=ALU.add), R=S5, W=S5)
            T.op('dve', lambda e: e.tensor_tensor(out=fre, in0=fre, in1=den, op=ALU.mult), R=S5, W=S5)
            T.op('dve', lambda e: e.tensor_tensor(out=fim, in0=abi, in1=are[:], op=ALU.mult), R=S5 + ['are'], W=S5)
            T.op('dve', lambda e: e.tensor_tensor(out=t0_, in0=t1_, in1=aim[:], op=ALU.mult), R=S5 + ['aim'], W=S5)
            T.op('dve', lambda e: e.tensor_tensor(out=fim, in0=fim, in1=t0_, op=ALU.subtract), R=S5, W=S5)
            T.op('dve', lambda e: e.tensor_tensor(out=fim, in0=fim, in1=den, op=ALU.mult), R=S5, W=S5)
            T.op('dve', lambda e: e.tensor_scalar(out=nfim, in0=fim, scalar1=-1.0, scalar2=None, op0=ALU.mult), R=S5, W=S5)
            bst = xt[:, 640:1152].rearrange("p (c t q) -> p c t q", c=2, t=16)
            with nc.allow_non_contiguous_dma(reason="small one-time parameter loads"):
                T.dma('sp', 'c0', lambda e: e.dma_start(out=bst[:, 0, :, :],
                                                        in_=b_re_d.rearrange("(t two) n q -> (two n) t q", two=2)), W=['bst'])
                T.dma('sp', 'c0', lambda e: e.dma_start(out=bst[:, 1, :, :],
                                                        in_=b_im_d.rearrange("(t two) n q -> (two n) t q", two=2)), W=['bst'])
            BT_b = sb("BT_b", [128, 16, 2, 128], BF16)
            bexp = xt[:, 1152:1408].rearrange("p (c n) -> p c n", c=2); bt1 = xt[:, 1408:1440].rearrange("p (c n) -> p c n", c=2)
            for i in range(16):
                ga, gb = (2 * i) % 8, (2 * i + 1) % 8
                T.op('dve', lambda e: e.memset(bexp[:], 0.0), W=['bexp'])
                T.op('dve', lambda e, i=i: e.tensor_scalar(out=bt1[:, 0, :], in0=bst[:, 0, i, :], scalar1=s5t[:, 3, i:i + 1],
                                                           scalar2=None, op0=ALU.mult), R=['bst', 's5t'], W=['bt1'])
                T.op('dve', lambda e, i=i: e.scalar_tensor_tensor(out=bt1[:, 0, :], in0=bst[:, 1, i, :], scalar=s5t[:, 5, i:i + 1],
                                                                  in1=bt1[:, 0, :], op0=ALU.mult, op1=ALU.add),
                     R=['bst', 's5t', 'bt1'], W=['bt1'])
                T.op('dve', lambda e, i=i: e.tensor_scalar(out=bt1[:, 1, :], in0=bst[:, 1, i, :], scalar1=s5t[:, 3, i:i + 1],
                                                           scalar2=None, op0=ALU.mult), R=['bst', 's5t', 'bt1'], W=['bt1'])
                T.op('dve', lambda e, i=i: e.scalar_tensor_tensor(out=bt1[:, 1, :], in0=bst[:, 0, i, :], scalar=s5t[:, 4, i:i + 1],
                                                                  in1=bt1[:, 1, :], op0=ALU.mult, op1=ALU.add),
                     R=['bst', 's5t', 'bt1'], W=['bt1'])
                for c in range(2):
                    T.op('dve', lambda e, c=c, ga=ga: e.tensor_copy(out=bexp[0:64, c, ga * 16:ga * 16 + 16], in_=bt1[0:64, c, :]),
                         R=['bt1', 'bexp'], W=['bexp'])
                    T.op('dve', lambda e, c=c, gb=gb: e.tensor_copy(out=bexp[64:128, c, gb * 16:gb * 16 + 16], in_=bt1[64:128, c, :]),
                         R=['bt1', 'bexp'], W=['bexp'])
                for c in range(2):
                    transpose_f32(BT_b[:, i, c, :], bexp[:, c, :], 128, 128, 'BT_b', 'bexp')
            CT_b = sb("CT_b", [128, 16, 2, 32], BF16)
            T.op('dve', lambda e: e.memset(CT_b[:], 0.0), W=['CT_b'])
            cpad = xt[:, 1536:1664]; cT = xt[:, 1664:1792]
            for c, cd in enumerate((c_re_d, c_im_d)):
                cv = cd.rearrange("g p n -> (g p) n")
                for c4 in range(4):
                    T.dma('sp', 'c0', lambda e, c4=c4, cv=cv: e.dma_start(out=cpad[:, 0:64], in_=cv[c4 * 128:(c4 + 1) * 128, :]), W=['cpad'])
                    T.dma('sp', 'c0', lambda e, c4=c4, cv=cv: e.dma_start(out=cpad[:, 64:128], in_=cv[c4 * 128:(c4 + 1) * 128, :]), W=['cpad'])
                    transpose_f32(cT[:], cpad[:], 128, 128, 'cT', 'cpad', scale=(1.0 if c == 0 else -1.0))
                    for k in range(4):
                        i = 4 * c4 + k
                        T.op('dve', lambda e, i=i, k=k, c=c: e.tensor_copy(out=CT_b[0:64, i, c, 0:16],
                                                                          in_=cT[0:64, (2 * k) * 16:(2 * k) * 16 + 16]),
                             R=['cT', 'CT_b'], W=['CT_b'])
                        T.op('dve', lambda e, i=i, k=k, c=c: e.tensor_copy(out=CT_b[64:128, i, c, 16:32],
                                                                          in_=cT[64:128, (2 * k + 1) * 16:(2 * k + 1) * 16 + 16]),
                             R=['cT', 'CT_b'], W=['CT_b'])

            dbg('theta', theta[:], [128, 16], ['theta']); dbg('rmag', rmag[:], [128, 16], ['rmag'])
            dbg('costab', costab[:].rearrange("p a b -> p (a b)"), [128, 16 * LS], ['costab'])
            dbg('sintab', sintab[:].rearrange("p a b -> p (a b)"), [128, 16 * LS], ['sintab'])
            dbg('s5t', xt[:, 512:640], [128, 128], ['s5t'])
            chk('s5')
            hre = sb("hre", [128, 16, NSEQ]); him = sb("him", [128, 16, NSEQ])
            T.op('dve', lambda e: e.memset(hre[:], 0.0), W=['hst'])
            T.op('dve', lambda e: e.memset(him[:], 0.0), R=['hst'], W=['hst'])
            GS = 2
            s5tmp = [sb("s5tmp%d" % k, [128, GS, 128]) for k in range(4)]
            bpr = sb("bpr", [128, GS, 128]); bpi = sb("bpi", [128, GS, 128])
            d0t = sb("d0t", [128, GS, 128])
            h_b = sb("h_b", [128, GS, 2, 128], BF16)
            hend = sb("hend", [128, 4, GS, NSEQ])

            dbg_once = []

            def ssm_step(uT_get, ncols, S, L, col0, want_out, gis=None):
                mcol = 0 if S == 1 else 128
                for gi in (range(16 // GS) if gis is None else gis):
                    p, pk = nextA()
                    pv = p[:, 0:GS * 2 * ncols].rearrange("p (j c n) -> p j c n", j=GS, c=2)
                    for j in range(GS):
                        i = gi * GS + j
                        for c in range(2):
                            T.op('pe', lambda e, i=i, j=j, c=c, pv=pv: e.matmul(pv[:, j, c, :], lhsT=BT_b[:, i, c, :],
                                                                               rhs=uT_get(i // 4), start=True, stop=True),
                                 R=['BT_b', 'uT'], W=[pk])
                    isl = slice(gi * GS, gi * GS + GS)

                    def tab(tb):
                        a = tb[:, isl, 0:L]
                        if S == 1:
                            return a
                        return a.unsqueeze(2).to_broadcast([128, GS, S, L])

                    def v4(t):
                        a = t[:, :, 0:ncols]
                        if S == 1:
                            return a
                        return a.rearrange("p j (s l) -> p j s l", l=L)

                    def pvc(c):
                        a = pv[:, :, c, :]
                        if S == 1:
                            return a
                        return a.rearrange("p j (s l) -> p j s l", l=L)
                    t1, t2, t3, t4 = s5tmp
                    T.op('dve', lambda e: e.tensor_tensor(out=v4(t1), in0=pvc(0), in1=tab(costab), op=ALU.mult), R=[pk] + TAB, W=['s5tmp0'])
                    T.op('dve', lambda e: e.tensor_tensor(out=v4(t2), in0=pvc(1), in1=tab(sintab), op=ALU.mult), R=[pk] + TAB, W=['s5tmp1'])
                    T.op('dve', lambda e: e.tensor_tensor(out=v4(t3), in0=pvc(1), in1=tab(costab), op=ALU.mult), R=[pk] + TAB, W=['s5tmp2'])
                    T.op('dve', lambda e: e.tensor_tensor(out=v4(t4), in0=pvc(0), in1=tab(sintab), op=ALU.mult), R=[pk] + TAB, W=['s5tmp3'])
                    T.op('pool', lambda e: e.tensor_tensor(out=bpr[:, :, 0:ncols], in0=t1[:, :, 0:ncols], in1=t2[:, :, 0:ncols], op=ALU.add),
                         R=['s5tmp0', 's5tmp1'], W=['bpr'])
                    T.op('pool', lambda e: e.tensor_tensor(out=bpi[:, :, 0:ncols], in0=t3[:, :, 0:ncols], in1=t4[:, :, 0:ncols], op=ALU.subtract),
                         R=['s5tmp2', 's5tmp3'], W=['bpi'])
                    if DBG and not dbg_once and gi == 0:
                        dbg('t1', t1[:].rearrange("p a b -> p (a b)"), [128, 256], ['s5tmp0'])
                        dbg('uT', uT[:].rearrange("p a b -> p (a b)"), [128, 512], ['uT'], BF16)
                        dbg('ubg', u_bg[:, 0, :], [128, 512], ['u_bg'], BF16)
                        dbg('wup', w_up_b[:, 0:2, :].rearrange("p a b -> p (a b)"), [128, 1664], ['w_up_b'], BF16)
                        dbg('gcol', gcol_in[:], [128, 16], ['gcol_in'])
                        dbg('xnT', xnT1[:].rearrange("p a b -> p (a b)"), [128, 2048], ['xnT1'], BF16)
                        dbg('BT', BT_b[:, 0:2, :, :].rearrange("p a b c -> p (a b c)"), [128, 512], ['BT_b'], BF16)
                        dbg('t4', t4[:].rearrange("p a b -> p (a b)"), [128, 256], ['s5tmp3'])
                        dbg('bpr0', bpr[:].rearrange("p a b -> p (a b)"), [128, 256], ['bpr'])
                        dbg('bpi0', bpi[:].rearrange("p a b -> p (a b)"), [128, 256], ['bpi'])
                    for j in range(GS):
                        i = gi * GS + j
                        T.op('dve', lambda e, i=i, j=j: e.tensor_scalar(out=d0t[:, j, 0:ncols], in0=smask[:, mcol:mcol + ncols],
                                                                         scalar1=rmag[:, i:i + 1], scalar2=None, op0=ALU.mult),
                             R=['smask', 'rmag', 'd0t'], W=['d0t'])
                        fr = bpr[:, j, 0:ncols].rearrange("p (s l) -> p s l", l=L)[:, :, 0]
                        fi = bpi[:, j, 0:ncols].rearrange("p (s l) -> p s l", l=L)[:, :, 0]
                        T.op('dve', lambda e, i=i, fr=fr: e.scalar_tensor_tensor(out=fr, in0=hre[:, i, 0:S], scalar=rmag[:, i:i + 1],
                                                                                  in1=fr, op0=ALU.mult, op1=ALU.add),
                             R=['hst', 'rmag', 'bpr'], W=['bpr'])
                        T.op('dve', lambda e, i=i, fi=fi: e.scalar_tensor_tensor(out=fi, in0=him[:, i, 0:S], scalar=rmag[:, i:i + 1],
                                                                                  in1=fi, op0=ALU.mult, op1=ALU.add),
                             R=['hst', 'rmag', 'bpi'], W=['bpi'])
                    for j in range(GS):
                        T.op('dve', lambda e, j=j: e.tensor_tensor_scan(out=bpr[:, j, 0:ncols], data0=d0t[:, j, 0:ncols],
                                                                        data1=bpr[:, j, 0:ncols], initial=0.0,
                                                                        op0=ALU.mult, op1=ALU.add), R=['d0t', 'bpr'], W=['bpr'])
                        T.op('dve', lambda e, j=j: e.tensor_tensor_scan(out=bpi[:, j, 0:ncols], data0=d0t[:, j, 0:ncols],
                                                                        data1=bpi[:, j, 0:ncols], initial=0.0,
                                                                        op0=ALU.mult, op1=ALU.add), R=['d0t', 'bpi'], W=['bpi'])
                    if DBG and not dbg_once and gi == 0:
                        dbg('bpr1', bpr[:].rearrange("p a b -> p (a b)"), [128, 256], ['bpr'])
                        dbg('bpi1', bpi[:].rearrange("p a b -> p (a b)"), [128, 256], ['bpi'])
                        dbg('d0t', d0t[:].rearrange("p a b -> p (a b)"), [128, 256], ['d0t'])
                        dbg_once.append(1)
                    gl_r = bpr[:, :, 0:ncols].rearrange("p j (s l) -> p j s l", l=L)[:, :, :, L - 1]
                    gl_i = bpi[:, :, 0:ncols].rearrange("p j (s l) -> p j s l", l=L)[:, :, :, L - 1]
                    cl = costab[:, isl, L - 1:L].to_broadcast([128, GS, S])
                    sl = sintab[:, isl, L - 1:L].to_broadcast([128, GS, S])
                    e1, e2, e3, e4 = [hend[:, k, :, 0:S] for k in range(4)]
                    T.op('pool', lambda e: e.tensor_tensor(out=e1, in0=gl_r, in1=cl, op=ALU.mult), R=['bpr'] + TAB, W=['hend'])
                    T.op('pool', lambda e: e.tensor_tensor(out=e2, in0=gl_i, in1=sl, op=ALU.mult), R=['bpi', 'hend'] + TAB, W=['hend'])
                    T.op('pool', lambda e: e.tensor_tensor(out=e3, in0=gl_r, in1=sl, op=ALU.mult), R=['bpr', 'hend'] + TAB, W=['hend'])
                    T.op('pool', lambda e: e.tensor_tensor(out=e4, in0=gl_i, in1=cl, op=ALU.mult), R=['bpi', 'hend'] + TAB, W=['hend'])
                    T.op('pool', lambda e: e.tensor_tensor(out=hre[:, isl, 0:S], in0=e1, in1=e2, op=ALU.subtract), R=['hend', 'hst'], W=['hst'])
                    T.op('pool', lambda e: e.tensor_tensor(out=him[:, isl, 0:S], in0=e3, in1=e4, op=ALU.add), R=['hend', 'hst'], W=['hst'])
                    if want_out:
                        T.op('pool', lambda e: e.tensor_tensor(out=v4(t1), in0=v4(bpr), in1=tab(costab), op=ALU.mult), R=['bpr'] + TAB, W=['s5tmp0'])
                        T.op('pool', lambda e: e.tensor_tensor(out=v4(t2), in0=v4(bpi), in1=tab(sintab), op=ALU.mult), R=['bpi'] + TAB, W=['s5tmp1'])
                        T.op('dve', lambda e: e.tensor_tensor(out=v4(t3), in0=v4(bpr), in1=tab(sintab), op=ALU.mult), R=['bpr'] + TAB, W=['s5tmp2'])
                        T.op('dve', lambda e: e.tensor_tensor(out=v4(t4), in0=v4(bpi), in1=tab(costab), op=ALU.mult), R=['bpi'] + TAB, W=['s5tmp3'])
                        T.op('pool', lambda e: e.tensor_tensor(out=h_b[:, :, 0, col0:col0 + ncols], in0=t1[:, :, 0:ncols],
                                                               in1=t2[:, :, 0:ncols], op=ALU.subtract), R=['s5tmp0', 's5tmp1', 'h_b'], W=['h_b'])
                        T.op('pool', lambda e: e.tensor_tensor(out=h_b[:, :, 1, col0:col0 + ncols], in0=t3[:, :, 0:ncols],
                                                               in1=t4[:, :, 0:ncols], op=ALU.add), R=['s5tmp2', 's5tmp3', 'h_b'], W=['h_b'])

            T.barrier()
            xnb = sb("xnb", [128, D], BF16)
            ropet = sb("ropet", [128, 128])
            sm = sb("sm", [128, 64])
            junk = sb("junk", [128, 512], BF16)
            ckvn_f = sb("ckvn_f", [128, 256]); krr_f = sb("krr_f", [128, 64]); krt = sb("krt", [128, 64])
            ckvn_b = sb("ckvn_b", [128, 256], BF16)
            krr_b = sb("krr_b", [128, 64], BF16)
            uT = sb("uT", [128, 4, 128], BF16); u_bg = sb("u_bg", [128, GT, 512], BF16)
            ckvT_all = sb("ckvT_all", [128, 2, NKT * 128], BF16)
            krT_all = sb("krT_all", [64, NKT * 128], BF16)
            KA = max(NKT, 16)
            ckv1_all = sb("ckv1_all", [128, KA, 256], BF16)
            c1flat = ckv1_all[:].rearrange("p a b -> p (a b)")
            rk_all = sb("rk_all", [128, NKT, 8])
            kbias = sb("kbias", [128, NKT])
            T.dma('sp', 'c0', lambda e: e.dma_start(out=kbias[:], in_=kbias_d[:, :]), W=['kbias'])
            ckvT_s = sb("ckvT_s", [128, 2, 128], BF16); krT_s = sb("krT_s", [64, 128], BF16)

            def rstd_of(src_ap, n, dst_ap, skey):
                T.op('act', lambda e: e.activation(out=junk[:, 0:n], in_=src_ap, func=AF.Square, accum_out=dst_ap),
                     R=[skey, 'sm'], W=['junk', 'sm'])
                T.op('act', lambda e: e.activation(out=dst_ap, in_=dst_ap, func=AF.Sqrt, scale=1.0 / n, bias=EPS), R=['sm'], W=['sm'])
                T.op('dve', lambda e: e.reciprocal(out=dst_ap, in_=dst_ap), R=['sm'], W=['sm'])

            def norm_transpose(src_d_ap, dst_fn, dkey):
                T.dma('sp', 'xld', lambda e: e.dma_start(out=xt[:], in_=src_d_ap), W=['xt'])
                T.op('act', lambda e: e.activation(out=xnb[:, 0:1024], in_=xt[:, 0:1024], func=AF.Square, accum_out=sm[:, 0:1]),
                     R=['xt', 'sm'], W=['xnb', 'sm'])
                T.op('act', lambda e: e.activation(out=xnb[:, 1024:2048], in_=xt[:, 1024:2048], func=AF.Square, accum_out=sm[:, 1:2]),
                     R=['xt', 'sm'], W=['xnb', 'sm'])
                T.op('dve', lambda e: e.tensor_tensor(out=sm[:, 0:1], in0=sm[:, 0:1], in1=sm[:, 1:2], op=ALU.add), R=['sm'], W=['sm'])
                T.op('act', lambda e: e.activation(out=sm[:, 0:1], in_=sm[:, 0:1], func=AF.Sqrt, scale=1.0 / D, bias=EPS), R=['sm'], W=['sm'])
                T.op('dve', lambda e: e.reciprocal(out=sm[:, 0:1], in_=sm[:, 0:1]), R=['sm'], W=['sm'])
                T.op('dve', lambda e: e.tensor_scalar(out=xnb[:], in0=xt[:], scalar1=sm[:, 0:1], scalar2=None, op0=ALU.mult),
                     R=['xt', 'sm', 'xnb'], W=['xnb'])
                for half in range(2):
                    p, pk = nextT()
                    for k in range(8):
                        kc = half * 8 + k
                        T.op('pe', lambda e, k=k, kc=kc, p=p: e.transpose(out=p[:, k * 128:(k + 1) * 128],
                                                                          in_=xnb[:, kc * 128:(kc + 1) * 128], identity=identb[:]),
                             R=['xnb', 'identb'], W=[pk])
                    if half == 0:
                        T.op('act', lambda e, p=p, half=half: e.copy(out=dst_fn(half), in_=p[:, :]), R=[pk, dkey], W=[dkey])
                    else:
                        T.op('dve', lambda e, p=p, half=half: e.tensor_copy(out=dst_fn(half), in_=p[:, :]), R=[pk, dkey], W=[dkey])

            sqb = sb("sqb", [128, 1024], BF16)
            kst = [sb("kst%d" % i, [128, 16]) for i in range(2)]
            kslot = [0]

            def key_norms(cT, cTk, kr_ap, krk, nk, rk_dst, rkk):
                sl_ = kslot[0]; kslot[0] = 1 - sl_
                ks, ksk = kst[sl_], 'kst%d' % sl_
                pa, pak = nextA()
                pb, pbk = nextA()
                for hb, (pp, ppk) in enumerate(((pa, pak), (pb, pbk))):
                    for kc in range(2):
                        T.op('pe', lambda e, kc=kc, pp=pp, hb=hb: e.matmul(pp[0:nk, :], lhsT=cT[:, kc, :],
                                                                          rhs=w_uk_b[:, kc, hb * 512:(hb + 1) * 512],
                                                                          start=(kc == 0), stop=(kc == 1)),
                             R=[cTk, 'w_uk_b'], W=[ppk])
                    T.op('act', lambda e, pp=pp, hb=hb: e.activation(out=sqb[0:nk, hb * 512:(hb + 1) * 512], in_=pp[0:nk, :],
                                                                     func=AF.Square), R=[ppk], W=['sqb%d' % hb])
                T.op('dve', lambda e: e.tensor_reduce(out=ks[0:nk, 0:8], in_=sqb[0:nk, :].rearrange("p (h d) -> p h d", d=128),
                                                      axis=AX.X, op=ALU.add), R=['sqb0', 'sqb1'], W=[ksk])
                T.op('act', lambda e: e.activation(out=junk[0:nk, 0:64], in_=kr_ap, func=AF.Square, accum_out=ks[0:nk, 8:9]),
                     R=[krk, ksk], W=['junk', ksk])
                T.op('dve', lambda e: e.tensor_scalar(out=ks[0:nk, 0:8], in0=ks[0:nk, 0:8], scalar1=ks[0:nk, 8:9], scalar2=None,
                                                      op0=ALU.add), R=[ksk], W=[ksk])
                T.op('act', lambda e: e.activation(out=ks[0:nk, 0:8], in_=ks[0:nk, 0:8], func=AF.Ln, scale=1.0 / 192, bias=EPS),
                     R=[ksk], W=[ksk])
                T.op('act', lambda e: e.activation(out=rk_dst, in_=ks[0:nk, 0:8], func=AF.Exp, scale=-0.5), R=[ksk, rkk], W=[rkk])

            def latent_post(pck, pckk, row0, out_ckv, out_kr, orow0, cT_dst, cTk, kT_dst, kTk, c1_dst, c1k, rk_dst):
                T.dma('sp', 'rld', lambda e: e.dma_start(out=ropet[:], in_=rope[row0:row0 + 128, :]), W=['ropet'])
                rstd_of(pck[:, 0:256], 256, sm[:, 2:3], pckk)
                T.op('dve', lambda e: e.scalar_tensor_tensor(out=ckvn_f[:], in0=pck[:, 0:256], scalar=sm[:, 2:3], in1=gkv_bc[:],
                                                             op0=ALU.mult, op1=ALU.mult), R=[pckk, 'sm', 'gkv_bc'], W=['ckvn_f'])
                T.op('dve', lambda e: e.tensor_tensor(out=krr_f[:], in0=pck[:, 256:320], in1=ropet[:, 0:64], op=ALU.mult),
                     R=[pckk, 'ropet'], W=['krr_f'])
                T.op('dve', lambda e: e.tensor_tensor(out=krt[:, 0:32], in0=pck[:, 288:320], in1=ropet[:, 64:96], op=ALU.mult),
                     R=[pckk, 'ropet'], W=['krt'])
                T.op('dve', lambda e: e.tensor_tensor(out=krt[:, 32:64], in0=pck[:, 256:288], in1=ropet[:, 96:128], op=ALU.mult),
                     R=[pckk, 'ropet', 'krt'], W=['krt'])
                T.op('dve', lambda e: e.tensor_tensor(out=krr_f[:], in0=krr_f[:], in1=krt[:], op=ALU.add), R=['krr_f', 'krt'], W=['krr_f'])
                if out_ckv is not None:
                    T.dma('sp', 'ost', lambda e: e.dma_start(out=out_ckv[orow0:orow0 + 128, :], in_=ckvn_f[:]), R=['ckvn_f'])
                    T.dma('sp', 'ost', lambda e: e.dma_start(out=out_kr[orow0:orow0 + 128, :], in_=krr_f[:]), R=['krr_f'])
                T.op('pool', lambda e: e.tensor_copy(out=ckvn_b[:], in_=ckvn_f[:]), R=['ckvn_f'], W=['ckvn_b'])
                T.op('pool', lambda e: e.tensor_copy(out=krr_b[:], in_=krr_f[:]), R=['krr_f'], W=['krr_b'])
                if c1_dst is not None:
                    T.op('pool', lambda e: e.tensor_copy(out=c1_dst, in_=ckvn_f[:]), R=['ckvn_f', c1k], W=[c1k])
                p, pk = nextT()
                for kc in range(2):
                    T.op('pe', lambda e, kc=kc: e.transpose(out=p[:, kc * 128:(kc + 1) * 128], in_=ckvn_b[:, kc * 128:(kc + 1) * 128],
                                                            identity=identb[:]), R=['ckvn_b', 'identb'], W=[pk])
                T.op('pe', lambda e: e.transpose(out=p[0:64, 256:384], in_=krr_b[:], identity=identb[:]), R=['krr_b', 'identb'], W=[pk])
                T.op('act', lambda e: e.copy(out=cT_dst, in_=p[:, 0:256].rearrange("p (a b) -> p a b", a=2)), R=[pk, cTk], W=[cTk])
                T.op('act', lambda e: e.copy(out=kT_dst, in_=p[0:64, 256:384]), R=[pk, kTk], W=[kTk])
                if rk_dst is not None:
                    key_norms(cT_dst, cTk, krr_f[:], 'krr_f', 128, rk_dst, 'rk_all')

            def u_transpose(src_b_ap, skey):
                p, pk = nextT()
                for k in range(4):
                    T.op('pe', lambda e, k=k: e.transpose(out=p[:, k * 128:(k + 1) * 128], in_=src_b_ap[:, k * 128:(k + 1) * 128],
                                                          identity=identb[:]), R=[skey, 'identb'], W=[pk])
                T.op('dve', lambda e: e.tensor_copy(out=uT[:].rearrange("p a b -> p (a b)"), in_=p[:, 0:512]), R=[pk], W=['uT'])

            for t in range(NPREV):
                norm_transpose(xall[t * 128:(t + 1) * 128, :],
                               lambda half: xnT1[:, half * 8:(half + 1) * 8, :].rearrange("p a b -> p (a b)"), 'xnT1')
                pu, puk = nextA()
                for kc in range(16):
                    T.op('pe', lambda e, kc=kc: e.matmul(pu[:, 0:512], lhsT=xnT1[:, kc, :], rhs=w_up_b[:, kc, 0:512],
                                                         start=(kc == 0), stop=(kc == 15)), R=['xnT1', 'w_up_b'], W=[puk])
                pc, pck = nextA()
                for kc in range(16):
                    T.op('pe', lambda e, kc=kc: e.matmul(pc[:, 0:320], lhsT=xnT1[:, kc, :], rhs=w_up_b[:, kc, 512:832],
                                                         start=(kc == 0), stop=(kc == 15)), R=['xnT1', 'w_up_b'], W=[pck])
                T.op('act', lambda e: e.copy(out=u_bg[:, 0, :], in_=pu[:, 0:512]), R=[puk], W=['u_bg'])
                latent_post(pc, pck, t * 128, None, None, 0, ckvT_all[:, :, t * 128:(t + 1) * 128], 'ckvT_all',
                            krT_all[0:64, t * 128:(t + 1) * 128], 'krT_all', ckv1_all[:, t, :], 'ckv1_all', rk_all[:, t, :])
                u_transpose(u_bg[:, 0, :], 'u_bg')
                for sub in range(128 // LS):
                    ssm_step(lambda ct, sub=sub: uT[:, ct, sub * LS:(sub + 1) * LS], LS, 1, LS, 0, False)
            dbg('hre', hre[:, :, 0], [128, 16], ['hst']); dbg('him', him[:, :, 0], [128, 16], ['hst'])
            T.barrier()

            chk('P')
            memKT_b = sb("memKT_b", [128, 4, 256], BF16)
            memV_b = sb("memV_b", [128, 2, 512], BF16)
            wblk = sb("wblk", [128, 16, 256], BF16)
            yf = sb("yf", [128, 512]); yg = sb("yg", [128, 512])
            ob = sb("ob", [128, 512], BF16)

            def stream_wblock(w_d, c0, ncols, gcol):
                for q4 in range(4):
                    T.dma('sp', 'wst', lambda e, q4=q4: e.dma_start(
                        out=stg[:, 0:4 * ncols].rearrange("p (k c) -> p k c", k=4),
                        in_=w_d[q4 * 512:(q4 + 1) * 512, c0:c0 + ncols].rearrange("(k p) c -> p k c", p=128)), W=['stg'])
                    for k in range(4):
                        kc = q4 * 4 + k
                        if k % 2 == 0:
                            T.op('dve', lambda e, k=k, kc=kc: e.tensor_scalar(out=wblk[:, kc, 0:ncols], in0=stg[:, k * ncols:(k + 1) * ncols],
                                                                              scalar1=gcol[:, kc:kc + 1], scalar2=None, op0=ALU.mult),
                                 R=['stg', 'wblk'], W=['wblk'])
                        else:
                            T.op('act', lambda e, k=k, kc=kc: e.activation(out=wblk[:, kc, 0:ncols], in_=stg[:, k * ncols:(k + 1) * ncols],
                                                                           func=AF.Copy, scale=gcol[:, kc:kc + 1]),
                                 R=['stg', 'wblk'], W=['wblk'])
                return wblk, 'wblk'

            for mt in range(2):
                norm_transpose(mem_d[mt * 128:(mt + 1) * 128, :],
                               lambda half, mt=mt: memT[:, mt, half * 8:(half + 1) * 8, :].rearrange("p a b -> p (a b)"), 'memT')
            chk('M0')
            for which, w_d in enumerate((w_mk_d, w_mv_d)):
                for cb in range(2):
                    wb, wk = stream_wblock(w_d, cb * 256, 256, gcol_mem)
                    chk('M1_%d_%d' % (which, cb))
                    for mt in range(2):
                        p, pk = nextA()
                        for kc in range(16):
                            T.op('pe', lambda e, kc=kc, mt=mt, p=p: e.matmul(p[:, 0:256], lhsT=memT[:, mt, kc, :], rhs=wb[:, kc, 0:256],
                                                                             start=(kc == 0), stop=(kc == 15)), R=['memT', wk], W=[pk])
                        chk('M2')
                        if which == 1:
                            T.op('act', lambda e, p=p: e.copy(out=yf[:, 0:256], in_=p[:, 0:256]), R=[pk], W=['yf'])
                            T.op('dve', lambda e, p=p, mt=mt, cb=cb: e.tensor_copy(out=memV_b[:, mt, cb * 256:(cb + 1) * 256], in_=p[:, 0:256]),
                                 R=[pk, 'memV_b'], W=['memV_b'])
                            T.dma('sp', 'ost', lambda e, mt=mt, cb=cb: e.dma_start(out=memv_o[mt * 128:(mt + 1) * 128, cb * 256:(cb + 1) * 256],
                                                                                   in_=yf[:, 0:256]), R=['yf'])
                        else:
                            for hh in range(2):
                                rstd_of(p[:, hh * 128:(hh + 1) * 128], 128, sm[:, 3:4], pk)
                                T.op('dve', lambda e, p=p, hh=hh: e.scalar_tensor_tensor(out=yf[:, hh * 128:(hh + 1) * 128],
                                                                                         in0=p[:, hh * 128:(hh + 1) * 128], scalar=sm[:, 3:4],
                                                                                         in1=gmk_bc[:], op0=ALU.mult, op1=ALU.mult),
                                     R=[pk, 'sm', 'gmk_bc', 'yf'], W=['yf'])
                            chk('M3')
                            T.op('dve', lambda e: e.tensor_copy(out=ob[:, 0:256], in_=yf[:, 0:256]), R=['yf'], W=['ob'])
                            T.dma('sp', 'ost', lambda e, mt=mt, cb=cb: e.dma_start(out=memk_o[mt * 128:(mt + 1) * 128, cb * 256:(cb + 1) * 256],
                                                                                   in_=yf[:, 0:256]), R=['yf'])
                            chk('M4')
                            pt_, ptk = nextT()
                            for hh in range(2):
                                T.op('pe', lambda e, hh=hh, pt_=pt_: e.transpose(out=pt_[:, hh * 128:(hh + 1) * 128], in_=ob[:, hh * 128:(hh + 1) * 128],
                                                                                 identity=identb[:]), R=['ob', 'identb'], W=[ptk])
                            T.op('act', lambda e, cb=cb, mt=mt, pt_=pt_: e.copy(out=memKT_b[:, cb * 2:cb * 2 + 2, mt * 128:(mt + 1) * 128],
                                                                                in_=pt_[:, 0:256].rearrange("p (a b) -> p a b", a=2)),
                                 R=[ptk, 'memKT_b'], W=['memKT_b'])
                            chk('M5')
                            if cb == 1 and mt == 1:
                                chk('M6')
            T.barrier()

            chk('M')
            ssq = sb("ssq", [128, GT, 8])
            qf = xt[:, 0:1536].rearrange("p (h c) -> p h c", h=8); qs_b = sb("qs_b", [128, 8, 192], BF16)
            wk6 = sb("wk6", [128, 1536])
            qsq = wk6[:, :].rearrange("p (h c) -> p h c", h=8)
            ymT8 = wk6[:, 0:1024].rearrange("p (h c) -> p h c", h=8)
            ymT4 = wk6[:, 0:4 * GC].rearrange("p (h c) -> p h c", h=4)
            cq_b = sb("cq_b", [128, 512], BF16); cqT = sb("cqT", [128, 4, 128], BF16)
            qabsT = sb("qabsT", [128, 2, 8, 128], BF16)
            qav = qabsT[:].rearrange("p a h c -> p a (h c)")
            PT = sb("PT", [128, GC], BF16)
            OTn = sb("OTn", [128, 2, GC], BF16)
            rcp = sb("rcp", [128, GC])
            xres = sb("xres", [128, 256]); yout = sb("yout", [128, 256])
            idx_b = sb("idx_b", [128, NPG], I32); idxf = sb("idxf", [128, NPG])
            iot = sb("iot", [128, 1], I32); iotf = sb("iotf", [128, 1])
            pgc = [sb("pgc%d" % i, [128, 256]) for i in range(2)]
            pgk = [sb("pgk%d" % i, [128, 64]) for i in range(2)]
            pgcb2 = [sb("pgcb%d" % i, [128, 256], BF16) for i in range(2)]; pgkb2 = [sb("pgkb%d" % i, [128, 64], BF16) for i in range(2)]
            pgT2 = [sb("pgT%d" % i, [128, 2, 128], BF16) for i in range(2)]; pgkT2 = [sb("pgkT%d" % i, [64, 128], BF16) for i in range(2)]
            rkp2 = [sb("rkp%d" % i, [128, 8]) for i in range(2)]; scf2 = [sb("scf%d" % i, [128, 64]) for i in range(2)]
            PTs2 = [sb("PTs%d" % i, [128, 64], BF16) for i in range(2)]
            mini = sb("mini", [8, 256], BF16)
            mkb = c1flat[:, 0:1024].rearrange("p (t c) -> p t c", t=2); mvb = c1flat[:, 1024:2048].rearrange("p (t c) -> p t c", t=2)
            mkT_s = c1flat[:, 2048:3072].rearrange("p (h c) -> p h c", h=4)
            mcf = [xt[:, 0:1024].rearrange("p (t c) -> p t c", t=2), xt[:, 1024:2048].rearrange("p (t c) -> p t c", t=2)]

            def in_proj_block(ng, c0, ncols, consume):
                stream_wblock(w_in, c0, ncols, gcol_in)
                for ti in range(ng):
                    p, pk = nextA()
                    for kc in range(16):
                        T.op('pe', lambda e, kc=kc, ti=ti, p=p: e.matmul(p[:, 0:ncols], lhsT=xnT_g[:, ti, kc, :], rhs=wblk[:, kc, 0:ncols],
                                                                         start=(kc == 0), stop=(kc == 15)), R=['xnT_g', 'wblk'], W=[pk])
                    consume(ti, p, pk)

            def mem_attend(ncols, qcols, KT, KTk, V, Vk, out_fn):
                for h in range(4):
                    acc, acck = psC[0], 'psC0'
                    lb, lbk = psC[1], 'psC1'
                    for kt in range(2):
                        p, pk = nextA()
                        T.op('pe', lambda e, kt=kt, h=h, p=p: e.matmul(p[:, 0:ncols], lhsT=KT[:, h, kt * 128:(kt + 1) * 128],
                                                                       rhs=QmT[:, h, qcols], start=True, stop=True), R=[KTk, 'QmT'], W=[pk])
                        T.op('act', lambda e, p=p: e.activation(out=PT[:, 0:ncols], in_=p[:, 0:ncols], func=AF.Exp), R=[pk], W=['PT'])
                        T.op('pe', lambda e, kt=kt, h=h: e.matmul(acc[:, 0:ncols], lhsT=V[:, kt, h * 128:(h + 1) * 128], rhs=PT[:, 0:ncols],
                                                                  start=(kt == 0), stop=(kt == 1)), R=[Vk, 'PT'], W=[acck])
                        T.op('pe', lambda e, kt=kt: e.matmul(lb[:, 0:ncols], lhsT=onesb[:], rhs=PT[:, 0:ncols],
                                                             start=(kt == 0), stop=(kt == 1)), R=['onesb', 'PT'], W=[lbk])
                    T.op('dve', lambda e: e.reciprocal(out=rcp[:, 0:ncols], in_=lb[:, 0:ncols]), R=[lbk], W=['rcp'])
                    T.op('dve', lambda e, h=h: e.tensor_tensor(out=out_fn(h), in0=acc[:, 0:ncols], in1=rcp[:, 0:ncols], op=ALU.mult),
                         R=[acck, 'rcp', 'wk6'], W=['wk6'])

            def finish_branch(ti, src_ap, skey, n, gate0):
                rstd_of(src_ap, n, sm[:, 4:5], skey)
                T.op('dve', lambda e: e.scalar_tensor_tensor(out=gates[:, ti, gate0:gate0 + n], in0=src_ap, scalar=sm[:, 4:5],
                                                             in1=gates[:, ti, gate0:gate0 + n], op0=ALU.mult, op1=ALU.mult),
                     R=[skey, 'sm', 'gates'], W=['gates'])

            def mem_branch_finish(ng, get_cols):
                for ti in range(ng):
                    p, pk = nextA()
                    for h in range(4):
                        T.op('pe', lambda e, h=h, ti=ti, p=p: e.transpose(out=p[:, h * 128:(h + 1) * 128], in_=get_cols(h, ti), identity=identf[:]),
                             R=['wk6', 'identf'], W=[pk])
                    T.op('act', lambda e, p=p: e.copy(out=yf[:], in_=p[:, 0:512]), R=[pk], W=['yf'])
                    finish_branch(ti, yf[:], 'yf', 512, 1536)

            def q_path(ti, row0):
                T.dma('sp', 'rld', lambda e: e.dma_start(out=ropet[:], in_=rope[row0:row0 + 128, :]), W=['ropet'])
                rstd_of(yf[:], 512, sm[:, 5:6], 'yf')
                T.op('dve', lambda e: e.tensor_scalar(out=cq_b[:], in0=yf[:], scalar1=sm[:, 5:6], scalar2=None, op0=ALU.mult),
                     R=['yf', 'sm'], W=['cq_b'])
                p, pk = nextT()
                for k in range(4):
                    T.op('pe', lambda e, k=k, p=p: e.transpose(out=p[:, k * 128:(k + 1) * 128], in_=cq_b[:, k * 128:(k + 1) * 128],
                                                               identity=identb[:]), R=['cq_b', 'identb'], W=[pk])
                T.op('dve', lambda e, p=p: e.tensor_copy(out=cqT[:].rearrange("p a b -> p (a b)"), in_=p[:, 0:512]), R=[pk], W=['cqT'])
                for blk in range(3):
                    pq, pqk = nextA()
                    for k in range(4):
                        T.op('pe', lambda e, k=k, blk=blk, pq=pq: e.matmul(pq[:, 0:512], lhsT=cqT[:, k, :],
                                                                          rhs=w_uq_b[:, k, blk * 512:(blk + 1) * 512],
                                                                          start=(k == 0), stop=(k == 3)), R=['cqT', 'w_uq_b'], W=[pqk])
                    T.op('act', lambda e, blk=blk, pq=pq: e.copy(out=qf[:].rearrange("p h c -> p (h c)")[:, blk * 512:(blk + 1) * 512],
                                                                 in_=pq[:, 0:512]), R=[pqk, 'xt'], W=['xt'])
                cc = ropet[:, 0:64].unsqueeze(1).to_broadcast([128, 8, 64])
                ns = ropet[:, 64:96].unsqueeze(1).to_broadcast([128, 8, 32])
                ps_ = ropet[:, 96:128].unsqueeze(1).to_broadcast([128, 8, 32])
                T.op('dve', lambda e: e.tensor_tensor(out=qsq[:, :, 0:32], in0=qf[:, :, 160:192], in1=ns, op=ALU.mult), R=['xt', 'ropet'], W=['wk6'])
                T.op('dve', lambda e: e.tensor_tensor(out=qsq[:, :, 32:64], in0=qf[:, :, 128:160], in1=ps_, op=ALU.mult), R=['xt', 'ropet', 'wk6'], W=['wk6'])
                T.op('dve', lambda e: e.tensor_tensor(out=qf[:, :, 128:192], in0=qf[:, :, 128:192], in1=cc, op=ALU.mult), R=['xt', 'ropet', 'wk6'], W=['xt'])
                T.op('dve', lambda e: e.tensor_tensor(out=qf[:, :, 128:192], in0=qf[:, :, 128:192], in1=qsq[:, :, 0:64], op=ALU.add), R=['xt', 'wk6'], W=['xt'])
                T.op('pool', lambda e: e.tensor_tensor(out=qsq[:], in0=qf[:], in1=qf[:], op=ALU.mult), R=['xt', 'wk6'], W=['wk6'])
                T.op('dve', lambda e: e.tensor_reduce(out=sm[:, 24:32], in_=qsq[:], axis=AX.X, op=ALU.add), R=['wk6', 'sm'], W=['sm'])
                T.op('act', lambda e: e.activation(out=sm[:, 24:32], in_=sm[:, 24:32], func=AF.Sqrt, scale=1.0 / 192, bias=EPS), R=['sm'], W=['sm'])
                T.op('dve', lambda e: e.reciprocal(out=sm[:, 24:32], in_=sm[:, 24:32]), R=['sm'], W=['sm'])
                T.op('dve', lambda e: e.tensor_tensor(out=qf[:], in0=qf[:], in1=sm[:, 24:32].unsqueeze(2).to_broadcast([128, 8, 192]), op=ALU.mult),
                     R=['xt', 'sm'], W=['xt'])
                T.op('dve', lambda e: e.tensor_tensor(out=qs_b[:], in0=qf[:], in1=gqk_bc[:].unsqueeze(1).to_broadcast([128, 8, 192]), op=ALU.mult),
                     R=['xt', 'gqk_bc'], W=['qs_b'])
                for h in range(8):
                    p, pk = nextT()
                    T.op('pe', lambda e, h=h, p=p: e.transpose(out=p[:, 0:128], in_=qs_b[:, h, 0:128], identity=identb[:]), R=['qs_b', 'identb'], W=[pk])
                    T.op('pe', lambda e, h=h, p=p: e.transpose(out=p[0:64, 128:256], in_=qs_b[:, h, 128:192], identity=identb[:]), R=['qs_b', 'identb'], W=[pk])
                    T.op('act', lambda e, h=h, p=p: e.copy(out=QTn[:, h, ti * 128:(ti + 1) * 128], in_=p[:, 0:128]), R=[pk, 'QTn'], W=['QTn'])
                    T.op('dve', lambda e, h=h, p=p: e.tensor_copy(out=QTr[0:64, h, ti * 128:(ti + 1) * 128], in_=p[0:64, 128:256]), R=[pk, 'QTr'], W=['QTr'])

            def prompt_attention(ng, ncg, own0):
                for h in range(8):
                    for kc in range(2):
                        p, pk = nextA()
                        T.op('pe', lambda e, kc=kc, h=h, p=p: e.matmul(p[:, 0:ncg], lhsT=w_ukT_b[:, h, kc * 128:(kc + 1) * 128],
                                                                       rhs=QTn[:, h, 0:ncg], start=True, stop=True), R=['w_ukT_b', 'QTn'], W=[pk])
                        T.op('act', lambda e, kc=kc, p=p: e.copy(out=qav[:, kc, 0:ncg], in_=p[:, 0:ncg]), R=[pk, 'qabsT'], W=['qabsT'])
                    nkt = NPREV + own0 + ng
                    for kt in range(nkt):
                        rel = kt - (NPREV + own0)
                        c0 = max(rel, 0) * 128
                        ncol = ncg - c0
                        kcols = slice(kt * 128, (kt + 1) * 128)
                        p, pk = nextA()
                        for kc in range(2):
                            T.op('pe', lambda e, kc=kc, p=p, kcols=kcols, c0=c0, ncol=ncol: e.matmul(
                                p[:, 0:ncol], lhsT=ckvT_all[:, kc, kcols], rhs=qav[:, kc, c0:ncg], start=(kc == 0), stop=False),
                                R=['ckvT_all', 'qabsT'], W=[pk])
                        T.op('pe', lambda e, h=h, p=p, kcols=kcols, c0=c0, ncol=ncol: e.matmul(
                            p[:, 0:ncol], lhsT=krT_all[0:64, kcols], rhs=QTr[0:64, h, c0:ncg], start=False, stop=True),
                            R=['krT_all', 'QTr'], W=[pk])
                        T.op('act', lambda e, kt=kt, h=h, p=p, c0=c0, ncol=ncol: e.activation(
                            out=PT[:, c0:ncg], in_=p[:, 0:ncol], func=AF.Exp, scale=rk_all[:, kt, h:h + 1], bias=kbias[:, kt:kt + 1]),
                            R=[pk, 'rk_all', 'kbias'], W=['PT'])
                        if rel >= 0:
                            T.op('pool', lambda e, c0=c0: e.tensor_tensor(out=PT[:, c0:c0 + 128], in0=PT[:, c0:c0 + 128], in1=trib[:], op=ALU.mult),
                                 R=['PT', 'trib'], W=['PT'])
                        first, last = (kt == 0), (kt == nkt - 1)
                        for kc in range(2):
                            T.op('pe', lambda e, kc=kc, kt=kt, c0=c0, first=first, last=last: e.matmul(
                                psC[kc][:, c0:ncg], lhsT=ckv1_all[:, kt, kc * 128:(kc + 1) * 128], rhs=PT[:, c0:ncg], start=first, stop=last),
                                R=['ckv1_all', 'PT'], W=['psC%d' % kc])
                        T.op('pe', lambda e, c0=c0, first=first, last=last: e.matmul(psC[2][:, c0:ncg], lhsT=onesb[:], rhs=PT[:, c0:ncg],
                                                                                     start=first, stop=last),
                             R=['onesb', 'PT'], W=['psC2'])
                    T.op('dve', lambda e: e.reciprocal(out=rcp[:, 0:ncg], in_=psC[2][:, 0:ncg]), R=['psC2'], W=['rcp'])
                    for kc in range(2):
                        T.op('dve', lambda e, kc=kc: e.tensor_tensor(out=OTn[:, kc, 0:ncg], in0=psC[kc][:, 0:ncg], in1=rcp[:, 0:ncg], op=ALU.mult),
                             R=['psC%d' % kc, 'rcp', 'OTn'], W=['OTn'])
                    for ti in range(ng):
                        p, pk = nextA()
                        for kc in range(2):
                            T.op('pe', lambda e, kc=kc, ti=ti, h=h, p=p: e.matmul(p[:, 0:128], lhsT=OTn[:, kc, ti * 128:(ti + 1) * 128],
                                                                                 rhs=w_uv_b[:, kc, h * 128:(h + 1) * 128],
                                                                                 start=(kc == 0), stop=(kc == 1)), R=['OTn', 'w_uv_b'], W=[pk])
                        T.op('dve', lambda e, ti=ti, h=h, p=p: e.tensor_copy(out=ymla[:, ti, h * 128:(h + 1) * 128], in_=p[:, 0:128]),
                             R=[pk, 'ymla'], W=['ymla'])
                        T.op('act', lambda e, ti=ti, h=h, p=p: e.activation(out=junk[:, 0:128], in_=p[:, 0:128], func=AF.Square,
                                                                            accum_out=ssq[:, ti, h:h + 1]), R=[pk, 'ssq'], W=['junk', 'ssq'])
                mem_attend(ncg, slice(0, ncg), memKT_b, 'memKT_b', memV_b, 'memV_b', lambda h: ymT4[:, h, 0:ncg])
                mem_branch_finish(ng, lambda h, ti: ymT4[:, h, ti * 128:(ti + 1) * 128])

            def sample_attention():
                T.op('pool', lambda e: e.iota(iot[:], pattern=[[0, 1]], base=0, channel_multiplier=1), W=['iot'])
                T.op('dve', lambda e: e.tensor_copy(out=iotf[:], in_=iot[:]), R=['iot'], W=['iotf'])
                for h in range(8):
                    for kc in range(2):
                        p, pk = nextA()
                        T.op('pe', lambda e, kc=kc, h=h, p=p: e.matmul(p[:, 0:128], lhsT=w_ukT_b[:, h, kc * 128:(kc + 1) * 128],
                                                                       rhs=QTn[:, h, 0:128], start=True, stop=True), R=['w_ukT_b', 'QTn'], W=[pk])
                        T.op('act', lambda e, kc=kc, h=h, p=p: e.copy(out=qabsT[:, kc, h, :], in_=p[:, 0:128]), R=[pk, 'qabsT'], W=['qabsT'])
                acc = psC[0]
                o0 = acc[:, 0:64]; o1 = acc[:, 64:128]; lb = acc[:, 128:192]
                for b in range(NSEQ):
                    bc = slice(b * 8, (b + 1) * 8)
                    T.dma('sp', 'c0', lambda e, b=b: e.dma_start(out=idx_b[:], in_=ptab[b].partition_broadcast(128)), W=['idx_b'])
                    T.op('dve', lambda e: e.tensor_copy(out=idxf[:], in_=idx_b[:]), R=['idx_b'], W=['idxf'])
                    T.op('dve', lambda e: e.tensor_scalar(out=idxf[:], in0=idxf[:], scalar1=128.0, scalar2=iotf[:, 0:1], op0=ALU.mult, op1=ALU.add),
                         R=['idxf', 'iotf'], W=['idxf'])
                    T.op('dve', lambda e: e.tensor_copy(out=idx_b[:], in_=idxf[:]), R=['idxf'], W=['idx_b'])

                    def attend(sl_, cT, cTk, krT_ap, krTk, c1, c1k, rk_ap, rkk, nk, first, last, mask):
                        p, pk = psC[1 + sl_], 'psC%d' % (1 + sl_)
                        scf, scfk = scf2[sl_], 'scf%d' % sl_
                        PTs, PTk = PTs2[sl_], 'PTs%d' % sl_
                        pv = p[0:nk, 0:64]
                        for kc in range(2):
                            T.op('pe', lambda e, kc=kc: e.matmul(pv, lhsT=cT[:, kc, :], rhs=qabsT[:, kc, :, bc], start=(kc == 0), stop=False),
                                 R=[cTk, 'qabsT'], W=[pk])
                        T.op('pe', lambda e: e.matmul(pv, lhsT=krT_ap, rhs=QTr[0:64, :, bc], start=False, stop=True), R=[krTk, 'QTr'], W=[pk])
                        T.op('dve', lambda e: e.tensor_tensor(out=scf[0:nk, :].rearrange("p (h q) -> p h q", q=8),
                                                              in0=pv.rearrange("p (h q) -> p h q", q=8),
                                                              in1=rk_ap.unsqueeze(2).to_broadcast([nk, 8, 8]), op=ALU.mult),
                             R=[pk, rkk], W=[scfk])
                        T.op('act', lambda e: e.activation(out=PTs[0:nk, 0:64], in_=scf[0:nk, :], func=AF.Exp), R=[scfk], W=[PTk])
                        if mask:
                            T.op('dve', lambda e: e.tensor_tensor(out=PTs[0:nk, 0:64].rearrange("p (h q) -> p h q", q=8),
                                                                  in0=PTs[0:nk, 0:64].rearrange("p (h q) -> p h q", q=8),
                                                                  in1=trib[0:nk, 0:8].unsqueeze(1).to_broadcast([nk, 8, 8]), op=ALU.mult),
                                 R=[PTk, 'trib'], W=[PTk])
                        T.op('pe', lambda e: e.matmul(o0, lhsT=c1[0:nk, 0:128], rhs=PTs[0:nk, 0:64], start=first, stop=last,
                                                      skip_group_check=True), R=[c1k, PTk], W=['psC0'])
                        T.op('pe', lambda e: e.matmul(o1, lhsT=c1[0:nk, 128:256], rhs=PTs[0:nk, 0:64], start=False, stop=last,
                                                      skip_group_check=True), R=[c1k, PTk], W=['psC0'])
                        T.op('pe', lambda e: e.matmul(lb, lhsT=onesb[0:nk, :], rhs=PTs[0:nk, 0:64], start=False, stop=last,
                                                      skip_group_check=True), R=['onesb', PTk], W=['psC0'])

                    for pg in range(NPG):
                        s = pg % 2
                        pgcb, pgkb, pgT, pgkT, rkp = pgcb2[s], pgkb2[s], pgT2[s], pgkT2[s], rkp2[s]
                        T.dma('pool', 'pgc%d' % s, lambda e, s=s, pg=pg: e.indirect_dma_start(
                            out=pgc[s][:], out_offset=None, in_=cckv[:, :],
                            in_offset=bass.IndirectOffsetOnAxis(ap=idx_b[:, pg:pg + 1], axis=0)), R=['idx_b'], W=['pgc%d' % s])
                        T.dma('pool', 'pgk%d' % s, lambda e, s=s, pg=pg: e.indirect_dma_start(
                            out=pgk[s][:], out_offset=None, in_=ckr[:, :],
                            in_offset=bass.IndirectOffsetOnAxis(ap=idx_b[:, pg:pg + 1], axis=0)), R=['idx_b'], W=['pgk%d' % s])
                        T.op('dve', lambda e, s=s, pgcb=pgcb: e.tensor_copy(out=pgcb[:], in_=pgc[s][:]), R=['pgc%d' % s], W=['pgcb%d' % s])
                        T.op('dve', lambda e, s=s, pgkb=pgkb: e.tensor_copy(out=pgkb[:], in_=pgk[s][:]), R=['pgk%d' % s], W=['pgkb%d' % s])
                        p, pk = nextT()
                        for kc in range(2):
                            T.op('pe', lambda e, kc=kc, p=p, pgcb=pgcb: e.transpose(out=p[:, kc * 128:(kc + 1) * 128], in_=pgcb[:, kc * 128:(kc + 1) * 128],
                                                                                   identity=identb[:]), R=['pgcb%d' % s, 'identb'], W=[pk])
                        T.op('pe', lambda e, p=p, pgkb=pgkb: e.transpose(out=p[0:64, 256:384], in_=pgkb[:], identity=identb[:]), R=['pgkb%d' % s, 'identb'], W=[pk])
                        T.op('dve', lambda e, p=p, pgT=pgT: e.tensor_copy(out=pgT[:], in_=p[:, 0:256].rearrange("p (a b) -> p a b", a=2)), R=[pk], W=['pgT%d' % s])
                        T.op('dve', lambda e, p=p, pgkT=pgkT: e.tensor_copy(out=pgkT[:], in_=p[0:64, 256:384]), R=[pk], W=['pgkT%d' % s])
                        key_norms(pgT[:], 'pgT%d' % s, pgk[s][:], 'pgk%d' % s, 128, rkp[:], 'rkp%d' % s)
                        attend(s, pgT[:], 'pgT%d' % s, pgkT[:], 'pgkT%d' % s, pgcb, 'pgcb%d' % s, rkp[:], 'rkp%d' % s, 128, pg == 0, False, False)
                    s = NPG % 2
                    rkp = rkp2[s]; scfm = scf2[1 - s]
                    p, pk = nextT()
                    for kc in range(2):
                        T.op('pe', lambda e, kc=kc, p=p: e.transpose(out=p[0:8, kc * 128:(kc + 1) * 128], in_=ckvT_s[:, kc, bc], identity=identb[:]),
                             R=['ckvT_s', 'identb'], W=[pk])
                    T.op('pe', lambda e, p=p: e.transpose(out=p[0:8, 256:320], in_=krT_s[0:64, bc], identity=identb[0:64, 0:64]),
                         R=['krT_s', 'identb'], W=[pk])
                    T.op('act', lambda e, p=p: e.copy(out=mini[:], in_=p[0:8, 0:256]), R=[pk], W=['mini'])
                    T.op('dve', lambda e, p=p: e.tensor_copy(out=scfm[0:8, 0:64], in_=p[0:8, 256:320]), R=[pk], W=['scf%d' % (1 - s)])
                    key_norms(ckvT_s[:, :, bc], 'ckvT_s', scfm[0:8, 0:64], 'scf%d' % (1 - s), 8, rkp[0:8, :], 'rkp%d' % s)
                    attend(s, ckvT_s[:, :, bc], 'ckvT_s', krT_s[0:64, bc], 'krT_s', mini, 'mini', rkp[0:8, :], 'rkp%d' % s, 8, NPG == 0, True, True)
                    T.op('dve', lambda e: e.reciprocal(out=rcp[:, 0:64], in_=lb), R=['psC0'], W=['rcp'])
                    T.op('dve', lambda e: e.tensor_tensor(out=OTn[:, 0, 0:64], in0=o0, in1=rcp[:, 0:64], op=ALU.mult), R=['psC0', 'rcp'], W=['OTn'])
                    T.op('dve', lambda e: e.tensor_tensor(out=OTn[:, 1, 0:64], in0=o1, in1=rcp[:, 0:64], op=ALU.mult), R=['psC0', 'rcp', 'OTn'], W=['OTn'])
                    p, pk = nextA()
                    for h in range(8):
                        for kc in range(2):
                            T.op('pe', lambda e, kc=kc, h=h, p=p: e.matmul(p[:, h * 8:(h + 1) * 8], lhsT=w_uv_b[:, kc, h * 128:(h + 1) * 128],
                                                                           rhs=OTn[:, kc, h * 8:(h + 1) * 8], start=(kc == 0), stop=(kc == 1)),
                                 R=['w_uv_b', 'OTn'], W=[pk])
                    T.op('act', lambda e, p=p: e.copy(out=ymT8[:, :, bc], in_=p[:, 0:64].rearrange("p (h q) -> p h q", q=8)), R=[pk, 'wk6'], W=['wk6'])
                for h in range(8):
                    p, pk = nextA()
                    T.op('pe', lambda e, h=h, p=p: e.transpose(out=p[:, 0:128], in_=ymT8[:, h, :], identity=identf[:]), R=['wk6', 'identf'], W=[pk])
                    T.op('dve', lambda e, h=h, p=p: e.tensor_copy(out=ymla[:, 0, h * 128:(h + 1) * 128], in_=p[:, 0:128]), R=[pk, 'ymla'], W=['ymla'])
                    T.op('act', lambda e, h=h, p=p: e.activation(out=junk[:, 0:128], in_=p[:, 0:128], func=AF.Square, accum_out=ssq[:, 0, h:h + 1]),
                         R=[pk, 'ssq'], W=['junk', 'ssq'])
                for b in range(NSEQ):
                    bc = slice(b * 8, (b + 1) * 8)
                    T.dma('sp', 'mck', lambda e, b=b: e.dma_start(out=mcf[0], in_=cmk[b * 256:(b + 1) * 256, :].rearrange("(t p) c -> p t c", p=128)),
                          W=['xt'])
                    T.dma('sp', 'mcv', lambda e, b=b: e.dma_start(out=mcf[1], in_=cmv[b * 256:(b + 1) * 256, :].rearrange("(t p) c -> p t c", p=128)),
                          W=['xt'])
                    T.op('pool', lambda e: e.tensor_copy(out=mkb[:], in_=mcf[0]), R=['xt'], W=['mkb'])
                    T.op('pool', lambda e: e.tensor_copy(out=mvb[:], in_=mcf[1]), R=['xt'], W=['mvb'])
                    for kt in range(2):
                        p, pk = nextT()
                        for h in range(4):
                            T.op('pe', lambda e, kt=kt, h=h, p=p: e.transpose(out=p[:, h * 128:(h + 1) * 128], in_=mkb[:, kt, h * 128:(h + 1) * 128],
                                                                              identity=identb[:]), R=['mkb', 'identb'], W=[pk])
                        T.op('act', lambda e, kt=kt, p=p: e.copy(out=mkT_s[:, :, kt * 128:(kt + 1) * 128],
                                                                in_=p[:, 0:512].rearrange("p (a b) -> p a b", a=4)), R=[pk, 'mkT_s'], W=['mkT_s'])
                    mem_attend(8, bc, mkT_s, 'mkT_s', mvb, 'mvb', lambda h, bc=bc: ymT4[:, h, bc])
                mem_branch_finish(1, lambda h, ti: ymT4[:, h, 0:128])

            def run_group(tiles, is_sample):
                ng = len(tiles)
                ncg = ng * 128
                for ti, (row0, _) in enumerate(tiles):
                    norm_transpose(xall[row0:row0 + 128, :],
                                   lambda half, ti=ti: xnT_g[:, ti, half * 8:(half + 1) * 8, :].rearrange("p a b -> p (a b)"), 'xnT_g')

                def c_u(half):
                    def f(ti, p, pk):
                        T.op('act', lambda e: e.copy(out=u_bg[:, ti, half * 256:(half + 1) * 256], in_=p[:, 0:256]), R=[pk, 'u_bg'], W=['u_bg'])
                    return f
                in_proj_block(ng, C_U, 256, c_u(0))
                in_proj_block(ng, C_U + 256, 256, c_u(1))

                def c_gate(goff):
                    def f(ti, p, pk):
                        T.op('act', lambda e: e.activation(out=gates[:, ti, goff:goff + 256], in_=p[:, 0:256], func=AF.Silu),
                             R=[pk, 'gates'], W=['gates'])
                    return f
                in_proj_block(ng, C_GS, 256, c_gate(0))
                in_proj_block(ng, C_GS + 256, 256, c_gate(256))
                chk('G0')
                for ti, (row0, own) in enumerate(tiles):
                    u_transpose(u_bg[:, ti, :], 'u_bg')
                    py, pyk = psC[2], 'psC2'
                    for gi in range(16 // GS):
                        if is_sample:
                            ssm_step(lambda ct: uT[:, ct, :], 128, NSEQ, 8, 0, True, gis=[gi])
                        else:
                            for sub in range(128 // LS):
                                ssm_step(lambda ct, sub=sub: uT[:, ct, sub * LS:(sub + 1) * LS], LS, 1, LS, sub * LS, True, gis=[gi])
                        for j in range(GS):
                            i = gi * GS + j
                            T.op('pe', lambda e, i=i, j=j: e.matmul(py[:, i * 32:(i + 1) * 32], lhsT=h_b[:, j, 0, :], rhs=CT_b[:, i, 0, :],
                                                                    start=True, stop=False), R=['h_b', 'CT_b'], W=[pyk])
                            T.op('pe', lambda e, i=i, j=j: e.matmul(py[:, i * 32:(i + 1) * 32], lhsT=h_b[:, j, 1, :], rhs=CT_b[:, i, 1, :],
                                                                    start=False, stop=False), R=['h_b', 'CT_b'], W=[pyk])
                            ct, off = i // 4, (i % 4) * 32
                            T.op('pe', lambda e, i=i, ct=ct, off=off: e.matmul(py[:, i * 32:(i + 1) * 32], lhsT=uT[:, ct, :],
                                                                              rhs=diagD_b[:, ct, off:off + 32], start=False, stop=True),
                                 R=['uT', 'diagD_b'], W=[pyk])
                    T.op('act', lambda e: e.copy(out=yf[:], in_=py[:, 0:512]), R=[pyk], W=['yf'])
                    T.op('pool', lambda e: e.tensor_tensor(out=yg[:], in0=yf[:], in1=yf[:], op=ALU.mult), R=['yf'], W=['yg'])
                    T.op('pool', lambda e: e.tensor_scalar(out=yg[:], in0=yg[:], scalar1=0.044715, scalar2=1.0, op0=ALU.mult, op1=ALU.add),
                         R=['yg'], W=['yg'])
                    T.op('pool', lambda e: e.tensor_tensor(out=yg[:], in0=yg[:], in1=yf[:], op=ALU.mult), R=['yg', 'yf'], W=['yg'])
                    T.op('act', lambda e: e.activation(out=yg[:], in_=yg[:], func=AF.Sigmoid, scale=GELU_K), R=['yg'], W=['yg'])
                    T.op('dve', lambda e: e.tensor_tensor(out=yg[:], in0=yg[:], in1=yf[:], op=ALU.mult), R=['yg', 'yf'], W=['yg'])
                    T.op('dve', lambda e: e.tensor_copy(out=ob[:, 0:512], in_=yg[:]), R=['yg'], W=['ob'])
                    p, pk = nextT()
                    for k in range(4):
                        T.op('pe', lambda e, k=k, p=p: e.transpose(out=p[:, k * 128:(k + 1) * 128], in_=ob[:, k * 128:(k + 1) * 128],
                                                                   identity=identb[:]), R=['ob', 'identb'], W=[pk])
                    T.op('act', lambda e, p=p: e.copy(out=cqT[:].rearrange("p a b -> p (a b)"), in_=p[:, 0:512]), R=[pk], W=['cqT'])
                    pz, pzk = nextA()
                    for k in range(4):
                        T.op('pe', lambda e, k=k, pz=pz: e.matmul(pz[:, 0:512], lhsT=cqT[:, k, :], rhs=glu_w_b[:, k, :], start=(k == 0), stop=False),
                             R=['cqT', 'glu_w_b'], W=[pzk])
                    T.op('pe', lambda e, pz=pz: e.matmul(pz[:, 0:512], lhsT=onesb[0:1, :], rhs=glub_b[0:1, :], start=False, stop=True),
                         R=['onesb', 'glub_b'], W=[pzk])
                    T.op('act', lambda e, pz=pz: e.activation(out=yf[:], in_=pz[:, 0:512], func=AF.Sigmoid), R=[pzk, 'yf'], W=['yf'])
                    T.op('dve', lambda e: e.tensor_tensor(out=yf[:], in0=yf[:], in1=yg[:], op=ALU.mult), R=['yf', 'yg'], W=['yf'])
                    finish_branch(ti, yf[:], 'yf', 512, 0)

                chk('G1')
                lat_ps = []
                stream_wblock(w_in, C_CKV, 256, gcol_in)
                for ti in range(ng):
                    p, pk = psC[ti], 'psC%d' % ti
                    for kc in range(16):
                        T.op('pe', lambda e, kc=kc, ti=ti, p=p: e.matmul(p[:, 0:256], lhsT=xnT_g[:, ti, kc, :], rhs=wblk[:, kc, 0:256],
                                                                         start=(kc == 0), stop=(kc == 15)), R=['xnT_g', 'wblk'], W=[pk])
                    lat_ps.append((p, pk))
                stream_wblock(w_in, C_KR, 64, gcol_in)
                for ti, (row0, own) in enumerate(tiles):
                    p, pk = lat_ps[ti]
                    for kc in range(16):
                        T.op('pe', lambda e, kc=kc, ti=ti, p=p: e.matmul(p[:, 256:320], lhsT=xnT_g[:, ti, kc, :], rhs=wblk[:, kc, 0:64],
                                                                         start=(kc == 0), stop=(kc == 15)), R=['xnT_g', 'wblk', pk], W=[pk])
                    if is_sample:
                        latent_post(p, pk, row0, ckv_s, kr_s, 0, ckvT_s[:], 'ckvT_s', krT_s[0:64, :], 'krT_s', None, None, None)
                    else:
                        kt = NPREV + own
                        latent_post(p, pk, row0, ckv_p, kr_p, own * 128, ckvT_all[:, :, kt * 128:(kt + 1) * 128], 'ckvT_all',
                                    krT_all[0:64, kt * 128:(kt + 1) * 128], 'krT_all', ckv1_all[:, kt, :], 'ckv1_all', rk_all[:, kt, :])

                chk('G2')
                cq_ps = []
                stream_wblock(w_in, C_CQ, 256, gcol_in)
                for ti in range(ng):
                    p, pk = psC[ti], 'psC%d' % ti
                    for kc in range(16):
                        T.op('pe', lambda e, kc=kc, ti=ti, p=p: e.matmul(p[:, 0:256], lhsT=xnT_g[:, ti, kc, :], rhs=wblk[:, kc, 0:256],
                                                                         start=(kc == 0), stop=(kc == 15)), R=['xnT_g', 'wblk'], W=[pk])
                    cq_ps.append((p, pk))
                stream_wblock(w_in, C_CQ + 256, 256, gcol_in)
                for ti, (row0, own) in enumerate(tiles):
                    p, pk = cq_ps[ti]
                    for kc in range(16):
                        T.op('pe', lambda e, kc=kc, ti=ti, p=p: e.matmul(p[:, 256:512], lhsT=xnT_g[:, ti, kc, :], rhs=wblk[:, kc, 0:256],
                                                                         start=(kc == 0), stop=(kc == 15)), R=['xnT_g', 'wblk', pk], W=[pk])
                    T.op('act', lambda e, p=p: e.copy(out=yf[:], in_=p[:, 0:512]), R=[pk], W=['yf'])
                    q_path(ti, row0)

                chk('G3')
                for b4 in range(4):
                    in_proj_block(ng, C_GM + b4 * 256, 256, c_gate(512 + b4 * 256))

                def c_qm(half):
                    def f(ti, p, pk):
                        for hh in range(2):
                            h = half * 2 + hh
                            rstd_of(p[:, hh * 128:(hh + 1) * 128], 128, sm[:, 6:7], pk)
                            T.op('dve', lambda e, hh=hh, h=h: e.scalar_tensor_tensor(out=cq_b[:, h * 128:(h + 1) * 128],
                                                                                     in0=p[:, hh * 128:(hh + 1) * 128], scalar=sm[:, 6:7],
                                                                                     in1=gmq_bc[:], op0=ALU.mult, op1=ALU.mult),
                                 R=[pk, 'sm', 'gmq_bc', 'cq_b'], W=['cq_b'])
                        pt_, ptk = nextT()
                        for hh in range(2):
                            h = half * 2 + hh
                            T.op('pe', lambda e, hh=hh, h=h, pt_=pt_: e.transpose(out=pt_[:, hh * 128:(hh + 1) * 128], in_=cq_b[:, h * 128:(h + 1) * 128],
                                                                                  identity=identb[:]), R=['cq_b', 'identb'], W=[ptk])
                        T.op('act', lambda e, pt_=pt_: e.copy(out=QmT[:, half * 2:half * 2 + 2, ti * 128:(ti + 1) * 128],
                                                              in_=pt_[:, 0:256].rearrange("p (a b) -> p a b", a=2)), R=[ptk, 'QmT'], W=['QmT'])
                    return f
                in_proj_block(ng, C_QM, 256, c_qm(0))
                in_proj_block(ng, C_QM + 256, 256, c_qm(1))
                in_proj_block(ng, C_GME, 256, c_gate(1536))
                in_proj_block(ng, C_GME + 256, 256, c_gate(1792))

                chk('G4')
                T.op('dve', lambda e: e.memset(ssq[:], 0.0), W=['ssq'])
                if not is_sample:
                    prompt_attention(ng, ncg, tiles[0][1])
                else:
                    sample_attention()

                chk('G5')
                for ti in range(ng):
                    T.op('dve', lambda e, ti=ti: e.tensor_reduce(out=sm[:, 7:8], in_=ssq[:, ti, 0:8], axis=AX.X, op=ALU.add), R=['ssq', 'sm'], W=['sm'])
                    T.op('act', lambda e: e.activation(out=sm[:, 7:8], in_=sm[:, 7:8], func=AF.Sqrt, scale=1.0 / 1024, bias=EPS), R=['sm'], W=['sm'])
                    T.op('dve', lambda e: e.reciprocal(out=sm[:, 7:8], in_=sm[:, 7:8]), R=['sm'], W=['sm'])
                    T.op('dve', lambda e, ti=ti: e.scalar_tensor_tensor(out=gates[:, ti, 512:1536], in0=ymla[:, ti, :], scalar=sm[:, 7:8],
                                                                        in1=gates[:, ti, 512:1536], op0=ALU.mult, op1=ALU.mult),
                         R=['ymla', 'sm', 'gates'], W=['gates'])
                    for half in range(2):
                        p, pk = nextT()
                        for k in range(8):
                            kc = half * 8 + k
                            T.op('pe', lambda e, k=k, kc=kc, p=p, ti=ti: e.transpose(out=p[:, k * 128:(k + 1) * 128],
                                                                                     in_=gates[:, ti, kc * 128:(kc + 1) * 128], identity=identb[:]),
                                 R=['gates', 'identb'], W=[pk])
                        T.op('act', lambda e, p=p, half=half, ti=ti: e.copy(out=xnT_g[:, ti, half * 8:(half + 1) * 8, :].rearrange("p a b -> p (a b)"),
                                                                           in_=p[:, :]), R=[pk, 'xnT_g'], W=['xnT_g'])
                for cb in range(8):
                    stream_wblock(w_out_d, cb * 256, 256, gcol_out)
                    for ti, (row0, own) in enumerate(tiles):
                        T.dma('sp', 'xres', lambda e, row0=row0, cb=cb: e.dma_start(out=xres[:], in_=xall[row0:row0 + 128, cb * 256:(cb + 1) * 256]),
                              W=['xres'])
                        p, pk = nextA()
                        for kc in range(16):
                            T.op('pe', lambda e, kc=kc, ti=ti, p=p: e.matmul(p[:, 0:256], lhsT=xnT_g[:, ti, kc, :], rhs=wblk[:, kc, 0:256],
                                                                             start=(kc == 0), stop=(kc == 15)), R=['xnT_g', 'wblk'], W=[pk])
                        T.op('dve', lambda e, p=p: e.tensor_tensor(out=yout[:], in0=p[:, 0:256], in1=xres[:], op=ALU.add), R=[pk, 'xres'], W=['yout'])
                        dst = y_s if is_sample else y_p
                        r0 = 0 if is_sample else own * 128
                        T.dma('sp', 'ost', lambda e, dst=dst, r0=r0, cb=cb: e.dma_start(out=dst[r0:r0 + 128, cb * 256:(cb + 1) * 256], in_=yout[:]),
                              R=['yout'])

            own_tiles = [((NPREV + o) * 128, o) for o in range(NOWN)]
            for g0 in range(0, NOWN, GT):
                run_group(own_tiles[g0:g0 + GT], False)
            stf = sb("stf", [16, 128])
            for src, dst in ((hre, sp_re), (him, sp_im)):
                transpose_f32(stf[:], src[:, :, 0], 128, 16, 'stf', 'hst')
                T.dma('sp', 'ost', lambda e, dst=dst: e.dma_start(out=dst[:, :], in_=stf[:]), R=['stf'])
            chk('O')
            T.barrier()
            stin = xt[0:NSEQ, :]
            for src_d, dstt in ((st_re, hre), (st_im, him)):
                T.dma('sp', 'c0', lambda e, src_d=src_d: e.dma_start(out=stin, in_=src_d[:, :]), W=['xt'])
                for i in range(16):
                    transpose_f32(dstt[:, i, :], stin[:, i * 128:(i + 1) * 128], NSEQ, 128, 'hst', 'xt')
            run_group([(NKT * 128, None)], True)
            for src, dst in ((hre, ss_re), (him, ss_im)):
                for i in range(16):
                    transpose_f32(stin[:, i * 128:(i + 1) * 128], src[:, i, :], 128, NSEQ, 'xt', 'hst')
                T.dma('sp', 'ost', lambda e, dst=dst: e.dma_start(out=dst[:, :], in_=stin), R=['xt'])
        except _Stop:
            pass
        T.finish('sp')
        print("[kernel] instructions emitted:", T.nins, "sbuf left:", nc.sbuf_bytes_remaining, {k: v for k, v in T.cnt.items() if k in ("pe", "act", "dve", "pool")}, flush=True)
    return nc


def rope_table(pos):
    half = 32
    inv = (10000.0 ** (-np.arange(half, dtype=np.float32) / half)).astype(np.float32)
    ang = pos.astype(np.float32)[:, None] * inv[None, :]
    c, s = np.cos(ang).astype(np.float32), np.sin(ang).astype(np.float32)
    return np.concatenate([c, c, -s, s], axis=1).astype(np.float32)


WNAMES = ['norm_g', 'w_in', 'ssm_a_re', 'ssm_a_im', 'ssm_log_dt', 'ssm_b_re', 'ssm_b_im', 'ssm_c_re', 'ssm_c_im', 'ssm_d',
          'ssm_glu_w', 'ssm_glu_b', 'mla_q_norm_g', 'mla_w_uq', 'mla_kv_norm_g', 'mla_w_ukv', 'mla_qk_norm_q', 'mla_qk_norm_k',
          'mem_norm_g', 'mem_w_k', 'mem_w_v', 'mem_qk_norm_q', 'mem_qk_norm_k', 'out_norm_ssm', 'out_norm_mla', 'out_norm_mem',
          'w_out']


def make_in_maps(inp, SEQ, NPG, NPOOL, PAST):
    CH = SEQ // 4
    NOWN = CH // 128
    NPREV = 3 * NOWN
    NKT = NPREV + NOWN
    f32 = np.float32
    xp = np.asarray(inp['x_prompt'], f32); xs = np.asarray(inp['x_sample'], f32)
    ident = np.eye(128, dtype=f32)
    tri = np.triu(np.ones((128, 128), f32))
    smask = np.ones((128, 256), f32); smask[:, 0] = 0.0
    smask[:, 128::8] = 0.0
    tau = np.tile(np.arange(1, LS + 1, dtype=f32)[None, :], (128, 1))
    cckv = np.ascontiguousarray(np.asarray(inp['cache_ckv'], f32).reshape(NPOOL * 128, 256))
    ckr = np.ascontiguousarray(np.asarray(inp['cache_krope'], f32).reshape(NPOOL * 128, 64))
    wts = {n: np.ascontiguousarray(np.asarray(inp[n], f32)) for n in WNAMES}
    maps = []
    for c in range(8):
        s, j = c // 4, c % 4
        xall = np.zeros(((NKT + 1) * 128, D), f32)
        pos = np.zeros((NKT + 1) * 128, f32)
        kb = np.zeros((128, NKT), f32)
        for k in range(3):
            src = j - 3 + k
            r0 = k * CH
            if src >= 0:
                xall[r0:r0 + CH] = xp[s, src * CH:(src + 1) * CH]
                pos[r0:r0 + CH] = np.arange(src * CH, (src + 1) * CH)
            else:
                kb[:, k * NOWN:(k + 1) * NOWN] = NEG
        xall[3 * CH:4 * CH] = xp[s, j * CH:(j + 1) * CH]
        pos[3 * CH:4 * CH] = np.arange(j * CH, (j + 1) * CH)
        xall[4 * CH:] = xs[c * NSEQ:(c + 1) * NSEQ].reshape(128, D)
        pos[4 * CH:] = np.tile(PAST + np.arange(8), NSEQ)
        m = {
            'xall': xall, 'rope': rope_table(pos), 'kbias': kb,
            'mem': np.ascontiguousarray(np.asarray(inp['mem_prompt'], f32)[s]),
            'cckv': cckv, 'ckr': ckr,
            'cmk': np.ascontiguousarray(np.asarray(inp['cache_mem_k'], f32)[c * NSEQ:(c + 1) * NSEQ].reshape(NSEQ * 256, 512)),
            'cmv': np.ascontiguousarray(np.asarray(inp['cache_mem_v'], f32)[c * NSEQ:(c + 1) * NSEQ].reshape(NSEQ * 256, 512)),
            'st_re': np.ascontiguousarray(np.asarray(inp['state_ssm_re'], f32)[c * NSEQ:(c + 1) * NSEQ].reshape(NSEQ, 2048)),
            'st_im': np.ascontiguousarray(np.asarray(inp['state_ssm_im'], f32)[c * NSEQ:(c + 1) * NSEQ].reshape(NSEQ, 2048)),
            'ptab': np.ascontiguousarray(np.asarray(inp['page_table'], np.int32)[c * NSEQ:(c + 1) * NSEQ]),
            'ident': ident, 'tri': tri, 'smask': smask, 'tau': tau,
        }
        m.update(wts)
        maps.append(m)
    return maps


def assemble(r, SEQ):
    y_prompt = np.stack([np.concatenate([r[s * 4 + j]['y_p'] for j in range(4)], 0) for s in range(2)])
    ckv_p = np.stack([np.concatenate([r[s * 4 + j]['ckv_p'] for j in range(4)], 0) for s in range(2)])
    kr_p = np.stack([np.concatenate([r[s * 4 + j]['kr_p'] for j in range(4)], 0) for s in range(2)])
    y_sample = np.concatenate([r[c]['y_s'].reshape(NSEQ, 8, D) for c in range(8)], 0)
    ckv_s = np.concatenate([r[c]['ckv_s'].reshape(NSEQ, 8, 256) for c in range(8)], 0)
    kr_s = np.concatenate([r[c]['kr_s'].reshape(NSEQ, 8, 64) for c in range(8)], 0)
    memk = np.stack([r[s * 4]['memk'].reshape(256, 4, 128) for s in range(2)])
    memv = np.stack([r[s * 4]['memv'].reshape(256, 4, 128) for s in range(2)])
    sp_re = np.stack([r[s * 4 + 3]['sp_re'].reshape(32, 64) for s in range(2)])
    sp_im = np.stack([r[s * 4 + 3]['sp_im'].reshape(32, 64) for s in range(2)])
    ss_re = np.concatenate([r[c]['ss_re'].reshape(NSEQ, 32, 64) for c in range(8)], 0)
    ss_im = np.concatenate([r[c]['ss_im'].reshape(NSEQ, 32, 64) for c in range(8)], 0)
    outs = (y_prompt, y_sample, ckv_p, kr_p, ckv_s, kr_s, memk, memv, sp_re, sp_im, ss_re, ss_im)
    return tuple(np.ascontiguousarray(o, dtype=np.float32) for o in outs)


def run(inp, SEQ, PAST, stop=None, cores=None):
    NPG = PAST // 128
    NPOOL = int(np.asarray(inp['cache_ckv']).shape[0])
    nc = build(SEQ, NPG, NPOOL, stop)
    maps = make_in_maps(inp, SEQ, NPG, NPOOL, PAST)
    if cores is not None:
        res = run_bass_kernel_spmd(nc, [maps[c] for c in cores], core_ids=list(range(len(cores))))
        return {c: res.results[i] for i, c in enumerate(cores)}
    res = run_bass_kernel_spmd(nc, maps, core_ids=list(range(8)))
    return assemble(res.results, SEQ)


def kernel(**inputs):
    return run(inputs, 4096, 8192)
```

```python
import contextlib
import numpy as np
import concourse.bass as bass
import concourse.mybir as mybir
from concourse.bass_utils import run_bass_kernel_spmd

F32 = mybir.dt.float32
BF16 = mybir.dt.bfloat16
I32 = mybir.dt.int32
AF = mybir.ActivationFunctionType
ALU = mybir.AluOpType
AX = mybir.AxisListType

D = 2048
IN_W = 3904
EPS = 1e-6
NEG = -30000.0
NSEQ = 16
LS = 64
GT = 2
GC = GT * 128
C_U, C_GS, C_CQ, C_CKV, C_KR, C_GM, C_QM, C_GME = 0, 512, 1024, 1536, 1792, 1856, 2880, 3392
GELU_K = 2.0 * 0.7978845608028654


class Tracker:
    def __init__(self, nc, es):
        self.nc = nc
        self.es = es
        self.eng = {'pe': nc.tensor, 'act': nc.scalar, 'dve': nc.vector, 'pool': nc.gpsimd, 'sp': nc.sync}
        self.sem = {}
        self.cnt = {}
        for n in ['pe', 'act', 'dve', 'pool']:
            self.sem[n] = es.enter_context(nc.semaphore('c_' + n))
            self.cnt[n] = 0
        self.seen = {n: {} for n in self.eng}
        self.lastw = {}
        self.readers = {}
        self.nins = 0

    def _waits(self, e, reads, writes):
        need = {}

        def add(t):
            if t is None:
                return
            sn, v = t
            if need.get(sn, 0) < v:
                need[sn] = v
        for k in reads:
            add(self.lastw.get(k))
            if k.startswith('ps'):
                for t in self.readers.get(k, ()):
                    if t[0] != e:
                        add(t)
        for k in writes:
            add(self.lastw.get(k))
            for t in self.readers.get(k, ()):
                add(t)
        for sn, v in need.items():
            if sn == 'pe' and e == 'pe':
                continue
            if self.seen[e].get(sn, 0) < v:
                self.eng[e].wait_ge(self.sem[sn], v)
                self.seen[e][sn] = v
                self.nins += 1

    def _commit(self, tok, reads, writes):
        for k in reads:
            self.readers.setdefault(k, []).append(tok)
        for k in writes:
            self.lastw[k] = tok
            self.readers[k] = []

    def op(self, e, fn, R=(), W=()):
        self._waits(e, R, W)
        ins = fn(self.eng[e])
        self.cnt[e] += 1
        ins.then_inc(self.sem[e], 1)
        self.nins += 1
        self._commit((e, self.cnt[e]), R, W)

    def dma(self, q, stream, fn, R=(), W=()):
        if stream not in self.sem:
            self.sem[stream] = self.es.enter_context(self.nc.semaphore('d_' + stream))
            self.cnt[stream] = 0
        self._waits(q, R, W)
        if self.cnt[stream] and self.seen[q].get(stream, 0) < self.cnt[stream]:
            self.eng[q].wait_ge(self.sem[stream], self.cnt[stream])
            self.seen[q][stream] = self.cnt[stream]
            self.nins += 1
        ins = fn(self.eng[q])
        self.cnt[stream] += 16
        ins.then_inc(self.sem[stream], 16)
        self.nins += 1
        self._commit((stream, self.cnt[stream]), R, W)

    def barrier(self):
        for e in self.eng:
            for sn, v in self.cnt.items():
                if v == 0:
                    continue
                if self.seen[e].get(sn, 0) < v:
                    self.eng[e].wait_ge(self.sem[sn], v)
                    self.seen[e][sn] = v
                    self.nins += 1
        self.lastw = {}
        self.readers = {}

    def finish(self, q='sp'):
        for sn, v in self.cnt.items():
            if v == 0:
                continue
            self.eng[q].wait_ge(self.sem[sn], v)


class _Stop(Exception):
    pass


def build(SEQ, NPG, NPOOL, stop=None):
    CH = SEQ // 4
    NOWN = CH // 128
    NPREV = 3 * NOWN
    NKT = NPREV + NOWN
    NT = NKT + 1
    nc = bass.Bass("TRN2", target_bir_lowering=False)

    def din(name, shape, dt=F32):
        return nc.dram_tensor(name, list(shape), dt, kind="ExternalInput").ap()

    def dout(name, shape):
        return nc.dram_tensor(name, list(shape), F32, kind="ExternalOutput").ap()

    xall = din("xall", [NT * 128, D])
    rope = din("rope", [NT * 128, 128])
    kbias_d = din("kbias", [128, NKT])
    mem_d = din("mem", [256, D])
    cckv = din("cckv", [NPOOL * 128, 256])
    ckr = din("ckr", [NPOOL * 128, 64])
    cmk = din("cmk", [NSEQ * 256, 512])
    cmv = din("cmv", [NSEQ * 256, 512])
    st_re = din("st_re", [NSEQ, 2048])
    st_im = din("st_im", [NSEQ, 2048])
    ptab = din("ptab", [NSEQ, NPG], I32)
    ident_d = din("ident", [128, 128])
    tri_d = din("tri", [128, 128])
    smask_d = din("smask", [128, 256])
    tau_d = din("tau", [128, LS])
    norm_g = din("norm_g", [D]); w_in = din("w_in", [D, IN_W])
    a_re_d = din("ssm_a_re", [32, 64]); a_im_d = din("ssm_a_im", [32, 64]); ldt_d = din("ssm_log_dt", [32])
    b_re_d = din("ssm_b_re", [32, 64, 16]); b_im_d = din("ssm_b_im", [32, 64, 16])
    c_re_d = din("ssm_c_re", [32, 16, 64]); c_im_d = din("ssm_c_im", [32, 16, 64])
    ssm_d_d = din("ssm_d", [512]); glu_w_d = din("ssm_glu_w", [512, 512]); glu_b_d = din("ssm_glu_b", [512])
    gq_d = din("mla_q_norm_g", [512]); w_uq_d = din("mla_w_uq", [512, 1536])
    gkv_d = din("mla_kv_norm_g", [256]); w_ukv_d = din("mla_w_ukv", [256, 2048])
    gqq_d = din("mla_qk_norm_q", [192]); gqk_d = din("mla_qk_norm_k", [192])
    gmem_d = din("mem_norm_g", [D]); w_mk_d = din("mem_w_k", [D, 512]); w_mv_d = din("mem_w_v", [D, 512])
    gmq_d = din("mem_qk_norm_q", [128]); gmk_d = din("mem_qk_norm_k", [128])
    go_s_d = din("out_norm_ssm", [512]); go_m_d = din("out_norm_mla", [1024]); go_e_d = din("out_norm_mem", [512])
    w_out_d = din("w_out", [D, D])

    y_p = dout("y_p", [CH, D]); y_s = dout("y_s", [128, D])
    ckv_p = dout("ckv_p", [CH, 256]); kr_p = dout("kr_p", [CH, 64])
    ckv_s = dout("ckv_s", [128, 256]); kr_s = dout("kr_s", [128, 64])
    memk_o = dout("memk", [256, 512]); memv_o = dout("memv", [256, 512])
    sp_re = dout("sp_re", [16, 128]); sp_im = dout("sp_im", [16, 128])
    ss_re = dout("ss_re", [NSEQ, 2048]); ss_im = dout("ss_im", [NSEQ, 2048])

    es = contextlib.ExitStack()
    with es:
        T = Tracker(nc, es)

        def chk(name):
            if stop == name:
                raise _Stop()

        import os as _os
        DBG = _os.environ.get('KDBG')

        def dbg(name, ap, shape, keys, dt=F32):
            if not DBG:
                return
            d = nc.dram_tensor("dbg_" + name, list(shape), dt, kind="ExternalOutput").ap()
            T.dma('sp', 'dbg', lambda e: e.dma_start(out=d, in_=ap, allow_slow_non_contiguous=True), R=keys)

        try:
            def sb(name, shape, dt=F32):
                return es.enter_context(nc.sbuf_tensor("s_" + name, list(shape), dt))

            def ps(name, shape, dt=F32):
                return es.enter_context(nc.psum_tensor("p_" + name, list(shape), dt))

            psT = [ps("psT%d" % i, [128, 1024], BF16) for i in range(2)]
            psA = [ps("psA%d" % i, [128, 512], F32) for i in range(3)]
            psC = [ps("psC%d" % i, [128, 512], F32) for i in range(3)]
            rr = {'T': 0, 'A': 0}

            def nextT():
                i = rr['T']; rr['T'] = (i + 1) % 2
                return psT[i], 'psT%d' % i

            def nextA():
                i = rr['A']; rr['A'] = (i + 1) % 3
                return psA[i], 'psA%d' % i

            identf = sb("identf", [128, 128]); identb = sb("identb", [128, 128], BF16)
            trib = sb("trib", [128, 128], BF16); onesb = sb("onesb", [128, 128], BF16)
            smask = sb("smask", [128, 256]); taur = sb("taur", [128, LS])
            stg = sb("stg", [128, 1024])
            T.dma('sp', 'c0', lambda e: e.dma_start(out=identf[:], in_=ident_d[:, :]), W=['identf'])
            T.dma('sp', 'c0', lambda e: e.dma_start(out=stg[:, 0:128], in_=tri_d[:, :]), W=['stg'])
            T.dma('sp', 'c0', lambda e: e.dma_start(out=smask[:], in_=smask_d[:, :]), W=['smask'])
            T.dma('sp', 'c0', lambda e: e.dma_start(out=taur[:], in_=tau_d[:, :]), W=['taur'])
            T.op('dve', lambda e: e.tensor_copy(out=identb[:], in_=identf[:]), R=['identf'], W=['identb'])
            T.op('dve', lambda e: e.tensor_copy(out=trib[:], in_=stg[:, 0:128]), R=['stg'], W=['trib'])
            T.op('dve', lambda e: e.memset(onesb[:], 1.0), W=['onesb'])

            def transpose_f32(dst_ap, src_ap, npart, nfree, dkey, skey, scale=None):
                p, pk = nextA()
                T.op('pe', lambda e: e.transpose(out=p[0:nfree, 0:npart], in_=src_ap, identity=identf[0:npart, 0:npart]),
                     R=[skey, 'identf'], W=[pk])
                if scale is None:
                    T.op('dve', lambda e: e.tensor_copy(out=dst_ap, in_=p[0:nfree, 0:npart]), R=[pk], W=[dkey])
                else:
                    T.op('dve', lambda e: e.tensor_scalar(out=dst_ap, in0=p[0:nfree, 0:npart], scalar1=scale, scalar2=None,
                                                          op0=ALU.mult), R=[pk], W=[dkey])

            def load_col_into(dst_ap, dkey, vec_d, n):
                k = n // 128
                T.dma('sp', 'c0', lambda e: e.dma_start(out=stg[0:k, 0:128], in_=vec_d.rearrange("(k p) -> k p", p=128)),
                      W=['stg'])
                transpose_f32(dst_ap, stg[0:k, 0:128], k, 128, dkey, 'stg')

            def load_col(name, vec_d, n):
                t = sb(name, [128, n // 128])
                load_col_into(t[:], name, vec_d, n)
                return t

            def load_bc(name, vec_d, n):
                t = sb(name, [128, n])
                T.dma('sp', 'c0', lambda e: e.dma_start(out=t[:], in_=vec_d.partition_broadcast(128)), W=[name])
                return t

            gcol_in = load_col("gcol_in", norm_g, D)
            gcol_q = load_col("gcol_q", gq_d, 512)
            gcol_mem = load_col("gcol_mem", gmem_d, D)
            gcol_out = sb("gcol_out", [128, 16])
            load_col_into(gcol_out[:, 0:4], 'gcol_out', go_s_d, 512)
            load_col_into(gcol_out[:, 4:12], 'gcol_out', go_m_d, 1024)
            load_col_into(gcol_out[:, 12:16], 'gcol_out', go_e_d, 512)
            dcol = load_col("dcol", ssm_d_d, 512)
            diagD_b = sb("diagD_b", [128, 4, 128], BF16)
            for ct in range(4):
                T.op('dve', lambda e, ct=ct: e.tensor_scalar(out=diagD_b[:, ct, :], in0=identf[:], scalar1=dcol[:, ct:ct + 1],
                                                             scalar2=None, op0=ALU.mult), R=['identf', 'dcol'], W=['diagD_b'])
            gkv_bc = load_bc("gkv_bc", gkv_d, 256)
            gmk_bc = load_bc("gmk_bc", gmk_d, 128)
            gqk_bc = load_bc("gqk_bc", gqq_d, 192)
            T.dma('sp', 'c0', lambda e: e.dma_start(out=stg[:, 0:192], in_=gqk_d.partition_broadcast(128)), W=['stg'])
            T.op('dve', lambda e: e.scalar_tensor_tensor(out=gqk_bc[:], in0=gqk_bc[:], scalar=192.0 ** -0.5, in1=stg[:, 0:192],
                                                         op0=ALU.mult, op1=ALU.mult), R=['gqk_bc', 'stg'], W=['gqk_bc'])
            gmq_bc = load_bc("gmq_bc", gmq_d, 128)
            T.op('dve', lambda e: e.tensor_scalar(out=gmq_bc[:], in0=gmq_bc[:], scalar1=128.0 ** -0.5, scalar2=None,
                                                  op0=ALU.mult), R=['gmq_bc'], W=['gmq_bc'])
            glub_b = sb("glub_b", [1, 512], BF16)
            T.dma('sp', 'c0', lambda e: e.dma_start(out=stg[0:1, 0:512], in_=glu_b_d.rearrange("(o n) -> o n", o=1)),
                  W=['stg'])
            T.op('dve', lambda e: e.tensor_copy(out=glub_b[:], in_=stg[0:1, 0:512]), R=['stg'], W=['glub_b'])

            def load_w_rows(dst, dkey, w_d, kc_n, c0, ncols, gcol, dcol0=0):
                for kc in range(kc_n):
                    for cc0 in range(0, ncols, 1024):
                        n = min(1024, ncols - cc0)
                        T.dma('sp', 'wst', lambda e, kc=kc, cc0=cc0, n=n: e.dma_start(
                            out=stg[:, 0:n], in_=w_d[kc * 128:(kc + 1) * 128, c0 + cc0:c0 + cc0 + n]), W=['stg'])
                        if gcol is None:
                            T.op('dve', lambda e, kc=kc, cc0=cc0, n=n: e.tensor_copy(out=dst[:, kc, dcol0 + cc0:dcol0 + cc0 + n],
                                                                                     in_=stg[:, 0:n]), R=['stg'], W=[dkey])
                        else:
                            T.op('dve', lambda e, kc=kc, cc0=cc0, n=n: e.tensor_scalar(out=dst[:, kc, dcol0 + cc0:dcol0 + cc0 + n],
                                                                                       in0=stg[:, 0:n], scalar1=gcol[:, kc:kc + 1],
                                                                                       scalar2=None, op0=ALU.mult), R=['stg'], W=[dkey])

            w_uq_b = sb("w_uq_b", [128, 4, 1536], BF16)
            load_w_rows(w_uq_b, 'w_uq_b', w_uq_d, 4, 0, 1536, gcol_q)
            glu_w_b = sb("glu_w_b", [128, 4, 512], BF16)
            load_w_rows(glu_w_b, 'glu_w_b', glu_w_d, 4, 0, 512, None)
            w_uk_b = sb("w_uk_b", [128, 2, 1024], BF16)
            w_uv_b = sb("w_uv_b", [128, 2, 1024], BF16)
            for kc in range(2):
                for hf in range(2):
                    T.dma('sp', 'wst', lambda e, kc=kc, hf=hf: e.dma_start(
                        out=stg[:, 0:1024], in_=w_ukv_d[kc * 128:(kc + 1) * 128, hf * 1024:(hf + 1) * 1024]), W=['stg'])
                    sv = stg[:, 0:1024].rearrange("p (h c) -> p h c", c=256)
                    T.op('dve', lambda e, kc=kc, hf=hf, sv=sv: e.tensor_copy(
                        out=w_uk_b[:, kc, hf * 512:(hf + 1) * 512].rearrange("p (h c) -> p h c", c=128), in_=sv[:, :, 0:128]),
                        R=['stg'], W=['w_uk_b'])
                    T.op('dve', lambda e, kc=kc, hf=hf, sv=sv: e.tensor_copy(
                        out=w_uv_b[:, kc, hf * 512:(hf + 1) * 512].rearrange("p (h c) -> p h c", c=128), in_=sv[:, :, 128:256]),
                        R=['stg'], W=['w_uv_b'])
            w_ukT_b = sb("w_ukT_b", [128, 8, 256], BF16)
            for h in range(8):
                p, pk = nextT()
                for kc in range(2):
                    T.op('pe', lambda e, h=h, kc=kc, p=p: e.transpose(out=p[:, kc * 128:(kc + 1) * 128],
                                                                      in_=w_uk_b[:, kc, h * 128:(h + 1) * 128],
                                                                      identity=identb[:]), R=['w_uk_b', 'identb'], W=[pk])
                T.op('act', lambda e, h=h, p=p: e.copy(out=w_ukT_b[:, h, :], in_=p[:, 0:256]), R=[pk], W=['w_ukT_b'])

            chk('w')
            arena = sb("arena", [128, 15360], BF16)
            w_up_b = arena[:, 0:13312].rearrange("p (k c) -> p k c", k=16)
            xnT1 = arena[:, 13312:15360].rearrange("p (k c) -> p k c", k=16)
            memT = arena[:, 8192:12288].rearrange("p (t k c) -> p t k c", t=2, k=16)
            gates = arena[:, 0:GT * 2048].rearrange("p (t c) -> p t c", t=GT)
            QTn = arena[:, 4096:4096 + 8 * GC].rearrange("p (h c) -> p h c", h=8)
            QTr = arena[:, 6144:6144 + 8 * GC].rearrange("p (h c) -> p h c", h=8)
            xnT_g = arena[:, 8192:8192 + GT * 2048].rearrange("p (t k c) -> p t k c", t=GT, k=16)
            ymla = arena[:, 12288:12288 + GT * 1024].rearrange("p (t c) -> p t c", t=GT)
            QmT = arena[:, 14336:14336 + 4 * GC].rearrange("p (h c) -> p h c", h=4)

            load_w_rows(w_up_b, 'w_up_b', w_in, 16, C_U, 512, gcol_in, 0)
            load_w_rows(w_up_b, 'w_up_b', w_in, 16, C_CKV, 320, gcol_in, 512)

            xt = sb("xt", [128, D])
            are = sb("are", [128, 16]); aim = sb("aim", [128, 16]); dts = sb("dts", [128, 16])
            with nc.allow_non_contiguous_dma(reason="small one-time parameter loads"):
                T.dma('sp', 'c0', lambda e: e.dma_start(out=are[:], in_=a_re_d.rearrange("(t two) n -> (two n) t", two=2)),
                      W=['are'])
                T.dma('sp', 'c0', lambda e: e.dma_start(out=aim[:], in_=a_im_d.rearrange("(t two) n -> (two n) t", two=2)),
                      W=['aim'])
                lv = ldt_d.rearrange("(t two) -> two t", two=2)
                for hh in range(2):
                    T.dma('sp', 'c0', lambda e, hh=hh: e.dma_start(out=dts[hh * 64:(hh + 1) * 64, :],
                                                                   in_=lv[hh].partition_broadcast(64)), W=['dts'])
            T.op('act', lambda e: e.activation(out=dts[:], in_=dts[:], func=AF.Exp), R=['dts'], W=['dts'])
            theta = sb("theta", [128, 16]); rmag = sb("rmag", [128, 16])
            T.op('dve', lambda e: e.tensor_tensor(out=theta[:], in0=dts[:], in1=aim[:], op=ALU.mult), R=['dts', 'aim'], W=['theta'])
            T.op('dve', lambda e: e.tensor_tensor(out=rmag[:], in0=dts[:], in1=are[:], op=ALU.mult), R=['dts', 'are'], W=['rmag'])
            T.op('act', lambda e: e.activation(out=rmag[:], in_=rmag[:], func=AF.Exp), R=['rmag'], W=['rmag'])
            costab = sb("costab", [128, 16, LS]); sintab = sb("sintab", [128, 16, LS])
            angw = xt[:, 0:4 * LS]; angk = sb("angk", [128, 4 * LS], I32); angm = xt[:, 256:256 + 4 * LS]
            TAB = ['sintab', 'costab']

            def sin_table(dst, dkey, phase):
                for q4 in range(4):
                    for j in range(4):
                        i = q4 * 4 + j
                        T.op('dve', lambda e, i=i, j=j: e.tensor_scalar(out=angw[:, j * LS:(j + 1) * LS], in0=taur[:],
                                                                        scalar1=theta[:, i:i + 1], scalar2=None, op0=ALU.mult),
                             R=['taur', 'theta', 'angw'], W=['angw'])
                    T.op('dve', lambda e: e.tensor_scalar(out=angw[:], in0=angw[:], scalar1=1.0 / (2 * np.pi), scalar2=phase,
                                                          op0=ALU.mult, op1=ALU.add), R=['angw'], W=['angw'])
                    T.op('dve', lambda e: e.tensor_copy(out=angk[:], in_=angw[:]), R=['angw'], W=['angk'])
                    T.op('dve', lambda e: e.tensor_copy(out=angm[:], in_=angk[:]), R=['angk'], W=['angm'])
                    T.op('dve', lambda e: e.tensor_tensor(out=angw[:], in0=angw[:], in1=angm[:], op=ALU.subtract),
                         R=['angw', 'angm'], W=['angw'])
                    T.op('dve', lambda e: e.tensor_scalar(out=angm[:], in0=angw[:], scalar1=0.5, scalar2=None, op0=ALU.is_gt),
                         R=['angw'], W=['angm'])
                    T.op('dve', lambda e: e.tensor_tensor(out=angw[:], in0=angw[:], in1=angm[:], op=ALU.subtract),
                         R=['angw', 'angm'], W=['angw'])
                    T.op('dve', lambda e: e.tensor_scalar(out=angm[:], in0=angw[:], scalar1=-0.5, scalar2=None, op0=ALU.is_lt),
                         R=['angw'], W=['angm'])
                    T.op('dve', lambda e: e.tensor_tensor(out=angw[:], in0=angw[:], in1=angm[:], op=ALU.add),
                         R=['angw', 'angm'], W=['angw'])
                    T.op('act', lambda e, q4=q4: e.activation(out=dst[:, q4 * 4:(q4 + 1) * 4, :].rearrange("p a b -> p (a b)"), in_=angw[:],
                                                              func=AF.Sin, scale=2 * np.pi), R=['angw'], W=[dkey])

            sin_table(sintab, 'sintab', 0.0)
            sin_table(costab, 'costab', 0.25)
            s5t = xt[:, 512:640].rearrange("p (a b) -> p a b", a=8)
            abr, abi, den, fre, fim, nfim, t0_, t1_ = [s5t[:, i, :] for i in range(8)]
            S5 = ['s5t']
            T.op('dve', lambda e: e.tensor_tensor(out=abr, in0=rmag[:], in1=costab[:, :, 0], op=ALU.mult), R=['rmag'] + TAB, W=S5)
            T.op('dve', lambda e: e.tensor_tensor(out=abi, in0=rmag[:], in1=sintab[:, :, 0], op=ALU.mult), R=['rmag'] + TAB + S5, W=S5)
            T.op('dve', lambda e: e.tensor_tensor(out=den, in0=are[:], in1=are[:], op=ALU.mult), R=['are'] + S5, W=S5)
            T.op('dve', lambda e: e.tensor_tensor(out=t0_, in0=aim[:], in1=aim[:], op=ALU.mult), R=['aim'] + S5, W=S5)
            T.op('dve', lambda e: e.tensor_tensor(out=den, in0=den, in1=t0_, op=ALU.add), R=S5, W=S5)
            T.op('dve', lambda e: e.reciprocal(out=den, in_=den), R=S5, W=S5)
            T.op('dve', lambda e: e.tensor_scalar(out=t1_, in0=abr, scalar1=-1.0, scalar2=None, op0=ALU.add), R=S5, W=S5)
            T.op('dve', lambda e: e.tensor_tensor(out=fre, in0=t1_, in1=are[:], op=ALU.mult), R=S5 + ['are'], W=S5)
            T.op('dve', lambda e: e.tensor_tensor(out=t0_, in0=abi, in1=aim[:], op=ALU.mult), R=S5 + ['aim'], W=S5)
            T.op('dve', lambda e: e.tensor_tensor(out=fre, in0=fre, in1=t0_, op=ALU.add), R=S5, W=S5)
            T.op('dve', lambda e: e.tensor_tensor(out=fre, in0=fre, in1=den, op=ALU.mult), R=S5, W=S5)
            T.op('dve', lambda e: e.tensor_tensor(out=fim, in0=abi, in1=are[:], op=ALU.mult), R=S5 + ['are'], W=S5)
            T.op('dve', lambda e: e.tensor_tensor(out=t0_, in0=t1_, in1=aim[:], op=ALU.mult), R=S5 + ['aim'], W=S5)
            T.op('dve', lambda e: e.tensor_tensor(out=fim, in0=fim, in1=t0_, op=ALU.subtract), R=S5, W=S5)
            T.op('dve', lambda e: e.tensor_tensor(out=fim, in0=fim, in1=den, op=ALU.mult), R=S5, W=S5)
            T.op('dve', lambda e: e.tensor_scalar(out=nfim, in0=fim, scalar1=-1.0, scalar2=None, op0=ALU.mult), R=S5, W=S5)
            bst = xt[:, 640:1152].rearrange("p (c t q) -> p c t q", c=2, t=16)
            with nc.allow_non_contiguous_dma(reason="small one-time parameter loads"):
                T.dma('sp', 'c0', lambda e: e.dma_start(out=bst[:, 0, :, :],
                                                        in_=b_re_d.rearrange("(t two) n q -> (two n) t q", two=2)), W=['bst'])
                T.dma('sp', 'c0', lambda e: e.dma_start(out=bst[:, 1, :, :],
                                                        in_=b_im_d.rearrange("(t two) n q -> (two n) t q", two=2)), W=['bst'])
            BT_b = sb("BT_b", [128, 16, 2, 128], BF16)
            bexp = xt[:, 1152:1408].rearrange("p (c n) -> p c n", c=2); bt1 = xt[:, 1408:1440].rearrange("p (c n) -> p c n", c=2)
            for i in range(16):
                ga, gb = (2 * i) % 8, (2 * i + 1) % 8
                T.op('dve', lambda e: e.memset(bexp[:], 0.0), W=['bexp'])
                T.op('dve', lambda e, i=i: e.tensor_scalar(out=bt1[:, 0, :], in0=bst[:, 0, i, :], scalar1=s5t[:, 3, i:i + 1],
                                                           scalar2=None, op0=ALU.mult), R=['bst', 's5t'], W=['bt1'])
                T.op('dve', lambda e, i=i: e.scalar_tensor_tensor(out=bt1[:, 0, :], in0=bst[:, 1, i, :], scalar=s5t[:, 5, i:i + 1],
                                                                  in1=bt1[:, 0, :], op0=ALU.mult, op1=ALU.add),
                     R=['bst', 's5t', 'bt1'], W=['bt1'])
                T.op('dve', lambda e, i=i: e.tensor_scalar(out=bt1[:, 1, :], in0=bst[:, 1, i, :], scalar1=s5t[:, 3, i:i + 1],
                                                           scalar2=None, op0=ALU.mult), R=['bst', 's5t', 'bt1'], W=['bt1'])
                T.op('dve', lambda e, i=i: e.scalar_tensor_tensor(out=bt1[:, 1, :], in0=bst[:, 0, i, :], scalar=s5t[:, 4, i:i + 1],
                                                                  in1=bt1[:, 1, :], op0=ALU.mult, op1=ALU.add),
                     R=['bst', 's5t', 'bt1'], W=['bt1'])
                for c in range(2):
                    T.op('dve', lambda e, c=c, ga=ga: e.tensor_copy(out=bexp[0:64, c, ga * 16:ga * 16 + 16], in_=bt1[0:64, c, :]),
                         R=['bt1', 'bexp'], W=['bexp'])
                    T.op('dve', lambda e, c=c, gb=gb: e.tensor_copy(out=bexp[64:128, c, gb * 16:gb * 16 + 16], in_=bt1[64:128, c, :]),
                         R=['bt1', 'bexp'], W=['bexp'])
                for c in range(2):
                    transpose_f32(BT_b[:, i, c, :], bexp[:, c, :], 128, 128, 'BT_b', 'bexp')
            CT_b = sb("CT_b", [128, 16, 2, 32], BF16)
            T.op('dve', lambda e: e.memset(CT_b[:], 0.0), W=['CT_b'])
            cpad = xt[:, 1536:1664]; cT = xt[:, 1664:1792]
            for c, cd in enumerate((c_re_d, c_im_d)):
                cv = cd.rearrange("g p n -> (g p) n")
                for c4 in range(4):
                    T.dma('sp', 'c0', lambda e, c4=c4, cv=cv: e.dma_start(out=cpad[:, 0:64], in_=cv[c4 * 128:(c4 + 1) * 128, :]), W=['cpad'])
                    T.dma('sp', 'c0', lambda e, c4=c4, cv=cv: e.dma_start(out=cpad[:, 64:128], in_=cv[c4 * 128:(c4 + 1) * 128, :]), W=['cpad'])
                    transpose_f32(cT[:], cpad[:], 128, 128, 'cT', 'cpad', scale=(1.0 if c == 0 else -1.0))
                    for k in range(4):
                        i = 4 * c4 + k
                        T.op('dve', lambda e, i=i, k=k, c=c: e.tensor_copy(out=CT_b[0:64, i, c, 0:16],
                                                                          in_=cT[0:64, (2 * k) * 16:(2 * k) * 16 + 16]),
                             R=['cT', 'CT_b'], W=['CT_b'])
                        T.op('dve', lambda e, i=i, k=k, c=c: e.tensor_copy(out=CT_b[64:128, i, c, 16:32],
                                                                          in_=cT[64:128, (2 * k + 1) * 16:(2 * k + 1) * 16 + 16]),
                             R=['cT', 'CT_b'], W=['CT_b'])

            dbg('theta', theta[:], [128, 16], ['theta']); dbg('rmag', rmag[:], [128, 16], ['rmag'])
            dbg('costab', costab[:].rearrange("p a b -> p (a b)"), [128, 16 * LS], ['costab'])
            dbg('sintab', sintab[:].rearrange("p a b -> p (a b)"), [128, 16 * LS], ['sintab'])
            dbg('s5t', xt[:, 512:640], [128, 128], ['s5t'])
            chk('s5')
            hre = sb("hre", [128, 16, NSEQ]); him = sb("him", [128, 16, NSEQ])
            T.op('dve', lambda e: e.memset(hre[:], 0.0), W=['hst'])
            T.op('dve', lambda e: e.memset(him[:], 0.0), R=['hst'], W=['hst'])
            GS = 2
            s5tmp = [sb("s5tmp%d" % k, [128, GS, 128]) for k in range(4)]
            bpr = sb("bpr", [128, GS, 128]); bpi = sb("bpi", [128, GS, 128])
            d0t = sb("d0t", [128, GS, 128])
            h_b = sb("h_b", [128, GS, 2, 128], BF16)
            hend = sb("hend", [128, 4, GS, NSEQ])

            dbg_once = []

            def ssm_step(uT_get, ncols, S, L, col0, want_out, gis=None):
                mcol = 0 if S == 1 else 128
                for gi in (range(16 // GS) if gis is None else gis):
                    p, pk = nextA()
                    pv = p[:, 0:GS * 2 * ncols].rearrange("p (j c n) -> p j c n", j=GS, c=2)
                    for j in range(GS):
                        i = gi * GS + j
                        for c in range(2):
                            T.op('pe', lambda e, i=i, j=j, c=c, pv=pv: e.matmul(pv[:, j, c, :], lhsT=BT_b[:, i, c, :],
                                                                               rhs=uT_get(i // 4), start=True, stop=True),
                                 R=['BT_b', 'uT'], W=[pk])
                    isl = slice(gi * GS, gi * GS + GS)

                    def tab(tb):
                        a = tb[:, isl, 0:L]
                        if S == 1:
                            return a
                        return a.unsqueeze(2).to_broadcast([128, GS, S, L])

                    def v4(t):
                        a = t[:, :, 0:ncols]
                        if S == 1:
                            return a
                        return a.rearrange("p j (s l) -> p j s l", l=L)

                    def pvc(c):
                        a = pv[:, :, c, :]
                        if S == 1:
                            return a
                        return a.rearrange("p j (s l) -> p j s l", l=L)
                    t1, t2, t3, t4 = s5tmp
                    T.op('dve', lambda e: e.tensor_tensor(out=v4(t1), in0=pvc(0), in1=tab(costab), op=ALU.mult), R=[pk] + TAB, W=['s5tmp0'])
                    T.op('dve', lambda e: e.tensor_tensor(out=v4(t2), in0=pvc(1), in1=tab(sintab), op=ALU.mult), R=[pk] + TAB, W=['s5tmp1'])
                    T.op('dve', lambda e: e.tensor_tensor(out=v4(t3), in0=pvc(1), in1=tab(costab), op=ALU.mult), R=[pk] + TAB, W=['s5tmp2'])
                    T.op('dve', lambda e: e.tensor_tensor(out=v4(t4), in0=pvc(0), in1=tab(sintab), op=ALU.mult), R=[pk] + TAB, W=['s5tmp3'])
                    T.op('pool', lambda e: e.tensor_tensor(out=bpr[:, :, 0:ncols], in0=t1[:, :, 0:ncols], in1=t2[:, :, 0:ncols], op=ALU.add),
                         R=['s5tmp0', 's5tmp1'], W=['bpr'])
                    T.op('pool', lambda e: e.tensor_tensor(out=bpi[:, :, 0:ncols], in0=t3[:, :, 0:ncols], in1=t4[:, :, 0:ncols], op=ALU.subtract),
                         R=['s5tmp2', 's5tmp3'], W=['bpi'])
                    if DBG and not dbg_once and gi == 0:
                        dbg('t1', t1[:].rearrange("p a b -> p (a b)"), [128, 256], ['s5tmp0'])
                        dbg('uT', uT[:].rearrange("p a b -> p (a b)"), [128, 512], ['uT'], BF16)
                        dbg('ubg', u_bg[:, 0, :], [128, 512], ['u_bg'], BF16)
                        dbg('wup', w_up_b[:, 0:2, :].rearrange("p a b -> p (a b)"), [128, 1664], ['w_up_b'], BF16)
                        dbg('gcol', gcol_in[:], [128, 16], ['gcol_in'])
                        dbg('xnT', xnT1[:].rearrange("p a b -> p (a b)"), [128, 2048], ['xnT1'], BF16)
                        dbg('BT', BT_b[:, 0:2, :, :].rearrange("p a b c -> p (a b c)"), [128, 512], ['BT_b'], BF16)
                        dbg('t4', t4[:].rearrange("p a b -> p (a b)"), [128, 256], ['s5tmp3'])
                        dbg('bpr0', bpr[:].rearrange("p a b -> p (a b)"), [128, 256], ['bpr'])
                        dbg('bpi0', bpi[:].rearrange("p a b -> p (a b)"), [128, 256], ['bpi'])
                    if S == 1:
                        for j in range(GS):
                            i = gi * GS + j
                            rb = rmag[:, i:i + 1].to_broadcast([128, ncols])
                            T.op('dve', lambda e, j=j, i=i, rb=rb: e.tensor_tensor_scan(out=bpr[:, j, 0:ncols], data0=rb, data1=bpr[:, j, 0:ncols],
                                                                                       initial=hre[:, i, 0:1], op0=ALU.mult, op1=ALU.add),
                                 R=['rmag', 'bpr', 'hst'], W=['bpr'])
                            T.op('dve', lambda e, j=j, i=i, rb=rb: e.tensor_tensor_scan(out=bpi[:, j, 0:ncols], data0=rb, data1=bpi[:, j, 0:ncols],
                                                                                       initial=him[:, i, 0:1], op0=ALU.mult, op1=ALU.add),
                                 R=['rmag', 'bpi', 'hst'], W=['bpi'])
                    else:
                        for j in range(GS):
                            i = gi * GS + j
                            T.op('dve', lambda e, i=i, j=j: e.tensor_scalar(out=d0t[:, j, 0:ncols], in0=smask[:, mcol:mcol + ncols],
                                                                            scalar1=rmag[:, i:i + 1], scalar2=None, op0=ALU.mult),
                                 R=['smask', 'rmag', 'd0t'], W=['d0t'])
                            fr = bpr[:, j, 0:ncols].rearrange("p (s l) -> p s l", l=L)[:, :, 0]
                            fi = bpi[:, j, 0:ncols].rearrange("p (s l) -> p s l", l=L)[:, :, 0]
                            T.op('dve', lambda e, i=i, fr=fr: e.scalar_tensor_tensor(out=fr, in0=hre[:, i, 0:S], scalar=rmag[:, i:i + 1],
                                                                                     in1=fr, op0=ALU.mult, op1=ALU.add),
                                 R=['hst', 'rmag', 'bpr'], W=['bpr'])
                            T.op('dve', lambda e, i=i, fi=fi: e.scalar_tensor_tensor(out=fi, in0=him[:, i, 0:S], scalar=rmag[:, i:i + 1],
                                                                                     in1=fi, op0=ALU.mult, op1=ALU.add),
                                 R=['hst', 'rmag', 'bpi'], W=['bpi'])
                        for j in range(GS):
                            T.op('dve', lambda e, j=j: e.tensor_tensor_scan(out=bpr[:, j, 0:ncols], data0=d0t[:, j, 0:ncols],
                                                                            data1=bpr[:, j, 0:ncols], initial=0.0,
                                                                            op0=ALU.mult, op1=ALU.add), R=['d0t', 'bpr'], W=['bpr'])
                            T.op('dve', lambda e, j=j: e.tensor_tensor_scan(out=bpi[:, j, 0:ncols], data0=d0t[:, j, 0:ncols],
                                                                            data1=bpi[:, j, 0:ncols], initial=0.0,
                                                                            op0=ALU.mult, op1=ALU.add), R=['d0t', 'bpi'], W=['bpi'])
                    if DBG and not dbg_once and gi == 0:
                        dbg('bpr1', bpr[:].rearrange("p a b -> p (a b)"), [128, 256], ['bpr'])
                        dbg('bpi1', bpi[:].rearrange("p a b -> p (a b)"), [128, 256], ['bpi'])
                        dbg('d0t', d0t[:].rearrange("p a b -> p (a b)"), [128, 256], ['d0t'])
                        dbg_once.append(1)
                    gl_r = bpr[:, :, 0:ncols].rearrange("p j (s l) -> p j s l", l=L)[:, :, :, L - 1]
                    gl_i = bpi[:, :, 0:ncols].rearrange("p j (s l) -> p j s l", l=L)[:, :, :, L - 1]
                    cl = costab[:, isl, L - 1:L].to_broadcast([128, GS, S])
                    sl = sintab[:, isl, L - 1:L].to_broadcast([128, GS, S])
                    e1, e2, e3, e4 = [hend[:, k, :, 0:S] for k in range(4)]
                    T.op('pool', lambda e: e.tensor_tensor(out=e1, in0=gl_r, in1=cl, op=ALU.mult), R=['bpr'] + TAB, W=['hend'])
                    T.op('pool', lambda e: e.tensor_tensor(out=e2, in0=gl_i, in1=sl, op=ALU.mult), R=['bpi', 'hend'] + TAB, W=['hend'])
                    T.op('pool', lambda e: e.tensor_tensor(out=e3, in0=gl_r, in1=sl, op=ALU.mult), R=['bpr', 'hend'] + TAB, W=['hend'])
                    T.op('pool', lambda e: e.tensor_tensor(out=e4, in0=gl_i, in1=cl, op=ALU.mult), R=['bpi', 'hend'] + TAB, W=['hend'])
                    T.op('pool', lambda e: e.tensor_tensor(out=hre[:, isl, 0:S], in0=e1, in1=e2, op=ALU.subtract), R=['hend', 'hst'], W=['hst'])
                    T.op('pool', lambda e: e.tensor_tensor(out=him[:, isl, 0:S], in0=e3, in1=e4, op=ALU.add), R=['hend', 'hst'], W=['hst'])
                    if want_out:
                        T.op('pool', lambda e: e.tensor_tensor(out=v4(t1), in0=v4(bpr), in1=tab(costab), op=ALU.mult), R=['bpr'] + TAB, W=['s5tmp0'])
                        T.op('pool', lambda e: e.tensor_tensor(out=v4(t2), in0=v4(bpi), in1=tab(sintab), op=ALU.mult), R=['bpi'] + TAB, W=['s5tmp1'])
                        T.op('dve', lambda e: e.tensor_tensor(out=v4(t3), in0=v4(bpr), in1=tab(sintab), op=ALU.mult), R=['bpr'] + TAB, W=['s5tmp2'])
                        T.op('dve', lambda e: e.tensor_tensor(out=v4(t4), in0=v4(bpi), in1=tab(costab), op=ALU.mult), R=['bpi'] + TAB, W=['s5tmp3'])
                        T.op('pool', lambda e: e.tensor_tensor(out=h_b[:, :, 0, col0:col0 + ncols], in0=t1[:, :, 0:ncols],
                                                               in1=t2[:, :, 0:ncols], op=ALU.subtract), R=['s5tmp0', 's5tmp1', 'h_b'], W=['h_b'])
                        T.op('pool', lambda e: e.tensor_tensor(out=h_b[:, :, 1, col0:col0 + ncols], in0=t3[:, :, 0:ncols],
                                                               in1=t4[:, :, 0:ncols], op=ALU.add), R=['s5tmp2', 's5tmp3', 'h_b'], W=['h_b'])

            T.barrier()
            xnb = sb("xnb", [128, D], BF16)
            ropet = sb("ropet", [128, 128])
            sm = sb("sm", [128, 64])
            junk = sb("junk", [128, 512], BF16)
            ckvn_f = sb("ckvn_f", [128, 256]); krr_f = sb("krr_f", [128, 64]); krt = sb("krt", [128, 64])
            ckvn_b = sb("ckvn_b", [128, 256], BF16)
            krr_b = sb("krr_b", [128, 64], BF16)
            uT = sb("uT", [128, 4, 128], BF16); u_bg = sb("u_bg", [128, GT, 512], BF16)
            ckvT_all = sb("ckvT_all", [128, 2, NKT * 128], BF16)
            krT_all = sb("krT_all", [64, NKT * 128], BF16)
            KA = max(NKT, 16)
            ckv1_all = sb("ckv1_all", [128, KA, 256], BF16)
            c1flat = ckv1_all[:].rearrange("p a b -> p (a b)")
            rk_all = sb("rk_all", [128, NKT, 8])
            kbias = sb("kbias", [128, NKT])
            T.dma('sp', 'c0', lambda e: e.dma_start(out=kbias[:], in_=kbias_d[:, :]), W=['kbias'])
            ckvT_s = sb("ckvT_s", [128, 2, 128], BF16); krT_s = sb("krT_s", [64, 128], BF16)

            def rstd_of(src_ap, n, dst_ap, skey):
                T.op('act', lambda e: e.activation(out=junk[:, 0:n], in_=src_ap, func=AF.Square, accum_out=dst_ap),
                     R=[skey, 'sm'], W=['junk', 'sm'])
                T.op('act', lambda e: e.activation(out=dst_ap, in_=dst_ap, func=AF.Sqrt, scale=1.0 / n, bias=EPS), R=['sm'], W=['sm'])
                T.op('dve', lambda e: e.reciprocal(out=dst_ap, in_=dst_ap), R=['sm'], W=['sm'])

            def norm_transpose(src_d_ap, dst_fn, dkey):
                T.dma('sp', 'xld', lambda e: e.dma_start(out=xt[:], in_=src_d_ap), W=['xt'])
                T.op('act', lambda e: e.activation(out=xnb[:, 0:1024], in_=xt[:, 0:1024], func=AF.Square, accum_out=sm[:, 0:1]),
                     R=['xt', 'sm'], W=['xnb', 'sm'])
                T.op('act', lambda e: e.activation(out=xnb[:, 1024:2048], in_=xt[:, 1024:2048], func=AF.Square, accum_out=sm[:, 1:2]),
                     R=['xt', 'sm'], W=['xnb', 'sm'])
                T.op('dve', lambda e: e.tensor_tensor(out=sm[:, 0:1], in0=sm[:, 0:1], in1=sm[:, 1:2], op=ALU.add), R=['sm'], W=['sm'])
                T.op('act', lambda e: e.activation(out=sm[:, 0:1], in_=sm[:, 0:1], func=AF.Sqrt, scale=1.0 / D, bias=EPS), R=['sm'], W=['sm'])
                T.op('dve', lambda e: e.reciprocal(out=sm[:, 0:1], in_=sm[:, 0:1]), R=['sm'], W=['sm'])
                T.op('dve', lambda e: e.tensor_scalar(out=xnb[:], in0=xt[:], scalar1=sm[:, 0:1], scalar2=None, op0=ALU.mult),
                     R=['xt', 'sm', 'xnb'], W=['xnb'])
                for half in range(2):
                    p, pk = nextT()
                    for k in range(8):
                        kc = half * 8 + k
                        T.op('pe', lambda e, k=k, kc=kc, p=p: e.transpose(out=p[:, k * 128:(k + 1) * 128],
                                                                          in_=xnb[:, kc * 128:(kc + 1) * 128], identity=identb[:]),
                             R=['xnb', 'identb'], W=[pk])
                    if half == 0:
                        T.op('act', lambda e, p=p, half=half: e.copy(out=dst_fn(half), in_=p[:, :]), R=[pk, dkey], W=[dkey])
                    else:
                        T.op('dve', lambda e, p=p, half=half: e.tensor_copy(out=dst_fn(half), in_=p[:, :]), R=[pk, dkey], W=[dkey])

            sqb = sb("sqb", [128, 1024], BF16)
            kst = [sb("kst%d" % i, [128, 16]) for i in range(2)]
            kslot = [0]

            def key_norms(cT, cTk, kr_ap, krk, nk, rk_dst, rkk):
                sl_ = kslot[0]; kslot[0] = 1 - sl_
                ks, ksk = kst[sl_], 'kst%d' % sl_
                pa, pak = nextA()
                pb, pbk = nextA()
                for hb, (pp, ppk) in enumerate(((pa, pak), (pb, pbk))):
                    for kc in range(2):
                        T.op('pe', lambda e, kc=kc, pp=pp, hb=hb: e.matmul(pp[0:nk, :], lhsT=cT[:, kc, :],
                                                                          rhs=w_uk_b[:, kc, hb * 512:(hb + 1) * 512],
                                                                          start=(kc == 0), stop=(kc == 1)),
                             R=[cTk, 'w_uk_b'], W=[ppk])
                    T.op('act', lambda e, pp=pp, hb=hb: e.activation(out=sqb[0:nk, hb * 512:(hb + 1) * 512], in_=pp[0:nk, :],
                                                                     func=AF.Square), R=[ppk], W=['sqb%d' % hb])
                T.op('dve', lambda e: e.tensor_reduce(out=ks[0:nk, 0:8], in_=sqb[0:nk, :].rearrange("p (h d) -> p h d", d=128),
                                                      axis=AX.X, op=ALU.add), R=['sqb0', 'sqb1'], W=[ksk])
                T.op('act', lambda e: e.activation(out=junk[0:nk, 0:64], in_=kr_ap, func=AF.Square, accum_out=ks[0:nk, 8:9]),
                     R=[krk, ksk], W=['junk', ksk])
                T.op('dve', lambda e: e.tensor_scalar(out=ks[0:nk, 0:8], in0=ks[0:nk, 0:8], scalar1=ks[0:nk, 8:9], scalar2=None,
                                                      op0=ALU.add), R=[ksk], W=[ksk])
                T.op('act', lambda e: e.activation(out=ks[0:nk, 0:8], in_=ks[0:nk, 0:8], func=AF.Ln, scale=1.0 / 192, bias=EPS),
                     R=[ksk], W=[ksk])
                T.op('act', lambda e: e.activation(out=rk_dst, in_=ks[0:nk, 0:8], func=AF.Exp, scale=-0.5), R=[ksk, rkk], W=[rkk])

            def latent_post(pck, pckk, row0, out_ckv, out_kr, orow0, cT_dst, cTk, kT_dst, kTk, c1_dst, c1k, rk_dst):
                T.dma('sp', 'rld', lambda e: e.dma_start(out=ropet[:], in_=rope[row0:row0 + 128, :]), W=['ropet'])
                rstd_of(pck[:, 0:256], 256, sm[:, 2:3], pckk)
                T.op('dve', lambda e: e.scalar_tensor_tensor(out=ckvn_f[:], in0=pck[:, 0:256], scalar=sm[:, 2:3], in1=gkv_bc[:],
                                                             op0=ALU.mult, op1=ALU.mult), R=[pckk, 'sm', 'gkv_bc'], W=['ckvn_f'])
                T.op('dve', lambda e: e.tensor_tensor(out=krr_f[:], in0=pck[:, 256:320], in1=ropet[:, 0:64], op=ALU.mult),
                     R=[pckk, 'ropet'], W=['krr_f'])
                T.op('dve', lambda e: e.tensor_tensor(out=krt[:, 0:32], in0=pck[:, 288:320], in1=ropet[:, 64:96], op=ALU.mult),
                     R=[pckk, 'ropet'], W=['krt'])
                T.op('dve', lambda e: e.tensor_tensor(out=krt[:, 32:64], in0=pck[:, 256:288], in1=ropet[:, 96:128], op=ALU.mult),
                     R=[pckk, 'ropet', 'krt'], W=['krt'])
                T.op('dve', lambda e: e.tensor_tensor(out=krr_f[:], in0=krr_f[:], in1=krt[:], op=ALU.add), R=['krr_f', 'krt'], W=['krr_f'])
                if out_ckv is not None:
                    T.dma('sp', 'ost', lambda e: e.dma_start(out=out_ckv[orow0:orow0 + 128, :], in_=ckvn_f[:]), R=['ckvn_f'])
                    T.dma('sp', 'ost', lambda e: e.dma_start(out=out_kr[orow0:orow0 + 128, :], in_=krr_f[:]), R=['krr_f'])
                T.op('pool', lambda e: e.tensor_copy(out=ckvn_b[:], in_=ckvn_f[:]), R=['ckvn_f'], W=['ckvn_b'])
                T.op('pool', lambda e: e.tensor_copy(out=krr_b[:], in_=krr_f[:]), R=['krr_f'], W=['krr_b'])
                if c1_dst is not None:
                    T.op('pool', lambda e: e.tensor_copy(out=c1_dst, in_=ckvn_f[:]), R=['ckvn_f', c1k], W=[c1k])
                p, pk = nextT()
                for kc in range(2):
                    T.op('pe', lambda e, kc=kc: e.transpose(out=p[:, kc * 128:(kc + 1) * 128], in_=ckvn_b[:, kc * 128:(kc + 1) * 128],
                                                            identity=identb[:]), R=['ckvn_b', 'identb'], W=[pk])
                T.op('pe', lambda e: e.transpose(out=p[0:64, 256:384], in_=krr_b[:], identity=identb[:]), R=['krr_b', 'identb'], W=[pk])
                T.op('act', lambda e: e.copy(out=cT_dst, in_=p[:, 0:256].rearrange("p (a b) -> p a b", a=2)), R=[pk, cTk], W=[cTk])
                T.op('act', lambda e: e.copy(out=kT_dst, in_=p[0:64, 256:384]), R=[pk, kTk], W=[kTk])
                if rk_dst is not None:
                    key_norms(cT_dst, cTk, krr_f[:], 'krr_f', 128, rk_dst, 'rk_all')

            def u_transpose(src_b_ap, skey):
                p, pk = nextT()
                for k in range(4):
                    T.op('pe', lambda e, k=k: e.transpose(out=p[:, k * 128:(k + 1) * 128], in_=src_b_ap[:, k * 128:(k + 1) * 128],
                                                          identity=identb[:]), R=[skey, 'identb'], W=[pk])
                T.op('dve', lambda e: e.tensor_copy(out=uT[:].rearrange("p a b -> p (a b)"), in_=p[:, 0:512]), R=[pk], W=['uT'])

            for t in range(NPREV):
                norm_transpose(xall[t * 128:(t + 1) * 128, :],
                               lambda half: xnT1[:, half * 8:(half + 1) * 8, :].rearrange("p a b -> p (a b)"), 'xnT1')
                pu, puk = nextA()
                for kc in range(16):
                    T.op('pe', lambda e, kc=kc: e.matmul(pu[:, 0:512], lhsT=xnT1[:, kc, :], rhs=w_up_b[:, kc, 0:512],
                                                         start=(kc == 0), stop=(kc == 15)), R=['xnT1', 'w_up_b'], W=[puk])
                pc, pck = nextA()
                for kc in range(16):
                    T.op('pe', lambda e, kc=kc: e.matmul(pc[:, 0:320], lhsT=xnT1[:, kc, :], rhs=w_up_b[:, kc, 512:832],
                                                         start=(kc == 0), stop=(kc == 15)), R=['xnT1', 'w_up_b'], W=[pck])
                T.op('act', lambda e: e.copy(out=u_bg[:, 0, :], in_=pu[:, 0:512]), R=[puk], W=['u_bg'])
                latent_post(pc, pck, t * 128, None, None, 0, ckvT_all[:, :, t * 128:(t + 1) * 128], 'ckvT_all',
                            krT_all[0:64, t * 128:(t + 1) * 128], 'krT_all', ckv1_all[:, t, :], 'ckv1_all', rk_all[:, t, :])
                u_transpose(u_bg[:, 0, :], 'u_bg')
                for sub in range(128 // LS):
                    ssm_step(lambda ct, sub=sub: uT[:, ct, sub * LS:(sub + 1) * LS], LS, 1, LS, 0, False)
            dbg('hre', hre[:, :, 0], [128, 16], ['hst']); dbg('him', him[:, :, 0], [128, 16], ['hst'])
            T.barrier()

            chk('P')
            memKT_b = sb("memKT_b", [128, 4, 256], BF16)
            memV_b = sb("memV_b", [128, 2, 512], BF16)
            wblk = sb("wblk", [128, 16, 256], BF16)
            yf = sb("yf", [128, 512]); yg = sb("yg", [128, 512])
            ob = sb("ob", [128, 512], BF16)

            def stream_wblock(w_d, c0, ncols, gcol):
                for q4 in range(4):
                    T.dma('sp', 'wst', lambda e, q4=q4: e.dma_start(
                        out=stg[:, 0:4 * ncols].rearrange("p (k c) -> p k c", k=4),
                        in_=w_d[q4 * 512:(q4 + 1) * 512, c0:c0 + ncols].rearrange("(k p) c -> p k c", p=128)), W=['stg'])
                    for k in range(4):
                        kc = q4 * 4 + k
                        if k % 2 == 0:
                            T.op('dve', lambda e, k=k, kc=kc: e.tensor_scalar(out=wblk[:, kc, 0:ncols], in0=stg[:, k * ncols:(k + 1) * ncols],
                                                                              scalar1=gcol[:, kc:kc + 1], scalar2=None, op0=ALU.mult),
                                 R=['stg', 'wblk'], W=['wblk'])
                        else:
                            T.op('act', lambda e, k=k, kc=kc: e.activation(out=wblk[:, kc, 0:ncols], in_=stg[:, k * ncols:(k + 1) * ncols],
                                                                           func=AF.Copy, scale=gcol[:, kc:kc + 1]),
                                 R=['stg', 'wblk'], W=['wblk'])
                return wblk, 'wblk'

            for mt in range(2):
                norm_transpose(mem_d[mt * 128:(mt + 1) * 128, :],
                               lambda half, mt=mt: memT[:, mt, half * 8:(half + 1) * 8, :].rearrange("p a b -> p (a b)"), 'memT')
            chk('M0')
            for which, w_d in enumerate((w_mk_d, w_mv_d)):
                for cb in range(2):
                    wb, wk = stream_wblock(w_d, cb * 256, 256, gcol_mem)
                    chk('M1_%d_%d' % (which, cb))
                    for mt in range(2):
                        p, pk = nextA()
                        for kc in range(16):
                            T.op('pe', lambda e, kc=kc, mt=mt, p=p: e.matmul(p[:, 0:256], lhsT=memT[:, mt, kc, :], rhs=wb[:, kc, 0:256],
                                                                             start=(kc == 0), stop=(kc == 15)), R=['memT', wk], W=[pk])
                        chk('M2')
                        if which == 1:
                            T.op('act', lambda e, p=p: e.copy(out=yf[:, 0:256], in_=p[:, 0:256]), R=[pk], W=['yf'])
                            T.op('dve', lambda e, p=p, mt=mt, cb=cb: e.tensor_copy(out=memV_b[:, mt, cb * 256:(cb + 1) * 256], in_=p[:, 0:256]),
                                 R=[pk, 'memV_b'], W=['memV_b'])
                            T.dma('sp', 'ost', lambda e, mt=mt, cb=cb: e.dma_start(out=memv_o[mt * 128:(mt + 1) * 128, cb * 256:(cb + 1) * 256],
                                                                                   in_=yf[:, 0:256]), R=['yf'])
                        else:
                            for hh in range(2):
                                rstd_of(p[:, hh * 128:(hh + 1) * 128], 128, sm[:, 3:4], pk)
                                T.op('dve', lambda e, p=p, hh=hh: e.scalar_tensor_tensor(out=yf[:, hh * 128:(hh + 1) * 128],
                                                                                         in0=p[:, hh * 128:(hh + 1) * 128], scalar=sm[:, 3:4],
                                                                                         in1=gmk_bc[:], op0=ALU.mult, op1=ALU.mult),
                                     R=[pk, 'sm', 'gmk_bc', 'yf'], W=['yf'])
                            chk('M3')
                            T.op('dve', lambda e: e.tensor_copy(out=ob[:, 0:256], in_=yf[:, 0:256]), R=['yf'], W=['ob'])
                            T.dma('sp', 'ost', lambda e, mt=mt, cb=cb: e.dma_start(out=memk_o[mt * 128:(mt + 1) * 128, cb * 256:(cb + 1) * 256],
                                                                                   in_=yf[:, 0:256]), R=['yf'])
                            chk('M4')
                            pt_, ptk = nextT()
                            for hh in range(2):
                                T.op('pe', lambda e, hh=hh, pt_=pt_: e.transpose(out=pt_[:, hh * 128:(hh + 1) * 128], in_=ob[:, hh * 128:(hh + 1) * 128],
                                                                                 identity=identb[:]), R=['ob', 'identb'], W=[ptk])
                            T.op('act', lambda e, cb=cb, mt=mt, pt_=pt_: e.copy(out=memKT_b[:, cb * 2:cb * 2 + 2, mt * 128:(mt + 1) * 128],
                                                                                in_=pt_[:, 0:256].rearrange("p (a b) -> p a b", a=2)),
                                 R=[ptk, 'memKT_b'], W=['memKT_b'])
                            chk('M5')
                            if cb == 1 and mt == 1:
                                chk('M6')
            T.barrier()

            chk('M')
            ssq = sb("ssq", [128, GT, 8])
            qf = xt[:, 0:1536].rearrange("p (h c) -> p h c", h=8); qs_b = sb("qs_b", [128, 8, 192], BF16)
            wk6 = sb("wk6", [128, 1536])
            qsq = wk6[:, :].rearrange("p (h c) -> p h c", h=8)
            ymT8 = wk6[:, 0:1024].rearrange("p (h c) -> p h c", h=8)
            ymT4 = wk6[:, 0:4 * GC].rearrange("p (h c) -> p h c", h=4)
            cq_b = sb("cq_b", [128, 512], BF16); cqT = sb("cqT", [128, 4, 128], BF16)
            qabsT = sb("qabsT", [128, 2, 8, 128], BF16)
            qav = qabsT[:].rearrange("p a h c -> p a (h c)")
            PT = sb("PT", [128, GC], BF16)
            OTn = sb("OTn", [128, 2, GC], BF16)
            rcp = sb("rcp", [128, GC])
            xres = sb("xres", [128, 256]); yout = sb("yout", [128, 256])
            idx_b = sb("idx_b", [128, NPG], I32); idxf = sb("idxf", [128, NPG])
            iot = sb("iot", [128, 1], I32); iotf = sb("iotf", [128, 1])
            pgc = [sb("pgc%d" % i, [128, 256]) for i in range(2)]
            pgk = [sb("pgk%d" % i, [128, 64]) for i in range(2)]
            pgcb2 = [sb("pgcb%d" % i, [128, 257], BF16) for i in range(2)]; pgkb2 = [sb("pgkb%d" % i, [128, 64], BF16) for i in range(2)]
            pgT2 = [sb("pgT%d" % i, [128, 2, 128], BF16) for i in range(2)]; pgkT2 = [sb("pgkT%d" % i, [64, 128], BF16) for i in range(2)]
            rkp2 = [sb("rkp%d" % i, [128, 8]) for i in range(2)]; scf2 = [sb("scf%d" % i, [128, 64]) for i in range(2)]
            PTs2 = [sb("PTs%d" % i, [128, 64], BF16) for i in range(2)]
            mini = sb("mini", [8, 257], BF16)
            for i_ in range(2):
                T.op('dve', lambda e, i_=i_: e.memset(pgcb2[i_][:, 256:257], 1.0), W=['pgcb%d' % i_])
            T.op('dve', lambda e: e.memset(mini[:, 256:257], 1.0), W=['mini'])
            mkb = c1flat[:, 0:1024].rearrange("p (t c) -> p t c", t=2); mvb = c1flat[:, 1024:2048].rearrange("p (t c) -> p t c", t=2)
            mkT_s = c1flat[:, 2048:3072].rearrange("p (h c) -> p h c", h=4)
            mcf = [xt[:, 0:1024].rearrange("p (t c) -> p t c", t=2), xt[:, 1024:2048].rearrange("p (t c) -> p t c", t=2)]

            def in_proj_block(ng, c0, ncols, consume):
                stream_wblock(w_in, c0, ncols, gcol_in)
                for ti in range(ng):
                    p, pk = nextA()
                    for kc in range(16):
                        T.op('pe', lambda e, kc=kc, ti=ti, p=p: e.matmul(p[:, 0:ncols], lhsT=xnT_g[:, ti, kc, :], rhs=wblk[:, kc, 0:ncols],
                                                                         start=(kc == 0), stop=(kc == 15)), R=['xnT_g', 'wblk'], W=[pk])
                    consume(ti, p, pk)

            def mem_attend(ncols, qcols, KT, KTk, V, Vk, out_fn):
                for h in range(4):
                    acc, acck = psC[0], 'psC0'
                    lb, lbk = psC[1], 'psC1'
                    for kt in range(2):
                        p, pk = nextA()
                        T.op('pe', lambda e, kt=kt, h=h, p=p: e.matmul(p[:, 0:ncols], lhsT=KT[:, h, kt * 128:(kt + 1) * 128],
                                                                       rhs=QmT[:, h, qcols], start=True, stop=True), R=[KTk, 'QmT'], W=[pk])
                        T.op('act', lambda e, p=p: e.activation(out=PT[:, 0:ncols], in_=p[:, 0:ncols], func=AF.Exp), R=[pk], W=['PT'])
                        T.op('pe', lambda e, kt=kt, h=h: e.matmul(acc[:, 0:ncols], lhsT=V[:, kt, h * 128:(h + 1) * 128], rhs=PT[:, 0:ncols],
                                                                  start=(kt == 0), stop=(kt == 1)), R=[Vk, 'PT'], W=[acck])
                        T.op('pe', lambda e, kt=kt: e.matmul(lb[:, 0:ncols], lhsT=onesb[:], rhs=PT[:, 0:ncols],
                                                             start=(kt == 0), stop=(kt == 1)), R=['onesb', 'PT'], W=[lbk])
                    T.op('dve', lambda e: e.reciprocal(out=rcp[:, 0:ncols], in_=lb[:, 0:ncols]), R=[lbk], W=['rcp'])
                    T.op('dve', lambda e, h=h: e.tensor_tensor(out=out_fn(h), in0=acc[:, 0:ncols], in1=rcp[:, 0:ncols], op=ALU.mult),
                         R=[acck, 'rcp', 'wk6'], W=['wk6'])

            def finish_branch(ti, src_ap, skey, n, gate0):
                rstd_of(src_ap, n, sm[:, 4:5], skey)
                T.op('dve', lambda e: e.scalar_tensor_tensor(out=gates[:, ti, gate0:gate0 + n], in0=src_ap, scalar=sm[:, 4:5],
                                                             in1=gates[:, ti, gate0:gate0 + n], op0=ALU.mult, op1=ALU.mult),
                     R=[skey, 'sm', 'gates'], W=['gates'])

            def mem_branch_finish(ng, get_cols):
                for ti in range(ng):
                    p, pk = nextA()
                    for h in range(4):
                        T.op('pe', lambda e, h=h, ti=ti, p=p: e.transpose(out=p[:, h * 128:(h + 1) * 128], in_=get_cols(h, ti), identity=identf[:]),
                             R=['wk6', 'identf'], W=[pk])
                    T.op('act', lambda e, p=p: e.copy(out=yf[:], in_=p[:, 0:512]), R=[pk], W=['yf'])
                    finish_branch(ti, yf[:], 'yf', 512, 1536)

            def q_path(ti, row0):
                T.dma('sp', 'rld', lambda e: e.dma_start(out=ropet[:], in_=rope[row0:row0 + 128, :]), W=['ropet'])
                rstd_of(yf[:], 512, sm[:, 5:6], 'yf')
                T.op('dve', lambda e: e.tensor_scalar(out=cq_b[:], in0=yf[:], scalar1=sm[:, 5:6], scalar2=None, op0=ALU.mult),
                     R=['yf', 'sm'], W=['cq_b'])
                p, pk = nextT()
                for k in range(4):
                    T.op('pe', lambda e, k=k, p=p: e.transpose(out=p[:, k * 128:(k + 1) * 128], in_=cq_b[:, k * 128:(k + 1) * 128],
                                                               identity=identb[:]), R=['cq_b', 'identb'], W=[pk])
                T.op('dve', lambda e, p=p: e.tensor_copy(out=cqT[:].rearrange("p a b -> p (a b)"), in_=p[:, 0:512]), R=[pk], W=['cqT'])
                for blk in range(3):
                    pq, pqk = nextA()
                    for k in range(4):
                        T.op('pe', lambda e, k=k, blk=blk, pq=pq: e.matmul(pq[:, 0:512], lhsT=cqT[:, k, :],
                                                                          rhs=w_uq_b[:, k, blk * 512:(blk + 1) * 512],
                                                                          start=(k == 0), stop=(k == 3)), R=['cqT', 'w_uq_b'], W=[pqk])
                    T.op('act', lambda e, blk=blk, pq=pq: e.copy(out=qf[:].rearrange("p h c -> p (h c)")[:, blk * 512:(blk + 1) * 512],
                                                                 in_=pq[:, 0:512]), R=[pqk, 'xt'], W=['xt'])
                cc = ropet[:, 0:64].unsqueeze(1).to_broadcast([128, 8, 64])
                ns = ropet[:, 64:96].unsqueeze(1).to_broadcast([128, 8, 32])
                ps_ = ropet[:, 96:128].unsqueeze(1).to_broadcast([128, 8, 32])
                T.op('dve', lambda e: e.tensor_tensor(out=qsq[:, :, 0:32], in0=qf[:, :, 160:192], in1=ns, op=ALU.mult), R=['xt', 'ropet'], W=['wk6'])
                T.op('dve', lambda e: e.tensor_tensor(out=qsq[:, :, 32:64], in0=qf[:, :, 128:160], in1=ps_, op=ALU.mult), R=['xt', 'ropet', 'wk6'], W=['wk6'])
                T.op('dve', lambda e: e.tensor_tensor(out=qf[:, :, 128:192], in0=qf[:, :, 128:192], in1=cc, op=ALU.mult), R=['xt', 'ropet', 'wk6'], W=['xt'])
                T.op('dve', lambda e: e.tensor_tensor(out=qf[:, :, 128:192], in0=qf[:, :, 128:192], in1=qsq[:, :, 0:64], op=ALU.add), R=['xt', 'wk6'], W=['xt'])
                T.op('pool', lambda e: e.tensor_tensor(out=qsq[:], in0=qf[:], in1=qf[:], op=ALU.mult), R=['xt', 'wk6'], W=['wk6'])
                T.op('dve', lambda e: e.tensor_reduce(out=sm[:, 24:32], in_=qsq[:], axis=AX.X, op=ALU.add), R=['wk6', 'sm'], W=['sm'])
                T.op('act', lambda e: e.activation(out=sm[:, 24:32], in_=sm[:, 24:32], func=AF.Sqrt, scale=1.0 / 192, bias=EPS), R=['sm'], W=['sm'])
                T.op('dve', lambda e: e.reciprocal(out=sm[:, 24:32], in_=sm[:, 24:32]), R=['sm'], W=['sm'])
                T.op('dve', lambda e: e.tensor_tensor(out=qf[:], in0=qf[:], in1=sm[:, 24:32].unsqueeze(2).to_broadcast([128, 8, 192]), op=ALU.mult),
                     R=['xt', 'sm'], W=['xt'])
                T.op('dve', lambda e: e.tensor_tensor(out=qs_b[:], in0=qf[:], in1=gqk_bc[:].unsqueeze(1).to_broadcast([128, 8, 192]), op=ALU.mult),
                     R=['xt', 'gqk_bc'], W=['qs_b'])
                for h in range(8):
                    p, pk = nextT()
                    T.op('pe', lambda e, h=h, p=p: e.transpose(out=p[:, 0:128], in_=qs_b[:, h, 0:128], identity=identb[:]), R=['qs_b', 'identb'], W=[pk])
                    T.op('pe', lambda e, h=h, p=p: e.transpose(out=p[0:64, 128:256], in_=qs_b[:, h, 128:192], identity=identb[:]), R=['qs_b', 'identb'], W=[pk])
                    T.op('act', lambda e, h=h, p=p: e.copy(out=QTn[:, h, ti * 128:(ti + 1) * 128], in_=p[:, 0:128]), R=[pk, 'QTn'], W=['QTn'])
                    T.op('dve', lambda e, h=h, p=p: e.tensor_copy(out=QTr[0:64, h, ti * 128:(ti + 1) * 128], in_=p[0:64, 128:256]), R=[pk, 'QTr'], W=['QTr'])

            def prompt_attention(ng, ncg, own0):
                for h in range(8):
                    for kc in range(2):
                        p, pk = nextA()
                        T.op('pe', lambda e, kc=kc, h=h, p=p: e.matmul(p[:, 0:ncg], lhsT=w_ukT_b[:, h, kc * 128:(kc + 1) * 128],
                                                                       rhs=QTn[:, h, 0:ncg], start=True, stop=True), R=['w_ukT_b', 'QTn'], W=[pk])
                        T.op('act', lambda e, kc=kc, p=p: e.copy(out=qav[:, kc, 0:ncg], in_=p[:, 0:ncg]), R=[pk, 'qabsT'], W=['qabsT'])
                    nkt = NPREV + own0 + ng
                    for kt in range(nkt):
                        rel = kt - (NPREV + own0)
                        c0 = max(rel, 0) * 128
                        ncol = ncg - c0
                        kcols = slice(kt * 128, (kt + 1) * 128)
                        p, pk = nextA()
                        for kc in range(2):
                            T.op('pe', lambda e, kc=kc, p=p, kcols=kcols, c0=c0, ncol=ncol: e.matmul(
                                p[:, 0:ncol], lhsT=ckvT_all[:, kc, kcols], rhs=qav[:, kc, c0:ncg], start=(kc == 0), stop=False),
                                R=['ckvT_all', 'qabsT'], W=[pk])
                        T.op('pe', lambda e, h=h, p=p, kcols=kcols, c0=c0, ncol=ncol: e.matmul(
                            p[:, 0:ncol], lhsT=krT_all[0:64, kcols], rhs=QTr[0:64, h, c0:ncg], start=False, stop=True),
                            R=['krT_all', 'QTr'], W=[pk])
                        T.op('act', lambda e, kt=kt, h=h, p=p, c0=c0, ncol=ncol: e.activation(
                            out=PT[:, c0:ncg], in_=p[:, 0:ncol], func=AF.Exp, scale=rk_all[:, kt, h:h + 1], bias=kbias[:, kt:kt + 1]),
                            R=[pk, 'rk_all', 'kbias'], W=['PT'])
                        if rel >= 0:
                            T.op('pool', lambda e, c0=c0: e.tensor_tensor(out=PT[:, c0:c0 + 128], in0=PT[:, c0:c0 + 128], in1=trib[:], op=ALU.mult),
                                 R=['PT', 'trib'], W=['PT'])
                        first, last = (kt == 0), (kt == nkt - 1)
                        for kc in range(2):
                            T.op('pe', lambda e, kc=kc, kt=kt, c0=c0, first=first, last=last: e.matmul(
                                psC[kc][:, c0:ncg], lhsT=ckv1_all[:, kt, kc * 128:(kc + 1) * 128], rhs=PT[:, c0:ncg], start=first, stop=last),
                                R=['ckv1_all', 'PT'], W=['psC%d' % kc])
                        T.op('pe', lambda e, c0=c0, first=first, last=last: e.matmul(psC[2][:, c0:ncg], lhsT=onesb[:], rhs=PT[:, c0:ncg],
                                                                                     start=first, stop=last),
                             R=['onesb', 'PT'], W=['psC2'])
                    T.op('dve', lambda e: e.reciprocal(out=rcp[:, 0:ncg], in_=psC[2][:, 0:ncg]), R=['psC2'], W=['rcp'])
                    for kc in range(2):
                        T.op('dve', lambda e, kc=kc: e.tensor_tensor(out=OTn[:, kc, 0:ncg], in0=psC[kc][:, 0:ncg], in1=rcp[:, 0:ncg], op=ALU.mult),
                             R=['psC%d' % kc, 'rcp', 'OTn'], W=['OTn'])
                    for ti in range(ng):
                        p, pk = nextA()
                        for kc in range(2):
                            T.op('pe', lambda e, kc=kc, ti=ti, h=h, p=p: e.matmul(p[:, 0:128], lhsT=OTn[:, kc, ti * 128:(ti + 1) * 128],
                                                                                 rhs=w_uv_b[:, kc, h * 128:(h + 1) * 128],
                                                                                 start=(kc == 0), stop=(kc == 1)), R=['OTn', 'w_uv_b'], W=[pk])
                        T.op('dve', lambda e, ti=ti, h=h, p=p: e.tensor_copy(out=ymla[:, ti, h * 128:(h + 1) * 128], in_=p[:, 0:128]),
                             R=[pk, 'ymla'], W=['ymla'])
                        T.op('act', lambda e, ti=ti, h=h, p=p: e.activation(out=junk[:, 0:128], in_=p[:, 0:128], func=AF.Square,
                                                                            accum_out=ssq[:, ti, h:h + 1]), R=[pk, 'ssq'], W=['junk', 'ssq'])
                mem_attend(ncg, slice(0, ncg), memKT_b, 'memKT_b', memV_b, 'memV_b', lambda h: ymT4[:, h, 0:ncg])
                mem_branch_finish(ng, lambda h, ti: ymT4[:, h, ti * 128:(ti + 1) * 128])

            def sample_attention():
                T.op('pool', lambda e: e.iota(iot[:], pattern=[[0, 1]], base=0, channel_multiplier=1), W=['iot'])
                T.op('dve', lambda e: e.tensor_copy(out=iotf[:], in_=iot[:]), R=['iot'], W=['iotf'])
                for h in range(8):
                    for kc in range(2):
                        p, pk = nextA()
                        T.op('pe', lambda e, kc=kc, h=h, p=p: e.matmul(p[:, 0:128], lhsT=w_ukT_b[:, h, kc * 128:(kc + 1) * 128],
                                                                       rhs=QTn[:, h, 0:128], start=True, stop=True), R=['w_ukT_b', 'QTn'], W=[pk])
                        T.op('act', lambda e, kc=kc, h=h, p=p: e.copy(out=qabsT[:, kc, h, :], in_=p[:, 0:128]), R=[pk, 'qabsT'], W=['qabsT'])
                acc = psC[0]
                for b in range(NSEQ):
                    bc = slice(b * 8, (b + 1) * 8)
                    T.dma('sp', 'c0', lambda e, b=b: e.dma_start(out=idx_b[:], in_=ptab[b].partition_broadcast(128)), W=['idx_b'])
                    T.op('dve', lambda e: e.tensor_copy(out=idxf[:], in_=idx_b[:]), R=['idx_b'], W=['idxf'])
                    T.op('dve', lambda e: e.tensor_scalar(out=idxf[:], in0=idxf[:], scalar1=128.0, scalar2=iotf[:, 0:1], op0=ALU.mult, op1=ALU.add),
                         R=['idxf', 'iotf'], W=['idxf'])
                    T.op('dve', lambda e: e.tensor_copy(out=idx_b[:], in_=idxf[:]), R=['idxf'], W=['idx_b'])

                    def attend(sl_, cT, cTk, krT_ap, krTk, c1, c1k, rk_ap, rkk, nk, first, last, mask):
                        p, pk = psC[1 + sl_], 'psC%d' % (1 + sl_)
                        scf, scfk = scf2[sl_], 'scf%d' % sl_
                        PTs, PTk = PTs2[sl_], 'PTs%d' % sl_
                        pv = p[0:nk, 0:64]
                        for kc in range(2):
                            T.op('pe', lambda e, kc=kc: e.matmul(pv, lhsT=cT[:, kc, :], rhs=qabsT[:, kc, :, bc], start=(kc == 0), stop=False),
                                 R=[cTk, 'qabsT'], W=[pk])
                        T.op('pe', lambda e: e.matmul(pv, lhsT=krT_ap, rhs=QTr[0:64, :, bc], start=False, stop=True), R=[krTk, 'QTr'], W=[pk])
                        T.op('dve', lambda e: e.tensor_tensor(out=scf[0:nk, :].rearrange("p (h q) -> p h q", q=8),
                                                              in0=pv.rearrange("p (h q) -> p h q", q=8),
                                                              in1=rk_ap.unsqueeze(2).to_broadcast([nk, 8, 8]), op=ALU.mult),
                             R=[pk, rkk], W=[scfk])
                        T.op('act', lambda e: e.activation(out=PTs[0:nk, 0:64], in_=scf[0:nk, :], func=AF.Exp), R=[scfk], W=[PTk])
                        if mask:
                            T.op('dve', lambda e: e.tensor_tensor(out=PTs[0:nk, 0:64].rearrange("p (h q) -> p h q", q=8),
                                                                  in0=PTs[0:nk, 0:64].rearrange("p (h q) -> p h q", q=8),
                                                                  in1=trib[0:nk, 0:8].unsqueeze(1).to_broadcast([nk, 8, 8]), op=ALU.mult),
                                 R=[PTk, 'trib'], W=[PTk])
                        T.op('pe', lambda e: e.matmul(acc[0:64, 0:257], lhsT=PTs[0:nk, 0:64], rhs=c1[0:nk, 0:257], start=first, stop=last),
                             R=[c1k, PTk], W=['psC0'])

                    for pg in range(NPG):
                        s = pg % 2
                        pgcb, pgkb, pgT, pgkT, rkp = pgcb2[s], pgkb2[s], pgT2[s], pgkT2[s], rkp2[s]
                        T.dma('pool', 'pgc%d' % s, lambda e, s=s, pg=pg: e.indirect_dma_start(
                            out=pgc[s][:], out_offset=None, in_=cckv[:, :],
                            in_offset=bass.IndirectOffsetOnAxis(ap=idx_b[:, pg:pg + 1], axis=0)), R=['idx_b'], W=['pgc%d' % s])
                        T.dma('pool', 'pgk%d' % s, lambda e, s=s, pg=pg: e.indirect_dma_start(
                            out=pgk[s][:], out_offset=None, in_=ckr[:, :],
                            in_offset=bass.IndirectOffsetOnAxis(ap=idx_b[:, pg:pg + 1], axis=0)), R=['idx_b'], W=['pgk%d' % s])
                        T.op('dve', lambda e, s=s, pgcb=pgcb: e.tensor_copy(out=pgcb[:, 0:256], in_=pgc[s][:]), R=['pgc%d' % s, 'pgcb%d' % s], W=['pgcb%d' % s])
                        T.op('dve', lambda e, s=s, pgkb=pgkb: e.tensor_copy(out=pgkb[:], in_=pgk[s][:]), R=['pgk%d' % s], W=['pgkb%d' % s])
                        p, pk = nextT()
                        for kc in range(2):
                            T.op('pe', lambda e, kc=kc, p=p, pgcb=pgcb: e.transpose(out=p[:, kc * 128:(kc + 1) * 128], in_=pgcb[:, kc * 128:(kc + 1) * 128],
                                                                                   identity=identb[:]), R=['pgcb%d' % s, 'identb'], W=[pk])
                        T.op('pe', lambda e, p=p, pgkb=pgkb: e.transpose(out=p[0:64, 256:384], in_=pgkb[:], identity=identb[:]), R=['pgkb%d' % s, 'identb'], W=[pk])
                        T.op('dve', lambda e, p=p, pgT=pgT: e.tensor_copy(out=pgT[:], in_=p[:, 0:256].rearrange("p (a b) -> p a b", a=2)), R=[pk], W=['pgT%d' % s])
                        T.op('dve', lambda e, p=p, pgkT=pgkT: e.tensor_copy(out=pgkT[:], in_=p[0:64, 256:384]), R=[pk], W=['pgkT%d' % s])
                        key_norms(pgT[:], 'pgT%d' % s, pgk[s][:], 'pgk%d' % s, 128, rkp[:], 'rkp%d' % s)
                        attend(s, pgT[:], 'pgT%d' % s, pgkT[:], 'pgkT%d' % s, pgcb, 'pgcb%d' % s, rkp[:], 'rkp%d' % s, 128, pg == 0, False, False)
                    s = NPG % 2
                    rkp = rkp2[s]; scfm = scf2[1 - s]
                    p, pk = nextT()
                    for kc in range(2):
                        T.op('pe', lambda e, kc=kc, p=p: e.transpose(out=p[0:8, kc * 128:(kc + 1) * 128], in_=ckvT_s[:, kc, bc], identity=identb[:]),
                             R=['ckvT_s', 'identb'], W=[pk])
                    T.op('pe', lambda e, p=p: e.transpose(out=p[0:8, 256:320], in_=krT_s[0:64, bc], identity=identb[0:64, 0:64]),
                         R=['krT_s', 'identb'], W=[pk])
                    T.op('act', lambda e, p=p: e.copy(out=mini[:, 0:256], in_=p[0:8, 0:256]), R=[pk, 'mini'], W=['mini'])
                    T.op('dve', lambda e, p=p: e.tensor_copy(out=scfm[0:8, 0:64], in_=p[0:8, 256:320]), R=[pk], W=['scf%d' % (1 - s)])
                    key_norms(ckvT_s[:, :, bc], 'ckvT_s', scfm[0:8, 0:64], 'scf%d' % (1 - s), 8, rkp[0:8, :], 'rkp%d' % s)
                    attend(s, ckvT_s[:, :, bc], 'ckvT_s', krT_s[0:64, bc], 'krT_s', mini, 'mini', rkp[0:8, :], 'rkp%d' % s, 8, NPG == 0, True, True)
                    T.op('dve', lambda e: e.reciprocal(out=rcp[0:64, 0:1], in_=acc[0:64, 256:257]), R=['psC0'], W=['rcp'])
                    T.op('dve', lambda e: e.tensor_scalar(out=ob[0:64, 0:256], in0=acc[0:64, 0:256], scalar1=rcp[0:64, 0:1], scalar2=None,
                                                          op0=ALU.mult), R=['psC0', 'rcp'], W=['ob'])
                    p, pk = nextT()
                    for kc in range(2):
                        T.op('pe', lambda e, kc=kc, p=p: e.transpose(out=p[:, kc * 64:(kc + 1) * 64], in_=ob[0:64, kc * 128:(kc + 1) * 128],
                                                                     identity=identb[0:64, 0:64]), R=['ob', 'identb'], W=[pk])
                    T.op('act', lambda e, p=p: e.copy(out=OTn[:, :, 0:64], in_=p[:, 0:128].rearrange("p (a b) -> p a b", a=2)), R=[pk, 'OTn'], W=['OTn'])
                    p, pk = nextA()
                    for h in range(8):
                        for kc in range(2):
                            T.op('pe', lambda e, kc=kc, h=h, p=p: e.matmul(p[:, h * 8:(h + 1) * 8], lhsT=w_uv_b[:, kc, h * 128:(h + 1) * 128],
                                                                           rhs=OTn[:, kc, h * 8:(h + 1) * 8], start=(kc == 0), stop=(kc == 1)),
                                 R=['w_uv_b', 'OTn'], W=[pk])
                    T.op('act', lambda e, p=p: e.copy(out=ymT8[:, :, bc], in_=p[:, 0:64].rearrange("p (h q) -> p h q", q=8)), R=[pk, 'wk6'], W=['wk6'])
                for h in range(8):
                    p, pk = nextA()
                    T.op('pe', lambda e, h=h, p=p: e.transpose(out=p[:, 0:128], in_=ymT8[:, h, :], identity=identf[:]), R=['wk6', 'identf'], W=[pk])
                    T.op('dve', lambda e, h=h, p=p: e.tensor_copy(out=ymla[:, 0, h * 128:(h + 1) * 128], in_=p[:, 0:128]), R=[pk, 'ymla'], W=['ymla'])
                    T.op('act', lambda e, h=h, p=p: e.activation(out=junk[:, 0:128], in_=p[:, 0:128], func=AF.Square, accum_out=ssq[:, 0, h:h + 1]),
                         R=[pk, 'ssq'], W=['junk', 'ssq'])
                for b in range(NSEQ):
                    bc = slice(b * 8, (b + 1) * 8)
                    T.dma('sp', 'mck', lambda e, b=b: e.dma_start(out=mcf[0], in_=cmk[b * 256:(b + 1) * 256, :].rearrange("(t p) c -> p t c", p=128)),
                          W=['xt'])
                    T.dma('sp', 'mcv', lambda e, b=b: e.dma_start(out=mcf[1], in_=cmv[b * 256:(b + 1) * 256, :].rearrange("(t p) c -> p t c", p=128)),
                          W=['xt'])
                    T.op('pool', lambda e: e.tensor_copy(out=mkb[:], in_=mcf[0]), R=['xt'], W=['mkb'])
                    T.op('pool', lambda e: e.tensor_copy(out=mvb[:], in_=mcf[1]), R=['xt'], W=['mvb'])
                    for kt in range(2):
                        p, pk = nextT()
                        for h in range(4):
                            T.op('pe', lambda e, kt=kt, h=h, p=p: e.transpose(out=p[:, h * 128:(h + 1) * 128], in_=mkb[:, kt, h * 128:(h + 1) * 128],
                                                                              identity=identb[:]), R=['mkb', 'identb'], W=[pk])
                        T.op('act', lambda e, kt=kt, p=p: e.copy(out=mkT_s[:, :, kt * 128:(kt + 1) * 128],
                                                                in_=p[:, 0:512].rearrange("p (a b) -> p a b", a=4)), R=[pk, 'mkT_s'], W=['mkT_s'])
                    mem_attend(8, bc, mkT_s, 'mkT_s', mvb, 'mvb', lambda h, bc=bc: ymT4[:, h, bc])
                mem_branch_finish(1, lambda h, ti: ymT4[:, h, 0:128])

            def run_group(tiles, is_sample):
                ng = len(tiles)
                ncg = ng * 128
                for ti, (row0, _) in enumerate(tiles):
                    norm_transpose(xall[row0:row0 + 128, :],
                                   lambda half, ti=ti: xnT_g[:, ti, half * 8:(half + 1) * 8, :].rearrange("p a b -> p (a b)"), 'xnT_g')

                def c_u(half):
                    def f(ti, p, pk):
                        T.op('act', lambda e: e.copy(out=u_bg[:, ti, half * 256:(half + 1) * 256], in_=p[:, 0:256]), R=[pk, 'u_bg'], W=['u_bg'])
                    return f
                in_proj_block(ng, C_U, 256, c_u(0))
                in_proj_block(ng, C_U + 256, 256, c_u(1))

                def c_gate(goff):
                    def f(ti, p, pk):
                        T.op('act', lambda e: e.activation(out=gates[:, ti, goff:goff + 256], in_=p[:, 0:256], func=AF.Silu),
                             R=[pk, 'gates'], W=['gates'])
                    return f
                in_proj_block(ng, C_GS, 256, c_gate(0))
                in_proj_block(ng, C_GS + 256, 256, c_gate(256))
                chk('G0')
                for ti, (row0, own) in enumerate(tiles):
                    u_transpose(u_bg[:, ti, :], 'u_bg')
                    py, pyk = psC[2], 'psC2'
                    for gi in range(16 // GS):
                        if is_sample:
                            ssm_step(lambda ct: uT[:, ct, :], 128, NSEQ, 8, 0, True, gis=[gi])
                        else:
                            for sub in range(128 // LS):
                                ssm_step(lambda ct, sub=sub: uT[:, ct, sub * LS:(sub + 1) * LS], LS, 1, LS, sub * LS, True, gis=[gi])
                        for j in range(GS):
                            i = gi * GS + j
                            T.op('pe', lambda e, i=i, j=j: e.matmul(py[:, i * 32:(i + 1) * 32], lhsT=h_b[:, j, 0, :], rhs=CT_b[:, i, 0, :],
                                                                    start=True, stop=False), R=['h_b', 'CT_b'], W=[pyk])
                            T.op('pe', lambda e, i=i, j=j: e.matmul(py[:, i * 32:(i + 1) * 32], lhsT=h_b[:, j, 1, :], rhs=CT_b[:, i, 1, :],
                                                                    start=False, stop=False), R=['h_b', 'CT_b'], W=[pyk])
                            ct, off = i // 4, (i % 4) * 32
                            T.op('pe', lambda e, i=i, ct=ct, off=off: e.matmul(py[:, i * 32:(i + 1) * 32], lhsT=uT[:, ct, :],
                                                                              rhs=diagD_b[:, ct, off:off + 32], start=False, stop=True),
                                 R=['uT', 'diagD_b'], W=[pyk])
                    T.op('act', lambda e: e.copy(out=yf[:], in_=py[:, 0:512]), R=[pyk], W=['yf'])
                    T.op('pool', lambda e: e.tensor_tensor(out=yg[:], in0=yf[:], in1=yf[:], op=ALU.mult), R=['yf'], W=['yg'])
                    T.op('pool', lambda e: e.tensor_scalar(out=yg[:], in0=yg[:], scalar1=0.044715, scalar2=1.0, op0=ALU.mult, op1=ALU.add),
                         R=['yg'], W=['yg'])
                    T.op('pool', lambda e: e.tensor_tensor(out=yg[:], in0=yg[:], in1=yf[:], op=ALU.mult), R=['yg', 'yf'], W=['yg'])
                    T.op('act', lambda e: e.activation(out=yg[:], in_=yg[:], func=AF.Sigmoid, scale=GELU_K), R=['yg'], W=['yg'])
                    T.op('dve', lambda e: e.tensor_tensor(out=yg[:], in0=yg[:], in1=yf[:], op=ALU.mult), R=['yg', 'yf'], W=['yg'])
                    T.op('dve', lambda e: e.tensor_copy(out=ob[:, 0:512], in_=yg[:]), R=['yg'], W=['ob'])
                    p, pk = nextT()
                    for k in range(4):
                        T.op('pe', lambda e, k=k, p=p: e.transpose(out=p[:, k * 128:(k + 1) * 128], in_=ob[:, k * 128:(k + 1) * 128],
                                                                   identity=identb[:]), R=['ob', 'identb'], W=[pk])
                    T.op('act', lambda e, p=p: e.copy(out=cqT[:].rearrange("p a b -> p (a b)"), in_=p[:, 0:512]), R=[pk], W=['cqT'])
                    pz, pzk = nextA()
                    for k in range(4):
                        T.op('pe', lambda e, k=k, pz=pz: e.matmul(pz[:, 0:512], lhsT=cqT[:, k, :], rhs=glu_w_b[:, k, :], start=(k == 0), stop=False),
                             R=['cqT', 'glu_w_b'], W=[pzk])
                    T.op('pe', lambda e, pz=pz: e.matmul(pz[:, 0:512], lhsT=onesb[0:1, :], rhs=glub_b[0:1, :], start=False, stop=True),
                         R=['onesb', 'glub_b'], W=[pzk])
                    T.op('act', lambda e, pz=pz: e.activation(out=yf[:], in_=pz[:, 0:512], func=AF.Sigmoid), R=[pzk, 'yf'], W=['yf'])
                    T.op('dve', lambda e: e.tensor_tensor(out=yf[:], in0=yf[:], in1=yg[:], op=ALU.mult), R=['yf', 'yg'], W=['yf'])
                    finish_branch(ti, yf[:], 'yf', 512, 0)

                chk('G1')
                lat_ps = []
                stream_wblock(w_in, C_CKV, 256, gcol_in)
                for ti in range(ng):
                    p, pk = psC[ti], 'psC%d' % ti
                    for kc in range(16):
                        T.op('pe', lambda e, kc=kc, ti=ti, p=p: e.matmul(p[:, 0:256], lhsT=xnT_g[:, ti, kc, :], rhs=wblk[:, kc, 0:256],
                                                                         start=(kc == 0), stop=(kc == 15)), R=['xnT_g', 'wblk'], W=[pk])
                    lat_ps.append((p, pk))
                stream_wblock(w_in, C_KR, 64, gcol_in)
                for ti, (row0, own) in enumerate(tiles):
                    p, pk = lat_ps[ti]
                    for kc in range(16):
                        T.op('pe', lambda e, kc=kc, ti=ti, p=p: e.matmul(p[:, 256:320], lhsT=xnT_g[:, ti, kc, :], rhs=wblk[:, kc, 0:64],
                                                                         start=(kc == 0), stop=(kc == 15)), R=['xnT_g', 'wblk', pk], W=[pk])
                    if is_sample:
                        latent_post(p, pk, row0, ckv_s, kr_s, 0, ckvT_s[:], 'ckvT_s', krT_s[0:64, :], 'krT_s', None, None, None)
                    else:
                        kt = NPREV + own
                        latent_post(p, pk, row0, ckv_p, kr_p, own * 128, ckvT_all[:, :, kt * 128:(kt + 1) * 128], 'ckvT_all',
                                    krT_all[0:64, kt * 128:(kt + 1) * 128], 'krT_all', ckv1_all[:, kt, :], 'ckv1_all', rk_all[:, kt, :])

                chk('G2')
                cq_ps = []
                stream_wblock(w_in, C_CQ, 256, gcol_in)
                for ti in range(ng):
                    p, pk = psC[ti], 'psC%d' % ti
                    for kc in range(16):
                        T.op('pe', lambda e, kc=kc, ti=ti, p=p: e.matmul(p[:, 0:256], lhsT=xnT_g[:, ti, kc, :], rhs=wblk[:, kc, 0:256],
                                                                         start=(kc == 0), stop=(kc == 15)), R=['xnT_g', 'wblk'], W=[pk])
                    cq_ps.append((p, pk))
                stream_wblock(w_in, C_CQ + 256, 256, gcol_in)
                for ti, (row0, own) in enumerate(tiles):
                    p, pk = cq_ps[ti]
                    for kc in range(16):
                        T.op('pe', lambda e, kc=kc, ti=ti, p=p: e.matmul(p[:, 256:512], lhsT=xnT_g[:, ti, kc, :], rhs=wblk[:, kc, 0:256],
                                                                         start=(kc == 0), stop=(kc == 15)), R=['xnT_g', 'wblk', pk], W=[pk])
                    T.op('act', lambda e, p=p: e.copy(out=yf[:], in_=p[:, 0:512]), R=[pk], W=['yf'])
                    q_path(ti, row0)

                chk('G3')
                for b4 in range(4):
                    in_proj_block(ng, C_GM + b4 * 256, 256, c_gate(512 + b4 * 256))

                def c_qm(half):
                    def f(ti, p, pk):
                        for hh in range(2):
                            h = half * 2 + hh
                            rstd_of(p[:, hh * 128:(hh + 1) * 128], 128, sm[:, 6:7], pk)
                            T.op('dve', lambda e, hh=hh, h=h: e.scalar_tensor_tensor(out=cq_b[:, h * 128:(h + 1) * 128],
                                                                                     in0=p[:, hh * 128:(hh + 1) * 128], scalar=sm[:, 6:7],
                                                                                     in1=gmq_bc[:], op0=ALU.mult, op1=ALU.mult),
                                 R=[pk, 'sm', 'gmq_bc', 'cq_b'], W=['cq_b'])
                        pt_, ptk = nextT()
                        for hh in range(2):
                            h = half * 2 + hh
                            T.op('pe', lambda e, hh=hh, h=h, pt_=pt_: e.transpose(out=pt_[:, hh * 128:(hh + 1) * 128], in_=cq_b[:, h * 128:(h + 1) * 128],
                                                                                  identity=identb[:]), R=['cq_b', 'identb'], W=[ptk])
                        T.op('act', lambda e, pt_=pt_: e.copy(out=QmT[:, half * 2:half * 2 + 2, ti * 128:(ti + 1) * 128],
                                                              in_=pt_[:, 0:256].rearrange("p (a b) -> p a b", a=2)), R=[ptk, 'QmT'], W=['QmT'])
                    return f
                in_proj_block(ng, C_QM, 256, c_qm(0))
                in_proj_block(ng, C_QM + 256, 256, c_qm(1))
                in_proj_block(ng, C_GME, 256, c_gate(1536))
                in_proj_block(ng, C_GME + 256, 256, c_gate(1792))

                chk('G4')
                T.op('dve', lambda e: e.memset(ssq[:], 0.0), W=['ssq'])
                if not is_sample:
                    prompt_attention(ng, ncg, tiles[0][1])
                else:
                    sample_attention()

                chk('G5')
                for ti in range(ng):
                    T.op('dve', lambda e, ti=ti: e.tensor_reduce(out=sm[:, 7:8], in_=ssq[:, ti, 0:8], axis=AX.X, op=ALU.add), R=['ssq', 'sm'], W=['sm'])
                    T.op('act', lambda e: e.activation(out=sm[:, 7:8], in_=sm[:, 7:8], func=AF.Sqrt, scale=1.0 / 1024, bias=EPS), R=['sm'], W=['sm'])
                    T.op('dve', lambda e: e.reciprocal(out=sm[:, 7:8], in_=sm[:, 7:8]), R=['sm'], W=['sm'])
                    T.op('dve', lambda e, ti=ti: e.scalar_tensor_tensor(out=gates[:, ti, 512:1536], in0=ymla[:, ti, :], scalar=sm[:, 7:8],
                                                                        in1=gates[:, ti, 512:1536], op0=ALU.mult, op1=ALU.mult),
                         R=['ymla', 'sm', 'gates'], W=['gates'])
                    for half in range(2):
                        p, pk = nextT()
                        for k in range(8):
                            kc = half * 8 + k
                            T.op('pe', lambda e, k=k, kc=kc, p=p, ti=ti: e.transpose(out=p[:, k * 128:(k + 1) * 128],
                                                                                     in_=gates[:, ti, kc * 128:(kc + 1) * 128], identity=identb[:]),
                                 R=['gates', 'identb'], W=[pk])
                        T.op('act', lambda e, p=p, half=half, ti=ti: e.copy(out=xnT_g[:, ti, half * 8:(half + 1) * 8, :].rearrange("p a b -> p (a b)"),
                                                                           in_=p[:, :]), R=[pk, 'xnT_g'], W=['xnT_g'])
                for cb in range(8):
                    stream_wblock(w_out_d, cb * 256, 256, gcol_out)
                    for ti, (row0, own) in enumerate(tiles):
                        T.dma('sp', 'xres', lambda e, row0=row0, cb=cb: e.dma_start(out=xres[:], in_=xall[row0:row0 + 128, cb * 256:(cb + 1) * 256]),
                              W=['xres'])
                        p, pk = nextA()
                        for kc in range(16):
                            T.op('pe', lambda e, kc=kc, ti=ti, p=p: e.matmul(p[:, 0:256], lhsT=xnT_g[:, ti, kc, :], rhs=wblk[:, kc, 0:256],
                                                                             start=(kc == 0), stop=(kc == 15)), R=['xnT_g', 'wblk'], W=[pk])
                        T.op('dve', lambda e, p=p: e.tensor_tensor(out=yout[:], in0=p[:, 0:256], in1=xres[:], op=ALU.add), R=[pk, 'xres'], W=['yout'])
                        dst = y_s if is_sample else y_p
                        r0 = 0 if is_sample else own * 128
                        T.dma('sp', 'ost', lambda e, dst=dst, r0=r0, cb=cb: e.dma_start(out=dst[r0:r0 + 128, cb * 256:(cb + 1) * 256], in_=yout[:]),
                              R=['yout'])

            own_tiles = [((NPREV + o) * 128, o) for o in range(NOWN)]
            for g0 in range(0, NOWN, GT):
                run_group(own_tiles[g0:g0 + GT], False)
            stf = sb("stf", [16, 128])
            for src, dst in ((hre, sp_re), (him, sp_im)):
                transpose_f32(stf[:], src[:, :, 0], 128, 16, 'stf', 'hst')
                T.dma('sp', 'ost', lambda e, dst=dst: e.dma_start(out=dst[:, :], in_=stf[:]), R=['stf'])
            chk('O')
            T.barrier()
            stin = xt[0:NSEQ, :]
            for src_d, dstt in ((st_re, hre), (st_im, him)):
                T.dma('sp', 'c0', lambda e, src_d=src_d: e.dma_start(out=stin, in_=src_d[:, :]), W=['xt'])
                for i in range(16):
                    transpose_f32(dstt[:, i, :], stin[:, i * 128:(i + 1) * 128], NSEQ, 128, 'hst', 'xt')
            run_group([(NKT * 128, None)], True)
            for src, dst in ((hre, ss_re), (him, ss_im)):
                for i in range(16):
                    transpose_f32(stin[:, i * 128:(i + 1) * 128], src[:, i, :], 128, NSEQ, 'xt', 'hst')
                T.dma('sp', 'ost', lambda e, dst=dst: e.dma_start(out=dst[:, :], in_=stin), R=['xt'])
        except _Stop:
            pass
        T.finish('sp')
        print("[kernel] instructions emitted:", T.nins, "sbuf left:", nc.sbuf_bytes_remaining, {k: v for k, v in T.cnt.items() if k in ("pe", "act", "dve", "pool")}, flush=True)
    return nc


def rope_table(pos):
    half = 32
    inv = (10000.0 ** (-np.arange(half, dtype=np.float32) / half)).astype(np.float32)
    ang = pos.astype(np.float32)[:, None] * inv[None, :]
    c, s = np.cos(ang).astype(np.float32), np.sin(ang).astype(np.float32)
    return np.concatenate([c, c, -s, s], axis=1).astype(np.float32)


WNAMES = ['norm_g', 'w_in', 'ssm_a_re', 'ssm_a_im', 'ssm_log_dt', 'ssm_b_re', 'ssm_b_im', 'ssm_c_re', 'ssm_c_im', 'ssm_d',
          'ssm_glu_w', 'ssm_glu_b', 'mla_q_norm_g', 'mla_w_uq', 'mla_kv_norm_g', 'mla_w_ukv', 'mla_qk_norm_q', 'mla_qk_norm_k',
          'mem_norm_g', 'mem_w_k', 'mem_w_v', 'mem_qk_norm_q', 'mem_qk_norm_k', 'out_norm_ssm', 'out_norm_mla', 'out_norm_mem',
          'w_out']


def make_in_maps(inp, SEQ, NPG, NPOOL, PAST):
    CH = SEQ // 4
    NOWN = CH // 128
    NPREV = 3 * NOWN
    NKT = NPREV + NOWN
    f32 = np.float32
    xp = np.asarray(inp['x_prompt'], f32); xs = np.asarray(inp['x_sample'], f32)
    ident = np.eye(128, dtype=f32)
    tri = np.triu(np.ones((128, 128), f32))
    smask = np.ones((128, 256), f32); smask[:, 0] = 0.0
    smask[:, 128::8] = 0.0
    tau = np.tile(np.arange(1, LS + 1, dtype=f32)[None, :], (128, 1))
    cckv = np.ascontiguousarray(np.asarray(inp['cache_ckv'], f32).reshape(NPOOL * 128, 256))
    ckr = np.ascontiguousarray(np.asarray(inp['cache_krope'], f32).reshape(NPOOL * 128, 64))
    wts = {n: np.ascontiguousarray(np.asarray(inp[n], f32)) for n in WNAMES}
    maps = []
    for c in range(8):
        s, j = c // 4, c % 4
        xall = np.zeros(((NKT + 1) * 128, D), f32)
        pos = np.zeros((NKT + 1) * 128, f32)
        kb = np.zeros((128, NKT), f32)
        for k in range(3):
            src = j - 3 + k
            r0 = k * CH
            if src >= 0:
                xall[r0:r0 + CH] = xp[s, src * CH:(src + 1) * CH]
                pos[r0:r0 + CH] = np.arange(src * CH, (src + 1) * CH)
            else:
                kb[:, k * NOWN:(k + 1) * NOWN] = NEG
        xall[3 * CH:4 * CH] = xp[s, j * CH:(j + 1) * CH]
        pos[3 * CH:4 * CH] = np.arange(j * CH, (j + 1) * CH)
        xall[4 * CH:] = xs[c * NSEQ:(c + 1) * NSEQ].reshape(128, D)
        pos[4 * CH:] = np.tile(PAST + np.arange(8), NSEQ)
        m = {
            'xall': xall, 'rope': rope_table(pos), 'kbias': kb,
            'mem': np.ascontiguousarray(np.asarray(inp['mem_prompt'], f32)[s]),
            'cckv': cckv, 'ckr': ckr,
            'cmk': np.ascontiguousarray(np.asarray(inp['cache_mem_k'], f32)[c * NSEQ:(c + 1) * NSEQ].reshape(NSEQ * 256, 512)),
            'cmv': np.ascontiguousarray(np.asarray(inp['cache_mem_v'], f32)[c * NSEQ:(c + 1) * NSEQ].reshape(NSEQ * 256, 512)),
            'st_re': np.ascontiguousarray(np.asarray(inp['state_ssm_re'], f32)[c * NSEQ:(c + 1) * NSEQ].reshape(NSEQ, 2048)),
            'st_im': np.ascontiguousarray(np.asarray(inp['state_ssm_im'], f32)[c * NSEQ:(c + 1) * NSEQ].reshape(NSEQ, 2048)),
            'ptab': np.ascontiguousarray(np.asarray(inp['page_table'], np.int32)[c * NSEQ:(c + 1) * NSEQ]),
            'ident': ident, 'tri': tri, 'smask': smask, 'tau': tau,
        }
        m.update(wts)
        maps.append(m)
    return maps


def assemble(r, SEQ):
    y_prompt = np.stack([np.concatenate([r[s * 4 + j]['y_p'] for j in range(4)], 0) for s in range(2)])
    ckv_p = np.stack([np.concatenate([r[s * 4 + j]['ckv_p'] for j in range(4)], 0) for s in range(2)])
    kr_p = np.stack([np.concatenate([r[s * 4 + j]['kr_p'] for j in range(4)], 0) for s in range(2)])
    y_sample = np.concatenate([r[c]['y_s'].reshape(NSEQ, 8, D) for c in range(8)], 0)
    ckv_s = np.concatenate([r[c]['ckv_s'].reshape(NSEQ, 8, 256) for c in range(8)], 0)
    kr_s = np.concatenate([r[c]['kr_s'].reshape(NSEQ, 8, 64) for c in range(8)], 0)
    memk = np.stack([r[s * 4]['memk'].reshape(256, 4, 128) for s in range(2)])
    memv = np.stack([r[s * 4]['memv'].reshape(256, 4, 128) for s in range(2)])
    sp_re = np.stack([r[s * 4 + 3]['sp_re'].reshape(32, 64) for s in range(2)])
    sp_im = np.stack([r[s * 4 + 3]['sp_im'].reshape(32, 64) for s in range(2)])
    ss_re = np.concatenate([r[c]['ss_re'].reshape(NSEQ, 32, 64) for c in range(8)], 0)
    ss_im = np.concatenate([r[c]['ss_im'].reshape(NSEQ, 32, 64) for c in range(8)], 0)
    outs = (y_prompt, y_sample, ckv_p, kr_p, ckv_s, kr_s, memk, memv, sp_re, sp_im, ss_re, ss_im)
    return tuple(np.ascontiguousarray(o, dtype=np.float32) for o in outs)


def run(inp, SEQ, PAST, stop=None, cores=None):
    NPG = PAST // 128
    NPOOL = int(np.asarray(inp['cache_ckv']).shape[0])
    nc = build(SEQ, NPG, NPOOL, stop)
    maps = make_in_maps(inp, SEQ, NPG, NPOOL, PAST)
    if cores is not None:
        res = run_bass_kernel_spmd(nc, [maps[c] for c in cores], core_ids=list(range(len(cores))))
        return {c: res.results[i] for i, c in enumerate(cores)}
    res = run_bass_kernel_spmd(nc, maps, core_ids=list(range(8)))
    return assemble(res.results, SEQ)


def kernel(**inputs):
    return run(inputs, 4096, 8192)
```

```python
import contextlib
import numpy as np
import concourse.bass as bass
import concourse.mybir as mybir
from concourse.bass_utils import run_bass_kernel_spmd

F32 = mybir.dt.float32
BF16 = mybir.dt.bfloat16
I32 = mybir.dt.int32
AF = mybir.ActivationFunctionType
ALU = mybir.AluOpType
AX = mybir.AxisListType

D = 2048
IN_W = 3904
EPS = 1e-6
NEG = -30000.0
NSEQ = 16
LS = 64
GT = 2
GC = GT * 128
C_U, C_GS, C_CQ, C_CKV, C_KR, C_GM, C_QM, C_GME = 0, 512, 1024, 1536, 1792, 1856, 2880, 3392
GELU_K = 2.0 * 0.7978845608028654


class Tracker:
    def __init__(self, nc, es):
        self.nc = nc
        self.es = es
        self.eng = {'pe': nc.tensor, 'act': nc.scalar, 'dve': nc.vector, 'pool': nc.gpsimd, 'sp': nc.sync}
        self.sem = {}
        self.cnt = {}
        for n in ['pe', 'act', 'dve', 'pool']:
            self.sem[n] = es.enter_context(nc.semaphore('c_' + n))
            self.cnt[n] = 0
        self.seen = {n: {} for n in self.eng}
        self.lastw = {}
        self.readers = {}
        self.nins = 0

    def _waits(self, e, reads, writes):
        need = {}

        def add(t):
            if t is None:
                return
            sn, v = t
            if need.get(sn, 0) < v:
                need[sn] = v
        for k in reads:
            add(self.lastw.get(k))
            if k.startswith('ps'):
                for t in self.readers.get(k, ()):
                    if t[0] != e:
                        add(t)
        for k in writes:
            add(self.lastw.get(k))
            for t in self.readers.get(k, ()):
                add(t)
        for sn, v in need.items():
            if sn == 'pe' and e == 'pe':
                continue
            if self.seen[e].get(sn, 0) < v:
                self.eng[e].wait_ge(self.sem[sn], v)
                self.seen[e][sn] = v
                self.nins += 1

    def _commit(self, tok, reads, writes):
        for k in reads:
            self.readers.setdefault(k, []).append(tok)
        for k in writes:
            self.lastw[k] = tok
            self.readers[k] = []

    def op(self, e, fn, R=(), W=()):
        self._waits(e, R, W)
        ins = fn(self.eng[e])
        self.cnt[e] += 1
        ins.then_inc(self.sem[e], 1)
        self.nins += 1
        self._commit((e, self.cnt[e]), R, W)

    def dma(self, q, stream, fn, R=(), W=()):
        if stream not in self.sem:
            self.sem[stream] = self.es.enter_context(self.nc.semaphore('d_' + stream))
            self.cnt[stream] = 0
        self._waits(q, R, W)
        if self.cnt[stream] and self.seen[q].get(stream, 0) < self.cnt[stream]:
            self.eng[q].wait_ge(self.sem[stream], self.cnt[stream])
            self.seen[q][stream] = self.cnt[stream]
            self.nins += 1
        ins = fn(self.eng[q])
        self.cnt[stream] += 16
        ins.then_inc(self.sem[stream], 16)
        self.nins += 1
        self._commit((stream, self.cnt[stream]), R, W)

    def barrier(self):
        for e in self.eng:
            for sn, v in self.cnt.items():
                if v == 0:
                    continue
                if self.seen[e].get(sn, 0) < v:
                    self.eng[e].wait_ge(self.sem[sn], v)
                    self.seen[e][sn] = v
                    self.nins += 1
        self.lastw = {}
        self.readers = {}

    def finish(self, q='sp'):
        for sn, v in self.cnt.items():
            if v == 0:
                continue
            self.eng[q].wait_ge(self.sem[sn], v)


class _Stop(Exception):
    pass


def build(SEQ, NPG, NPOOL, stop=None):
    CH = SEQ // 4
    NOWN = CH // 128
    NPREV = 3 * NOWN
    NKT = NPREV + NOWN
    NT = NKT + 1
    nc = bass.Bass("TRN2", target_bir_lowering=False)

    def din(name, shape, dt=F32):
        return nc.dram_tensor(name, list(shape), dt, kind="ExternalInput").ap()

    def dout(name, shape):
        return nc.dram_tensor(name, list(shape), F32, kind="ExternalOutput").ap()

    xall = din("xall", [NT * 128, D])
    rope = din("rope", [NT * 128, 128])
    kbias_d = din("kbias", [128, NKT])
    mem_d = din("mem", [256, D])
    cckv = din("cckv", [NPOOL * 128, 256])
    ckr = din("ckr", [NPOOL * 128, 64])
    cmk = din("cmk", [NSEQ * 256, 512])
    cmv = din("cmv", [NSEQ * 256, 512])
    st_re = din("st_re", [NSEQ, 2048])
    st_im = din("st_im", [NSEQ, 2048])
    ptab = din("ptab", [NSEQ, NPG], I32)
    ident_d = din("ident", [128, 128])
    tri_d = din("tri", [128, 128])
    smask_d = din("smask", [128, 256])
    tau_d = din("tau", [128, LS])
    norm_g = din("norm_g", [D]); w_in = din("w_in", [D, IN_W])
    a_re_d = din("ssm_a_re", [32, 64]); a_im_d = din("ssm_a_im", [32, 64]); ldt_d = din("ssm_log_dt", [32])
    b_re_d = din("ssm_b_re", [32, 64, 16]); b_im_d = din("ssm_b_im", [32, 64, 16])
    c_re_d = din("ssm_c_re", [32, 16, 64]); c_im_d = din("ssm_c_im", [32, 16, 64])
    ssm_d_d = din("ssm_d", [512]); glu_w_d = din("ssm_glu_w", [512, 512]); glu_b_d = din("ssm_glu_b", [512])
    gq_d = din("mla_q_norm_g", [512]); w_uq_d = din("mla_w_uq", [512, 1536])
    gkv_d = din("mla_kv_norm_g", [256]); w_ukv_d = din("mla_w_ukv", [256, 2048])
    gqq_d = din("mla_qk_norm_q", [192]); gqk_d = din("mla_qk_norm_k", [192])
    gmem_d = din("mem_norm_g", [D]); w_mk_d = din("mem_w_k", [D, 512]); w_mv_d = din("mem_w_v", [D, 512])
    gmq_d = din("mem_qk_norm_q", [128]); gmk_d = din("mem_qk_norm_k", [128])
    go_s_d = din("out_norm_ssm", [512]); go_m_d = din("out_norm_mla", [1024]); go_e_d = din("out_norm_mem", [512])
    w_out_d = din("w_out", [D, D])

    y_p = dout("y_p", [CH, D]); y_s = dout("y_s", [128, D])
    ckv_p = dout("ckv_p", [CH, 256]); kr_p = dout("kr_p", [CH, 64])
    ckv_s = dout("ckv_s", [128, 256]); kr_s = dout("kr_s", [128, 64])
    memk_o = dout("memk", [256, 512]); memv_o = dout("memv", [256, 512])
    sp_re = dout("sp_re", [16, 128]); sp_im = dout("sp_im", [16, 128])
    ss_re = dout("ss_re", [NSEQ, 2048]); ss_im = dout("ss_im", [NSEQ, 2048])

    es = contextlib.ExitStack()
    with es:
        T = Tracker(nc, es)

        def chk(name):
            if stop == name:
                raise _Stop()

        import os as _os
        DBG = _os.environ.get('KDBG')

        def dbg(name, ap, shape, keys, dt=F32):
            if not DBG:
                return
            d = nc.dram_tensor("dbg_" + name, list(shape), dt, kind="ExternalOutput").ap()
            T.dma('sp', 'dbg', lambda e: e.dma_start(out=d, in_=ap, allow_slow_non_contiguous=True), R=keys)

        try:
            def sb(name, shape, dt=F32):
                return es.enter_context(nc.sbuf_tensor("s_" + name, list(shape), dt))

            def ps(name, shape, dt=F32):
                return es.enter_context(nc.psum_tensor("p_" + name, list(shape), dt))

            psT = [ps("psT%d" % i, [128, 1024], BF16) for i in range(2)]
            psA = [ps("psA%d" % i, [128, 512], F32) for i in range(3)]
            psC = [ps("psC%d" % i, [128, 512], F32) for i in range(3)]
            rr = {'T': 0, 'A': 0}

            def nextT():
                i = rr['T']; rr['T'] = (i + 1) % 2
                return psT[i], 'psT%d' % i

            def nextA():
                i = rr['A']; rr['A'] = (i + 1) % 3
                return psA[i], 'psA%d' % i

            identf = sb("identf", [128, 128]); identb = sb("identb", [128, 128], BF16)
            trib = sb("trib", [128, 128], BF16); onesb = sb("onesb", [128, 128], BF16)
            smask = sb("smask", [128, 256]); taur = sb("taur", [128, LS])
            stg = sb("stg", [128, 1024])
            T.dma('sp', 'c0', lambda e: e.dma_start(out=identf[:], in_=ident_d[:, :]), W=['identf'])
            T.dma('sp', 'c0', lambda e: e.dma_start(out=stg[:, 0:128], in_=tri_d[:, :]), W=['stg'])
            T.dma('sp', 'c0', lambda e: e.dma_start(out=smask[:], in_=smask_d[:, :]), W=['smask'])
            T.dma('sp', 'c0', lambda e: e.dma_start(out=taur[:], in_=tau_d[:, :]), W=['taur'])
            T.op('dve', lambda e: e.tensor_copy(out=identb[:], in_=identf[:]), R=['identf'], W=['identb'])
            T.op('dve', lambda e: e.tensor_copy(out=trib[:], in_=stg[:, 0:128]), R=['stg'], W=['trib'])
            T.op('dve', lambda e: e.memset(onesb[:], 1.0), W=['onesb'])

            def transpose_f32(dst_ap, src_ap, npart, nfree, dkey, skey, scale=None):
                p, pk = nextA()
                T.op('pe', lambda e: e.transpose(out=p[0:nfree, 0:npart], in_=src_ap, identity=identf[0:npart, 0:npart]),
                     R=[skey, 'identf'], W=[pk])
                if scale is None:
                    T.op('dve', lambda e: e.tensor_copy(out=dst_ap, in_=p[0:nfree, 0:npart]), R=[pk], W=[dkey])
                else:
                    T.op('dve', lambda e: e.tensor_scalar(out=dst_ap, in0=p[0:nfree, 0:npart], scalar1=scale, scalar2=None,
                                                          op0=ALU.mult), R=[pk], W=[dkey])

            def load_col_into(dst_ap, dkey, vec_d, n):
                k = n // 128
                T.dma('sp', 'c0', lambda e: e.dma_start(out=stg[0:k, 0:128], in_=vec_d.rearrange("(k p) -> k p", p=128)),
                      W=['stg'])
                transpose_f32(dst_ap, stg[0:k, 0:128], k, 128, dkey, 'stg')

            def load_col(name, vec_d, n):
                t = sb(name, [128, n // 128])
                load_col_into(t[:], name, vec_d, n)
                return t

            def load_bc(name, vec_d, n):
                t = sb(name, [128, n])
                T.dma('sp', 'c0', lambda e: e.dma_start(out=t[:], in_=vec_d.partition_broadcast(128)), W=[name])
                return t

            gcol_in = load_col("gcol_in", norm_g, D)
            gcol_q = load_col("gcol_q", gq_d, 512)
            gcol_mem = load_col("gcol_mem", gmem_d, D)
            gcol_out = sb("gcol_out", [128, 16])
            load_col_into(gcol_out[:, 0:4], 'gcol_out', go_s_d, 512)
            load_col_into(gcol_out[:, 4:12], 'gcol_out', go_m_d, 1024)
            load_col_into(gcol_out[:, 12:16], 'gcol_out', go_e_d, 512)
            dcol = load_col("dcol", ssm_d_d, 512)
            diagD_b = sb("diagD_b", [128, 4, 128], BF16)
            for ct in range(4):
                T.op('dve', lambda e, ct=ct: e.tensor_scalar(out=diagD_b[:, ct, :], in0=identf[:], scalar1=dcol[:, ct:ct + 1],
                                                             scalar2=None, op0=ALU.mult), R=['identf', 'dcol'], W=['diagD_b'])
            gkv_bc = load_bc("gkv_bc", gkv_d, 256)
            gmk_bc = load_bc("gmk_bc", gmk_d, 128)
            gqk_bc = load_bc("gqk_bc", gqq_d, 192)
            T.dma('sp', 'c0', lambda e: e.dma_start(out=stg[:, 0:192], in_=gqk_d.partition_broadcast(128)), W=['stg'])
            T.op('dve', lambda e: e.scalar_tensor_tensor(out=gqk_bc[:], in0=gqk_bc[:], scalar=192.0 ** -0.5, in1=stg[:, 0:192],
                                                         op0=ALU.mult, op1=ALU.mult), R=['gqk_bc', 'stg'], W=['gqk_bc'])
            gmq_bc = load_bc("gmq_bc", gmq_d, 128)
            T.op('dve', lambda e: e.tensor_scalar(out=gmq_bc[:], in0=gmq_bc[:], scalar1=128.0 ** -0.5, scalar2=None,
                                                  op0=ALU.mult), R=['gmq_bc'], W=['gmq_bc'])
            glub_b = sb("glub_b", [1, 512], BF16)
            T.dma('sp', 'c0', lambda e: e.dma_start(out=stg[0:1, 0:512], in_=glu_b_d.rearrange("(o n) -> o n", o=1)),
                  W=['stg'])
            T.op('dve', lambda e: e.tensor_copy(out=glub_b[:], in_=stg[0:1, 0:512]), R=['stg'], W=['glub_b'])

            def load_w_rows(dst, dkey, w_d, kc_n, c0, ncols, gcol, dcol0=0):
                for kc in range(kc_n):
                    for cc0 in range(0, ncols, 1024):
                        n = min(1024, ncols - cc0)
                        T.dma('sp', 'wst', lambda e, kc=kc, cc0=cc0, n=n: e.dma_start(
                            out=stg[:, 0:n], in_=w_d[kc * 128:(kc + 1) * 128, c0 + cc0:c0 + cc0 + n]), W=['stg'])
                        if gcol is None:
                            T.op('dve', lambda e, kc=kc, cc0=cc0, n=n: e.tensor_copy(out=dst[:, kc, dcol0 + cc0:dcol0 + cc0 + n],
                                                                                     in_=stg[:, 0:n]), R=['stg'], W=[dkey])
                        else:
                            T.op('dve', lambda e, kc=kc, cc0=cc0, n=n: e.tensor_scalar(out=dst[:, kc, dcol0 + cc0:dcol0 + cc0 + n],
                                                                                       in0=stg[:, 0:n], scalar1=gcol[:, kc:kc + 1],
                                                                                       scalar2=None, op0=ALU.mult), R=['stg'], W=[dkey])

            w_uq_b = sb("w_uq_b", [128, 4, 1536], BF16)
            load_w_rows(w_uq_b, 'w_uq_b', w_uq_d, 4, 0, 1536, gcol_q)
            glu_w_b = sb("glu_w_b", [128, 4, 512], BF16)
            load_w_rows(glu_w_b, 'glu_w_b', glu_w_d, 4, 0, 512, None)
            w_uk_b = sb("w_uk_b", [128, 2, 1024], BF16)
            w_uv_b = sb("w_uv_b", [128, 2, 1024], BF16)
            for kc in range(2):
                for hf in range(2):
                    T.dma('sp', 'wst', lambda e, kc=kc, hf=hf: e.dma_start(
                        out=stg[:, 0:1024], in_=w_ukv_d[kc * 128:(kc + 1) * 128, hf * 1024:(hf + 1) * 1024]), W=['stg'])
                    sv = stg[:, 0:1024].rearrange("p (h c) -> p h c", c=256)
                    T.op('dve', lambda e, kc=kc, hf=hf, sv=sv: e.tensor_copy(
                        out=w_uk_b[:, kc, hf * 512:(hf + 1) * 512].rearrange("p (h c) -> p h c", c=128), in_=sv[:, :, 0:128]),
                        R=['stg'], W=['w_uk_b'])
                    T.op('dve', lambda e, kc=kc, hf=hf, sv=sv: e.tensor_copy(
                        out=w_uv_b[:, kc, hf * 512:(hf + 1) * 512].rearrange("p (h c) -> p h c", c=128), in_=sv[:, :, 128:256]),
                        R=['stg'], W=['w_uv_b'])
            w_ukT_b = sb("w_ukT_b", [128, 8, 256], BF16)
            for h in range(8):
                p, pk = nextT()
                for kc in range(2):
                    T.op('pe', lambda e, h=h, kc=kc, p=p: e.transpose(out=p[:, kc * 128:(kc + 1) * 128],
                                                                      in_=w_uk_b[:, kc, h * 128:(h + 1) * 128],
                                                                      identity=identb[:]), R=['w_uk_b', 'identb'], W=[pk])
                T.op('act', lambda e, h=h, p=p: e.copy(out=w_ukT_b[:, h, :], in_=p[:, 0:256]), R=[pk], W=['w_ukT_b'])

            chk('w')
            arena = sb("arena", [128, 15360], BF16)
            w_up_b = arena[:, 0:13312].rearrange("p (k c) -> p k c", k=16)
            xnT1 = arena[:, 13312:15360].rearrange("p (k c) -> p k c", k=16)
            memT = arena[:, 8192:12288].rearrange("p (t k c) -> p t k c", t=2, k=16)
            gates = arena[:, 0:GT * 2048].rearrange("p (t c) -> p t c", t=GT)
            QTn = arena[:, 4096:4096 + 8 * GC].rearrange("p (h c) -> p h c", h=8)
            QTr = arena[:, 6144:6144 + 8 * GC].rearrange("p (h c) -> p h c", h=8)
            xnT_g = arena[:, 8192:8192 + GT * 2048].rearrange("p (t k c) -> p t k c", t=GT, k=16)
            ymla = arena[:, 12288:12288 + GT * 1024].rearrange("p (t c) -> p t c", t=GT)
            QmT = arena[:, 14336:14336 + 4 * GC].rearrange("p (h c) -> p h c", h=4)

            load_w_rows(w_up_b, 'w_up_b', w_in, 16, C_U, 512, gcol_in, 0)
            load_w_rows(w_up_b, 'w_up_b', w_in, 16, C_CKV, 320, gcol_in, 512)

            xt = sb("xt", [128, D])
            are = sb("are", [128, 16]); aim = sb("aim", [128, 16]); dts = sb("dts", [128, 16])
            with nc.allow_non_contiguous_dma(reason="small one-time parameter loads"):
                T.dma('sp', 'c0', lambda e: e.dma_start(out=are[:], in_=a_re_d.rearrange("(t two) n -> (two n) t", two=2)),
                      W=['are'])
                T.dma('sp', 'c0', lambda e: e.dma_start(out=aim[:], in_=a_im_d.rearrange("(t two) n -> (two n) t", two=2)),
                      W=['aim'])
                lv = ldt_d.rearrange("(t two) -> two t", two=2)
                for hh in range(2):
                    T.dma('sp', 'c0', lambda e, hh=hh: e.dma_start(out=dts[hh * 64:(hh + 1) * 64, :],
                                                                   in_=lv[hh].partition_broadcast(64)), W=['dts'])
            T.op('act', lambda e: e.activation(out=dts[:], in_=dts[:], func=AF.Exp), R=['dts'], W=['dts'])
            theta = sb("theta", [128, 16]); rmag = sb("rmag", [128, 16])
            T.op('dve', lambda e: e.tensor_tensor(out=theta[:], in0=dts[:], in1=aim[:], op=ALU.mult), R=['dts', 'aim'], W=['theta'])
            T.op('dve', lambda e: e.tensor_tensor(out=rmag[:], in0=dts[:], in1=are[:], op=ALU.mult), R=['dts', 'are'], W=['rmag'])
            T.op('act', lambda e: e.activation(out=rmag[:], in_=rmag[:], func=AF.Exp), R=['rmag'], W=['rmag'])
            costab = sb("costab", [128, 16, LS]); sintab = sb("sintab", [128, 16, LS])
            angw = xt[:, 0:4 * LS]; angk = sb("angk", [128, 4 * LS], I32); angm = xt[:, 256:256 + 4 * LS]
            TAB = ['sintab', 'costab']

            def sin_table(dst, dkey, phase):
                for q4 in range(4):
                    for j in range(4):
                        i = q4 * 4 + j
                        T.op('dve', lambda e, i=i, j=j: e.tensor_scalar(out=angw[:, j * LS:(j + 1) * LS], in0=taur[:],
                                                                        scalar1=theta[:, i:i + 1], scalar2=None, op0=ALU.mult),
                             R=['taur', 'theta', 'angw'], W=['angw'])
                    T.op('dve', lambda e: e.tensor_scalar(out=angw[:], in0=angw[:], scalar1=1.0 / (2 * np.pi), scalar2=phase,
                                                          op0=ALU.mult, op1=ALU.add), R=['angw'], W=['angw'])
                    T.op('dve', lambda e: e.tensor_copy(out=angk[:], in_=angw[:]), R=['angw'], W=['angk'])
                    T.op('dve', lambda e: e.tensor_copy(out=angm[:], in_=angk[:]), R=['angk'], W=['angm'])
                    T.op('dve', lambda e: e.tensor_tensor(out=angw[:], in0=angw[:], in1=angm[:], op=ALU.subtract),
                         R=['angw', 'angm'], W=['angw'])
                    T.op('dve', lambda e: e.tensor_scalar(out=angm[:], in0=angw[:], scalar1=0.5, scalar2=None, op0=ALU.is_gt),
                         R=['angw'], W=['angm'])
                    T.op('dve', lambda e: e.tensor_tensor(out=angw[:], in0=angw[:], in1=angm[:], op=ALU.subtract),
                         R=['angw', 'angm'], W=['angw'])
                    T.op('dve', lambda e: e.tensor_scalar(out=angm[:], in0=angw[:], scalar1=-0.5, scalar2=None, op0=ALU.is_lt),
                         R=['angw'], W=['angm'])
                    T.op('dve', lambda e: e.tensor_tensor(out=angw[:], in0=angw[:], in1=angm[:], op=ALU.add),
                         R=['angw', 'angm'], W=['angw'])
                    T.op('act', lambda e, q4=q4: e.activation(out=dst[:, q4 * 4:(q4 + 1) * 4, :].rearrange("p a b -> p (a b)"), in_=angw[:],
                                                              func=AF.Sin, scale=2 * np.pi), R=['angw'], W=[dkey])

            sin_table(sintab, 'sintab', 0.0)
            sin_table(costab, 'costab', 0.25)
            s5t = xt[:, 512:640].rearrange("p (a b) -> p a b", a=8)
            abr, abi, den, fre, fim, nfim, t0_, t1_ = [s5t[:, i, :] for i in range(8)]
            S5 = ['s5t']
            T.op('dve', lambda e: e.tensor_tensor(out=abr, in0=rmag[:], in1=costab[:, :, 0], op=ALU.mult), R=['rmag'] + TAB, W=S5)
            T.op('dve', lambda e: e.tensor_tensor(out=abi, in0=rmag[:], in1=sintab[:, :, 0], op=ALU.mult), R=['rmag'] + TAB + S5, W=S5)
            T.op('dve', lambda e: e.tensor_tensor(out=den, in0=are[:], in1=are[:], op=ALU.mult), R=['are'] + S5, W=S5)
            T.op('dve', lambda e: e.tensor_tensor(out=t0_, in0=aim[:], in1=aim[:], op=ALU.mult), R=['aim'] + S5, W=S5)
            T.op('dve', lambda e: e.tensor_tensor(out=den, in0=den, in1=t0_, op=ALU.add), R=S5, W=S5)
            T.op('dve', lambda e: e.reciprocal(out=den, in_=den), R=S5, W=S5)
            T.op('dve', lambda e: e.tensor_scalar(out=t1_, in0=abr, scalar1=-1.0, scalar2=None, op0=ALU.add), R=S5, W=S5)
            T.op('dve', lambda e: e.tensor_tensor(out=fre, in0=t1_, in1=are[:], op=ALU.mult), R=S5 + ['are'], W=S5)
            T.op('dve', lambda e: e.tensor_tensor(out=t0_, in0=abi, in1=aim[:], op=ALU.mult), R=S5 + ['aim'], W=S5)
            T.op('dve', lambda e: e.tensor_tensor(out=fre, in0=fre, in1=t0_, op=ALU.add), R=S5, W=S5)
            T.op('dve', lambda e: e.tensor_tensor(out=fre, in0=fre, in1=den, op=ALU.mult), R=S5, W=S5)
            T.op('dve', lambda e: e.tensor_tensor(out=fim, in0=abi, in1=are[:], op=ALU.mult), R=S5 + ['are'], W=S5)
            T.op('dve', lambda e: e.tensor_tensor(out=t0_, in0=t1_, in1=aim[:], op=ALU.mult), R=S5 + ['aim'], W=S5)
            T.op('dve', lambda e: e.tensor_tensor(out=fim, in0=fim, in1=t0_, op=ALU.subtract), R=S5, W=S5)
            T.op('dve', lambda e: e.tensor_tensor(out=fim, in0=fim, in1=den, op=ALU.mult), R=S5, W=S5)
            T.op('dve', lambda e: e.tensor_scalar(out=nfim, in0=fim, scalar1=-1.0, scalar2=None, op0=ALU.mult), R=S5, W=S5)
            bst = xt[:, 640:1152].rearrange("p (c t q) -> p c t q", c=2, t=16)
            with nc.allow_non_contiguous_dma(reason="small one-time parameter loads"):
                T.dma('sp', 'c0', lambda e: e.dma_start(out=bst[:, 0, :, :],
                                                        in_=b_re_d.rearrange("(t two) n q -> (two n) t q", two=2)), W=['bst'])
                T.dma('sp', 'c0', lambda e: e.dma_start(out=bst[:, 1, :, :],
                                                        in_=b_im_d.rearrange("(t two) n q -> (two n) t q", two=2)), W=['bst'])
            BT_b = sb("BT_b", [128, 16, 2, 128], BF16)
            bexp = xt[:, 1152:1408].rearrange("p (c n) -> p c n", c=2); bt1 = xt[:, 1408:1440].rearrange("p (c n) -> p c n", c=2)
            for i in range(16):
                ga, gb = (2 * i) % 8, (2 * i + 1) % 8
                T.op('dve', lambda e: e.memset(bexp[:], 0.0), W=['bexp'])
                T.op('dve', lambda e, i=i: e.tensor_scalar(out=bt1[:, 0, :], in0=bst[:, 0, i, :], scalar1=s5t[:, 3, i:i + 1],
                                                           scalar2=None, op0=ALU.mult), R=['bst', 's5t'], W=['bt1'])
                T.op('dve', lambda e, i=i: e.scalar_tensor_tensor(out=bt1[:, 0, :], in0=bst[:, 1, i, :], scalar=s5t[:, 5, i:i + 1],
                                                                  in1=bt1[:, 0, :], op0=ALU.mult, op1=ALU.add),
                     R=['bst', 's5t', 'bt1'], W=['bt1'])
                T.op('dve', lambda e, i=i: e.tensor_scalar(out=bt1[:, 1, :], in0=bst[:, 1, i, :], scalar1=s5t[:, 3, i:i + 1],
                                                           scalar2=None, op0=ALU.mult), R=['bst', 's5t', 'bt1'], W=['bt1'])
                T.op('dve', lambda e, i=i: e.scalar_tensor_tensor(out=bt1[:, 1, :], in0=bst[:, 0, i, :], scalar=s5t[:, 4, i:i + 1],
                                                                  in1=bt1[:, 1, :], op0=ALU.mult, op1=ALU.add),
                     R=['bst', 's5t', 'bt1'], W=['bt1'])
                for c in range(2):
                    T.op('dve', lambda e, c=c, ga=ga: e.tensor_copy(out=bexp[0:64, c, ga * 16:ga * 16 + 16], in_=bt1[0:64, c, :]),
                         R=['bt1', 'bexp'], W=['bexp'])
                    T.op('dve', lambda e, c=c, gb=gb: e.tensor_copy(out=bexp[64:128, c, gb * 16:gb * 16 + 16], in_=bt1[64:128, c, :]),
                         R=['bt1', 'bexp'], W=['bexp'])
                for c in range(2):
                    transpose_f32(BT_b[:, i, c, :], bexp[:, c, :], 128, 128, 'BT_b', 'bexp')
            CT_b = sb("CT_b", [128, 16, 2, 32], BF16)
            T.op('dve', lambda e: e.memset(CT_b[:], 0.0), W=['CT_b'])
            cpad = xt[:, 1536:1664]; cT = xt[:, 1664:1792]
            for c, cd in enumerate((c_re_d, c_im_d)):
                cv = cd.rearrange("g p n -> (g p) n")
                for c4 in range(4):
                    T.dma('sp', 'c0', lambda e, c4=c4, cv=cv: e.dma_start(out=cpad[:, 0:64], in_=cv[c4 * 128:(c4 + 1) * 128, :]), W=['cpad'])
                    T.dma('sp', 'c0', lambda e, c4=c4, cv=cv: e.dma_start(out=cpad[:, 64:128], in_=cv[c4 * 128:(c4 + 1) * 128, :]), W=['cpad'])
                    transpose_f32(cT[:], cpad[:], 128, 128, 'cT', 'cpad', scale=(1.0 if c == 0 else -1.0))
                    for k in range(4):
                        i = 4 * c4 + k
                        T.op('dve', lambda e, i=i, k=k, c=c: e.tensor_copy(out=CT_b[0:64, i, c, 0:16],
                                                                          in_=cT[0:64, (2 * k) * 16:(2 * k) * 16 + 16]),
                             R=['cT', 'CT_b'], W=['CT_b'])
                        T.op('dve', lambda e, i=i, k=k, c=c: e.tensor_copy(out=CT_b[64:128, i, c, 16:32],
                                                                          in_=cT[64:128, (2 * k + 1) * 16:(2 * k + 1) * 16 + 16]),
                             R=['cT', 'CT_b'], W=['CT_b'])

            dbg('theta', theta[:], [128, 16], ['theta']); dbg('rmag', rmag[:], [128, 16], ['rmag'])
            dbg('costab', costab[:].rearrange("p a b -> p (a b)"), [128, 16 * LS], ['costab'])
            dbg('sintab', sintab[:].rearrange("p a b -> p (a b)"), [128, 16 * LS], ['sintab'])
            dbg('s5t', xt[:, 512:640], [128, 128], ['s5t'])
            chk('s5')
            hre = sb("hre", [128, 16, NSEQ]); him = sb("him", [128, 16, NSEQ])
            T.op('dve', lambda e: e.memset(hre[:], 0.0), W=['hst'])
            T.op('dve', lambda e: e.memset(him[:], 0.0), R=['hst'], W=['hst'])
            GS = 2
            s5tmp = [sb("s5tmp%d" % k, [128, GS, 128]) for k in range(4)]
            bpr = sb("bpr", [128, GS, 128]); bpi = sb("bpi", [128, GS, 128])
            d0t = sb("d0t", [128, GS, 128])
            h_b = sb("h_b", [128, GS, 2, 128], BF16)
            hend = sb("hend", [128, 4, GS, NSEQ])

            dbg_once = []

            def ssm_step(uT_get, ncols, S, L, col0, want_out, gis=None):
                mcol = 0 if S == 1 else 128
                for gi in (range(16 // GS) if gis is None else gis):
                    p, pk = nextA()
                    pv = p[:, 0:GS * 2 * ncols].rearrange("p (j c n) -> p j c n", j=GS, c=2)
                    for j in range(GS):
                        i = gi * GS + j
                        for c in range(2):
                            T.op('pe', lambda e, i=i, j=j, c=c, pv=pv: e.matmul(pv[:, j, c, :], lhsT=BT_b[:, i, c, :],
                                                                               rhs=uT_get(i // 4), start=True, stop=True),
                                 R=['BT_b', 'uT'], W=[pk])
                    isl = slice(gi * GS, gi * GS + GS)

                    def tab(tb):
                        a = tb[:, isl, 0:L]
                        if S == 1:
                            return a
                        return a.unsqueeze(2).to_broadcast([128, GS, S, L])

                    def v4(t):
                        a = t[:, :, 0:ncols]
                        if S == 1:
                            return a
                        return a.rearrange("p j (s l) -> p j s l", l=L)

                    def pvc(c):
                        a = pv[:, :, c, :]
                        if S == 1:
                            return a
                        return a.rearrange("p j (s l) -> p j s l", l=L)
                    t1, t2, t3, t4 = s5tmp
                    T.op('dve', lambda e: e.tensor_tensor(out=v4(t1), in0=pvc(0), in1=tab(costab), op=ALU.mult), R=[pk] + TAB, W=['s5tmp0'])
                    T.op('dve', lambda e: e.tensor_tensor(out=v4(t2), in0=pvc(1), in1=tab(sintab), op=ALU.mult), R=[pk] + TAB, W=['s5tmp1'])
                    T.op('dve', lambda e: e.tensor_tensor(out=v4(t3), in0=pvc(1), in1=tab(costab), op=ALU.mult), R=[pk] + TAB, W=['s5tmp2'])
                    T.op('dve', lambda e: e.tensor_tensor(out=v4(t4), in0=pvc(0), in1=tab(sintab), op=ALU.mult), R=[pk] + TAB, W=['s5tmp3'])
                    T.op('pool', lambda e: e.tensor_tensor(out=bpr[:, :, 0:ncols], in0=t1[:, :, 0:ncols], in1=t2[:, :, 0:ncols], op=ALU.add),
                         R=['s5tmp0', 's5tmp1'], W=['bpr'])
                    T.op('pool', lambda e: e.tensor_tensor(out=bpi[:, :, 0:ncols], in0=t3[:, :, 0:ncols], in1=t4[:, :, 0:ncols], op=ALU.subtract),
                         R=['s5tmp2', 's5tmp3'], W=['bpi'])
                    if DBG and not dbg_once and gi == 0:
                        dbg('t1', t1[:].rearrange("p a b -> p (a b)"), [128, 256], ['s5tmp0'])
                        dbg('uT', uT[:].rearrange("p a b -> p (a b)"), [128, 512], ['uT'], BF16)
                        dbg('ubg', u_bg[:, 0, :], [128, 512], ['u_bg'], BF16)
                        dbg('wup', w_up_b[:, 0:2, :].rearrange("p a b -> p (a b)"), [128, 1664], ['w_up_b'], BF16)
                        dbg('gcol', gcol_in[:], [128, 16], ['gcol_in'])
                        dbg('xnT', xnT1[:].rearrange("p a b -> p (a b)"), [128, 2048], ['xnT1'], BF16)
                        dbg('BT', BT_b[:, 0:2, :, :].rearrange("p a b c -> p (a b c)"), [128, 512], ['BT_b'], BF16)
                        dbg('t4', t4[:].rearrange("p a b -> p (a b)"), [128, 256], ['s5tmp3'])
                        dbg('bpr0', bpr[:].rearrange("p a b -> p (a b)"), [128, 256], ['bpr'])
                        dbg('bpi0', bpi[:].rearrange("p a b -> p (a b)"), [128, 256], ['bpi'])
                    if S == 1:
                        for j in range(GS):
                            i = gi * GS + j
                            rb = rmag[:, i:i + 1].to_broadcast([128, ncols])
                            T.op('dve', lambda e, j=j, i=i, rb=rb: e.tensor_tensor_scan(out=bpr[:, j, 0:ncols], data0=rb, data1=bpr[:, j, 0:ncols],
                                                                                       initial=hre[:, i, 0:1], op0=ALU.mult, op1=ALU.add),
                                 R=['rmag', 'bpr', 'hst'], W=['bpr'])
                            T.op('dve', lambda e, j=j, i=i, rb=rb: e.tensor_tensor_scan(out=bpi[:, j, 0:ncols], data0=rb, data1=bpi[:, j, 0:ncols],
                                                                                       initial=him[:, i, 0:1], op0=ALU.mult, op1=ALU.add),
                                 R=['rmag', 'bpi', 'hst'], W=['bpi'])
                    else:
                        for j in range(GS):
                            i = gi * GS + j
                            T.op('dve', lambda e, i=i, j=j: e.tensor_scalar(out=d0t[:, j, 0:ncols], in0=smask[:, mcol:mcol + ncols],
                                                                            scalar1=rmag[:, i:i + 1], scalar2=None, op0=ALU.mult),
                                 R=['smask', 'rmag', 'd0t'], W=['d0t'])
                            fr = bpr[:, j, 0:ncols].rearrange("p (s l) -> p s l", l=L)[:, :, 0]
                            fi = bpi[:, j, 0:ncols].rearrange("p (s l) -> p s l", l=L)[:, :, 0]
                            T.op('dve', lambda e, i=i, fr=fr: e.scalar_tensor_tensor(out=fr, in0=hre[:, i, 0:S], scalar=rmag[:, i:i + 1],
                                                                                     in1=fr, op0=ALU.mult, op1=ALU.add),
                                 R=['hst', 'rmag', 'bpr'], W=['bpr'])
                            T.op('dve', lambda e, i=i, fi=fi: e.scalar_tensor_tensor(out=fi, in0=him[:, i, 0:S], scalar=rmag[:, i:i + 1],
                                                                                     in1=fi, op0=ALU.mult, op1=ALU.add),
                                 R=['hst', 'rmag', 'bpi'], W=['bpi'])
                        for j in range(GS):
                            T.op('dve', lambda e, j=j: e.tensor_tensor_scan(out=bpr[:, j, 0:ncols], data0=d0t[:, j, 0:ncols],
                                                                            data1=bpr[:, j, 0:ncols], initial=0.0,
                                                                            op0=ALU.mult, op1=ALU.add), R=['d0t', 'bpr'], W=['bpr'])
                            T.op('dve', lambda e, j=j: e.tensor_tensor_scan(out=bpi[:, j, 0:ncols], data0=d0t[:, j, 0:ncols],
                                                                            data1=bpi[:, j, 0:ncols], initial=0.0,
                                                                            op0=ALU.mult, op1=ALU.add), R=['d0t', 'bpi'], W=['bpi'])
                    if DBG and not dbg_once and gi == 0:
                        dbg('bpr1', bpr[:].rearrange("p a b -> p (a b)"), [128, 256], ['bpr'])
                        dbg('bpi1', bpi[:].rearrange("p a b -> p (a b)"), [128, 256], ['bpi'])
                        dbg('d0t', d0t[:].rearrange("p a b -> p (a b)"), [128, 256], ['d0t'])
                        dbg_once.append(1)
                    gl_r = bpr[:, :, 0:ncols].rearrange("p j (s l) -> p j s l", l=L)[:, :, :, L - 1]
                    gl_i = bpi[:, :, 0:ncols].rearrange("p j (s l) -> p j s l", l=L)[:, :, :, L - 1]
                    cl = costab[:, isl, L - 1:L].to_broadcast([128, GS, S])
                    sl = sintab[:, isl, L - 1:L].to_broadcast([128, GS, S])
                    e1, e2, e3, e4 = [hend[:, k, :, 0:S] for k in range(4)]
                    T.op('pool', lambda e: e.tensor_tensor(out=e1, in0=gl_r, in1=cl, op=ALU.mult), R=['bpr'] + TAB, W=['hend'])
                    T.op('pool', lambda e: e.tensor_tensor(out=e2, in0=gl_i, in1=sl, op=ALU.mult), R=['bpi', 'hend'] + TAB, W=['hend'])
                    T.op('pool', lambda e: e.tensor_tensor(out=e3, in0=gl_r, in1=sl, op=ALU.mult), R=['bpr', 'hend'] + TAB, W=['hend'])
                    T.op('pool', lambda e: e.tensor_tensor(out=e4, in0=gl_i, in1=cl, op=ALU.mult), R=['bpi', 'hend'] + TAB, W=['hend'])
                    T.op('pool', lambda e: e.tensor_tensor(out=hre[:, isl, 0:S], in0=e1, in1=e2, op=ALU.subtract), R=['hend', 'hst'], W=['hst'])
                    T.op('pool', lambda e: e.tensor_tensor(out=him[:, isl, 0:S], in0=e3, in1=e4, op=ALU.add), R=['hend', 'hst'], W=['hst'])
                    if want_out:
                        T.op('pool', lambda e: e.tensor_tensor(out=v4(t1), in0=v4(bpr), in1=tab(costab), op=ALU.mult), R=['bpr'] + TAB, W=['s5tmp0'])
                        T.op('pool', lambda e: e.tensor_tensor(out=v4(t2), in0=v4(bpi), in1=tab(sintab), op=ALU.mult), R=['bpi'] + TAB, W=['s5tmp1'])
                        T.op('dve', lambda e: e.tensor_tensor(out=v4(t3), in0=v4(bpr), in1=tab(sintab), op=ALU.mult), R=['bpr'] + TAB, W=['s5tmp2'])
                        T.op('dve', lambda e: e.tensor_tensor(out=v4(t4), in0=v4(bpi), in1=tab(costab), op=ALU.mult), R=['bpi'] + TAB, W=['s5tmp3'])
                        T.op('pool', lambda e: e.tensor_tensor(out=h_b[:, :, 0, col0:col0 + ncols], in0=t1[:, :, 0:ncols],
                                                               in1=t2[:, :, 0:ncols], op=ALU.subtract), R=['s5tmp0', 's5tmp1', 'h_b'], W=['h_b'])
                        T.op('pool', lambda e: e.tensor_tensor(out=h_b[:, :, 1, col0:col0 + ncols], in0=t3[:, :, 0:ncols],
                                                               in1=t4[:, :, 0:ncols], op=ALU.add), R=['s5tmp2', 's5tmp3', 'h_b'], W=['h_b'])

            T.barrier()
            xnb = sb("xnb", [128, D], BF16)
            ropet = sb("ropet", [128, 128])
            sm = sb("sm", [128, 64])
            junk = sb("junk", [128, 512], BF16)
            ckvn_f = sb("ckvn_f", [128, 256]); krr_f = sb("krr_f", [128, 64]); krt = sb("krt", [128, 64])
            ckvn_b = sb("ckvn_b", [128, 256], BF16)
            krr_b = sb("krr_b", [128, 64], BF16)
            uT = sb("uT", [128, 4, 128], BF16); u_bg = sb("u_bg", [128, GT, 512], BF16)
            ckvT_all = sb("ckvT_all", [128, 2, NKT * 128], BF16)
            krT_all = sb("krT_all", [64, NKT * 128], BF16)
            KA = max(NKT, 16)
            ckv1_all = sb("ckv1_all", [128, KA, 256], BF16)
            c1flat = ckv1_all[:].rearrange("p a b -> p (a b)")
            rk_all = sb("rk_all", [128, NKT, 8])
            kbias = sb("kbias", [128, NKT])
            T.dma('sp', 'c0', lambda e: e.dma_start(out=kbias[:], in_=kbias_d[:, :]), W=['kbias'])
            ckvT_s = sb("ckvT_s", [128, 2, 128], BF16); krT_s = sb("krT_s", [64, 128], BF16)

            def rstd_of(src_ap, n, dst_ap, skey):
                T.op('act', lambda e: e.activation(out=junk[:, 0:n], in_=src_ap, func=AF.Square, accum_out=dst_ap),
                     R=[skey, 'sm'], W=['junk', 'sm'])
                T.op('act', lambda e: e.activation(out=dst_ap, in_=dst_ap, func=AF.Sqrt, scale=1.0 / n, bias=EPS), R=['sm'], W=['sm'])
                T.op('dve', lambda e: e.reciprocal(out=dst_ap, in_=dst_ap), R=['sm'], W=['sm'])

            def norm_transpose(src_d_ap, dst_fn, dkey):
                T.dma('sp', 'xld', lambda e: e.dma_start(out=xt[:], in_=src_d_ap), W=['xt'])
                T.op('act', lambda e: e.activation(out=xnb[:, 0:1024], in_=xt[:, 0:1024], func=AF.Square, accum_out=sm[:, 0:1]),
                     R=['xt', 'sm'], W=['xnb', 'sm'])
                T.op('act', lambda e: e.activation(out=xnb[:, 1024:2048], in_=xt[:, 1024:2048], func=AF.Square, accum_out=sm[:, 1:2]),
                     R=['xt', 'sm'], W=['xnb', 'sm'])
                T.op('dve', lambda e: e.tensor_tensor(out=sm[:, 0:1], in0=sm[:, 0:1], in1=sm[:, 1:2], op=ALU.add), R=['sm'], W=['sm'])
                T.op('act', lambda e: e.activation(out=sm[:, 0:1], in_=sm[:, 0:1], func=AF.Sqrt, scale=1.0 / D, bias=EPS), R=['sm'], W=['sm'])
                T.op('dve', lambda e: e.reciprocal(out=sm[:, 0:1], in_=sm[:, 0:1]), R=['sm'], W=['sm'])
                T.op('dve', lambda e: e.tensor_scalar(out=xnb[:], in0=xt[:], scalar1=sm[:, 0:1], scalar2=None, op0=ALU.mult),
                     R=['xt', 'sm', 'xnb'], W=['xnb'])
                for half in range(2):
                    p, pk = nextT()
                    for k in range(8):
                        kc = half * 8 + k
                        T.op('pe', lambda e, k=k, kc=kc, p=p: e.transpose(out=p[:, k * 128:(k + 1) * 128],
                                                                          in_=xnb[:, kc * 128:(kc + 1) * 128], identity=identb[:]),
                             R=['xnb', 'identb'], W=[pk])
                    if half == 0:
                        T.op('act', lambda e, p=p, half=half: e.copy(out=dst_fn(half), in_=p[:, :]), R=[pk, dkey], W=[dkey])
                    else:
                        T.op('dve', lambda e, p=p, half=half: e.tensor_copy(out=dst_fn(half), in_=p[:, :]), R=[pk, dkey], W=[dkey])

            sqb = sb("sqb", [128, 1024], BF16)
            kst = [sb("kst%d" % i, [128, 16]) for i in range(2)]
            kslot = [0]

            def key_norms(cT, cTk, kr_ap, krk, nk, rk_dst, rkk):
                sl_ = kslot[0]; kslot[0] = 1 - sl_
                ks, ksk = kst[sl_], 'kst%d' % sl_
                pa, pak = nextA()
                pb, pbk = nextA()
                for hb, (pp, ppk) in enumerate(((pa, pak), (pb, pbk))):
                    for kc in range(2):
                        T.op('pe', lambda e, kc=kc, pp=pp, hb=hb: e.matmul(pp[0:nk, :], lhsT=cT[:, kc, :],
                                                                          rhs=w_uk_b[:, kc, hb * 512:(hb + 1) * 512],
                                                                          start=(kc == 0), stop=(kc == 1)),
                             R=[cTk, 'w_uk_b'], W=[ppk])
                    T.op('act', lambda e, pp=pp, hb=hb: e.activation(out=sqb[0:nk, hb * 512:(hb + 1) * 512], in_=pp[0:nk, :],
                                                                     func=AF.Square), R=[ppk], W=['sqb%d' % hb])
                T.op('dve', lambda e: e.tensor_reduce(out=ks[0:nk, 0:8], in_=sqb[0:nk, :].rearrange("p (h d) -> p h d", d=128),
                                                      axis=AX.X, op=ALU.add), R=['sqb0', 'sqb1'], W=[ksk])
                T.op('act', lambda e: e.activation(out=junk[0:nk, 0:64], in_=kr_ap, func=AF.Square, accum_out=ks[0:nk, 8:9]),
                     R=[krk, ksk], W=['junk', ksk])
                T.op('dve', lambda e: e.tensor_scalar(out=ks[0:nk, 0:8], in0=ks[0:nk, 0:8], scalar1=ks[0:nk, 8:9], scalar2=None,
                                                      op0=ALU.add), R=[ksk], W=[ksk])
                T.op('act', lambda e: e.activation(out=ks[0:nk, 0:8], in_=ks[0:nk, 0:8], func=AF.Ln, scale=1.0 / 192, bias=EPS),
                     R=[ksk], W=[ksk])
                T.op('act', lambda e: e.activation(out=rk_dst, in_=ks[0:nk, 0:8], func=AF.Exp, scale=-0.5), R=[ksk, rkk], W=[rkk])

            def latent_post(pck, pckk, row0, out_ckv, out_kr, orow0, cT_dst, cTk, kT_dst, kTk, c1_dst, c1k, rk_dst):
                T.dma('sp', 'rld', lambda e: e.dma_start(out=ropet[:], in_=rope[row0:row0 + 128, :]), W=['ropet'])
                rstd_of(pck[:, 0:256], 256, sm[:, 2:3], pckk)
                T.op('dve', lambda e: e.scalar_tensor_tensor(out=ckvn_f[:], in0=pck[:, 0:256], scalar=sm[:, 2:3], in1=gkv_bc[:],
                                                             op0=ALU.mult, op1=ALU.mult), R=[pckk, 'sm', 'gkv_bc'], W=['ckvn_f'])
                T.op('dve', lambda e: e.tensor_tensor(out=krr_f[:], in0=pck[:, 256:320], in1=ropet[:, 0:64], op=ALU.mult),
                     R=[pckk, 'ropet'], W=['krr_f'])
                T.op('dve', lambda e: e.tensor_tensor(out=krt[:, 0:32], in0=pck[:, 288:320], in1=ropet[:, 64:96], op=ALU.mult),
                     R=[pckk, 'ropet'], W=['krt'])
                T.op('dve', lambda e: e.tensor_tensor(out=krt[:, 32:64], in0=pck[:, 256:288], in1=ropet[:, 96:128], op=ALU.mult),
                     R=[pckk, 'ropet', 'krt'], W=['krt'])
                T.op('dve', lambda e: e.tensor_tensor(out=krr_f[:], in0=krr_f[:], in1=krt[:], op=ALU.add), R=['krr_f', 'krt'], W=['krr_f'])
                if out_ckv is not None:
                    T.dma('sp', 'ost', lambda e: e.dma_start(out=out_ckv[orow0:orow0 + 128, :], in_=ckvn_f[:]), R=['ckvn_f'])
                    T.dma('sp', 'ost', lambda e: e.dma_start(out=out_kr[orow0:orow0 + 128, :], in_=krr_f[:]), R=['krr_f'])
                T.op('pool', lambda e: e.tensor_copy(out=ckvn_b[:], in_=ckvn_f[:]), R=['ckvn_f'], W=['ckvn_b'])
                T.op('pool', lambda e: e.tensor_copy(out=krr_b[:], in_=krr_f[:]), R=['krr_f'], W=['krr_b'])
                if c1_dst is not None:
                    T.op('pool', lambda e: e.tensor_copy(out=c1_dst, in_=ckvn_f[:]), R=['ckvn_f', c1k], W=[c1k])
                p, pk = nextT()
                for kc in range(2):
                    T.op('pe', lambda e, kc=kc: e.transpose(out=p[:, kc * 128:(kc + 1) * 128], in_=ckvn_b[:, kc * 128:(kc + 1) * 128],
                                                            identity=identb[:]), R=['ckvn_b', 'identb'], W=[pk])
                T.op('pe', lambda e: e.transpose(out=p[0:64, 256:384], in_=krr_b[:], identity=identb[:]), R=['krr_b', 'identb'], W=[pk])
                T.op('act', lambda e: e.copy(out=cT_dst, in_=p[:, 0:256].rearrange("p (a b) -> p a b", a=2)), R=[pk, cTk], W=[cTk])
                T.op('act', lambda e: e.copy(out=kT_dst, in_=p[0:64, 256:384]), R=[pk, kTk], W=[kTk])
                if rk_dst is not None:
                    key_norms(cT_dst, cTk, krr_f[:], 'krr_f', 128, rk_dst, 'rk_all')

            def u_transpose(src_b_ap, skey):
                p, pk = nextT()
                for k in range(4):
                    T.op('pe', lambda e, k=k: e.transpose(out=p[:, k * 128:(k + 1) * 128], in_=src_b_ap[:, k * 128:(k + 1) * 128],
                                                          identity=identb[:]), R=[skey, 'identb'], W=[pk])
                T.op('dve', lambda e: e.tensor_copy(out=uT[:].rearrange("p a b -> p (a b)"), in_=p[:, 0:512]), R=[pk], W=['uT'])

            for t in range(NPREV):
                norm_transpose(xall[t * 128:(t + 1) * 128, :],
                               lambda half: xnT1[:, half * 8:(half + 1) * 8, :].rearrange("p a b -> p (a b)"), 'xnT1')
                pu, puk = nextA()
                for kc in range(16):
                    T.op('pe', lambda e, kc=kc: e.matmul(pu[:, 0:512], lhsT=xnT1[:, kc, :], rhs=w_up_b[:, kc, 0:512],
                                                         start=(kc == 0), stop=(kc == 15)), R=['xnT1', 'w_up_b'], W=[puk])
                pc, pck = nextA()
                for kc in range(16):
                    T.op('pe', lambda e, kc=kc: e.matmul(pc[:, 0:320], lhsT=xnT1[:, kc, :], rhs=w_up_b[:, kc, 512:832],
                                                         start=(kc == 0), stop=(kc == 15)), R=['xnT1', 'w_up_b'], W=[pck])
                T.op('act', lambda e: e.copy(out=u_bg[:, 0, :], in_=pu[:, 0:512]), R=[puk], W=['u_bg'])
                latent_post(pc, pck, t * 128, None, None, 0, ckvT_all[:, :, t * 128:(t + 1) * 128], 'ckvT_all',
                            krT_all[0:64, t * 128:(t + 1) * 128], 'krT_all', ckv1_all[:, t, :], 'ckv1_all', rk_all[:, t, :])
                u_transpose(u_bg[:, 0, :], 'u_bg')
                for sub in range(128 // LS):
                    ssm_step(lambda ct, sub=sub: uT[:, ct, sub * LS:(sub + 1) * LS], LS, 1, LS, 0, False)
            dbg('hre', hre[:, :, 0], [128, 16], ['hst']); dbg('him', him[:, :, 0], [128, 16], ['hst'])
            T.barrier()

            chk('P')
            memKT_b = sb("memKT_b", [128, 4, 256], BF16)
            memV_b = sb("memV_b", [128, 2, 512], BF16)
            wblk = sb("wblk", [128, 16, 256], BF16)
            yf = sb("yf", [128, 512]); yg = sb("yg", [128, 512])
            ob = sb("ob", [128, 512], BF16)

            def stream_wblock(w_d, c0, ncols, gcol):
                for q8 in range(8):
                    hf = q8 % 2
                    sg = stg[:, hf * 512:hf * 512 + 2 * ncols]
                    sk = 'stg%d' % hf
                    T.dma('sp', 'wst%d' % hf, lambda e, q8=q8, sg=sg: e.dma_start(
                        out=sg.rearrange("p (k c) -> p k c", k=2),
                        in_=w_d[q8 * 256:(q8 + 1) * 256, c0:c0 + ncols].rearrange("(k p) c -> p k c", p=128)), W=[sk])
                    for k in range(2):
                        kc = q8 * 2 + k
                        if k % 2 == 0:
                            T.op('dve', lambda e, k=k, kc=kc, sg=sg: e.tensor_scalar(out=wblk[:, kc, 0:ncols], in0=sg[:, k * ncols:(k + 1) * ncols],
                                                                                     scalar1=gcol[:, kc:kc + 1], scalar2=None, op0=ALU.mult),
                                 R=[sk, 'wblk'], W=['wblk'])
                        else:
                            T.op('act', lambda e, k=k, kc=kc, sg=sg: e.activation(out=wblk[:, kc, 0:ncols], in_=sg[:, k * ncols:(k + 1) * ncols],
                                                                                  func=AF.Copy, scale=gcol[:, kc:kc + 1]),
                                 R=[sk, 'wblk'], W=['wblk'])
                return wblk, 'wblk'

            for mt in range(2):
                norm_transpose(mem_d[mt * 128:(mt + 1) * 128, :],
                               lambda half, mt=mt: memT[:, mt, half * 8:(half + 1) * 8, :].rearrange("p a b -> p (a b)"), 'memT')
            chk('M0')
            for which, w_d in enumerate((w_mk_d, w_mv_d)):
                for cb in range(2):
                    wb, wk = stream_wblock(w_d, cb * 256, 256, gcol_mem)
                    chk('M1_%d_%d' % (which, cb))
                    for mt in range(2):
                        p, pk = nextA()
                        for kc in range(16):
                            T.op('pe', lambda e, kc=kc, mt=mt, p=p: e.matmul(p[:, 0:256], lhsT=memT[:, mt, kc, :], rhs=wb[:, kc, 0:256],
                                                                             start=(kc == 0), stop=(kc == 15)), R=['memT', wk], W=[pk])
                        chk('M2')
                        if which == 1:
                            T.op('act', lambda e, p=p: e.copy(out=yf[:, 0:256], in_=p[:, 0:256]), R=[pk], W=['yf'])
                            T.op('dve', lambda e, p=p, mt=mt, cb=cb: e.tensor_copy(out=memV_b[:, mt, cb * 256:(cb + 1) * 256], in_=p[:, 0:256]),
                                 R=[pk, 'memV_b'], W=['memV_b'])
                            T.dma('sp', 'ost', lambda e, mt=mt, cb=cb: e.dma_start(out=memv_o[mt * 128:(mt + 1) * 128, cb * 256:(cb + 1) * 256],
                                                                                   in_=yf[:, 0:256]), R=['yf'])
                        else:
                            for hh in range(2):
                                rstd_of(p[:, hh * 128:(hh + 1) * 128], 128, sm[:, 3:4], pk)
                                T.op('dve', lambda e, p=p, hh=hh: e.scalar_tensor_tensor(out=yf[:, hh * 128:(hh + 1) * 128],
                                                                                         in0=p[:, hh * 128:(hh + 1) * 128], scalar=sm[:, 3:4],
                                                                                         in1=gmk_bc[:], op0=ALU.mult, op1=ALU.mult),
                                     R=[pk, 'sm', 'gmk_bc', 'yf'], W=['yf'])
                            chk('M3')
                            T.op('dve', lambda e: e.tensor_copy(out=ob[:, 0:256], in_=yf[:, 0:256]), R=['yf'], W=['ob'])
                            T.dma('sp', 'ost', lambda e, mt=mt, cb=cb: e.dma_start(out=memk_o[mt * 128:(mt + 1) * 128, cb * 256:(cb + 1) * 256],
                                                                                   in_=yf[:, 0:256]), R=['yf'])
                            chk('M4')
                            pt_, ptk = nextT()
                            for hh in range(2):
                                T.op('pe', lambda e, hh=hh, pt_=pt_: e.transpose(out=pt_[:, hh * 128:(hh + 1) * 128], in_=ob[:, hh * 128:(hh + 1) * 128],
                                                                                 identity=identb[:]), R=['ob', 'identb'], W=[ptk])
                            T.op('act', lambda e, cb=cb, mt=mt, pt_=pt_: e.copy(out=memKT_b[:, cb * 2:cb * 2 + 2, mt * 128:(mt + 1) * 128],
                                                                                in_=pt_[:, 0:256].rearrange("p (a b) -> p a b", a=2)),
                                 R=[ptk, 'memKT_b'], W=['memKT_b'])
                            chk('M5')
                            if cb == 1 and mt == 1:
                                chk('M6')
            T.barrier()

            chk('M')
            ssq = sb("ssq", [128, GT, 8])
            qf = xt[:, 0:1536].rearrange("p (h c) -> p h c", h=8); qs_b = sb("qs_b", [128, 8, 192], BF16)
            wk6 = sb("wk6", [128, 1536])
            qsq = wk6[:, :].rearrange("p (h c) -> p h c", h=8)
            ymT8 = wk6[:, 0:1024].rearrange("p (h c) -> p h c", h=8)
            ymT4 = wk6[:, 0:4 * GC].rearrange("p (h c) -> p h c", h=4)
            cq_b = sb("cq_b", [128, 512], BF16); cqT = sb("cqT", [128, 4, 128], BF16)
            qabsT = sb("qabsT", [128, 2, 8, 128], BF16)
            qav = qabsT[:].rearrange("p a h c -> p a (h c)")
            PT = sb("PT", [128, GC], BF16)
            OTn = sb("OTn", [128, 2, GC], BF16)
            rcp = sb("rcp", [128, GC])
            xres = sb("xres", [128, 256]); yout = sb("yout", [128, 256])
            idx_b = sb("idx_b", [128, NPG], I32); idxf = sb("idxf", [128, NPG])
            iot = sb("iot", [128, 1], I32); iotf = sb("iotf", [128, 1])
            pgc = [sb("pgc%d" % i, [128, 256]) for i in range(2)]
            pgk = [sb("pgk%d" % i, [128, 64]) for i in range(2)]
            pgcb2 = [sb("pgcb%d" % i, [128, 257], BF16) for i in range(2)]; pgkb2 = [sb("pgkb%d" % i, [128, 64], BF16) for i in range(2)]
            pgT2 = [sb("pgT%d" % i, [128, 2, 128], BF16) for i in range(2)]; pgkT2 = [sb("pgkT%d" % i, [64, 128], BF16) for i in range(2)]
            rkp2 = [sb("rkp%d" % i, [128, 8]) for i in range(2)]; scf2 = [sb("scf%d" % i, [128, 64]) for i in range(2)]
            PTs2 = [sb("PTs%d" % i, [128, 64], BF16) for i in range(2)]
            mini = sb("mini", [8, 257], BF16)
            for i_ in range(2):
                T.op('dve', lambda e, i_=i_: e.memset(pgcb2[i_][:, 256:257], 1.0), W=['pgcb%d' % i_])
            T.op('dve', lambda e: e.memset(mini[:, 256:257], 1.0), W=['mini'])
            mkb = c1flat[:, 0:1024].rearrange("p (t c) -> p t c", t=2); mvb = c1flat[:, 1024:2048].rearrange("p (t c) -> p t c", t=2)
            mkT_s = c1flat[:, 2048:3072].rearrange("p (h c) -> p h c", h=4)
            mcf = [xt[:, 0:1024].rearrange("p (t c) -> p t c", t=2), xt[:, 1024:2048].rearrange("p (t c) -> p t c", t=2)]

            def in_proj_block(ng, c0, ncols, consume):
                stream_wblock(w_in, c0, ncols, gcol_in)
                for ti in range(ng):
                    p, pk = nextA()
                    for kc in range(16):
                        T.op('pe', lambda e, kc=kc, ti=ti, p=p: e.matmul(p[:, 0:ncols], lhsT=xnT_g[:, ti, kc, :], rhs=wblk[:, kc, 0:ncols],
                                                                         start=(kc == 0), stop=(kc == 15)), R=['xnT_g', 'wblk'], W=[pk])
                    consume(ti, p, pk)

            def mem_attend(ncols, qcols, KT, KTk, V, Vk, out_fn):
                for h in range(4):
                    acc, acck = psC[0], 'psC0'
                    lb, lbk = psC[1], 'psC1'
                    for kt in range(2):
                        p, pk = nextA()
                        T.op('pe', lambda e, kt=kt, h=h, p=p: e.matmul(p[:, 0:ncols], lhsT=KT[:, h, kt * 128:(kt + 1) * 128],
                                                                       rhs=QmT[:, h, qcols], start=True, stop=True), R=[KTk, 'QmT'], W=[pk])
                        T.op('act', lambda e, p=p: e.activation(out=PT[:, 0:ncols], in_=p[:, 0:ncols], func=AF.Exp), R=[pk], W=['PT'])
                        T.op('pe', lambda e, kt=kt, h=h: e.matmul(acc[:, 0:ncols], lhsT=V[:, kt, h * 128:(h + 1) * 128], rhs=PT[:, 0:ncols],
                                                                  start=(kt == 0), stop=(kt == 1)), R=[Vk, 'PT'], W=[acck])
                        T.op('pe', lambda e, kt=kt: e.matmul(lb[:, 0:ncols], lhsT=onesb[:], rhs=PT[:, 0:ncols],
                                                             start=(kt == 0), stop=(kt == 1)), R=['onesb', 'PT'], W=[lbk])
                    T.op('dve', lambda e: e.reciprocal(out=rcp[:, 0:ncols], in_=lb[:, 0:ncols]), R=[lbk], W=['rcp'])
                    T.op('dve', lambda e, h=h: e.tensor_tensor(out=out_fn(h), in0=acc[:, 0:ncols], in1=rcp[:, 0:ncols], op=ALU.mult),
                         R=[acck, 'rcp', 'wk6'], W=['wk6'])

            def finish_branch(ti, src_ap, skey, n, gate0):
                rstd_of(src_ap, n, sm[:, 4:5], skey)
                T.op('dve', lambda e: e.scalar_tensor_tensor(out=gates[:, ti, gate0:gate0 + n], in0=src_ap, scalar=sm[:, 4:5],
                                                             in1=gates[:, ti, gate0:gate0 + n], op0=ALU.mult, op1=ALU.mult),
                     R=[skey, 'sm', 'gates'], W=['gates'])

            def mem_branch_finish(ng, get_cols):
                for ti in range(ng):
                    p, pk = nextA()
                    for h in range(4):
                        T.op('pe', lambda e, h=h, ti=ti, p=p: e.transpose(out=p[:, h * 128:(h + 1) * 128], in_=get_cols(h, ti), identity=identf[:]),
                             R=['wk6', 'identf'], W=[pk])
                    T.op('act', lambda e, p=p: e.copy(out=yf[:], in_=p[:, 0:512]), R=[pk], W=['yf'])
                    finish_branch(ti, yf[:], 'yf', 512, 1536)

            def q_path(ti, row0):
                T.dma('sp', 'rld', lambda e: e.dma_start(out=ropet[:], in_=rope[row0:row0 + 128, :]), W=['ropet'])
                rstd_of(yf[:], 512, sm[:, 5:6], 'yf')
                T.op('dve', lambda e: e.tensor_scalar(out=cq_b[:], in0=yf[:], scalar1=sm[:, 5:6], scalar2=None, op0=ALU.mult),
                     R=['yf', 'sm'], W=['cq_b'])
                p, pk = nextT()
                for k in range(4):
                    T.op('pe', lambda e, k=k, p=p: e.transpose(out=p[:, k * 128:(k + 1) * 128], in_=cq_b[:, k * 128:(k + 1) * 128],
                                                               identity=identb[:]), R=['cq_b', 'identb'], W=[pk])
                T.op('dve', lambda e, p=p: e.tensor_copy(out=cqT[:].rearrange("p a b -> p (a b)"), in_=p[:, 0:512]), R=[pk], W=['cqT'])
                for blk in range(3):
                    pq, pqk = nextA()
                    for k in range(4):
                        T.op('pe', lambda e, k=k, blk=blk, pq=pq: e.matmul(pq[:, 0:512], lhsT=cqT[:, k, :],
                                                                          rhs=w_uq_b[:, k, blk * 512:(blk + 1) * 512],
                                                                          start=(k == 0), stop=(k == 3)), R=['cqT', 'w_uq_b'], W=[pqk])
                    T.op('act', lambda e, blk=blk, pq=pq: e.copy(out=qf[:].rearrange("p h c -> p (h c)")[:, blk * 512:(blk + 1) * 512],
                                                                 in_=pq[:, 0:512]), R=[pqk, 'xt'], W=['xt'])
                cc = ropet[:, 0:64].unsqueeze(1).to_broadcast([128, 8, 64])
                ns = ropet[:, 64:96].unsqueeze(1).to_broadcast([128, 8, 32])
                ps_ = ropet[:, 96:128].unsqueeze(1).to_broadcast([128, 8, 32])
                T.op('dve', lambda e: e.tensor_tensor(out=qsq[:, :, 0:32], in0=qf[:, :, 160:192], in1=ns, op=ALU.mult), R=['xt', 'ropet'], W=['wk6'])
                T.op('dve', lambda e: e.tensor_tensor(out=qsq[:, :, 32:64], in0=qf[:, :, 128:160], in1=ps_, op=ALU.mult), R=['xt', 'ropet', 'wk6'], W=['wk6'])
                T.op('dve', lambda e: e.tensor_tensor(out=qf[:, :, 128:192], in0=qf[:, :, 128:192], in1=cc, op=ALU.mult), R=['xt', 'ropet', 'wk6'], W=['xt'])
                T.op('dve', lambda e: e.tensor_tensor(out=qf[:, :, 128:192], in0=qf[:, :, 128:192], in1=qsq[:, :, 0:64], op=ALU.add), R=['xt', 'wk6'], W=['xt'])
                T.op('pool', lambda e: e.tensor_tensor(out=qsq[:], in0=qf[:], in1=qf[:], op=ALU.mult), R=['xt', 'wk6'], W=['wk6'])
                T.op('dve', lambda e: e.tensor_reduce(out=sm[:, 24:32], in_=qsq[:], axis=AX.X, op=ALU.add), R=['wk6', 'sm'], W=['sm'])
                T.op('act', lambda e: e.activation(out=sm[:, 24:32], in_=sm[:, 24:32], func=AF.Sqrt, scale=1.0 / 192, bias=EPS), R=['sm'], W=['sm'])
                T.op('dve', lambda e: e.reciprocal(out=sm[:, 24:32], in_=sm[:, 24:32]), R=['sm'], W=['sm'])
                T.op('dve', lambda e: e.tensor_tensor(out=qf[:], in0=qf[:], in1=sm[:, 24:32].unsqueeze(2).to_broadcast([128, 8, 192]), op=ALU.mult),
                     R=['xt', 'sm'], W=['xt'])
                T.op('dve', lambda e: e.tensor_tensor(out=qs_b[:], in0=qf[:], in1=gqk_bc[:].unsqueeze(1).to_broadcast([128, 8, 192]), op=ALU.mult),
                     R=['xt', 'gqk_bc'], W=['qs_b'])
                for h in range(8):
                    p, pk = nextT()
                    T.op('pe', lambda e, h=h, p=p: e.transpose(out=p[:, 0:128], in_=qs_b[:, h, 0:128], identity=identb[:]), R=['qs_b', 'identb'], W=[pk])
                    T.op('pe', lambda e, h=h, p=p: e.transpose(out=p[0:64, 128:256], in_=qs_b[:, h, 128:192], identity=identb[:]), R=['qs_b', 'identb'], W=[pk])
                    T.op('act', lambda e, h=h, p=p: e.copy(out=QTn[:, h, ti * 128:(ti + 1) * 128], in_=p[:, 0:128]), R=[pk, 'QTn'], W=['QTn'])
                    T.op('dve', lambda e, h=h, p=p: e.tensor_copy(out=QTr[0:64, h, ti * 128:(ti + 1) * 128], in_=p[0:64, 128:256]), R=[pk, 'QTr'], W=['QTr'])

            def prompt_attention(ng, ncg, own0):
                for h in range(8):
                    for kc in range(2):
                        p, pk = nextA()
                        T.op('pe', lambda e, kc=kc, h=h, p=p: e.matmul(p[:, 0:ncg], lhsT=w_ukT_b[:, h, kc * 128:(kc + 1) * 128],
                                                                       rhs=QTn[:, h, 0:ncg], start=True, stop=True), R=['w_ukT_b', 'QTn'], W=[pk])
                        T.op('act', lambda e, kc=kc, p=p: e.copy(out=qav[:, kc, 0:ncg], in_=p[:, 0:ncg]), R=[pk, 'qabsT'], W=['qabsT'])
                    nkt = NPREV + own0 + ng
                    for kt in range(nkt):
                        rel = kt - (NPREV + own0)
                        c0 = max(rel, 0) * 128
                        ncol = ncg - c0
                        kcols = slice(kt * 128, (kt + 1) * 128)
                        p, pk = nextA()
                        for kc in range(2):
                            T.op('pe', lambda e, kc=kc, p=p, kcols=kcols, c0=c0, ncol=ncol: e.matmul(
                                p[:, 0:ncol], lhsT=ckvT_all[:, kc, kcols], rhs=qav[:, kc, c0:ncg], start=(kc == 0), stop=False),
                                R=['ckvT_all', 'qabsT'], W=[pk])
                        T.op('pe', lambda e, h=h, p=p, kcols=kcols, c0=c0, ncol=ncol: e.matmul(
                            p[:, 0:ncol], lhsT=krT_all[0:64, kcols], rhs=QTr[0:64, h, c0:ncg], start=False, stop=True),
                            R=['krT_all', 'QTr'], W=[pk])
                        T.op('act', lambda e, kt=kt, h=h, p=p, c0=c0, ncol=ncol: e.activation(
                            out=PT[:, c0:ncg], in_=p[:, 0:ncol], func=AF.Exp, scale=rk_all[:, kt, h:h + 1], bias=kbias[:, kt:kt + 1]),
                            R=[pk, 'rk_all', 'kbias'], W=['PT'])
                        if rel >= 0:
                            T.op('pool', lambda e, c0=c0: e.tensor_tensor(out=PT[:, c0:c0 + 128], in0=PT[:, c0:c0 + 128], in1=trib[:], op=ALU.mult),
                                 R=['PT', 'trib'], W=['PT'])
                        first, last = (kt == 0), (kt == nkt - 1)
                        for kc in range(2):
                            T.op('pe', lambda e, kc=kc, kt=kt, c0=c0, first=first, last=last: e.matmul(
                                psC[kc][:, c0:ncg], lhsT=ckv1_all[:, kt, kc * 128:(kc + 1) * 128], rhs=PT[:, c0:ncg], start=first, stop=last),
                                R=['ckv1_all', 'PT'], W=['psC%d' % kc])
                        T.op('pe', lambda e, c0=c0, first=first, last=last: e.matmul(psC[2][:, c0:ncg], lhsT=onesb[:], rhs=PT[:, c0:ncg],
                                                                                     start=first, stop=last),
                             R=['onesb', 'PT'], W=['psC2'])
                    T.op('dve', lambda e: e.reciprocal(out=rcp[:, 0:ncg], in_=psC[2][:, 0:ncg]), R=['psC2'], W=['rcp'])
                    for kc in range(2):
                        T.op('dve', lambda e, kc=kc: e.tensor_tensor(out=OTn[:, kc, 0:ncg], in0=psC[kc][:, 0:ncg], in1=rcp[:, 0:ncg], op=ALU.mult),
                             R=['psC%d' % kc, 'rcp', 'OTn'], W=['OTn'])
                    for ti in range(ng):
                        p, pk = nextA()
                        for kc in range(2):
                            T.op('pe', lambda e, kc=kc, ti=ti, h=h, p=p: e.matmul(p[:, 0:128], lhsT=OTn[:, kc, ti * 128:(ti + 1) * 128],
                                                                                 rhs=w_uv_b[:, kc, h * 128:(h + 1) * 128],
                                                                                 start=(kc == 0), stop=(kc == 1)), R=['OTn', 'w_uv_b'], W=[pk])
                        T.op('dve', lambda e, ti=ti, h=h, p=p: e.tensor_copy(out=ymla[:, ti, h * 128:(h + 1) * 128], in_=p[:, 0:128]),
                             R=[pk, 'ymla'], W=['ymla'])
                        T.op('act', lambda e, ti=ti, h=h, p=p: e.activation(out=junk[:, 0:128], in_=p[:, 0:128], func=AF.Square,
                                                                            accum_out=ssq[:, ti, h:h + 1]), R=[pk, 'ssq'], W=['junk', 'ssq'])
                mem_attend(ncg, slice(0, ncg), memKT_b, 'memKT_b', memV_b, 'memV_b', lambda h: ymT4[:, h, 0:ncg])
                mem_branch_finish(ng, lambda h, ti: ymT4[:, h, ti * 128:(ti + 1) * 128])

            def sample_attention():
                T.op('pool', lambda e: e.iota(iot[:], pattern=[[0, 1]], base=0, channel_multiplier=1), W=['iot'])
                T.op('dve', lambda e: e.tensor_copy(out=iotf[:], in_=iot[:]), R=['iot'], W=['iotf'])
                for h in range(8):
                    for kc in range(2):
                        p, pk = nextA()
                        T.op('pe', lambda e, kc=kc, h=h, p=p: e.matmul(p[:, 0:128], lhsT=w_ukT_b[:, h, kc * 128:(kc + 1) * 128],
                                                                       rhs=QTn[:, h, 0:128], start=True, stop=True), R=['w_ukT_b', 'QTn'], W=[pk])
                        T.op('act', lambda e, kc=kc, h=h, p=p: e.copy(out=qabsT[:, kc, h, :], in_=p[:, 0:128]), R=[pk, 'qabsT'], W=['qabsT'])
                acc = psC[0]
                for b in range(NSEQ):
                    bc = slice(b * 8, (b + 1) * 8)
                    T.dma('sp', 'c0', lambda e, b=b: e.dma_start(out=idx_b[:], in_=ptab[b].partition_broadcast(128)), W=['idx_b'])
                    T.op('dve', lambda e: e.tensor_copy(out=idxf[:], in_=idx_b[:]), R=['idx_b'], W=['idxf'])
                    T.op('dve', lambda e: e.tensor_scalar(out=idxf[:], in0=idxf[:], scalar1=128.0, scalar2=iotf[:, 0:1], op0=ALU.mult, op1=ALU.add),
                         R=['idxf', 'iotf'], W=['idxf'])
                    T.op('dve', lambda e: e.tensor_copy(out=idx_b[:], in_=idxf[:]), R=['idxf'], W=['idx_b'])

                    def attend(sl_, cT, cTk, krT_ap, krTk, c1, c1k, rk_ap, rkk, nk, first, last, mask):
                        p, pk = psC[1 + sl_], 'psC%d' % (1 + sl_)
                        scf, scfk = scf2[sl_], 'scf%d' % sl_
                        PTs, PTk = PTs2[sl_], 'PTs%d' % sl_
                        pv = p[0:nk, 0:64]
                        for kc in range(2):
                            T.op('pe', lambda e, kc=kc: e.matmul(pv, lhsT=cT[:, kc, :], rhs=qabsT[:, kc, :, bc], start=(kc == 0), stop=False),
                                 R=[cTk, 'qabsT'], W=[pk])
                        T.op('pe', lambda e: e.matmul(pv, lhsT=krT_ap, rhs=QTr[0:64, :, bc], start=False, stop=True), R=[krTk, 'QTr'], W=[pk])
                        T.op('dve', lambda e: e.tensor_tensor(out=scf[0:nk, :].rearrange("p (h q) -> p h q", q=8),
                                                              in0=pv.rearrange("p (h q) -> p h q", q=8),
                                                              in1=rk_ap.unsqueeze(2).to_broadcast([nk, 8, 8]), op=ALU.mult),
                             R=[pk, rkk], W=[scfk])
                        T.op('act', lambda e: e.activation(out=PTs[0:nk, 0:64], in_=scf[0:nk, :], func=AF.Exp), R=[scfk], W=[PTk])
                        if mask:
                            T.op('dve', lambda e: e.tensor_tensor(out=PTs[0:nk, 0:64].rearrange("p (h q) -> p h q", q=8),
                                                                  in0=PTs[0:nk, 0:64].rearrange("p (h q) -> p h q", q=8),
                                                                  in1=trib[0:nk, 0:8].unsqueeze(1).to_broadcast([nk, 8, 8]), op=ALU.mult),
                                 R=[PTk, 'trib'], W=[PTk])
                        T.op('pe', lambda e: e.matmul(acc[0:64, 0:257], lhsT=PTs[0:nk, 0:64], rhs=c1[0:nk, 0:257], start=first, stop=last),
                             R=[c1k, PTk], W=['psC0'])

                    for pg in range(NPG):
                        s = pg % 2
                        pgcb, pgkb, pgT, pgkT, rkp = pgcb2[s], pgkb2[s], pgT2[s], pgkT2[s], rkp2[s]
                        T.dma('pool', 'pgc%d' % s, lambda e, s=s, pg=pg: e.indirect_dma_start(
                            out=pgc[s][:], out_offset=None, in_=cckv[:, :],
                            in_offset=bass.IndirectOffsetOnAxis(ap=idx_b[:, pg:pg + 1], axis=0)), R=['idx_b'], W=['pgc%d' % s])
                        T.dma('pool', 'pgk%d' % s, lambda e, s=s, pg=pg: e.indirect_dma_start(
                            out=pgk[s][:], out_offset=None, in_=ckr[:, :],
                            in_offset=bass.IndirectOffsetOnAxis(ap=idx_b[:, pg:pg + 1], axis=0)), R=['idx_b'], W=['pgk%d' % s])
                        T.op('dve', lambda e, s=s, pgcb=pgcb: e.tensor_copy(out=pgcb[:, 0:256], in_=pgc[s][:]), R=['pgc%d' % s, 'pgcb%d' % s], W=['pgcb%d' % s])
                        T.op('dve', lambda e, s=s, pgkb=pgkb: e.tensor_copy(out=pgkb[:], in_=pgk[s][:]), R=['pgk%d' % s], W=['pgkb%d' % s])
                        p, pk = nextT()
                        for kc in range(2):
                            T.op('pe', lambda e, kc=kc, p=p, pgcb=pgcb: e.transpose(out=p[:, kc * 128:(kc + 1) * 128], in_=pgcb[:, kc * 128:(kc + 1) * 128],
                                                                                   identity=identb[:]), R=['pgcb%d' % s, 'identb'], W=[pk])
                        T.op('pe', lambda e, p=p, pgkb=pgkb: e.transpose(out=p[0:64, 256:384], in_=pgkb[:], identity=identb[:]), R=['pgkb%d' % s, 'identb'], W=[pk])
                        T.op('dve', lambda e, p=p, pgT=pgT: e.tensor_copy(out=pgT[:], in_=p[:, 0:256].rearrange("p (a b) -> p a b", a=2)), R=[pk], W=['pgT%d' % s])
                        T.op('dve', lambda e, p=p, pgkT=pgkT: e.tensor_copy(out=pgkT[:], in_=p[0:64, 256:384]), R=[pk], W=['pgkT%d' % s])
                        key_norms(pgT[:], 'pgT%d' % s, pgk[s][:], 'pgk%d' % s, 128, rkp[:], 'rkp%d' % s)
                        attend(s, pgT[:], 'pgT%d' % s, pgkT[:], 'pgkT%d' % s, pgcb, 'pgcb%d' % s, rkp[:], 'rkp%d' % s, 128, pg == 0, False, False)
                    s = NPG % 2
                    rkp = rkp2[s]; scfm = scf2[1 - s]
                    p, pk = nextT()
                    for kc in range(2):
                        T.op('pe', lambda e, kc=kc, p=p: e.transpose(out=p[0:8, kc * 128:(kc + 1) * 128], in_=ckvT_s[:, kc, bc], identity=identb[:]),
                             R=['ckvT_s', 'identb'], W=[pk])
                    T.op('pe', lambda e, p=p: e.transpose(out=p[0:8, 256:320], in_=krT_s[0:64, bc], identity=identb[0:64, 0:64]),
                         R=['krT_s', 'identb'], W=[pk])
                    T.op('act', lambda e, p=p: e.copy(out=mini[:, 0:256], in_=p[0:8, 0:256]), R=[pk, 'mini'], W=['mini'])
                    T.op('dve', lambda e, p=p: e.tensor_copy(out=scfm[0:8, 0:64], in_=p[0:8, 256:320]), R=[pk], W=['scf%d' % (1 - s)])
                    key_norms(ckvT_s[:, :, bc], 'ckvT_s', scfm[0:8, 0:64], 'scf%d' % (1 - s), 8, rkp[0:8, :], 'rkp%d' % s)
                    attend(s, ckvT_s[:, :, bc], 'ckvT_s', krT_s[0:64, bc], 'krT_s', mini, 'mini', rkp[0:8, :], 'rkp%d' % s, 8, NPG == 0, True, True)
                    T.op('dve', lambda e: e.reciprocal(out=rcp[0:64, 0:1], in_=acc[0:64, 256:257]), R=['psC0'], W=['rcp'])
                    T.op('dve', lambda e: e.tensor_scalar(out=ob[0:64, 0:256], in0=acc[0:64, 0:256], scalar1=rcp[0:64, 0:1], scalar2=None,
                                                          op0=ALU.mult), R=['psC0', 'rcp'], W=['ob'])
                    p, pk = nextT()
                    for kc in range(2):
                        T.op('pe', lambda e, kc=kc, p=p: e.transpose(out=p[:, kc * 64:(kc + 1) * 64], in_=ob[0:64, kc * 128:(kc + 1) * 128],
                                                                     identity=identb[0:64, 0:64]), R=['ob', 'identb'], W=[pk])
                    T.op('act', lambda e, p=p: e.copy(out=OTn[:, :, 0:64], in_=p[:, 0:128].rearrange("p (a b) -> p a b", a=2)), R=[pk, 'OTn'], W=['OTn'])
                    p, pk = nextA()
                    for h in range(8):
                        for kc in range(2):
                            T.op('pe', lambda e, kc=kc, h=h, p=p: e.matmul(p[:, h * 8:(h + 1) * 8], lhsT=w_uv_b[:, kc, h * 128:(h + 1) * 128],
                                                                           rhs=OTn[:, kc, h * 8:(h + 1) * 8], start=(kc == 0), stop=(kc == 1)),
                                 R=['w_uv_b', 'OTn'], W=[pk])
                    T.op('act', lambda e, p=p: e.copy(out=ymT8[:, :, bc], in_=p[:, 0:64].rearrange("p (h q) -> p h q", q=8)), R=[pk, 'wk6'], W=['wk6'])
                for h in range(8):
                    p, pk = nextA()
                    T.op('pe', lambda e, h=h, p=p: e.transpose(out=p[:, 0:128], in_=ymT8[:, h, :], identity=identf[:]), R=['wk6', 'identf'], W=[pk])
                    T.op('dve', lambda e, h=h, p=p: e.tensor_copy(out=ymla[:, 0, h * 128:(h + 1) * 128], in_=p[:, 0:128]), R=[pk, 'ymla'], W=['ymla'])
                    T.op('act', lambda e, h=h, p=p: e.activation(out=junk[:, 0:128], in_=p[:, 0:128], func=AF.Square, accum_out=ssq[:, 0, h:h + 1]),
                         R=[pk, 'ssq'], W=['junk', 'ssq'])
                for b in range(NSEQ):
                    bc = slice(b * 8, (b + 1) * 8)
                    T.dma('sp', 'mck', lambda e, b=b: e.dma_start(out=mcf[0], in_=cmk[b * 256:(b + 1) * 256, :].rearrange("(t p) c -> p t c", p=128)),
                          W=['xt'])
                    T.dma('sp', 'mcv', lambda e, b=b: e.dma_start(out=mcf[1], in_=cmv[b * 256:(b + 1) * 256, :].rearrange("(t p) c -> p t c", p=128)),
                          W=['xt'])
                    T.op('pool', lambda e: e.tensor_copy(out=mkb[:], in_=mcf[0]), R=['xt'], W=['mkb'])
                    T.op('pool', lambda e: e.tensor_copy(out=mvb[:], in_=mcf[1]), R=['xt'], W=['mvb'])
                    for kt in range(2):
                        p, pk = nextT()
                        for h in range(4):
                            T.op('pe', lambda e, kt=kt, h=h, p=p: e.transpose(out=p[:, h * 128:(h + 1) * 128], in_=mkb[:, kt, h * 128:(h + 1) * 128],
                                                                              identity=identb[:]), R=['mkb', 'identb'], W=[pk])
                        T.op('act', lambda e, kt=kt, p=p: e.copy(out=mkT_s[:, :, kt * 128:(kt + 1) * 128],
                                                                in_=p[:, 0:512].rearrange("p (a b) -> p a b", a=4)), R=[pk, 'mkT_s'], W=['mkT_s'])
                    mem_attend(8, bc, mkT_s, 'mkT_s', mvb, 'mvb', lambda h, bc=bc: ymT4[:, h, bc])
                mem_branch_finish(1, lambda h, ti: ymT4[:, h, 0:128])

            def run_group(tiles, is_sample):
                ng = len(tiles)
                ncg = ng * 128
                for ti, (row0, _) in enumerate(tiles):
                    norm_transpose(xall[row0:row0 + 128, :],
                                   lambda half, ti=ti: xnT_g[:, ti, half * 8:(half + 1) * 8, :].rearrange("p a b -> p (a b)"), 'xnT_g')

                def c_u(half):
                    def f(ti, p, pk):
                        T.op('act', lambda e: e.copy(out=u_bg[:, ti, half * 256:(half + 1) * 256], in_=p[:, 0:256]), R=[pk, 'u_bg'], W=['u_bg'])
                    return f
                in_proj_block(ng, C_U, 256, c_u(0))
                in_proj_block(ng, C_U + 256, 256, c_u(1))

                def c_gate(goff):
                    def f(ti, p, pk):
                        T.op('act', lambda e: e.activation(out=gates[:, ti, goff:goff + 256], in_=p[:, 0:256], func=AF.Silu),
                             R=[pk, 'gates'], W=['gates'])
                    return f
                in_proj_block(ng, C_GS, 256, c_gate(0))
                in_proj_block(ng, C_GS + 256, 256, c_gate(256))
                chk('G0')
                for ti, (row0, own) in enumerate(tiles):
                    u_transpose(u_bg[:, ti, :], 'u_bg')
                    py, pyk = psC[2], 'psC2'
                    for gi in range(16 // GS):
                        if is_sample:
                            ssm_step(lambda ct: uT[:, ct, :], 128, NSEQ, 8, 0, True, gis=[gi])
                        else:
                            for sub in range(128 // LS):
                                ssm_step(lambda ct, sub=sub: uT[:, ct, sub * LS:(sub + 1) * LS], LS, 1, LS, sub * LS, True, gis=[gi])
                        for j in range(GS):
                            i = gi * GS + j
                            T.op('pe', lambda e, i=i, j=j: e.matmul(py[:, i * 32:(i + 1) * 32], lhsT=h_b[:, j, 0, :], rhs=CT_b[:, i, 0, :],
                                                                    start=True, stop=False), R=['h_b', 'CT_b'], W=[pyk])
                            T.op('pe', lambda e, i=i, j=j: e.matmul(py[:, i * 32:(i + 1) * 32], lhsT=h_b[:, j, 1, :], rhs=CT_b[:, i, 1, :],
                                                                    start=False, stop=False), R=['h_b', 'CT_b'], W=[pyk])
                            ct, off = i // 4, (i % 4) * 32
                            T.op('pe', lambda e, i=i, ct=ct, off=off: e.matmul(py[:, i * 32:(i + 1) * 32], lhsT=uT[:, ct, :],
                                                                              rhs=diagD_b[:, ct, off:off + 32], start=False, stop=True),
                                 R=['uT', 'diagD_b'], W=[pyk])
                    T.op('act', lambda e: e.copy(out=yf[:], in_=py[:, 0:512]), R=[pyk], W=['yf'])
                    T.op('pool', lambda e: e.tensor_tensor(out=yg[:], in0=yf[:], in1=yf[:], op=ALU.mult), R=['yf'], W=['yg'])
                    T.op('pool', lambda e: e.tensor_scalar(out=yg[:], in0=yg[:], scalar1=0.044715, scalar2=1.0, op0=ALU.mult, op1=ALU.add),
                         R=['yg'], W=['yg'])
                    T.op('pool', lambda e: e.tensor_tensor(out=yg[:], in0=yg[:], in1=yf[:], op=ALU.mult), R=['yg', 'yf'], W=['yg'])
                    T.op('act', lambda e: e.activation(out=yg[:], in_=yg[:], func=AF.Sigmoid, scale=GELU_K), R=['yg'], W=['yg'])
                    T.op('dve', lambda e: e.tensor_tensor(out=yg[:], in0=yg[:], in1=yf[:], op=ALU.mult), R=['yg', 'yf'], W=['yg'])
                    T.op('dve', lambda e: e.tensor_copy(out=ob[:, 0:512], in_=yg[:]), R=['yg'], W=['ob'])
                    p, pk = nextT()
                    for k in range(4):
                        T.op('pe', lambda e, k=k, p=p: e.transpose(out=p[:, k * 128:(k + 1) * 128], in_=ob[:, k * 128:(k + 1) * 128],
                                                                   identity=identb[:]), R=['ob', 'identb'], W=[pk])
                    T.op('act', lambda e, p=p: e.copy(out=cqT[:].rearrange("p a b -> p (a b)"), in_=p[:, 0:512]), R=[pk], W=['cqT'])
                    pz, pzk = nextA()
                    for k in range(4):
                        T.op('pe', lambda e, k=k, pz=pz: e.matmul(pz[:, 0:512], lhsT=cqT[:, k, :], rhs=glu_w_b[:, k, :], start=(k == 0), stop=False),
                             R=['cqT', 'glu_w_b'], W=[pzk])
                    T.op('pe', lambda e, pz=pz: e.matmul(pz[:, 0:512], lhsT=onesb[0:1, :], rhs=glub_b[0:1, :], start=False, stop=True),
                         R=['onesb', 'glub_b'], W=[pzk])
                    T.op('act', lambda e, pz=pz: e.activation(out=yf[:], in_=pz[:, 0:512], func=AF.Sigmoid), R=[pzk, 'yf'], W=['yf'])
                    T.op('dve', lambda e: e.tensor_tensor(out=yf[:], in0=yf[:], in1=yg[:], op=ALU.mult), R=['yf', 'yg'], W=['yf'])
                    finish_branch(ti, yf[:], 'yf', 512, 0)

                chk('G1')
                lat_ps = []
                stream_wblock(w_in, C_CKV, 256, gcol_in)
                for ti in range(ng):
                    p, pk = psC[ti], 'psC%d' % ti
                    for kc in range(16):
                        T.op('pe', lambda e, kc=kc, ti=ti, p=p: e.matmul(p[:, 0:256], lhsT=xnT_g[:, ti, kc, :], rhs=wblk[:, kc, 0:256],
                                                                         start=(kc == 0), stop=(kc == 15)), R=['xnT_g', 'wblk'], W=[pk])
                    lat_ps.append((p, pk))
                stream_wblock(w_in, C_KR, 64, gcol_in)
                for ti, (row0, own) in enumerate(tiles):
                    p, pk = lat_ps[ti]
                    for kc in range(16):
                        T.op('pe', lambda e, kc=kc, ti=ti, p=p: e.matmul(p[:, 256:320], lhsT=xnT_g[:, ti, kc, :], rhs=wblk[:, kc, 0:64],
                                                                         start=(kc == 0), stop=(kc == 15)), R=['xnT_g', 'wblk', pk], W=[pk])
                    if is_sample:
                        latent_post(p, pk, row0, ckv_s, kr_s, 0, ckvT_s[:], 'ckvT_s', krT_s[0:64, :], 'krT_s', None, None, None)
                    else:
                        kt = NPREV + own
                        latent_post(p, pk, row0, ckv_p, kr_p, own * 128, ckvT_all[:, :, kt * 128:(kt + 1) * 128], 'ckvT_all',
                                    krT_all[0:64, kt * 128:(kt + 1) * 128], 'krT_all', ckv1_all[:, kt, :], 'ckv1_all', rk_all[:, kt, :])

                chk('G2')
                cq_ps = []
                stream_wblock(w_in, C_CQ, 256, gcol_in)
                for ti in range(ng):
                    p, pk = psC[ti], 'psC%d' % ti
                    for kc in range(16):
                        T.op('pe', lambda e, kc=kc, ti=ti, p=p: e.matmul(p[:, 0:256], lhsT=xnT_g[:, ti, kc, :], rhs=wblk[:, kc, 0:256],
                                                                         start=(kc == 0), stop=(kc == 15)), R=['xnT_g', 'wblk'], W=[pk])
                    cq_ps.append((p, pk))
                stream_wblock(w_in, C_CQ + 256, 256, gcol_in)
                for ti, (row0, own) in enumerate(tiles):
                    p, pk = cq_ps[ti]
                    for kc in range(16):
                        T.op('pe', lambda e, kc=kc, ti=ti, p=p: e.matmul(p[:, 256:512], lhsT=xnT_g[:, ti, kc, :], rhs=wblk[:, kc, 0:256],
                                                                         start=(kc == 0), stop=(kc == 15)), R=['xnT_g', 'wblk', pk], W=[pk])
                    T.op('act', lambda e, p=p: e.copy(out=yf[:], in_=p[:, 0:512]), R=[pk], W=['yf'])
                    q_path(ti, row0)

                chk('G3')
                for b4 in range(4):
                    in_proj_block(ng, C_GM + b4 * 256, 256, c_gate(512 + b4 * 256))

                def c_qm(half):
                    def f(ti, p, pk):
                        for hh in range(2):
                            h = half * 2 + hh
                            rstd_of(p[:, hh * 128:(hh + 1) * 128], 128, sm[:, 6:7], pk)
                            T.op('dve', lambda e, hh=hh, h=h: e.scalar_tensor_tensor(out=cq_b[:, h * 128:(h + 1) * 128],
                                                                                     in0=p[:, hh * 128:(hh + 1) * 128], scalar=sm[:, 6:7],
                                                                                     in1=gmq_bc[:], op0=ALU.mult, op1=ALU.mult),
                                 R=[pk, 'sm', 'gmq_bc', 'cq_b'], W=['cq_b'])
                        pt_, ptk = nextT()
                        for hh in range(2):
                            h = half * 2 + hh
                            T.op('pe', lambda e, hh=hh, h=h, pt_=pt_: e.transpose(out=pt_[:, hh * 128:(hh + 1) * 128], in_=cq_b[:, h * 128:(h + 1) * 128],
                                                                                  identity=identb[:]), R=['cq_b', 'identb'], W=[ptk])
                        T.op('act', lambda e, pt_=pt_: e.copy(out=QmT[:, half * 2:half * 2 + 2, ti * 128:(ti + 1) * 128],
                                                              in_=pt_[:, 0:256].rearrange("p (a b) -> p a b", a=2)), R=[ptk, 'QmT'], W=['QmT'])
                    return f
                in_proj_block(ng, C_QM, 256, c_qm(0))
                in_proj_block(ng, C_QM + 256, 256, c_qm(1))
                in_proj_block(ng, C_GME, 256, c_gate(1536))
                in_proj_block(ng, C_GME + 256, 256, c_gate(1792))

                chk('G4')
                T.op('dve', lambda e: e.memset(ssq[:], 0.0), W=['ssq'])
                if not is_sample:
                    prompt_attention(ng, ncg, tiles[0][1])
                else:
                    sample_attention()

                chk('G5')
                for ti in range(ng):
                    T.op('dve', lambda e, ti=ti: e.tensor_reduce(out=sm[:, 7:8], in_=ssq[:, ti, 0:8], axis=AX.X, op=ALU.add), R=['ssq', 'sm'], W=['sm'])
                    T.op('act', lambda e: e.activation(out=sm[:, 7:8], in_=sm[:, 7:8], func=AF.Sqrt, scale=1.0 / 1024, bias=EPS), R=['sm'], W=['sm'])
                    T.op('dve', lambda e: e.reciprocal(out=sm[:, 7:8], in_=sm[:, 7:8]), R=['sm'], W=['sm'])
                    T.op('dve', lambda e, ti=ti: e.scalar_tensor_tensor(out=gates[:, ti, 512:1536], in0=ymla[:, ti, :], scalar=sm[:, 7:8],
                                                                        in1=gates[:, ti, 512:1536], op0=ALU.mult, op1=ALU.mult),
                         R=['ymla', 'sm', 'gates'], W=['gates'])
                    for half in range(2):
                        p, pk = nextT()
                        for k in range(8):
                            kc = half * 8 + k
                            T.op('pe', lambda e, k=k, kc=kc, p=p, ti=ti: e.transpose(out=p[:, k * 128:(k + 1) * 128],
                                                                                     in_=gates[:, ti, kc * 128:(kc + 1) * 128], identity=identb[:]),
                                 R=['gates', 'identb'], W=[pk])
                        T.op('act', lambda e, p=p, half=half, ti=ti: e.copy(out=xnT_g[:, ti, half * 8:(half + 1) * 8, :].rearrange("p a b -> p (a b)"),
                                                                           in_=p[:, :]), R=[pk, 'xnT_g'], W=['xnT_g'])
                for cb in range(8):
                    stream_wblock(w_out_d, cb * 256, 256, gcol_out)
                    for ti, (row0, own) in enumerate(tiles):
                        T.dma('sp', 'xres', lambda e, row0=row0, cb=cb: e.dma_start(out=xres[:], in_=xall[row0:row0 + 128, cb * 256:(cb + 1) * 256]),
                              W=['xres'])
                        p, pk = nextA()
                        for kc in range(16):
                            T.op('pe', lambda e, kc=kc, ti=ti, p=p: e.matmul(p[:, 0:256], lhsT=xnT_g[:, ti, kc, :], rhs=wblk[:, kc, 0:256],
                                                                             start=(kc == 0), stop=(kc == 15)), R=['xnT_g', 'wblk'], W=[pk])
                        T.op('dve', lambda e, p=p: e.tensor_tensor(out=yout[:], in0=p[:, 0:256], in1=xres[:], op=ALU.add), R=[pk, 'xres'], W=['yout'])
                        dst = y_s if is_sample else y_p
                        r0 = 0 if is_sample else own * 128
                        T.dma('sp', 'ost', lambda e, dst=dst, r0=r0, cb=cb: e.dma_start(out=dst[r0:r0 + 128, cb * 256:(cb + 1) * 256], in_=yout[:]),
                              R=['yout'])

            own_tiles = [((NPREV + o) * 128, o) for o in range(NOWN)]
            for g0 in range(0, NOWN, GT):
                run_group(own_tiles[g0:g0 + GT], False)
            stf = sb("stf", [16, 128])
            for src, dst in ((hre, sp_re), (him, sp_im)):
                transpose_f32(stf[:], src[:, :, 0], 128, 16, 'stf', 'hst')
                T.dma('sp', 'ost', lambda e, dst=dst: e.dma_start(out=dst[:, :], in_=stf[:]), R=['stf'])
            chk('O')
            T.barrier()
            stin = xt[0:NSEQ, :]
            for src_d, dstt in ((st_re, hre), (st_im, him)):
                T.dma('sp', 'c0', lambda e, src_d=src_d: e.dma_start(out=stin, in_=src_d[:, :]), W=['xt'])
                for i in range(16):
                    transpose_f32(dstt[:, i, :], stin[:, i * 128:(i + 1) * 128], NSEQ, 128, 'hst', 'xt')
            run_group([(NKT * 128, None)], True)
            for src, dst in ((hre, ss_re), (him, ss_im)):
                for i in range(16):
                    transpose_f32(stin[:, i * 128:(i + 1) * 128], src[:, i, :], 128, NSEQ, 'xt', 'hst')
                T.dma('sp', 'ost', lambda e, dst=dst: e.dma_start(out=dst[:, :], in_=stin), R=['xt'])
        except _Stop:
            pass
        T.finish('sp')
        print("[kernel] instructions emitted:", T.nins, "sbuf left:", nc.sbuf_bytes_remaining, {k: v for k, v in T.cnt.items() if k in ("pe", "act", "dve", "pool")}, flush=True)
    return nc


def rope_table(pos):
    half = 32
    inv = (10000.0 ** (-np.arange(half, dtype=np.float32) / half)).astype(np.float32)
    ang = pos.astype(np.float32)[:, None] * inv[None, :]
    c, s = np.cos(ang).astype(np.float32), np.sin(ang).astype(np.float32)
    return np.concatenate([c, c, -s, s], axis=1).astype(np.float32)


WNAMES = ['norm_g', 'w_in', 'ssm_a_re', 'ssm_a_im', 'ssm_log_dt', 'ssm_b_re', 'ssm_b_im', 'ssm_c_re', 'ssm_c_im', 'ssm_d',
          'ssm_glu_w', 'ssm_glu_b', 'mla_q_norm_g', 'mla_w_uq', 'mla_kv_norm_g', 'mla_w_ukv', 'mla_qk_norm_q', 'mla_qk_norm_k',
          'mem_norm_g', 'mem_w_k', 'mem_w_v', 'mem_qk_norm_q', 'mem_qk_norm_k', 'out_norm_ssm', 'out_norm_mla', 'out_norm_mem',
          'w_out']


def make_in_maps(inp, SEQ, NPG, NPOOL, PAST):
    CH = SEQ // 4
    NOWN = CH // 128
    NPREV = 3 * NOWN
    NKT = NPREV + NOWN
    f32 = np.float32
    xp = np.asarray(inp['x_prompt'], f32); xs = np.asarray(inp['x_sample'], f32)
    ident = np.eye(128, dtype=f32)
    tri = np.triu(np.ones((128, 128), f32))
    smask = np.ones((128, 256), f32); smask[:, 0] = 0.0
    smask[:, 128::8] = 0.0
    tau = np.tile(np.arange(1, LS + 1, dtype=f32)[None, :], (128, 1))
    cckv = np.ascontiguousarray(np.asarray(inp['cache_ckv'], f32).reshape(NPOOL * 128, 256))
    ckr = np.ascontiguousarray(np.asarray(inp['cache_krope'], f32).reshape(NPOOL * 128, 64))
    wts = {n: np.ascontiguousarray(np.asarray(inp[n], f32)) for n in WNAMES}
    maps = []
    for c in range(8):
        s, j = c // 4, c % 4
        xall = np.zeros(((NKT + 1) * 128, D), f32)
        pos = np.zeros((NKT + 1) * 128, f32)
        kb = np.zeros((128, NKT), f32)
        for k in range(3):
            src = j - 3 + k
            r0 = k * CH
            if src >= 0:
                xall[r0:r0 + CH] = xp[s, src * CH:(src + 1) * CH]
                pos[r0:r0 + CH] = np.arange(src * CH, (src + 1) * CH)
            else:
                kb[:, k * NOWN:(k + 1) * NOWN] = NEG
        xall[3 * CH:4 * CH] = xp[s, j * CH:(j + 1) * CH]
        pos[3 * CH:4 * CH] = np.arange(j * CH, (j + 1) * CH)
        xall[4 * CH:] = xs[c * NSEQ:(c + 1) * NSEQ].reshape(128, D)
        pos[4 * CH:] = np.tile(PAST + np.arange(8), NSEQ)
        m = {
            'xall': xall, 'rope': rope_table(pos), 'kbias': kb,
            'mem': np.ascontiguousarray(np.asarray(inp['mem_prompt'], f32)[s]),
            'cckv': cckv, 'ckr': ckr,
            'cmk': np.ascontiguousarray(np.asarray(inp['cache_mem_k'], f32)[c * NSEQ:(c + 1) * NSEQ].reshape(NSEQ * 256, 512)),
            'cmv': np.ascontiguousarray(np.asarray(inp['cache_mem_v'], f32)[c * NSEQ:(c + 1) * NSEQ].reshape(NSEQ * 256, 512)),
            'st_re': np.ascontiguousarray(np.asarray(inp['state_ssm_re'], f32)[c * NSEQ:(c + 1) * NSEQ].reshape(NSEQ, 2048)),
            'st_im': np.ascontiguousarray(np.asarray(inp['state_ssm_im'], f32)[c * NSEQ:(c + 1) * NSEQ].reshape(NSEQ, 2048)),
            'ptab': np.ascontiguousarray(np.asarray(inp['page_table'], np.int32)[c * NSEQ:(c + 1) * NSEQ]),
            'ident': ident, 'tri': tri, 'smask': smask, 'tau': tau,
        }
        m.update(wts)
        maps.append(m)
    return maps


def assemble(r, SEQ):
    y_prompt = np.stack([np.concatenate([r[s * 4 + j]['y_p'] for j in range(4)], 0) for s in range(2)])
    ckv_p = np.stack([np.concatenate([r[s * 4 + j]['ckv_p'] for j in range(4)], 0) for s in range(2)])
    kr_p = np.stack([np.concatenate([r[s * 4 + j]['kr_p'] for j in range(4)], 0) for s in range(2)])
    y_sample = np.concatenate([r[c]['y_s'].reshape(NSEQ, 8, D) for c in range(8)], 0)
    ckv_s = np.concatenate([r[c]['ckv_s'].reshape(NSEQ, 8, 256) for c in range(8)], 0)
    kr_s = np.concatenate([r[c]['kr_s'].reshape(NSEQ, 8, 64) for c in range(8)], 0)
    memk = np.stack([r[s * 4]['memk'].reshape(256, 4, 128) for s in range(2)])
    memv = np.stack([r[s * 4]['memv'].reshape(256, 4, 128) for s in range(2)])
    sp_re = np.stack([r[s * 4 + 3]['sp_re'].reshape(32, 64) for s in range(2)])
    sp_im = np.stack([r[s * 4 + 3]['sp_im'].reshape(32, 64) for s in range(2)])
    ss_re = np.concatenate([r[c]['ss_re'].reshape(NSEQ, 32, 64) for c in range(8)], 0)
    ss_im = np.concatenate([r[c]['ss_im'].reshape(NSEQ, 32, 64) for c in range(8)], 0)
    outs = (y_prompt, y_sample, ckv_p, kr_p, ckv_s, kr_s, memk, memv, sp_re, sp_im, ss_re, ss_im)
    return tuple(np.ascontiguousarray(o, dtype=np.float32) for o in outs)


def run(inp, SEQ, PAST, stop=None, cores=None):
    NPG = PAST // 128
    NPOOL = int(np.asarray(inp['cache_ckv']).shape[0])
    nc = build(SEQ, NPG, NPOOL, stop)
    maps = make_in_maps(inp, SEQ, NPG, NPOOL, PAST)
    if cores is not None:
        res = run_bass_kernel_spmd(nc, [maps[c] for c in cores], core_ids=list(range(len(cores))))
        return {c: res.results[i] for i, c in enumerate(cores)}
    res = run_bass_kernel_spmd(nc, maps, core_ids=list(range(8)))
    return assemble(res.results, SEQ)


def kernel(**inputs):
    return run(inputs, 4096, 8192)
```
